# Optimizing a Trainium2 kernel written in Bass

```python
import math
import jax, jax.numpy as jnp
from jax import lax
import numpy as np

D_MODEL = 1024
BATCH = 2
SEQ = 8192
DEPTH = 4
DEC_BATCH = 128
DEC_SEQ = 4
PAST_LEN = 8192
PAGE_SIZE = 128

N_MIXERS = 3
N_MLA = (DEPTH + 2) // 3
N_RWKV = (DEPTH + 1) // 3
N_S5 = DEPTH // 3
N_META = 16
D_FF = 2816
ALPHA = (2 * DEPTH) ** 0.25
BETA = (8 * DEPTH) ** -0.25
LN_EPS = 1e-5
RMS_EPS = 1e-6
MLA_HEADS = 16
Q_RANK = 768
KV_RANK = 256
NOPE_DIM = 64
ROPE_DIM = 32
QK_DIM = NOPE_DIM + ROPE_DIM
V_DIM = 64
ROPE_THETA = 10000.0
Q_BLOCK = 128
RW_HEAD = 64
RW_HEADS = D_MODEL // RW_HEAD
DECAY_LORA = 64
AAA_LORA = 64
GATE_LORA = 128
GN_EPS = 64e-5
S5_GROUP = 16
S5_GROUPS = D_MODEL // S5_GROUP
S5_STATE = 64

kernel_name = 'hybrid_mla_rwkv7_s5_macaron_step'


def layer_norm(x, g, b):
    xf = x.astype(jnp.float32)
    mu = jnp.mean(xf, -1, keepdims=True)
    var = jnp.mean(jnp.square(xf - mu), -1, keepdims=True)
    return ((xf - mu) * lax.rsqrt(var + LN_EPS) * g + b).astype(x.dtype)


def rms_norm(x, g):
    xf = x.astype(jnp.float32)
    return (xf * lax.rsqrt(jnp.mean(jnp.square(xf), -1, keepdims=True) + RMS_EPS) * g).astype(x.dtype)


def swiglu(x, w1, w3, w2):
    return (jax.nn.silu(x @ w1) * (x @ w3)) @ w2


def rope(x, pos):
    half = ROPE_DIM // 2
    inv = 1.0 / (ROPE_THETA ** (jnp.arange(half, dtype=jnp.float32) * (2.0 / ROPE_DIM)))
    ang = pos.astype(jnp.float32)[:, None] * inv
    ang = ang.reshape(ang.shape[:1] + (1,) * (x.ndim - 3) + (half,))
    c, s = jnp.cos(ang), jnp.sin(ang)
    xf = x.astype(jnp.float32)
    x1, x2 = xf[..., :half], xf[..., half:]
    return jnp.concatenate([x1 * c - x2 * s, x1 * s + x2 * c], -1).astype(x.dtype)


def causal_block_attention(q, k, v, n_lead, scale):
    B, L, H, dk = q.shape
    kpos = jnp.arange(L)

    def attend(qb, qpos):
        s = jnp.einsum('bqhd,bkhd->bhqk', qb, k).astype(jnp.float32) * scale
        s = jnp.where(kpos[None, :] <= qpos[:, None], s, -jnp.inf)
        p = jax.nn.softmax(s, axis=-1).astype(v.dtype)
        return jnp.einsum('bhqk,bkhv->bqhv', p, v)

    lead = attend(q[:, :n_lead], jnp.arange(n_lead))
    nb = (L - n_lead) // Q_BLOCK
    qr = q[:, n_lead:].reshape(B, nb, Q_BLOCK, H, dk).transpose(1, 0, 2, 3, 4)
    qpos = n_lead + jnp.arange(nb * Q_BLOCK).reshape(nb, Q_BLOCK)
    rest = lax.map(lambda a: attend(a[0], a[1]), (qr, qpos))
    rest = rest.transpose(1, 0, 2, 3, 4).reshape(B, nb * Q_BLOCK, H, v.shape[-1])
    return jnp.concatenate([lead, rest], axis=1)


def mla_mixer(x, pos, m, paged, p):
    B, T, _ = x.shape
    cq = rms_norm(x @ p['mla_w_dq'][m], p['mla_q_norm'][m])
    q = jnp.einsum('btq,qhd->bthd', cq, p['mla_w_uq'][m])
    q_nope, q_pe = q[..., :NOPE_DIM], rope(q[..., NOPE_DIM:], pos)
    ckv = x @ p['mla_w_dkv'][m]
    c = rms_norm(ckv[..., :KV_RANK], p['mla_kv_norm'][m])
    k_pe = rope(ckv[..., KV_RANK:], pos)
    w_uk, w_uv = p['mla_w_uk'][m], p['mla_w_uv'][m]
    scale = QK_DIM ** -0.5
    if paged is None:
        k_nope = jnp.einsum('btr,rhn->bthn', c, w_uk)
        v = jnp.einsum('btr,rhv->bthv', c, w_uv)
        qf = jnp.concatenate([q_nope, q_pe], -1)
        kf = jnp.concatenate([k_nope, jnp.broadcast_to(k_pe[:, :, None, :], (B, T, MLA_HEADS, ROPE_DIM))], -1)
        o = causal_block_attention(qf, kf, v, N_META, scale)
    else:
        lat_pool, kr_pool, page_table = paged
        past = page_table.shape[1] * PAGE_SIZE
        c_all = jnp.concatenate([lat_pool[m, page_table].reshape(B, past, KV_RANK).astype(c.dtype), c], 1)
        kr_all = jnp.concatenate([kr_pool[m, page_table].reshape(B, past, ROPE_DIM).astype(k_pe.dtype), k_pe], 1)
        q_lat = jnp.einsum('bthn,rhn->bthr', q_nope, w_uk)
        s = (jnp.einsum('bthr,bsr->bhts', q_lat, c_all)
             + jnp.einsum('bthd,bsd->bhts', q_pe, kr_all)).astype(jnp.float32) * scale
        kpos = jnp.arange(past + T)
        qpos = past + jnp.arange(T)
        s = jnp.where(kpos[None, :] <= qpos[:, None], s, -jnp.inf)
        pr = jax.nn.softmax(s, axis=-1).astype(c_all.dtype)
        o_lat = jnp.einsum('bhts,bsr->bthr', pr, c_all)
        o = jnp.einsum('bthr,rhv->bthv', o_lat, w_uv)
    return jnp.einsum('bthv,hvd->btd', o, p['mla_w_o'][m]), c, k_pe


def rwkv7_mixer(x, shift0, wkv0, m, p):
    B, T, D = x.shape
    f32 = jnp.float32
    xf = x.astype(f32)
    xprev = jnp.concatenate([shift0.astype(f32)[:, None], xf[:, :-1]], axis=1)
    xx = xprev - xf
    mu = p['rw_mu'][m]
    xr, xw, xk, xv, xa, xg = (xf + xx * mu[j] for j in range(6))
    r = xr @ p['rw_wr'][m]
    k = xk @ p['rw_wk'][m]
    v = xv @ p['rw_wv'][m]
    w = -jax.nn.softplus(-(p['rw_w0'][m] + jnp.tanh(xw @ p['rw_w1'][m]) @ p['rw_w2'][m])) - 0.5
    decay = jnp.exp(-jnp.exp(w.astype(f32)))
    a = jax.nn.sigmoid(p['rw_a0'][m] + (xa @ p['rw_a1'][m]) @ p['rw_a2'][m])
    g = jax.nn.sigmoid(xg @ p['rw_g1'][m]) @ p['rw_g2'][m]
    heads = lambda z: z.astype(f32).reshape(B, T, RW_HEADS, RW_HEAD)
    kk = heads(k * p['rw_k_k'][m])
    kk = kk * lax.rsqrt(jnp.maximum(jnp.sum(kk * kk, -1, keepdims=True), 1e-24))
    k = k * (1.0 + (a - 1.0) * p['rw_k_a'][m])
    r_h, k_h, v_h, w_h, a_h = heads(r), heads(k), heads(v), heads(decay), heads(a)

    def step(S, inp):
        r_t, w_t, k_t, v_t, kk_t, a_t = inp
        sa = jnp.einsum('bhvk,bhk->bhv', S, -kk_t)
        S = (S * w_t[:, :, None, :] + sa[..., None] * (kk_t * a_t)[:, :, None, :]
             + v_t[..., None] * k_t[:, :, None, :])
        return S, jnp.einsum('bhvk,bhk->bhv', S, r_t)

    tm = lambda z: jnp.swapaxes(z, 0, 1)
    wkv_T, ys = lax.scan(step, wkv0.astype(f32), tuple(tm(z) for z in (r_h, w_h, k_h, v_h, kk, a_h)))
    y = tm(ys)
    mean = jnp.mean(y, -1, keepdims=True)
    var = jnp.mean(jnp.square(y - mean), -1, keepdims=True)
    y = ((y - mean) * lax.rsqrt(var + GN_EPS)).reshape(B, T, D) * p['rw_lnx_g'][m] + p['rw_lnx_b'][m]
    bonus = jnp.sum(r_h * k_h * p['rw_r_k'][m], -1, keepdims=True) * v_h
    y = y + bonus.reshape(B, T, D)
    out = (y * g) @ p['rw_wo'][m]
    return out.astype(x.dtype), wkv_T, x[:, -1]


def cmul(ar, ai, br, bi):
    return ar * br - ai * bi, ar * bi + ai * br


def s5_combine(e1, e2):
    a1r, a1i, b1r, b1i = e1
    a2r, a2i, b2r, b2i = e2
    ar, ai = cmul(a2r, a2i, a1r, a1i)
    br, bi = cmul(a2r, a2i, b1r, b1i)
    return (ar, ai, br + b2r, bi + b2i)


def s5_mixer(x, h0_re, h0_im, m, p):
    B, T, D = x.shape
    f32 = jnp.float32
    u = x.astype(f32).reshape(B, T, S5_GROUPS, S5_GROUP)
    lre = p['s5_lam_re'][m].astype(f32)
    lim = p['s5_lam_im'][m].astype(f32)
    dt = jnp.exp(p['s5_log_dt'][m].astype(f32))[:, None]
    mag = jnp.exp(lre * dt)
    ab_re, ab_im = mag * jnp.cos(lim * dt), mag * jnp.sin(lim * dt)
    den = lre * lre + lim * lim
    nr = ab_re - 1.0
    co_re = (nr * lre + ab_im * lim) / den
    co_im = (ab_im * lre - nr * lim) / den
    bb_re, bb_im = cmul(co_re[..., None], co_im[..., None], p['s5_b_re'][m], p['s5_b_im'][m])
    bu_re = jnp.einsum('btgs,gps->btgp', u, bb_re)
    bu_im = jnp.einsum('btgs,gps->btgp', u, bb_im)
    a_re = jnp.broadcast_to(ab_re, (1, T) + ab_re.shape)
    a_im = jnp.broadcast_to(ab_im, (1, T) + ab_im.shape)
    acr, aci, bcr, bci = lax.associative_scan(s5_combine, (a_re, a_im, bu_re, bu_im), axis=1)
    hr, hi = cmul(acr, aci, h0_re.astype(f32)[:, None], h0_im.astype(f32)[:, None])
    hr, hi = hr + bcr, hi + bci
    y = (jnp.einsum('btgp,gsp->btgs', hr, p['s5_c_re'][m]) - jnp.einsum('btgp,gsp->btgs', hi, p['s5_c_im'][m])
         + p['s5_d'][m].reshape(S5_GROUPS, S5_GROUP) * u).reshape(B, T, D)
    z = jax.nn.gelu(y)
    out = (z @ p['s5_wv'][m]) * jax.nn.sigmoid(z @ p['s5_wg'][m])
    return out.astype(x.dtype), hr[:, -1], hi[:, -1]


def run_group(x, start, paged, rw_wkv0, rw_shift0, s5_re0, s5_im0, p):
    T = x.shape[1]
    pos = start + jnp.arange(T)
    lat_rows, kr_rows, wkv_out, shift_out, re_out, im_out = [], [], [], [], [], []
    for i in range(DEPTH):
        kind, m = i % N_MIXERS, i // N_MIXERS
        x = layer_norm(ALPHA * x + 0.5 * swiglu(x, p['ffn_w1'][i, 0], p['ffn_w3'][i, 0], p['ffn_w2'][i, 0]),
                       p['ln_g'][i, 0], p['ln_b'][i, 0])
        if kind == 0:
            h, c, k_pe = mla_mixer(x, pos, m, paged, p)
            lat_rows.append(c)
            kr_rows.append(k_pe)
        elif kind == 1:
            h, wkv, sh = rwkv7_mixer(x, rw_shift0[m], rw_wkv0[m], m, p)
            wkv_out.append(wkv)
            shift_out.append(sh)
        else:
            h, sr, si = s5_mixer(x, s5_re0[m], s5_im0[m], m, p)
            re_out.append(sr)
            im_out.append(si)
        x = layer_norm(ALPHA * x + h, p['ln_g'][i, 1], p['ln_b'][i, 1])
        x = layer_norm(ALPHA * x + 0.5 * swiglu(x, p['ffn_w1'][i, 1], p['ffn_w3'][i, 1], p['ffn_w2'][i, 1]),
                       p['ln_g'][i, 2], p['ln_b'][i, 2])
    return (x, jnp.stack(lat_rows), jnp.stack(kr_rows), jnp.stack(wkv_out), jnp.stack(shift_out),
            jnp.stack(re_out), jnp.stack(im_out))


def setup_inputs(seed: int = 0) -> dict:
    key = jax.random.key(seed)
    keys = iter(jax.random.split(key, 96))

    def nrm(shape, scale=1.0):
        return jax.random.normal(next(keys), shape, jnp.float32) * scale

    def unif(shape, lo, hi):
        return jax.random.uniform(next(keys), shape, jnp.float32, lo, hi)

    def gain(shape):
        return 1.0 + nrm(shape, 0.02)

    D = D_MODEL
    n_pages = PAST_LEN // PAGE_SIZE
    n_pool = (DEC_BATCH * n_pages * 5) // 4
    page_table = jax.random.permutation(next(keys), n_pool)[: DEC_BATCH * n_pages]
    page_table = page_table.reshape(DEC_BATCH, n_pages).astype(jnp.int32)
    lam_im0 = jnp.pi * jnp.arange(S5_STATE, dtype=jnp.float32)
    return {
        'x_prompt': nrm((BATCH, SEQ, D)),
        'x_sample': nrm((DEC_BATCH, DEC_SEQ, D)),
        'cache_mla_latent': nrm((N_MLA, n_pool, PAGE_SIZE, KV_RANK)),
        'cache_mla_krope': nrm((N_MLA, n_pool, PAGE_SIZE, ROPE_DIM)),
        'state_rwkv_wkv': nrm((N_RWKV, DEC_BATCH, RW_HEADS, RW_HEAD, RW_HEAD), 0.5),
        'state_rwkv_shift': nrm((N_RWKV, DEC_BATCH, D)),
        'state_s5_re': nrm((N_S5, DEC_BATCH, S5_GROUPS, S5_STATE), 0.1),
        'state_s5_im': nrm((N_S5, DEC_BATCH, S5_GROUPS, S5_STATE), 0.1),
        'page_table': page_table,
        'meta_tokens': nrm((N_META, D)),
        'ln_g': gain((DEPTH, 3, D)),
        'ln_b': nrm((DEPTH, 3, D), 0.02),
        'ffn_w1': nrm((DEPTH, 2, D, D_FF), D ** -0.5),
        'ffn_w3': nrm((DEPTH, 2, D, D_FF), D ** -0.5),
        'ffn_w2': nrm((DEPTH, 2, D_FF, D), BETA * D_FF ** -0.5),
        'mla_w_dq': nrm((N_MLA, D, Q_RANK), D ** -0.5),
        'mla_q_norm': gain((N_MLA, Q_RANK)),
        'mla_w_uq': nrm((N_MLA, Q_RANK, MLA_HEADS, QK_DIM), Q_RANK ** -0.5),
        'mla_w_dkv': nrm((N_MLA, D, KV_RANK + ROPE_DIM), D ** -0.5),
        'mla_kv_norm': gain((N_MLA, KV_RANK)),
        'mla_w_uk': nrm((N_MLA, KV_RANK, MLA_HEADS, NOPE_DIM), KV_RANK ** -0.5),
        'mla_w_uv': nrm((N_MLA, KV_RANK, MLA_HEADS, V_DIM), KV_RANK ** -0.5),
        'mla_w_o': nrm((N_MLA, MLA_HEADS, V_DIM, D), BETA * (MLA_HEADS * V_DIM) ** -0.5),
        'rw_mu': unif((N_RWKV, 6, D), 0.0, 1.0),
        'rw_wr': nrm((N_RWKV, D, D), D ** -0.5),
        'rw_wk': nrm((N_RWKV, D, D), D ** -0.5),
        'rw_wv': nrm((N_RWKV, D, D), D ** -0.5),
        'rw_w0': unif((N_RWKV, D), -6.5, -1.5),
        'rw_w1': nrm((N_RWKV, D, DECAY_LORA), D ** -0.5),
        'rw_w2': nrm((N_RWKV, DECAY_LORA, D), 0.1 * DECAY_LORA ** -0.5),
        'rw_a0': nrm((N_RWKV, D), 0.1),
        'rw_a1': nrm((N_RWKV, D, AAA_LORA), D ** -0.5),
        'rw_a2': nrm((N_RWKV, AAA_LORA, D), 0.1 * AAA_LORA ** -0.5),
        'rw_g1': nrm((N_RWKV, D, GATE_LORA), D ** -0.5),
        'rw_g2': nrm((N_RWKV, GATE_LORA, D), GATE_LORA ** -0.5),
        'rw_k_k': 0.85 + nrm((N_RWKV, D), 0.02),
        'rw_k_a': 1.0 + nrm((N_RWKV, D), 0.02),
        'rw_r_k': nrm((N_RWKV, RW_HEADS, RW_HEAD), 0.1),
        'rw_lnx_g': gain((N_RWKV, D)),
        'rw_lnx_b': nrm((N_RWKV, D), 0.02),
        'rw_wo': nrm((N_RWKV, D, D), BETA * D ** -0.5),
        's5_lam_re': -0.5 + nrm((N_S5, S5_GROUPS, S5_STATE), 0.01),
        's5_lam_im': lam_im0 + nrm((N_S5, S5_GROUPS, S5_STATE), 0.01),
        's5_log_dt': unif((N_S5, S5_GROUPS), math.log(1e-3), math.log(1e-1)),
        's5_b_re': nrm((N_S5, S5_GROUPS, S5_STATE, S5_GROUP), (2 * S5_GROUP) ** -0.5),
        's5_b_im': nrm((N_S5, S5_GROUPS, S5_STATE, S5_GROUP), (2 * S5_GROUP) ** -0.5),
        's5_c_re': nrm((N_S5, S5_GROUPS, S5_GROUP, S5_STATE), S5_STATE ** -0.5),
        's5_c_im': nrm((N_S5, S5_GROUPS, S5_GROUP, S5_STATE), S5_STATE ** -0.5),
        's5_d': nrm((N_S5, D)),
        's5_wv': nrm((N_S5, D, D), BETA * D ** -0.5),
        's5_wg': nrm((N_S5, D, D), D ** -0.5),
    }


def reference(x_prompt, x_sample, cache_mla_latent, cache_mla_krope, state_rwkv_wkv, state_rwkv_shift,
              state_s5_re, state_s5_im, page_table, meta_tokens, ln_g, ln_b, ffn_w1, ffn_w3, ffn_w2,
              mla_w_dq, mla_q_norm, mla_w_uq, mla_w_dkv, mla_kv_norm, mla_w_uk, mla_w_uv, mla_w_o,
              rw_mu, rw_wr, rw_wk, rw_wv, rw_w0, rw_w1, rw_w2, rw_a0, rw_a1, rw_a2, rw_g1, rw_g2,
              rw_k_k, rw_k_a, rw_r_k, rw_lnx_g, rw_lnx_b, rw_wo,
              s5_lam_re, s5_lam_im, s5_log_dt, s5_b_re, s5_b_im, s5_c_re, s5_c_im, s5_d, s5_wv, s5_wg):
    p = dict(ln_g=ln_g, ln_b=ln_b, ffn_w1=ffn_w1, ffn_w3=ffn_w3, ffn_w2=ffn_w2,
             mla_w_dq=mla_w_dq, mla_q_norm=mla_q_norm, mla_w_uq=mla_w_uq, mla_w_dkv=mla_w_dkv,
             mla_kv_norm=mla_kv_norm, mla_w_uk=mla_w_uk, mla_w_uv=mla_w_uv, mla_w_o=mla_w_o,
             rw_mu=rw_mu, rw_wr=rw_wr, rw_wk=rw_wk, rw_wv=rw_wv, rw_w0=rw_w0, rw_w1=rw_w1, rw_w2=rw_w2,
             rw_a0=rw_a0, rw_a1=rw_a1, rw_a2=rw_a2, rw_g1=rw_g1, rw_g2=rw_g2, rw_k_k=rw_k_k, rw_k_a=rw_k_a,
             rw_r_k=rw_r_k, rw_lnx_g=rw_lnx_g, rw_lnx_b=rw_lnx_b, rw_wo=rw_wo,
             s5_lam_re=s5_lam_re, s5_lam_im=s5_lam_im, s5_log_dt=s5_log_dt, s5_b_re=s5_b_re, s5_b_im=s5_b_im,
             s5_c_re=s5_c_re, s5_c_im=s5_c_im, s5_d=s5_d, s5_wv=s5_wv, s5_wg=s5_wg)
    B = x_prompt.shape[0]
    meta = jnp.broadcast_to(meta_tokens.astype(x_prompt.dtype)[None], (B, N_META, D_MODEL))
    x0 = jnp.concatenate([meta, x_prompt], axis=1)
    zw = jnp.zeros((N_RWKV, B, RW_HEADS, RW_HEAD, RW_HEAD), jnp.float32)
    zs = jnp.zeros((N_RWKV, B, D_MODEL), jnp.float32)
    zc = jnp.zeros((N_S5, B, S5_GROUPS, S5_STATE), jnp.float32)
    yp, lat_p, kr_p, wkv_p, sh_p, re_p, im_p = run_group(x0, 0, None, zw, zs, zc, zc, p)
    y_prompt = yp[:, N_META:]
    past_len = page_table.shape[1] * PAGE_SIZE
    y_sample, lat_s, kr_s, wkv_s, sh_s, re_s, im_s = run_group(
        x_sample, past_len, (cache_mla_latent, cache_mla_krope, page_table),
        state_rwkv_wkv, state_rwkv_shift, state_s5_re, state_s5_im, p)
    return (y_prompt, y_sample, lat_p, kr_p, lat_s, kr_s, wkv_p, sh_p, wkv_s, sh_s, re_p, im_p, re_s, im_s)
```

```python
from contextlib import ExitStack
import math
import numpy as np
import concourse.bass as bass
import concourse.mybir as mybir
from concourse.bass_utils import run_bass_kernel_spmd

F32 = mybir.dt.float32
BF16 = mybir.dt.bfloat16
I32 = mybir.dt.int32
AF = mybir.ActivationFunctionType
ALU = mybir.AluOpType
AX = mybir.AxisListType


DEBUG_BARRIER = False
DEBUG_TILES = None


class T:
    __slots__ = ("t", "wr", "rd", "name")

    def __init__(self, t, name=""):
        self.t = t
        self.wr = {}
        self.rd = {}
        self.name = name

    def __getitem__(self, k):
        return self.t[k]


class Sync:
    NDMA = 24

    def __init__(self, nc, es):
        self.nc = nc
        self.es = es
        self.eng = {"pe": nc.tensor, "act": nc.scalar, "dve": nc.vector, "pool": nc.gpsimd, "sp": nc.sync}
        self.sems = {}
        self.cnt = {}
        self.waited = {}
        for e in ("pe", "act", "dve", "pool"):
            self._mk("c_" + e)
        self.dma_rr = {}
        for q in ("sp", "pool", "act"):
            self.dma_rr[q] = 0
            for i in range(self.NDMA):
                self._mk("d_%s%d" % (q, i))
        self.n_ins = 0

    def _mk(self, name):
        self.sems[name] = self.es.enter_context(self.nc.semaphore(name))
        self.cnt[name] = 0

    def _wait(self, en, evs):
        own = "c_" + en
        e = self.eng[en]
        for s, v in evs.items():
            if s == own and en == "pe":
                continue
            if self.waited.get((en, s), 0) >= v:
                continue
            e.wait_ge(self.sems[s], v)
            self.waited[(en, s)] = v

    def op(self, en, fn, reads=(), writes=(), dma=False):
        evs = {}
        for t in reads:
            for s, v in t.wr.items():
                if evs.get(s, 0) < v:
                    evs[s] = v
        for t in writes:
            for d in (t.wr, t.rd):
                for s, v in d.items():
                    if evs.get(s, 0) < v:
                        evs[s] = v
        if dma:
            e = self.eng[en]
            for s, v in evs.items():
                if self.waited.get((en, s), 0) >= v:
                    continue
                e.wait_ge(self.sems[s], v)
                self.waited[(en, s)] = v
            i = self.dma_rr[en]
            self.dma_rr[en] = (i + 1) % self.NDMA
            sname = "d_%s%d" % (en, i)
            inc = 16
        else:
            self._wait(en, evs)
            sname = "c_" + en
            inc = 1
        ins = fn(self.eng[en])
        self.cnt[sname] += inc
        ins.then_inc(self.sems[sname], inc)
        v = self.cnt[sname]
        for t in writes:
            t.wr[sname] = v
        for t in reads:
            t.rd[sname] = v
        self.n_ins += 1
        return ins

    def barrier(self):
        snap = dict(self.cnt)
        for en in ("pe", "act", "dve", "pool", "sp"):
            self._wait(en, snap)

    def finish(self):
        self.barrier()


D = 1024
DFF = 2816
NH = 16
QR = 768
KVR = 256
NOPE = 64
ROPE = 32
QK = 96
VD = 64
LN_EPS = 1e-5
RMS_EPS = 1e-6
GN_EPS = 64e-5
N_META = 16
PAGE = 128
DEC_SEQ = 4


class Cfg:
    def __init__(self, ntf=64, nsq=16, npages=64, npool=10240, depth=4, n_cores=8, n_batch=2):
        self.ntf = ntf
        self.L = 128 * ntf + N_META
        self.seq = 128 * ntf
        self.ntp = ntf + 1
        self.nsq = nsq
        self.srows = nsq * DEC_SEQ
        self.nt = self.ntp + 1
        self.npages = npages
        self.past = npages * PAGE
        self.npool = npool
        self.depth = depth
        self.alpha = (2 * depth) ** 0.25
        self.n_mla = (depth + 2) // 3
        self.n_rwkv = (depth + 1) // 3
        self.n_s5 = depth // 3
        self.n_cores = n_cores
        self.n_batch = n_batch

    def rows(self, j):
        if j < self.ntf:
            return 128
        if j == self.ntf:
            return N_META
        return self.srows


class Builder:
    def __init__(self, nc, cfg):
        self.nc = nc
        self.cfg = cfg
        self.es = ExitStack()
        self.sy = Sync(nc, self.es)
        self.din = {}
        self.dout = {}
        self.uid = 0

    def sb(self, name, shape, dt=F32, st=None):
        self.uid += 1
        return T((st or self.es).enter_context(self.nc.sbuf_tensor("%s_%d" % (name, self.uid), list(shape), dt)), name)

    def dram_in(self, name, shape, dt=F32):
        t = T(self.nc.dram_tensor(name, list(shape), dt, kind="ExternalInput").ap(), name)
        self.din[name] = t
        return t

    def dram_out(self, name, shape, dt=F32):
        t = T(self.nc.dram_tensor(name, list(shape), dt, kind="ExternalOutput").ap(), name)
        self.dout[name] = t
        return t

    def dram_scr(self, name, shape, dt=F32):
        return T(self.nc.dram_tensor(name, list(shape), dt, kind="Internal").ap(), name)

    def pe(self, fn, r=(), w=()):
        return self.sy.op("pe", fn, r, w)

    def act(self, fn, r=(), w=()):
        return self.sy.op("act", fn, r, w)

    def dve(self, fn, r=(), w=()):
        return self.sy.op("dve", fn, r, w)

    def pool(self, fn, r=(), w=()):
        return self.sy.op("pool", fn, r, w)

    def dma(self, q, out, in_, r=(), w=()):
        return self.sy.op(q, lambda e: e.dma_start(out=out, in_=in_), r, w, dma=True)

    def setup_common(self):
        nc = self.nc
        self.ps = []
        for i in range(8):
            self.ps.append(T(self.es.enter_context(nc.psum_tensor("psb%d" % i, [128, 512], F32)), "ps%d" % i))
        self.ident = self.sb("ident", [128, 128], F32)
        self.identb = self.sb("identb", [128, 128], BF16)
        self.pool(lambda e: e.memset(self.ident[:], 0.0), w=[self.ident])
        self.pool(lambda e: e.affine_select(out=self.ident[:], in_=self.ident[:], pattern=[[-1, 128]], compare_op=ALU.not_equal,
                                            fill=1.0, base=0, channel_multiplier=1), r=[self.ident], w=[self.ident])
        self.dve(lambda e: e.tensor_copy(out=self.identb[:], in_=self.ident[:]), r=[self.ident], w=[self.identb])
        self.ones = self.sb("ones", [128, 128], F32)
        self.pool(lambda e: e.memset(self.ones[:], 1.0), w=[self.ones])
        self.eps = self.sb("eps", [128, 4], F32)
        for i, v in enumerate((LN_EPS, RMS_EPS, GN_EPS, 0.0)):
            self.dve(lambda e, i=i, v=v: e.memset(self.eps[:, i:i + 1], v), w=[self.eps])

    def psb(self, i):
        return self.ps[i].t[:].bitcast(BF16)

    def load_w(self, dst, src, st_q="pool"):
        K = src.shape[0]
        if K <= 128:
            self.dma(st_q, dst[0:K, 0, :], src, w=[dst])
        else:
            v = src.rearrange("(kc p) n -> p kc n", p=128)
            for kc in range(K // 128):
                self.dma(st_q, dst[:, kc, :], v[:, kc, :], w=[dst])

    def load_bcast(self, dst, src_row):
        n = src_row.shape[-1]
        self.dma("sp", dst[:, 0:n], src_row.rearrange("(o n) -> o n", o=1).broadcast_to([128, n]), w=[dst])

    def load_T(self, dst, dst_ap, src_rows, n):
        with ExitStack() as s1:
            tmp = self.sb("ldT", [n, 128], F32, s1)
            self.dma("sp", tmp[:, :], src_rows, w=[tmp])
            self.pe(lambda e: e.transpose(self.ps[0][:, 0:n], tmp[:, :], self.ident[0:n, 0:n]), r=[tmp, self.ident], w=[self.ps[0]])
            self.dve(lambda e: e.tensor_copy(out=dst_ap, in_=self.ps[0][:, 0:n]), r=[self.ps[0]], w=[dst])
            self.sy.barrier()

    def transpose_to(self, src, nch, dst, dst_ap, bank, dt=BF16, rows=128, evac="act", src_off=0, cw=128):
        per = (1024 if dt == BF16 else 512)
        assert nch * rows <= per
        if dt == BF16:
            pv = self.psb(self.ps.index(bank))
            idt = self.identb
        else:
            pv = bank.t[:]
            idt = self.ident
        for c in range(nch):
            self.pe(lambda e, c=c: e.transpose(pv[0:cw, c * rows:(c + 1) * rows], src[0:rows, src_off + c * cw: src_off + (c + 1) * cw],
                                               idt[0:rows, 0:rows]), r=[src, idt], w=[bank])
        i = pv[0:cw, 0:nch * rows].rearrange("p (c r) -> p c r", c=nch)
        if evac == "act":
            self.act(lambda e: e.copy(out=dst_ap, in_=i), r=[bank], w=[dst])
        else:
            self.dve(lambda e: e.tensor_copy(out=dst_ap, in_=i), r=[bank], w=[dst])

    def linear(self, bank, xT, W, n0, n1, nkc, M=128, kp=128):
        for kc in range(nkc):
            self.pe(lambda e, kc=kc: e.matmul(bank[0:M, 0:n1 - n0], lhsT=xT[0:kp, kc, 0:M], rhs=W[0:kp, kc, n0:n1],
                                              start=(kc == 0), stop=(kc == nkc - 1)), r=[xT, W], w=[bank])

    def rstd_from_ss(self, ss, n, eps_col, out):
        self.act(lambda e: e.activation(out=out[:, 0:1], in_=ss[:, 0:1], func=AF.Sqrt, bias=self.eps[:, eps_col:eps_col + 1], scale=1.0 / n),
                 r=[ss, self.eps], w=[out])
        self.dve(lambda e: e.reciprocal(out=out[:, 0:1], in_=out[:, 0:1]), r=[out], w=[out])

    def resid_ln(self, x, banks, c, G, Bt, out, tmp):
        alpha = self.cfg.alpha
        xa, y, junk, st = tmp["xa"], tmp["y"], tmp["junk"], tmp["st"]
        self.act(lambda e: e.activation(out=xa[:], in_=x[:], func=AF.Copy, scale=alpha), r=[x], w=[xa])
        for h in range(2):
            self.dve(lambda e, h=h: e.scalar_tensor_tensor(out=y[:, h * 512:(h + 1) * 512], in0=banks[h][:, :], scalar=float(c),
                                                         in1=xa[:, h * 512:(h + 1) * 512], op0=ALU.mult, op1=ALU.add,
                                                         accum_out=st[:, h:h + 1]), r=[banks[h], xa], w=[y, st])
        self.ln_core(y, G, Bt, out, junk, st)

    def ln_core(self, y, G, Bt, out, junk, st):
        self.dve(lambda e: e.tensor_scalar(out=st[:, 2:3], in0=st[:, 0:1], scalar1=st[:, 1:2], scalar2=-1.0 / D, op0=ALU.add, op1=ALU.mult),
                 r=[st], w=[st])
        self.act(lambda e: e.activation(out=junk[:], in_=y[:], func=AF.Square, bias=st[:, 2:3], scale=1.0, accum_out=st[:, 3:4]),
                 r=[y, st], w=[junk, st])
        self.rstd_from_ss(_col(st, 3), D, 0, _col(st, 4))
        self.dve(lambda e: e.tensor_scalar(out=junk[:], in0=y[:], scalar1=st[:, 2:3], scalar2=st[:, 4:5], op0=ALU.add, op1=ALU.mult),
                 r=[y, st], w=[junk])
        self.pool(lambda e: e.tensor_tensor(out=junk[:], in0=junk[:], in1=G[:], op=ALU.mult), r=[junk, G], w=[junk])
        self.dve(lambda e: e.tensor_tensor(out=out[:], in0=junk[:], in1=Bt[:], op=ALU.add), r=[junk, Bt], w=[out])


class _col:
    def __init__(self, t, c):
        self._t = t
        self.c = c

    @property
    def wr(self):
        return self._t.wr

    @property
    def rd(self):
        return self._t.rd

    def __getitem__(self, k):
        return self._t.t[:, self.c:self.c + 1]


def x0_src(cfg, d):
    def f(j):
        if j == 0:
            return [((0, N_META), d["meta_tokens"][:, :], d["meta_tokens"]),
                    ((N_META, 128), d["x_prompt"][0:128 - N_META, :], d["x_prompt"])]
        if j < cfg.ntp:
            r = cfg.rows(j)
            return [((0, r), d["x_prompt"][128 * j - N_META:128 * j - N_META + r, :], d["x_prompt"])]
        return [((0, cfg.srows), d["x_sample"][:, :], d["x_sample"])]
    return f


def y_dst(cfg, d):
    def f(j):
        if j == 0:
            return [((N_META, 128), d["y_prompt"][0:128 - N_META, :], d["y_prompt"])]
        if j < cfg.ntp:
            r = cfg.rows(j)
            return [((0, r), d["y_prompt"][128 * j - N_META:128 * j - N_META + r, :], d["y_prompt"])]
        return [((0, cfg.srows), d["y_sample"][:, :], d["y_sample"])]
    return f


def scr_map(cfg, X, Xt):
    def f(j):
        r = cfg.rows(j)
        return [((0, r), X[128 * j:128 * j + r, :], Xt[j])]
    f.X = X
    f.Xt = Xt
    return f


def stage_ffn(B, li, half, src, dst, tiles=None):
    cfg, d = B.cfg, B.din
    with ExitStack() as st:
        w1 = B.sb("w1", [128, 8, DFF], BF16, st)
        w3 = B.sb("w3", [128, 8, DFF], BF16, st)
        w2 = B.sb("w2", [128, 22, D], BF16, st)
        B.load_w(w1, d["ffn_w1"].t[li, half])
        B.load_w(w3, d["ffn_w3"].t[li, half])
        B.load_w(w2, d["ffn_w2"].t[li, half])
        G = B.sb("G", [128, D], F32, st)
        Bt = B.sb("Bt", [128, D], F32, st)
        lni = 0 if half == 0 else 2
        B.load_bcast(G, d["ln_g"].t[li, lni])
        B.load_bcast(Bt, d["ln_b"].t[li, lni])
        xs = [B.sb("xs", [128, D], F32, st) for _ in range(3)]
        xb = B.sb("xb", [128, D], BF16, st)
        xT = B.sb("xT", [128, 8, 128], BF16, st)
        sg = [B.sb("sg", [128, 512], F32, st) for _ in range(2)]
        g = B.sb("g", [128, DFF], BF16, st)
        gT = B.sb("gT", [128, 22, 128], BF16, st)
        tmp = dict(xa=B.sb("xa", [128, D], F32, st), y=B.sb("y", [128, D], F32, st), junk=B.sb("junk", [128, D], F32, st),
                   st=B.sb("st", [128, 8], F32, st))
        xo = [B.sb("xo", [128, D], F32, st) for _ in range(2)]
        for t in xs:
            B.dve(lambda e, t=t: e.memset(t[:], 0.0), w=[t])
        ps = B.ps
        tl = list(range(cfg.nt)) if tiles is None else tiles
        if DEBUG_TILES is not None:
            tl = DEBUG_TILES

        def load(j, buf):
            for (r0, r1), ap, tt in src(j):
                B.dma("sp", buf[r0:r1, :], ap, r=[tt], w=[buf])

        load(tl[0], xs[0])
        for k, j in enumerate(tl):
            x = xs[k % 3]
            if k + 1 < len(tl):
                load(tl[k + 1], xs[(k + 1) % 3])
            B.pool(lambda e: e.tensor_copy(out=xb[:], in_=x[:]), r=[x], w=[xb])
            B.transpose_to(xb, 8, xT, xT[:, :, :], ps[0])
            for gi in range(6):
                n0 = gi * 512
                n1 = min(DFF, n0 + 512)
                b1, b3 = ps[1 + 2 * (gi % 2)], ps[2 + 2 * (gi % 2)]
                B.linear(b1, xT, w1, n0, n1, 8)
                B.linear(b3, xT, w3, n0, n1, 8)
                s_ = sg[gi % 2]
                B.act(lambda e, s_=s_, b1=b1, n=n1 - n0: e.activation(out=s_[:, 0:n], in_=b1[:, 0:n], func=AF.Silu), r=[b1], w=[s_])
                B.dve(lambda e, s_=s_, b3=b3, n0=n0, n1=n1: e.tensor_tensor(out=g[:, n0:n1], in0=s_[:, 0:n1 - n0], in1=b3[:, 0:n1 - n0], op=ALU.mult),
                      r=[s_, b3], w=[g])
            for c0, nch in ((0, 8), (8, 8), (16, 6)):
                B.transpose_to(g, nch, gT, gT[:, c0:c0 + nch, :], ps[5], src_off=c0 * 128, evac="act" if c0 != 8 else "dve")
            for h in range(2):
                for fc in range(22):
                    B.pe(lambda e, h=h, fc=fc: e.matmul(ps[6 + h][:, :], lhsT=gT[:, fc, :], rhs=w2[:, fc, h * 512:(h + 1) * 512],
                                                       start=(fc == 0), stop=(fc == 21)), r=[gT, w2], w=[ps[6 + h]])
            o = xo[k % 2]
            B.resid_ln(x, [ps[6], ps[7]], 0.5, G, Bt, o, tmp)
            for (r0, r1), ap, tt in dst(j):
                B.dma("sp", ap, o[r0:r1, :], r=[o], w=[tt])
            if DEBUG_BARRIER:
                B.sy.barrier()
        B.sy.barrier()


WEIGHT_SHAPES = dict(
    meta_tokens=(N_META, D), ln_g=("depth", 3, D), ln_b=("depth", 3, D),
    ffn_w1=("depth", 2, D, DFF), ffn_w3=("depth", 2, D, DFF), ffn_w2=("depth", 2, DFF, D),
    mla_w_dq=("n_mla", D, QR), mla_q_norm=("n_mla", QR), mla_w_uq=("n_mla", QR, NH * QK),
    mla_w_dkv=("n_mla", D, KVR + ROPE), mla_kv_norm=("n_mla", KVR), mla_w_uk=("n_mla", KVR, NH * NOPE),
    mla_w_uv=("n_mla", KVR, NH * VD), mla_w_o=("n_mla", NH * VD, D),
    rw_mu=("n_rwkv", 6, D), rw_wr=("n_rwkv", D, D), rw_wk=("n_rwkv", D, D), rw_wv=("n_rwkv", D, D),
    rw_w0=("n_rwkv", D), rw_w1=("n_rwkv", D, 64), rw_w2=("n_rwkv", 64, D), rw_a0=("n_rwkv", D),
    rw_a1=("n_rwkv", D, 64), rw_a2=("n_rwkv", 64, D), rw_g1=("n_rwkv", D, 128), rw_g2=("n_rwkv", 128, D),
    rw_k_k=("n_rwkv", D), rw_k_a=("n_rwkv", D), rw_r_k=("n_rwkv", D), rw_lnx_g=("n_rwkv", D), rw_lnx_b=("n_rwkv", D),
    rw_wo=("n_rwkv", D, D),
    s5_lam_re=("n_s5", 64, 64), s5_lam_im=("n_s5", 64, 64), s5_log_dt=("n_s5", 64),
    s5_b_re=("n_s5", 64, 64, 16), s5_b_im=("n_s5", 64, 64, 16), s5_c_re=("n_s5", 64, 16, 64), s5_c_im=("n_s5", 64, 16, 64),
    s5_d=("n_s5", D), s5_wv=("n_s5", D, D), s5_wg=("n_s5", D, D),
)


def _shape(cfg, shp):
    return tuple(getattr(cfg, s) if isinstance(s, str) else s for s in shp)


def io_shapes(cfg):
    ins = dict(
        x_prompt=(cfg.seq, D), x_sample=(cfg.srows, D),
        cache_mla_latent=(cfg.n_mla * cfg.npool * PAGE, KVR), cache_mla_krope=(cfg.n_mla * cfg.npool * PAGE, ROPE),
        state_rwkv_wkv=(cfg.n_rwkv, cfg.nsq, NH, 64, 64), state_rwkv_shift=(cfg.n_rwkv, cfg.nsq, D),
        state_s5_re=(cfg.n_s5, cfg.nsq, 64, 64), state_s5_im=(cfg.n_s5, cfg.nsq, 64, 64),
        page_table=(cfg.nsq, cfg.npages),
    )
    for k, v in WEIGHT_SHAPES.items():
        ins[k] = _shape(cfg, v)
    outs = dict(
        y_prompt=(cfg.seq, D), y_sample=(cfg.srows, D),
        lat_p=(cfg.n_mla, cfg.L, KVR), kr_p=(cfg.n_mla, cfg.L, ROPE), lat_s=(cfg.n_mla, cfg.srows, KVR), kr_s=(cfg.n_mla, cfg.srows, ROPE),
        wkv_p=(cfg.n_rwkv, NH, 64, 64), sh_p=(cfg.n_rwkv, D), wkv_s=(cfg.n_rwkv, cfg.nsq, NH, 64, 64), sh_s=(cfg.n_rwkv, cfg.nsq, D),
        re_p=(cfg.n_s5, 64, 64), im_p=(cfg.n_s5, 64, 64), re_s=(cfg.n_s5, cfg.nsq, 64, 64), im_s=(cfg.n_s5, cfg.nsq, 64, 64),
    )
    return ins, outs


def build_program(cfg, nstages=None):
    nc = bass.Bass("TRN2", target_bir_lowering=False)
    B = Builder(nc, cfg)
    ins, outs = io_shapes(cfg)
    for k, shp in ins.items():
        B.dram_in(k, shp, I32 if k == "page_table" else F32)
    for k, shp in outs.items():
        B.dram_out(k, shp, F32)
    d = dict(B.din)
    d.update(B.dout)
    B.d = d
    B.setup_common()
    XA = B.dram_scr("XA", [cfg.nt * 128, D])
    XB = B.dram_scr("XB", [cfg.nt * 128, D])
    Xs = [XA, XB]
    Xt = [[T(None, "XA%d" % j) for j in range(cfg.nt)], [T(None, "XB%d" % j) for j in range(cfg.nt)]]
    B.X = Xs
    B.Xt = Xt
    total = 3 * cfg.depth
    if nstages is None:
        nstages = total
    for s in range(nstages):
        li, kind = s // 3, s % 3
        src = x0_src(cfg, d) if s == 0 else scr_map(cfg, Xs[(s - 1) % 2].t, Xt[(s - 1) % 2])
        dst = y_dst(cfg, d) if s == nstages - 1 else scr_map(cfg, Xs[s % 2].t, Xt[s % 2])
        if kind == 0:
            stage_ffn(B, li, 0, src, dst)
        elif kind == 2:
            stage_ffn(B, li, 1, src, dst)
        else:
            mk, m = li % 3, li // 3
            if mk == 0:
                stage_mla(B, li, m, src, dst)
            elif mk == 1:
                stage_rwkv(B, li, m, src, dst)
            else:
                stage_s5(B, li, m, src, dst)
    B.sy.finish()
    B.es.close()
    return nc, B


def shard_inputs(cfg, inputs, c):
    b = c % cfg.n_batch
    s0 = c * cfg.nsq
    m = {}
    m["x_prompt"] = np.ascontiguousarray(inputs["x_prompt"][b])
    m["x_sample"] = np.ascontiguousarray(inputs["x_sample"][s0:s0 + cfg.nsq]).reshape(cfg.srows, D)
    m["cache_mla_latent"] = inputs["cache_mla_latent"].reshape(-1, KVR)
    m["cache_mla_krope"] = inputs["cache_mla_krope"].reshape(-1, ROPE)
    m["state_rwkv_wkv"] = np.ascontiguousarray(inputs["state_rwkv_wkv"][:, s0:s0 + cfg.nsq])
    m["state_rwkv_shift"] = np.ascontiguousarray(inputs["state_rwkv_shift"][:, s0:s0 + cfg.nsq])
    m["state_s5_re"] = np.ascontiguousarray(inputs["state_s5_re"][:, s0:s0 + cfg.nsq])
    m["state_s5_im"] = np.ascontiguousarray(inputs["state_s5_im"][:, s0:s0 + cfg.nsq])
    m["page_table"] = np.ascontiguousarray(inputs["page_table"][s0:s0 + cfg.nsq]).astype(np.int32)
    for k, shp in WEIGHT_SHAPES.items():
        m[k] = np.ascontiguousarray(inputs[k]).reshape(_shape(cfg, shp))
    return m


def assemble(cfg, res):
    nb, nco = cfg.n_batch, cfg.n_cores
    def pb(name):
        return [res[b][name] for b in range(nb)]
    def sc(name):
        return [res[c][name] for c in range(nco)]
    y_p = np.stack(pb("y_prompt"), 0)
    y_s = np.concatenate(sc("y_sample"), 0).reshape(nco * cfg.nsq, DEC_SEQ, D)
    lat_p = np.stack(pb("lat_p"), 1)
    kr_p = np.stack(pb("kr_p"), 1)
    lat_s = np.concatenate([r.reshape(cfg.n_mla, cfg.nsq, DEC_SEQ, KVR) for r in sc("lat_s")], 1)
    kr_s = np.concatenate([r.reshape(cfg.n_mla, cfg.nsq, DEC_SEQ, ROPE) for r in sc("kr_s")], 1)
    wkv_p = np.stack(pb("wkv_p"), 1)
    sh_p = np.stack(pb("sh_p"), 1)
    wkv_s = np.concatenate(sc("wkv_s"), 1)
    sh_s = np.concatenate(sc("sh_s"), 1)
    re_p = np.stack(pb("re_p"), 1)
    im_p = np.stack(pb("im_p"), 1)
    re_s = np.concatenate(sc("re_s"), 1)
    im_s = np.concatenate(sc("im_s"), 1)
    return (y_p, y_s, lat_p, kr_p, lat_s, kr_s, wkv_p, sh_p, wkv_s, sh_s, re_p, im_p, re_s, im_s)


def run(cfg, inputs, nstages=None, trace=False):
    nc, B = build_program(cfg, nstages)
    in_maps = [shard_inputs(cfg, inputs, c) for c in range(cfg.n_cores)]
    res = run_bass_kernel_spmd(nc, in_maps, core_ids=list(range(cfg.n_cores)), trace=trace)
    return assemble(cfg, res.results), res


def kernel(**inputs):
    cfg = Cfg()
    out, _ = run(cfg, inputs)
    return tuple(np.ascontiguousarray(o, dtype=np.float32) for o in out)


def range_reduce_sin(B, ang, out, n, tmpf, tmpi):
    C1 = 6.28125
    C2 = 2 * math.pi - C1
    B.dve(lambda e: e.tensor_scalar(out=tmpf[:, 0:n], in0=ang[:, 0:n], scalar1=1.0 / (2 * math.pi), scalar2=None, op0=ALU.mult), r=[ang], w=[tmpf])
    B.dve(lambda e: e.tensor_copy(out=tmpi[:, 0:n], in_=tmpf[:, 0:n]), r=[tmpf], w=[tmpi])
    B.dve(lambda e: e.tensor_copy(out=tmpf[:, 0:n], in_=tmpi[:, 0:n]), r=[tmpi], w=[tmpf])
    B.dve(lambda e: e.scalar_tensor_tensor(out=out[:, 0:n], in0=tmpf[:, 0:n], scalar=-C1, in1=ang[:, 0:n], op0=ALU.mult, op1=ALU.add), r=[tmpf, ang], w=[out])
    B.dve(lambda e: e.scalar_tensor_tensor(out=out[:, 0:n], in0=tmpf[:, 0:n], scalar=-C2, in1=out[:, 0:n], op0=ALU.mult, op1=ALU.add), r=[tmpf, out], w=[out])
    B.dve(lambda e: e.tensor_scalar(out=out[:, 0:n], in0=out[:, 0:n], scalar1=math.pi, scalar2=-math.pi, op0=ALU.min, op1=ALU.max), r=[out], w=[out])
    B.act(lambda e: e.activation(out=out[:, 0:n], in_=out[:, 0:n], func=AF.Sin), r=[out], w=[out])


def setup_rope(B, stk):
    cfg = B.cfg
    nt = cfg.nt
    B.COS = B.sb("COS", [128, nt * 16], F32, stk)
    B.SIN = B.sb("SIN", [128, nt * 16], F32, stk)
    with ExitStack() as st:
        pos = B.sb("pos", [128, nt], F32, st)
        pi_ = B.sb("pi_", [128, 1], I32, st)
        ang = B.sb("ang", [128, nt * 16], F32, st)
        ang2 = B.sb("ang2", [128, nt * 16], F32, st)
        tf = B.sb("tf", [128, nt * 16], F32, st)
        ti = B.sb("ti", [128, nt * 16], I32, st)
        B.pool(lambda e: e.iota(pos[:, 0:cfg.ntp], pattern=[[128, cfg.ntp]], base=0, channel_multiplier=1, allow_small_or_imprecise_dtypes=True), w=[pos])
        B.pool(lambda e: e.iota(pi_[:], pattern=[[0, 1]], base=0, channel_multiplier=1), w=[pi_])
        B.dve(lambda e: e.tensor_single_scalar(out=pi_[:], in_=pi_[:], scalar=3, op=ALU.bitwise_and), r=[pi_], w=[pi_])
        B.dve(lambda e: e.tensor_copy(out=pos[:, cfg.ntp:nt], in_=pi_[:]), r=[pi_], w=[pos])
        B.dve(lambda e: e.tensor_scalar(out=pos[:, cfg.ntp:nt], in0=pos[:, cfg.ntp:nt], scalar1=float(cfg.past), scalar2=None, op0=ALU.add), r=[pos], w=[pos])
        a3 = ang[:].rearrange("p (j f) -> p j f", f=16)
        for f in range(16):
            inv = float(np.float32(1.0) / np.float32(10000.0) ** (np.float32(f) * np.float32(2.0 / ROPE)))
            B.dve(lambda e, f=f, inv=inv: e.tensor_scalar(out=a3[:, :, f], in0=pos[:, :], scalar1=inv, scalar2=None, op0=ALU.mult), r=[pos], w=[ang])
        B.dve(lambda e: e.tensor_scalar(out=ang2[:], in0=ang[:], scalar1=math.pi / 2, scalar2=None, op0=ALU.add), r=[ang], w=[ang2])
        range_reduce_sin(B, ang, B.SIN, nt * 16, tf, ti)
        range_reduce_sin(B, ang2, B.COS, nt * 16, tf, ti)
        B.sy.barrier()


def rope_apply(B, src, src_ap, dst, dst_ap, j, nh, t):
    c = B.COS[:, j * 16:(j + 1) * 16].unsqueeze(1).broadcast_to([128, nh, 16])
    s = B.SIN[:, j * 16:(j + 1) * 16].unsqueeze(1).broadcast_to([128, nh, 16])
    x1, x2 = src_ap[:, :, 0:16], src_ap[:, :, 16:32]
    def v(k):
        return t[k][:, 0:nh * 16].rearrange("p (h f) -> p h f", f=16)
    B.dve(lambda e: e.tensor_tensor(out=v(0), in0=x1, in1=c, op=ALU.mult), r=[src, B.COS], w=[t[0]])
    B.dve(lambda e: e.tensor_tensor(out=v(1), in0=x2, in1=s, op=ALU.mult), r=[src, B.SIN], w=[t[1]])
    B.dve(lambda e: e.tensor_tensor(out=v(2), in0=x1, in1=s, op=ALU.mult), r=[src, B.SIN], w=[t[2]])
    B.dve(lambda e: e.tensor_tensor(out=v(3), in0=x2, in1=c, op=ALU.mult), r=[src, B.COS], w=[t[3]])
    B.dve(lambda e: e.tensor_tensor(out=dst_ap[:, :, 0:16], in0=v(0), in1=v(1), op=ALU.subtract), r=[t[0], t[1]], w=[dst])
    B.dve(lambda e: e.tensor_tensor(out=dst_ap[:, :, 16:32], in0=v(2), in1=v(3), op=ALU.add), r=[t[2], t[3]], w=[dst])


def stage_mla(B, li, m, src, dst):
    cfg, d, ps = B.cfg, B.d, B.ps
    ntp, nt = cfg.ntp, cfg.nt
    NTOK = ntp * 128
    scale = QK ** -0.5
    nblk = (ntp + 3) // 4
    if not hasattr(B, "QTd"):
        B.QTd = B.dram_scr("QTd", [NH, QK, nblk * 512], BF16)
        B.OTd = B.dram_scr("OTd", [NH, VD, nblk * 512 + 128], BF16)
        B.CNd = B.dram_scr("CNd", [128, KVR + ROPE], F32)
    QTd, OTd, CNd = B.QTd, B.OTd, B.CNd
    with ExitStack() as st:
        w_uk = B.sb("w_uk", [128, 2, NH * NOPE], BF16, st)
        w_uv = B.sb("w_uv", [128, 2, NH * VD], BF16, st)
        B.load_w(w_uk, d["mla_w_uk"].t[m])
        B.load_w(w_uv, d["mla_w_uv"].t[m])
        qs_b = B.sb("qs_b", [128, NH, QK], BF16, st)
        sA = ExitStack()
        setup_rope(B, sA)
        w_dq = B.sb("w_dq", [128, 8, QR], BF16, sA)
        w_uq = B.sb("w_uq", [128, 6, NH * QK], BF16, sA)
        w_dkv = B.sb("w_dkv", [128, 8, KVR + ROPE], BF16, sA)
        B.load_w(w_dq, d["mla_w_dq"].t[m])
        B.load_w(w_uq, d["mla_w_uq"].t[m])
        B.load_w(w_dkv, d["mla_w_dkv"].t[m])
        Gq = B.sb("Gq", [128, QR], F32, sA)
        Gkv = B.sb("Gkv", [128, KVR], F32, sA)
        B.load_bcast(Gq, d["mla_q_norm"].t[m])
        B.load_bcast(Gkv, d["mla_kv_norm"].t[m])
        wukx = B.sb("wukx", [128, 2, NH, QK], BF16, sA)
        B.pool(lambda e: e.memset(wukx[:], 0.0), w=[wukx])
        B.pool(lambda e: e.tensor_copy(out=wukx[:, :, :, 0:NOPE], in_=w_uk[:, :, :].rearrange("p k (h n) -> p k h n", n=NOPE)), r=[w_uk], w=[wukx])
        sel = B.sb("sel", [32, QK], BF16, sA)
        B.pool(lambda e: e.memset(sel[:], 0.0), w=[sel])
        B.pool(lambda e: e.tensor_copy(out=sel[:, NOPE:QK], in_=B.identb[0:32, 0:32]), r=[B.identb], w=[sel])
        cT = B.sb("cT", [128, 2, NTOK], BF16, sA)
        kpeT = B.sb("kpeT", [32, NTOK], BF16, sA)
        qmax = B.sb("qmax", [128, NH], F32, sA)
        kmax = B.sb("kmax", [128, NH], F32, sA)
        B.dve(lambda e: e.memset(qmax[:], 0.0), w=[qmax])
        B.dve(lambda e: e.memset(kmax[:], 0.0), w=[kmax])
        st2 = ExitStack()
        xs = [B.sb("xs", [128, D], F32, st2) for _ in range(2)]
        xb = B.sb("xb", [128, D], BF16, st2)
        xT = B.sb("xT", [128, 8, 128], BF16, st2)
        stq = B.sb("stq", [128, 8], F32, st2)
        cqn = B.sb("cqn", [128, QR], BF16, st2)
        cqT = B.sb("cqT", [128, 6, 128], BF16, st2)
        qf = B.sb("qf", [128, NH * QK], F32, st2)
        qb = B.sb("qb", [128, NH, QK], BF16, st2)
        sq = B.sb("sq", [128, NH * QK], F32, st2)
        ss16 = B.sb("ss16", [128, NH], F32, st2)
        rt = [B.sb("rt", [128, NH * 16], F32, st2) for _ in range(4)]
        QTs = [B.sb("QTs", [QK, NH, 512], BF16, st2) for _ in range(2)]
        c32 = [B.sb("c32", [128, KVR + ROPE], F32, st2) for _ in range(2)]
        cb = B.sb("cb", [128, KVR + ROPE], BF16, st2)
        for t_ in xs + QTs:
            B.dve(lambda e, t_=t_: e.memset(t_[:], 0.0), w=[t_])
        qf3 = qf[:].rearrange("p (h q) -> p h q", q=QK)

        def load(j, buf):
            for (r0, r1), ap, tt in src(j):
                B.dma("sp", buf[r0:r1, :], ap, r=[tt], w=[buf])

        load(0, xs[0])
        for j in range(nt):
            x = xs[j % 2]
            rows = cfg.rows(j)
            sample = (j == ntp)
            B.pool(lambda e: e.tensor_copy(out=xb[:], in_=x[:]), r=[x], w=[xb])
            B.transpose_to(xb, 8, xT, xT[:, :, :], ps[0])
            if j + 1 < nt:
                load(j + 1, xs[(j + 1) % 2])
            B.linear(ps[1], xT, w_dq, 0, 512, 8)
            B.linear(ps[2], xT, w_dq, 512, QR, 8)
            B.act(lambda e: e.activation(out=sq[:, 0:512], in_=ps[1][:, 0:512], func=AF.Square, accum_out=stq[:, 0:1]), r=[ps[1]], w=[sq, stq])
            B.act(lambda e: e.activation(out=sq[:, 512:QR], in_=ps[2][:, 0:QR - 512], func=AF.Square, accum_out=stq[:, 1:2]), r=[ps[2]], w=[sq, stq])
            B.dve(lambda e: e.tensor_tensor(out=stq[:, 2:3], in0=stq[:, 0:1], in1=stq[:, 1:2], op=ALU.add), r=[stq], w=[stq])
            B.rstd_from_ss(_col(stq, 2), QR, 1, _col(stq, 3))
            B.dve(lambda e: e.scalar_tensor_tensor(out=cqn[:, 0:512], in0=ps[1][:, 0:512], scalar=stq[:, 3:4], in1=Gq[:, 0:512], op0=ALU.mult, op1=ALU.mult),
                  r=[ps[1], stq, Gq], w=[cqn])
            B.dve(lambda e: e.scalar_tensor_tensor(out=cqn[:, 512:QR], in0=ps[2][:, 0:QR - 512], scalar=stq[:, 3:4], in1=Gq[:, 512:QR], op0=ALU.mult, op1=ALU.mult),
                  r=[ps[2], stq, Gq], w=[cqn])
            B.transpose_to(cqn, 6, cqT, cqT[:, :, :], ps[0])
            for k3 in range(3):
                B.linear(ps[3 + k3], cqT, w_uq, 512 * k3, 512 * (k3 + 1), 6)
                B.act(lambda e, k3=k3: e.copy(out=qf[:, 512 * k3:512 * (k3 + 1)], in_=ps[3 + k3][:, :]), r=[ps[3 + k3]], w=[qf])
            qdst = qs_b if sample else qb
            B.dve(lambda e: e.tensor_copy(out=qdst[:, :, 0:NOPE], in_=qf3[:, :, 0:NOPE]), r=[qf], w=[qdst])
            rope_apply(B, qf, qf3[:, :, NOPE:QK], qdst, qdst[:, :, NOPE:QK], j, NH, rt)
            if not sample:
                B.pool(lambda e: e.tensor_tensor(out=sq[:], in0=qf[:], in1=qf[:], op=ALU.mult), r=[qf], w=[sq])
                B.dve(lambda e: e.tensor_reduce(out=ss16[:], in_=sq[:].rearrange("p (h q) -> p h q", q=QK), axis=AX.X, op=ALU.add), r=[sq], w=[ss16])
                B.dve(lambda e: e.tensor_tensor(out=qmax[:], in0=qmax[:], in1=ss16[:], op=ALU.max), r=[qmax, ss16], w=[qmax])
                QT_ = QTs[(j // 4) % 2]
                for hb in range(2):
                    bank = ps[6 + hb]
                    pv = B.psb(6 + hb)
                    for hh in range(8):
                        h = hb * 8 + hh
                        B.pe(lambda e, h=h, hh=hh, pv=pv: e.transpose(pv[0:QK, hh * 128:(hh + 1) * 128], qb[:, h, :], B.identb[:, :]), r=[qb, B.identb], w=[bank])
                    B.act(lambda e, hb=hb, pv=pv: e.copy(out=QT_[:, hb * 8:(hb + 1) * 8, (j % 4) * 128:(j % 4 + 1) * 128],
                                                         in_=pv[0:QK, 0:1024].rearrange("p (h t) -> p h t", t=128)), r=[bank], w=[QT_])
                if j % 4 == 3 or j == ntp - 1:
                    blk = j // 4
                    B.dma("sp", QTd.t.rearrange("h r t -> r h t")[:, :, blk * 512:(blk + 1) * 512], QT_[:, :, :], r=[QT_], w=[QTd])
            B.linear(ps[1], xT, w_dkv, 0, KVR + ROPE, 8)
            B.act(lambda e: e.activation(out=sq[:, 0:KVR], in_=ps[1][:, 0:KVR], func=AF.Square, accum_out=stq[:, 4:5]), r=[ps[1]], w=[sq, stq])
            B.rstd_from_ss(_col(stq, 4), KVR, 1, _col(stq, 5))
            c_ = c32[j % 2]
            B.dve(lambda e: e.scalar_tensor_tensor(out=c_[:, 0:KVR], in0=ps[1][:, 0:KVR], scalar=stq[:, 5:6], in1=Gkv[:, :], op0=ALU.mult, op1=ALU.mult),
                  r=[ps[1], stq, Gkv], w=[c_])
            rope_apply(B, ps[1], ps[1][:, KVR:KVR + ROPE].rearrange("p (o f) -> p o f", o=1), c_,
                       c_[:, KVR:KVR + ROPE].rearrange("p (o f) -> p o f", o=1), j, 1, rt)
            B.pool(lambda e: e.tensor_copy(out=cb[:], in_=c_[:]), r=[c_], w=[cb])
            if sample:
                B.dma("sp", d["lat_s"].t[m, :, :], c_[0:rows, 0:KVR], r=[c_], w=[d["lat_s"]])
                B.dma("sp", d["kr_s"].t[m, :, :], c_[0:rows, KVR:KVR + ROPE], r=[c_], w=[d["kr_s"]])
                B.dma("sp", CNd.t[0:rows, :], c_[0:rows, :], r=[c_], w=[CNd])
            else:
                B.dma("sp", d["lat_p"].t[m, 128 * j:128 * j + rows, :], c_[0:rows, 0:KVR], r=[c_], w=[d["lat_p"]])
                B.dma("sp", d["kr_p"].t[m, 128 * j:128 * j + rows, :], c_[0:rows, KVR:KVR + ROPE], r=[c_], w=[d["kr_p"]])
                B.transpose_to(cb, 2, cT, cT[:, :, j * 128:(j + 1) * 128], ps[0])
                pv0 = B.psb(0)
                B.pe(lambda e: e.transpose(pv0[0:ROPE, 0:128], cb[:, KVR:KVR + ROPE], B.identb[:, :]), r=[cb, B.identb], w=[ps[0]])
                B.act(lambda e: e.copy(out=kpeT[:, j * 128:(j + 1) * 128], in_=pv0[0:ROPE, 0:128]), r=[ps[0]], w=[kpeT])
                for k2 in range(2):
                    for kc in range(2):
                        B.pe(lambda e, k2=k2, kc=kc: e.matmul(ps[3 + k2][:, :], lhsT=cT[:, kc, j * 128:(j + 1) * 128], rhs=w_uk[:, kc, 512 * k2:512 * (k2 + 1)],
                                                             start=(kc == 0), stop=(kc == 1)), r=[cT, w_uk], w=[ps[3 + k2]])
                    B.act(lambda e, k2=k2: e.activation(out=sq[:, 512 * k2:512 * (k2 + 1)], in_=ps[3 + k2][:, :], func=AF.Square), r=[ps[3 + k2]], w=[sq])
                B.dve(lambda e: e.tensor_reduce(out=ss16[:], in_=sq[:, 0:NH * NOPE].rearrange("p (h q) -> p h q", q=NOPE), axis=AX.X, op=ALU.add), r=[sq], w=[ss16])
                B.act(lambda e: e.activation(out=sq[:, 1024:1024 + ROPE], in_=c_[:, KVR:KVR + ROPE], func=AF.Square, accum_out=stq[:, 6:7]), r=[c_], w=[sq, stq])
                B.dve(lambda e: e.tensor_scalar(out=ss16[:], in0=ss16[:], scalar1=stq[:, 6:7], scalar2=None, op0=ALU.add), r=[ss16, stq], w=[ss16])
                B.dve(lambda e: e.tensor_tensor(out=kmax[:], in0=kmax[:], in1=ss16[:], op=ALU.max), r=[kmax, ss16], w=[kmax])
        B.sy.barrier()
        st2.close()
        mla_prompt_attention(B, st, m, cT, kpeT, wukx, sel, w_uv, qmax, kmax, scale)
        sA.close()
        mla_sample_attention(B, st, m, qs_b, w_uk, w_uv, scale)
    mla_outproj(B, li, m, src, dst)


def bcast_partition_max(B, src, dst, bank, tmp):
    B.pe(lambda e: e.transpose(bank[0:NH, 0:128], src[:, 0:NH], B.ident[:, :]), r=[src, B.ident], w=[bank])
    B.dve(lambda e: e.tensor_reduce(out=tmp[0:NH, 0:1], in_=bank[0:NH, 0:128], axis=AX.X, op=ALU.max), r=[bank], w=[tmp])
    B.dve(lambda e: e.tensor_scalar(out=tmp[0:NH, 1:1 + NH], in0=B.ident[0:NH, 0:NH], scalar1=tmp[0:NH, 0:1], scalar2=None, op0=ALU.mult), r=[tmp, B.ident], w=[tmp])
    B.pe(lambda e: e.matmul(bank[:, 256:256 + NH], lhsT=B.ones[0:NH, :], rhs=tmp[0:NH, 1:1 + NH], start=True, stop=True), r=[B.ones, tmp], w=[bank])
    B.dve(lambda e: e.tensor_copy(out=dst[:, 0:NH], in_=bank[:, 256:256 + NH]), r=[bank], w=[dst])


def mla_prompt_attention(B, st, m, cT, kpeT, wukx, sel, w_uv, qmax, kmax, scale):
    cfg, d, ps = B.cfg, B.d, B.ps
    ntp = cfg.ntp
    NTOK = ntp * 128
    nblk = (ntp + 3) // 4
    QTd, OTd = B.QTd, B.OTd
    with ExitStack() as s2:
        tmpm = B.sb("tmpm", [128, 1 + NH], F32, s2)
        MQ = B.sb("MQ", [128, NH], F32, s2)
        MK = B.sb("MK", [128, NH], F32, s2)
        negM = B.sb("negM", [128, NH], F32, s2)
        bcast_partition_max(B, qmax, MQ, ps[0], tmpm)
        bcast_partition_max(B, kmax, MK, ps[0], tmpm)
        B.dve(lambda e: e.tensor_tensor(out=negM[:], in0=MQ[:], in1=MK[:], op=ALU.mult), r=[MQ, MK], w=[negM])
        B.act(lambda e: e.activation(out=negM[:], in_=negM[:], func=AF.Sqrt), r=[negM], w=[negM])
        B.dve(lambda e: e.tensor_scalar(out=negM[:], in0=negM[:], scalar1=-scale, scalar2=None, op0=ALU.mult), r=[negM], w=[negM])
        masks = []
        mf = B.sb("mf", [128, 512], F32, s2)
        for i in range(4):
            mk = B.sb("mask", [128, 512], BF16, s2)
            B.pool(lambda e: e.memset(mf[:], 1.0), w=[mf])
            B.pool(lambda e, i=i: e.affine_select(out=mf[:], in_=mf[:], pattern=[[1, 512]], compare_op=ALU.is_ge, fill=0.0, base=-128 * i, channel_multiplier=-1),
                   r=[mf], w=[mf])
            B.pool(lambda e, mk=mk: e.tensor_copy(out=mk[:], in_=mf[:]), r=[mf], w=[mk])
            masks.append(mk)
        KTh = [B.sb("KTh", [QK, NTOK], BF16, s2) for _ in range(2)]
        Vh = [B.sb("Vh", [128, ntp, VD + 1], BF16, s2) for _ in range(2)]
        for v_ in Vh:
            B.pool(lambda e, v_=v_: e.memset(v_[:], 1.0), w=[v_])
        QTq = [B.sb("QTq", [QK, 512], BF16, s2) for _ in range(2)]
        PT = [B.sb("PT", [128, 512], BF16, s2) for _ in range(3)]
        Osb = [B.sb("Osb", [VD + 1, 512], F32, s2) for _ in range(2)]
        rl = B.sb("rl", [VD, 512], F32, s2)
        On = [B.sb("On", [VD, 512], BF16, s2) for _ in range(2)]
        onesr = B.sb("onesr", [VD + 1, VD], F32, s2)
        B.pool(lambda e: e.memset(onesr[:], 1.0), w=[onesr])
        nstep = 0
        nq_i = 0
        for h in range(NH):
            KT, V = KTh[h % 2], Vh[h % 2]
            for b in range(nblk):
                n = min(512, NTOK - b * 512)
                bank = ps[1 + b % 2]
                for kc in range(2):
                    B.pe(lambda e, kc=kc, b=b, n=n, bank=bank: e.matmul(bank[0:QK, 0:n], lhsT=wukx[:, kc, h, :], rhs=cT[:, kc, b * 512:b * 512 + n],
                                                                        start=(kc == 0), stop=False), r=[wukx, cT], w=[bank])
                B.pe(lambda e, b=b, n=n, bank=bank: e.matmul(bank[0:QK, 0:n], lhsT=sel[:, :], rhs=kpeT[:, b * 512:b * 512 + n], start=False, stop=True),
                     r=[sel, kpeT], w=[bank])
                B.dve(lambda e, b=b, n=n, bank=bank: e.tensor_copy(out=KT[:, b * 512:b * 512 + n], in_=bank[0:QK, 0:n]), r=[bank], w=[KT])
            for t0 in range(0, ntp, 8):
                nt8 = min(8, ntp - t0)
                bank = ps[3]
                for i in range(nt8):
                    for kc in range(2):
                        B.pe(lambda e, i=i, kc=kc, t0=t0, bank=bank: e.matmul(bank[:, i * VD:(i + 1) * VD], lhsT=cT[:, kc, (t0 + i) * 128:(t0 + i + 1) * 128],
                                                                              rhs=w_uv[:, kc, h * VD:(h + 1) * VD], start=(kc == 0), stop=(kc == 1)),
                             r=[cT, w_uv], w=[bank])
                B.dve(lambda e, t0=t0, nt8=nt8, bank=bank: e.tensor_copy(out=V[:, t0:t0 + nt8, 0:VD], in_=bank[:, 0:nt8 * VD].rearrange("p (t v) -> p t v", v=VD)),
                      r=[bank], w=[V])
            for b in range(nblk):
                tiles = list(range(4 * b, min(4 * b + 4, ntp)))
                nq = 128 * len(tiles)
                Q = QTq[nq_i % 2]
                Ob = ps[6 + nq_i % 2]
                B.dma("sp", Q[:, 0:nq], QTd.t[h, :, b * 512:b * 512 + nq], r=[QTd], w=[Q])
                last_kt = tiles[-1]
                for kt in range(last_kt + 1):
                    kr = cfg.rows(kt)
                    Sb = ps[4 + nstep % 2]
                    P = PT[nstep % 3]
                    nstep += 1
                    B.pe(lambda e, kt=kt, kr=kr, Sb=Sb: e.matmul(Sb[0:kr, 0:nq], lhsT=KT[:, kt * 128:kt * 128 + kr], rhs=Q[:, 0:nq], start=True, stop=True),
                         r=[KT, Q], w=[Sb])
                    B.act(lambda e, kr=kr, Sb=Sb, P=P: e.activation(out=P[0:kr, 0:nq], in_=Sb[0:kr, 0:nq], func=AF.Exp, bias=negM[0:kr, h:h + 1], scale=scale),
                          r=[Sb, negM], w=[P])
                    if kt >= 4 * b:
                        mk = masks[kt - 4 * b]
                        B.pool(lambda e, kr=kr, P=P, mk=mk: e.tensor_tensor(out=P[0:kr, 0:nq], in0=P[0:kr, 0:nq], in1=mk[0:kr, 0:nq], op=ALU.mult), r=[P, mk], w=[P])
                    B.pe(lambda e, kt=kt, kr=kr, P=P: e.matmul(Ob[0:VD + 1, 0:nq], lhsT=V[0:kr, kt, :], rhs=P[0:kr, 0:nq], start=(kt == 0), stop=(kt == last_kt)),
                         r=[V, P], w=[Ob])
                O_ = Osb[nq_i % 2]
                On_ = On[nq_i % 2]
                B.act(lambda e: e.copy(out=O_[:, 0:nq], in_=Ob[0:VD + 1, 0:nq]), r=[Ob], w=[O_])
                B.pe(lambda e: e.matmul(ps[0][0:VD, 0:nq], lhsT=onesr[VD:VD + 1, :], rhs=O_[VD:VD + 1, 0:nq], start=True, stop=True), r=[onesr, O_], w=[ps[0]])
                B.dve(lambda e: e.reciprocal(out=rl[:, 0:nq], in_=ps[0][0:VD, 0:nq]), r=[ps[0]], w=[rl])
                B.dve(lambda e: e.tensor_tensor(out=On_[:, 0:nq], in0=O_[0:VD, 0:nq], in1=rl[:, 0:nq], op=ALU.mult), r=[O_, rl], w=[On_])
                B.dma("sp", OTd.t[h, :, b * 512:b * 512 + nq], On_[:, 0:nq], r=[On_], w=[OTd])
                nq_i += 1
        B.sy.barrier()


def mla_sample_attention(B, st, m, qs_b, w_uk, w_uv, scale):
    cfg, d, ps = B.cfg, B.d, B.ps
    nsq, srows, npg = cfg.nsq, cfg.srows, cfg.npages
    NK = npg * 128 + DEC_SEQ
    R = KVR + ROPE
    OTd, CNd = B.OTd, B.CNd
    nblk = (cfg.ntp + 3) // 4
    with ExitStack() as s2:
        wukT = B.sb("wukT", [NOPE, NH, KVR], BF16, s2)
        for h in range(NH):
            pv = B.psb(h % 2)
            for kc in range(2):
                B.pe(lambda e, h=h, kc=kc, pv=pv: e.transpose(pv[0:NOPE, kc * 128:(kc + 1) * 128], w_uk[:, kc, h * NOPE:(h + 1) * NOPE], B.identb[:, :]),
                     r=[w_uk, B.identb], w=[ps[h % 2]])
            B.act(lambda e, h=h, pv=pv: e.copy(out=wukT[:, h, :], in_=pv[0:NOPE, 0:KVR]), r=[ps[h % 2]], w=[wukT])
        qnT = B.sb("qnT", [NOPE, NH, srows], BF16, s2)
        pv = B.psb(2)
        for h in range(NH):
            B.pe(lambda e, h=h: e.transpose(pv[0:NOPE, h * srows:(h + 1) * srows], qs_b[0:srows, h, 0:NOPE], B.identb[0:srows, 0:srows]),
                 r=[qs_b, B.identb], w=[ps[2]])
        B.act(lambda e: e.copy(out=qnT[:, :, :], in_=pv[0:NOPE, 0:NH * srows].rearrange("p (h t) -> p h t", t=srows)), r=[ps[2]], w=[qnT])
        QL = B.sb("QL", [srows, NH, R], BF16, s2)
        for h in range(NH):
            bank = ps[3 + h % 2]
            B.pe(lambda e, h=h, bank=bank: e.matmul(bank[0:srows, 0:KVR], lhsT=qnT[:, h, :], rhs=wukT[:, h, :], start=True, stop=True), r=[qnT, wukT], w=[bank])
            B.act(lambda e, h=h, bank=bank: e.copy(out=QL[:, h, 0:KVR], in_=bank[0:srows, 0:KVR]), r=[bank], w=[QL])
        B.dve(lambda e: e.tensor_copy(out=QL[:, :, KVR:R], in_=qs_b[0:srows, :, NOPE:QK]), r=[qs_b], w=[QL])
        QLT = B.sb("QLT", [128, 3, srows, NH], BF16, s2)
        for c, (c0, cw) in enumerate(((0, 128), (128, 128), (256, 32))):
            for h0 in range(0, NH, 8):
                bank = ps[5 + (c + h0 // 8) % 2]
                pv = B.psb(5 + (c + h0 // 8) % 2)
                for hh in range(8):
                    h = h0 + hh
                    B.pe(lambda e, h=h, hh=hh, c0=c0, cw=cw, pv=pv: e.transpose(pv[0:cw, hh * srows:(hh + 1) * srows], QL[:, h, c0:c0 + cw], B.identb[0:srows, 0:srows]),
                         r=[QL, B.identb], w=[bank])
                B.act(lambda e, c=c, h0=h0, cw=cw, pv=pv: e.copy(out=QLT[0:cw, c, :, h0:h0 + 8].rearrange("p r h -> p h r"),
                                                              in_=pv[0:cw, 0:8 * srows].rearrange("p (h r) -> p h r", r=srows)), r=[bank], w=[QLT])
        ptb = B.sb("ptb", [128, nsq * npg], I32, s2)
        idx = B.sb("idx", [128, nsq * npg], I32, s2)
        pidx = B.sb("pidx", [128, 1], I32, s2)
        B.dma("sp", ptb[:, :], d["page_table"].t.rearrange("s g -> (s g)").rearrange("(o n) -> o n", o=1).broadcast_to([128, nsq * npg]), w=[ptb])
        B.pool(lambda e: e.iota(pidx[:], pattern=[[0, 1]], base=m * cfg.npool * 128, channel_multiplier=1), w=[pidx])
        B.pool(lambda e: e.tensor_scalar(out=idx[:], in0=ptb[:], scalar1=128, scalar2=None, op0=ALU.mult), r=[ptb], w=[idx])
        B.pool(lambda e: e.tensor_tensor(out=idx[:], in0=idx[:], in1=pidx[:].broadcast_to([128, nsq * npg]), op=ALU.add), r=[idx, pidx], w=[idx])
        mskf = B.sb("mskf", [64, DEC_SEQ], F32, s2)
        B.pool(lambda e: e.memset(mskf[:], 1.0), w=[mskf])
        B.pool(lambda e: e.affine_select(out=mskf[:], in_=mskf[:], pattern=[[-NH, DEC_SEQ]], compare_op=ALU.is_ge, fill=0.0, base=0, channel_multiplier=1),
               r=[mskf], w=[mskf])
        CP = [B.sb("CP", [128, npg + 1, R], BF16, s2) for _ in range(2)]
        CTs = [B.sb("CTs", [128, 3, 128], BF16, s2) for _ in range(3)]
        S_all = B.sb("S_all", [64, NK], F32, s2)
        Pb = B.sb("Pb", [64, NK], BF16, s2)
        Pn = B.sb("Pn", [64, DEC_SEQ], F32, s2)
        PTs = B.sb("PTs", [128, npg + 1, 64], BF16, s2)
        sm = B.sb("sm", [64, 8], F32, s2)
        OLs = B.sb("OLs", [64, KVR], BF16, s2)
        OLT = B.sb("OLT", [128, 2, NH, srows], BF16, s2)
        lat, kr_ = d["cache_mla_latent"], d["cache_mla_krope"]
        nct = 0
        for s in range(nsq):
            C = CP[s % 2]
            for g in range(npg):
                B.sy.op("pool", lambda e, g=g: e.indirect_dma_start(out=C[:, g, 0:KVR], out_offset=None, in_=lat.t,
                        in_offset=bass.IndirectOffsetOnAxis(ap=idx[:, s * npg + g:s * npg + g + 1], axis=0)), [idx, lat], [C], dma=True)
                B.sy.op("pool", lambda e, g=g: e.indirect_dma_start(out=C[:, g, KVR:R], out_offset=None, in_=kr_.t,
                        in_offset=bass.IndirectOffsetOnAxis(ap=idx[:, s * npg + g:s * npg + g + 1], axis=0)), [idx, kr_], [C], dma=True)
            B.dma("pool", C[0:DEC_SEQ, npg, :], CNd.t[s * DEC_SEQ:(s + 1) * DEC_SEQ, :], r=[CNd], w=[C])
            qcols = lambda c, kw: QLT[0:kw, c, s * DEC_SEQ:(s + 1) * DEC_SEQ, :].rearrange("p t h -> p (t h)")
            for g in range(npg + 1):
                kr = 128 if g < npg else DEC_SEQ
                CT_ = CTs[nct % 3]
                nct += 1
                bi = 1 + (g % 2)
                pv = B.psb(bi)
                for c, (c0, cw) in enumerate(((0, 128), (128, 128), (256, 32))):
                    B.pe(lambda e, g=g, kr=kr, c=c, c0=c0, cw=cw, pv=pv: e.transpose(pv[0:cw, c * 128:c * 128 + kr], C[0:kr, g, c0:c0 + cw], B.identb[0:kr, 0:kr]),
                         r=[C, B.identb], w=[ps[bi]])
                B.act(lambda e, kr=kr, CT_=CT_, pv=pv: e.copy(out=CT_[:, :, 0:kr], in_=pv[:, 0:384].rearrange("p (c k) -> p c k", k=128)[:, :, 0:kr]), r=[ps[bi]], w=[CT_])
                sb_i = 3 + (g // 4) % 2
                off = (g % 4) * 128
                for c, (c0, cw) in enumerate(((0, 128), (128, 128), (256, 32))):
                    B.pe(lambda e, c=c, cw=cw, kr=kr, CT_=CT_, sb_i=sb_i, off=off: e.matmul(ps[sb_i][0:64, off:off + kr], lhsT=qcols(c, cw), rhs=CT_[0:cw, c, 0:kr],
                                                                                        start=(c == 0), stop=(c == 2)), r=[QLT, CT_], w=[ps[sb_i]])
                if g % 4 == 3 or g == npg:
                    g0 = (g // 4) * 4
                    n = off + kr
                    B.dve(lambda e, g0=g0, n=n, sb_i=sb_i: e.tensor_scalar(out=S_all[:, g0 * 128:g0 * 128 + n], in0=ps[sb_i][0:64, 0:n], scalar1=scale, scalar2=None, op0=ALU.mult),
                          r=[ps[sb_i]], w=[S_all])
            B.dve(lambda e: e.tensor_reduce(out=sm[:, 0:1], in_=S_all[:, 0:NK], axis=AX.X, op=ALU.max), r=[S_all], w=[sm])
            B.dve(lambda e: e.tensor_scalar(out=sm[:, 1:2], in0=sm[:, 0:1], scalar1=-1.0, scalar2=None, op0=ALU.mult), r=[sm], w=[sm])
            B.act(lambda e: e.activation(out=Pb[:, 0:npg * 128], in_=S_all[:, 0:npg * 128], func=AF.Exp, bias=sm[:, 1:2], scale=1.0, accum_out=sm[:, 2:3]),
                  r=[S_all, sm], w=[Pb, sm])
            B.act(lambda e: e.activation(out=Pn[:, :], in_=S_all[:, npg * 128:NK], func=AF.Exp, bias=sm[:, 1:2], scale=1.0), r=[S_all, sm], w=[Pn])
            B.dve(lambda e: e.tensor_tensor(out=Pn[:, :], in0=Pn[:, :], in1=mskf[:, :], op=ALU.mult), r=[Pn, mskf], w=[Pn])
            B.dve(lambda e: e.tensor_reduce(out=sm[:, 3:4], in_=Pn[:, :], axis=AX.X, op=ALU.add), r=[Pn], w=[sm])
            B.dve(lambda e: e.tensor_copy(out=Pb[:, npg * 128:NK], in_=Pn[:, :]), r=[Pn], w=[Pb])
            B.dve(lambda e: e.tensor_tensor(out=sm[:, 4:5], in0=sm[:, 2:3], in1=sm[:, 3:4], op=ALU.add), r=[sm], w=[sm])
            B.dve(lambda e: e.reciprocal(out=sm[:, 5:6], in_=sm[:, 4:5]), r=[sm], w=[sm])
            for g0 in range(0, npg + 1, 8):
                n8 = min(8, npg + 1 - g0)
                bi = 5 + (g0 // 8) % 2
                pv = B.psb(bi)
                for i in range(n8):
                    g = g0 + i
                    kr = 128 if g < npg else DEC_SEQ
                    B.pe(lambda e, g=g, i=i, kr=kr, pv=pv: e.transpose(pv[0:kr, i * 64:(i + 1) * 64], Pb[:, g * 128:g * 128 + kr], B.identb[0:64, 0:64]),
                         r=[Pb, B.identb], w=[ps[bi]])
                nfull = n8 if g0 + n8 <= npg else n8 - 1
                if nfull:
                    B.act(lambda e, g0=g0, nfull=nfull, pv=pv: e.copy(out=PTs[:, g0:g0 + nfull, :], in_=pv[:, 0:nfull * 64].rearrange("p (g q) -> p g q", q=64)), r=[ps[bi]], w=[PTs])
                if nfull != n8:
                    B.act(lambda e, pv=pv, nfull=nfull: e.copy(out=PTs[0:DEC_SEQ, npg, :], in_=pv[0:DEC_SEQ, nfull * 64:(nfull + 1) * 64]), r=[ps[bi]], w=[PTs])
            for g in range(npg + 1):
                kr = 128 if g < npg else DEC_SEQ
                B.pe(lambda e, g=g, kr=kr: e.matmul(ps[7][0:64, 0:KVR], lhsT=PTs[0:kr, g, :], rhs=C[0:kr, g, 0:KVR], start=(g == 0), stop=(g == npg)), r=[PTs, C], w=[ps[7]])
            B.dve(lambda e: e.tensor_scalar(out=OLs[:, :], in0=ps[7][0:64, 0:KVR], scalar1=sm[:, 5:6], scalar2=None, op0=ALU.mult), r=[ps[7], sm], w=[OLs])
            pv = B.psb(0)
            for kc in range(2):
                B.pe(lambda e, kc=kc: e.transpose(pv[:, kc * 64:(kc + 1) * 64], OLs[:, kc * 128:(kc + 1) * 128], B.identb[0:64, 0:64]), r=[OLs, B.identb], w=[ps[0]])
            B.act(lambda e: e.copy(out=OLT[:, :, :, s * DEC_SEQ:(s + 1) * DEC_SEQ].rearrange("p k h t -> p k t h"),
                                   in_=pv[:, 0:128].rearrange("p (k t h) -> p k t h", k=2, t=DEC_SEQ)), r=[ps[0]], w=[OLT])
        OTs = B.sb("OTs", [VD, NH, srows], BF16, s2)
        for h in range(NH):
            bank = ps[1 + h % 2]
            for kc in range(2):
                B.pe(lambda e, h=h, kc=kc, bank=bank: e.matmul(bank[0:VD, 0:srows], lhsT=w_uv[:, kc, h * VD:(h + 1) * VD], rhs=OLT[:, kc, h, :], start=(kc == 0), stop=(kc == 1)),
                     r=[w_uv, OLT], w=[bank])
            B.act(lambda e, h=h, bank=bank: e.copy(out=OTs[:, h, :], in_=bank[0:VD, 0:srows]), r=[bank], w=[OTs])
        B.dma("sp", OTd.t.rearrange("h v t -> v h t")[:, :, nblk * 512:nblk * 512 + srows], OTs[:, :, :], r=[OTs], w=[OTd])
        B.sy.barrier()


def mla_outproj(B, li, m, src, dst):
    cfg, d, ps = B.cfg, B.d, B.ps
    ntp, nt = cfg.ntp, cfg.nt
    nblk = (ntp + 3) // 4
    OTd = B.OTd
    with ExitStack() as st:
        w_o = B.sb("w_o", [VD, NH, D], BF16, st)
        wv = d["mla_w_o"].t[m].rearrange("(h v) n -> v h n", v=VD)
        for h in range(NH):
            B.dma("pool", w_o[:, h, :], wv[:, h, :], w=[w_o])
        G = B.sb("G", [128, D], F32, st)
        Bt = B.sb("Bt", [128, D], F32, st)
        B.load_bcast(G, d["ln_g"].t[li, 1])
        B.load_bcast(Bt, d["ln_b"].t[li, 1])
        xs = [B.sb("xs", [128, D], F32, st) for _ in range(2)]
        OTg = [B.sb("OTg", [VD, NH, 512], BF16, st) for _ in range(2)]
        tmp = dict(xa=B.sb("xa", [128, D], F32, st), y=B.sb("y", [128, D], F32, st), junk=B.sb("junk", [128, D], F32, st),
                   st=B.sb("st", [128, 8], F32, st))
        xo = [B.sb("xo", [128, D], F32, st) for _ in range(2)]
        for t_ in xs:
            B.dve(lambda e, t_=t_: e.memset(t_[:], 0.0), w=[t_])
        for j in range(nt):
            x = xs[j % 2]
            for (r0, r1), ap, tt in src(j):
                B.dma("sp", x[r0:r1, :], ap, r=[tt], w=[x])
            if j < ntp:
                blk, off = j // 4, (j % 4) * 128
                OT = OTg[blk % 2]
                if j % 4 == 0:
                    n = min(512, ntp * 128 - blk * 512)
                    B.dma("sp", OT[:, :, 0:n], OTd.t.rearrange("h v t -> v h t")[:, :, blk * 512:blk * 512 + n], r=[OTd], w=[OT])
                M = 128
            else:
                OT = OTg[(nblk) % 2]
                off, M = 0, cfg.srows
                B.dma("sp", OT[:, :, 0:M], OTd.t.rearrange("h v t -> v h t")[:, :, nblk * 512:nblk * 512 + M], r=[OTd], w=[OT])
            for hf in range(2):
                for h in range(NH):
                    B.pe(lambda e, h=h, hf=hf, OT=OT, off=off, M=M: e.matmul(ps[6 + hf][0:M, :], lhsT=OT[:, h, off:off + M], rhs=w_o[:, h, hf * 512:(hf + 1) * 512],
                                                                           start=(h == 0), stop=(h == NH - 1)), r=[OT, w_o], w=[ps[6 + hf]])
            o = xo[j % 2]
            B.resid_ln(x, [ps[6], ps[7]], 1.0, G, Bt, o, tmp)
            for (r0, r1), ap, tt in dst(j):
                B.dma("sp", ap, o[r0:r1, :], r=[o], w=[tt])
        B.sy.barrier()


def resid_ln_sb(B, x, h, G, Bt, out, tmp):
    y, junk, st = tmp["y"], tmp["junk"], tmp["st"]
    B.dve(lambda e: e.memset(st[:, 1:2], 0.0), w=[st])
    B.dve(lambda e: e.scalar_tensor_tensor(out=y[:], in0=x[:], scalar=float(B.cfg.alpha), in1=h[:], op0=ALU.mult, op1=ALU.add, accum_out=st[:, 0:1]),
          r=[x, h], w=[y, st])
    B.ln_core(y, G, Bt, out, junk, st)


def stage_s5(B, li, m, src, dst):
    cfg, d, ps = B.cfg, B.d, B.ps
    ntp, nt, nsq, srows = cfg.ntp, cfg.nt, cfg.nsq, cfg.srows
    with ExitStack() as st:
        BbT = [B.sb("BbT", [128, 32, 128], BF16, st) for _ in range(2)]
        CTm = [B.sb("CTm", [128, 32, 128], BF16, st) for _ in range(2)]
        prm = B.sb("prm", [128, 12, 32], F32, st)
        Dp = B.sb("Dp", [128, 8], F32, st)
        wv = B.sb("wv", [128, 8, D], BF16, st)
        wg = B.sb("wg", [128, 8, D], BF16, st)
        B.load_w(wv, d["s5_wv"].t[m])
        B.load_w(wg, d["s5_wg"].t[m])
        G = B.sb("G", [128, D], F32, st)
        Bt = B.sb("Bt", [128, D], F32, st)
        B.load_bcast(G, d["ln_g"].t[li, 1])
        B.load_bcast(Bt, d["ln_b"].t[li, 1])
        B.load_T(Dp, Dp[:, :], d["s5_d"].t[m].rearrange("(c p) -> c p", p=128), 8)
        P = lambda k: prm[:, k, :]
        bufs = s5_alloc(B, st)
        sp = ExitStack()
        CS = B.sb("CS", [128, 32 * 128], F32, sp)
        SN = B.sb("SN", [128, 32 * 128], F32, sp)
        with ExitStack() as s1:
            raw = B.sb("raw", [32, 3, 128], F32, s1)
            ldt = B.sb("ldt", [32, 2], F32, s1)
            B.dma("sp", raw[:, 0, :], d["s5_lam_re"].t[m].rearrange("(s g) p -> s (g p)", g=2), w=[raw])
            B.dma("sp", raw[:, 1, :], d["s5_lam_im"].t[m].rearrange("(s g) p -> s (g p)", g=2), w=[raw])
            B.dma("sp", ldt[:, :], d["s5_log_dt"].t[m].rearrange("(s g) -> s g", g=2), w=[ldt])
            B.dve(lambda e: e.tensor_copy(out=raw[:, 2, :].rearrange("s (g p) -> s g p", g=2), in_=ldt[:, :].unsqueeze(2).broadcast_to([32, 2, 64])), r=[ldt], w=[raw])
            for k in range(3):
                B.pe(lambda e, k=k: e.transpose(ps[0][:, k * 32:(k + 1) * 32], raw[:, k, :], B.ident[0:32, 0:32]), r=[raw, B.ident], w=[ps[0]])
            B.dve(lambda e: e.tensor_copy(out=prm[:, 0:3, :], in_=ps[0][:, 0:96].rearrange("p (k s) -> p k s", k=3)), r=[ps[0]], w=[prm])
            B.act(lambda e: e.activation(out=P(2), in_=P(2), func=AF.Exp), r=[prm], w=[prm])
            B.dve(lambda e: e.tensor_tensor(out=P(9), in0=P(0), in1=P(2), op=ALU.mult), r=[prm], w=[prm])
            B.act(lambda e: e.activation(out=P(3), in_=P(9), func=AF.Exp), r=[prm], w=[prm])
            B.dve(lambda e: e.tensor_tensor(out=P(4), in0=P(1), in1=P(2), op=ALU.mult), r=[prm], w=[prm])
            tf = B.sb("tf", [128, 4096], F32, s1)
            ti = B.sb("ti", [128, 4096], I32, s1)
            a32 = B.sb("a32", [128, 4, 32], F32, s1)
            B.dve(lambda e: e.tensor_copy(out=a32[:, 0, :], in_=P(4)), r=[prm], w=[a32])
            B.dve(lambda e: e.tensor_scalar(out=a32[:, 1, :], in0=P(4), scalar1=math.pi / 2, scalar2=None, op0=ALU.add), r=[prm], w=[a32])
            a32f = T(a32.t[:].rearrange("p k s -> p (k s)"), "a32f")
            a32f.wr, a32f.rd = a32.wr, a32.rd
            sc_ = B.sb("sc_", [128, 64], F32, s1)
            range_reduce_sin_signed(B, a32f, sc_, 64, tf, ti)
            B.dve(lambda e: e.tensor_tensor(out=P(6), in0=P(3), in1=sc_[:, 0:32], op=ALU.mult), r=[prm, sc_], w=[prm])
            B.dve(lambda e: e.tensor_tensor(out=P(5), in0=P(3), in1=sc_[:, 32:64], op=ALU.mult), r=[prm, sc_], w=[prm])
            B.dve(lambda e: e.tensor_tensor(out=P(9), in0=P(0), in1=P(0), op=ALU.mult), r=[prm], w=[prm])
            B.dve(lambda e: e.tensor_tensor(out=P(10), in0=P(1), in1=P(1), op=ALU.mult), r=[prm], w=[prm])
            B.dve(lambda e: e.tensor_tensor(out=P(9), in0=P(9), in1=P(10), op=ALU.add), r=[prm], w=[prm])
            B.dve(lambda e: e.reciprocal(out=P(9), in_=P(9)), r=[prm], w=[prm])
            B.dve(lambda e: e.tensor_scalar(out=P(10), in0=P(5), scalar1=-1.0, scalar2=None, op0=ALU.add), r=[prm], w=[prm])
            B.dve(lambda e: e.tensor_tensor(out=P(7), in0=P(10), in1=P(0), op=ALU.mult), r=[prm], w=[prm])
            B.dve(lambda e: e.tensor_tensor(out=P(11), in0=P(6), in1=P(1), op=ALU.mult), r=[prm], w=[prm])
            B.dve(lambda e: e.tensor_tensor(out=P(7), in0=P(7), in1=P(11), op=ALU.add), r=[prm], w=[prm])
            B.dve(lambda e: e.tensor_tensor(out=P(7), in0=P(7), in1=P(9), op=ALU.mult), r=[prm], w=[prm])
            B.dve(lambda e: e.tensor_tensor(out=P(8), in0=P(6), in1=P(0), op=ALU.mult), r=[prm], w=[prm])
            B.dve(lambda e: e.tensor_tensor(out=P(11), in0=P(10), in1=P(1), op=ALU.mult), r=[prm], w=[prm])
            B.dve(lambda e: e.tensor_tensor(out=P(8), in0=P(8), in1=P(11), op=ALU.subtract), r=[prm], w=[prm])
            B.dve(lambda e: e.tensor_tensor(out=P(8), in0=P(8), in1=P(9), op=ALU.mult), r=[prm], w=[prm])
            t1 = B.sb("t1", [128, 128], F32, s1)
            B.pool(lambda e: e.iota(t1[:], pattern=[[1, 128]], base=1, channel_multiplier=0, allow_small_or_imprecise_dtypes=True), w=[t1])
            ang = B.sb("ang", [128, 4096], F32, s1)
            for s_ in range(32):
                B.dve(lambda e, s_=s_: e.tensor_scalar(out=ang[:, s_ * 128:(s_ + 1) * 128], in0=t1[:], scalar1=prm[:, 4, s_:s_ + 1], scalar2=None, op0=ALU.mult), r=[t1, prm], w=[ang])
            range_reduce_sin_signed(B, ang, SN, 4096, tf, ti)
            B.dve(lambda e: e.tensor_scalar(out=ang[:], in0=ang[:], scalar1=math.pi / 2, scalar2=None, op0=ALU.add), r=[ang], w=[ang])
            range_reduce_sin_signed(B, ang, CS, 4096, tf, ti)
            B.sy.barrier()
        with ExitStack() as s1:
            Braw = [B.sb("Braw", [128, 32, 16], F32, s1) for _ in range(2)]
            bb = [B.sb("bb", [128, 32, 16], F32, s1) for _ in range(2)]
            tb = B.sb("tb", [128, 32, 16], F32, s1)
            for k, nm in enumerate(("s5_b_re", "s5_b_im")):
                src_v = d[nm].t[m].rearrange("(s g) p i -> g p s i", g=2)
                for g2 in range(2):
                    B.dma("sp", Braw[k][g2 * 64:(g2 + 1) * 64, :, :], src_v[g2], w=[Braw[k]])
            cre = prm[:, 7, :].unsqueeze(2).broadcast_to([128, 32, 16])
            cim = prm[:, 8, :].unsqueeze(2).broadcast_to([128, 32, 16])
            B.dve(lambda e: e.tensor_tensor(out=bb[0][:], in0=Braw[0][:], in1=cre, op=ALU.mult), r=[Braw[0], prm], w=[bb[0]])
            B.dve(lambda e: e.tensor_tensor(out=tb[:], in0=Braw[1][:], in1=cim, op=ALU.mult), r=[Braw[1], prm], w=[tb])
            B.dve(lambda e: e.tensor_tensor(out=bb[0][:], in0=bb[0][:], in1=tb[:], op=ALU.subtract), r=[bb[0], tb], w=[bb[0]])
            B.dve(lambda e: e.tensor_tensor(out=bb[1][:], in0=Braw[1][:], in1=cre, op=ALU.mult), r=[Braw[1], prm], w=[bb[1]])
            B.dve(lambda e: e.tensor_tensor(out=tb[:], in0=Braw[0][:], in1=cim, op=ALU.mult), r=[Braw[0], prm], w=[tb])
            B.dve(lambda e: e.tensor_tensor(out=bb[1][:], in0=bb[1][:], in1=tb[:], op=ALU.add), r=[bb[1], tb], w=[bb[1]])
            Ep = B.sb("Ep", [128, 32, 128], F32, s1)
            for k in range(2):
                B.pool(lambda e: e.memset(Ep[:], 0.0), w=[Ep])
                E4 = Ep[:].rearrange("p (c q) n -> p c q n", q=4)
                b4 = bb[k][:].rearrange("p (c q) i -> p c q i", q=4)
                for g2 in range(2):
                    for q in range(4):
                        B.pool(lambda e, g2=g2, q=q, E4=E4, b4=b4: e.tensor_copy(out=E4[g2 * 64:(g2 + 1) * 64, :, q, q * 32 + g2 * 16:q * 32 + g2 * 16 + 16],
                                                                                 in_=b4[g2 * 64:(g2 + 1) * 64, :, q, :]), r=[bb[k]], w=[Ep])
                for s0 in range(0, 32, 4):
                    bank = ps[1 + (s0 // 4) % 2]
                    for i in range(4):
                        B.pe(lambda e, s0=s0, i=i, bank=bank: e.transpose(bank[:, i * 128:(i + 1) * 128], Ep[:, s0 + i, :], B.ident[:, :]), r=[Ep, B.ident], w=[bank])
                    B.act(lambda e, s0=s0, k=k, bank=bank: e.copy(out=BbT[k][:, s0:s0 + 4, :], in_=bank[:, :].rearrange("p (s n) -> p s n", n=128)), r=[bank], w=[BbT[k]])
            B.sy.barrier()
        with ExitStack() as s1:
            selc = B.sb("selc", [16, 8, 128], BF16, s1)
            B.pool(lambda e: e.memset(selc[:], 0.0), w=[selc])
            for q in range(4):
                for g2 in range(2):
                    B.pool(lambda e, q=q, g2=g2: e.tensor_copy(out=selc[:, q * 2 + g2, q * 32 + g2 * 16:q * 32 + g2 * 16 + 16], in_=B.identb[0:16, 0:16]), r=[B.identb], w=[selc])
            F = [B.sb("F", [16, 32, 128], BF16, s1) for _ in range(2)]
            for k, nm in enumerate(("s5_c_re", "s5_c_im")):
                src_v = d[nm].t[m].rearrange("(s g) o p -> g o s p", g=2)
                for g2 in range(2):
                    B.pool(lambda e, g2=g2: e.memset(F[g2][:], 0.0), w=[F[g2]])
                    B.dma("pool", F[g2][:, :, g2 * 64:(g2 + 1) * 64], src_v[g2], w=[F[g2]])
                for s0 in range(0, 32, 4):
                    bank = ps[3 + (s0 // 4) % 2]
                    for i in range(4):
                        s_ = s0 + i
                        q = s_ % 4
                        for g2 in range(2):
                            B.pe(lambda e, s_=s_, i=i, g2=g2, q=q, bank=bank: e.matmul(bank[:, i * 128:(i + 1) * 128], lhsT=F[g2][:, s_, :], rhs=selc[:, q * 2 + g2, :],
                                                                                   start=(g2 == 0), stop=(g2 == 1)), r=[F[g2], selc], w=[bank])
                    B.act(lambda e, s0=s0, k=k, bank=bank: e.activation(out=CTm[k][:, s0:s0 + 4, :], in_=bank[:, :].rearrange("p (s n) -> p s n", n=128), func=AF.Copy,
                                                                        scale=(1.0 if k == 0 else -1.0)), r=[bank], w=[CTm[k]])
            B.sy.barrier()
        s5_body(B, st, sp, bufs, li, m, src, dst, BbT, CTm, CS, SN, prm, Dp, wv, wg, G, Bt)


def range_reduce_sin_signed(B, ang, out, n, tmpf, tmpi):
    range_reduce_sin(B, ang, out, n, tmpf, tmpi)


def s5_alloc(B, st):
    bufs = {}
    bufs["xs"] = [B.sb("xs", [128, D], F32, st) for _ in range(2)]
    bufs["uT"] = B.sb("uT", [128, 8, 128], F32, st)
    bufs["uTb"] = B.sb("uTb", [128, 8, 128], BF16, st)
    bufs["HS"] = [B.sb("HS", [128, 32], F32, st) for _ in range(2)]
    bufs["yT"] = B.sb("yT", [128, 8, 128], F32, st)
    bufs["gt"] = B.sb("gt", [128, D], F32, st)
    bufs["zT"] = B.sb("zT", [128, 8, 128], BF16, st)
    bufs["sgt"] = B.sb("sgt", [128, D], F32, st)
    bufs["hbuf"] = B.sb("hbuf", [128, D], F32, st)
    bufs["y"] = B.sb("y", [128, D], F32, st)
    bufs["st"] = B.sb("st", [128, 8], F32, st)
    bufs["xo"] = [B.sb("xo", [128, D], F32, st) for _ in range(2)]
    return bufs


def s5_body(B, st, sp, bufs, li, m, src, dst, BbT, CTm, CS, SN, prm, Dp, wv, wg, G, Bt):
    cfg, d, ps = B.cfg, B.d, B.ps
    ntp, nt, nsq, srows = cfg.ntp, cfg.nt, cfg.nsq, cfg.srows
    xs, uT, uTb, HS, yT, gt, zT, sgt, hbuf, xo = (bufs[k] for k in ("xs", "uT", "uTb", "HS", "yT", "gt", "zT", "sgt", "hbuf", "xo"))
    bu = [B.sb("bu", [128, 512], F32, sp) for _ in range(2)]
    mt = [B.sb("mt", [128, 512], F32, sp) for _ in range(4)]
    z = [B.sb("z", [128, 512], F32, sp) for _ in range(2)]
    gs = [B.sb("gs", [128, 512], F32, sp) for _ in range(2)]
    hh = [B.sb("hh", [128, 512], F32, sp) for _ in range(2)]
    hb = [B.sb("hb", [128, 512], BF16, sp) for _ in range(2)]
    tmp = dict(y=bufs["y"], junk=gt, st=bufs["st"])
    for t_ in xs + HS:
        B.dve(lambda e, t_=t_: e.memset(t_[:], 0.0), w=[t_])
    ss = ExitStack()
    Hs = Hall = BUs = sm_ = None

    def load(j, buf):
        for (r0, r1), ap, tt in src(j):
            B.dma("sp", buf[r0:r1, :], ap, r=[tt], w=[buf])

    load(0, xs[0])
    for j in range(nt):
        x = xs[j % 2]
        rows = cfg.rows(j)
        sample = (j == ntp)
        N = srows if sample else 128
        if j + 1 < nt:
            load(j + 1, xs[(j + 1) % 2])
        if sample:
            B.sy.barrier()
            sp.close()
            Hs = [B.sb("Hs", [128, 32, nsq], F32, ss) for _ in range(2)]
            Hall = [B.sb("Hall", [128, 32, srows], F32, ss) for _ in range(2)]
            BUs = [B.sb("BUs", [128, 32, srows], F32, ss) for _ in range(2)]
            sm_ = [B.sb("sm_", [128, 32, nsq], F32, ss) for _ in range(4)]
            hin = B.sb("hin", [nsq, 4096], F32, ss)
            for k, nm in enumerate(("state_s5_re", "state_s5_im")):
                B.dma("sp", hin[:, :], d[nm].t[m].rearrange("s g p -> s (g p)"), w=[hin])
                for s_ in range(32):
                    B.pe(lambda e, s_=s_: e.transpose(ps[0][:, s_ * nsq:(s_ + 1) * nsq], hin[:, s_ * 128:(s_ + 1) * 128], B.ident[0:nsq, 0:nsq]), r=[hin, B.ident], w=[ps[0]])
                B.dve(lambda e, k=k: e.tensor_copy(out=Hs[k][:, :, :], in_=ps[0][:, 0:32 * nsq].rearrange("p (s q) -> p s q", q=nsq)), r=[ps[0]], w=[Hs[k]])
        for h2 in range(2):
            for c in range(4):
                B.pe(lambda e, h2=h2, c=c: e.transpose(ps[1 + h2][:, c * 128:(c + 1) * 128], x[:, (h2 * 4 + c) * 128:(h2 * 4 + c + 1) * 128], B.ident[:, :]), r=[x, B.ident], w=[ps[1 + h2]])
            B.act(lambda e, h2=h2: e.copy(out=uT[:, h2 * 4:(h2 + 1) * 4, :], in_=ps[1 + h2][:, :].rearrange("p (c t) -> p c t", t=128)), r=[ps[1 + h2]], w=[uT])
        B.pool(lambda e: e.tensor_copy(out=uTb[:], in_=uT[:]), r=[uT], w=[uTb])
        for c in range(8):
            for k in range(2):
                bank = ps[3 + k]
                for q in range(4):
                    B.pe(lambda e, k=k, q=q, c=c, bank=bank: e.matmul(bank[:, q * 128:q * 128 + N], lhsT=BbT[k][:, 4 * c + q, :], rhs=uTb[:, c, 0:N], start=True, stop=True),
                         r=[BbT[k], uTb], w=[bank])
            if sample:
                for k in range(2):
                    B.act(lambda e, k=k, c=c: e.copy(out=BUs[k][:, 4 * c:4 * c + 4, :], in_=ps[3 + k][:, :].rearrange("p (q t) -> p q t", t=128)[:, :, 0:N]), r=[ps[3 + k]], w=[BUs[k]])
                continue
            for k in range(2):
                B.act(lambda e, k=k: e.copy(out=bu[k][:], in_=ps[3 + k][:, :]), r=[ps[3 + k]], w=[bu[k]])
            cs = CS[:, c * 512:(c + 1) * 512]
            sn = SN[:, c * 512:(c + 1) * 512]
            B.dve(lambda e: e.tensor_tensor(out=mt[0][:], in0=bu[0][:], in1=cs, op=ALU.mult), r=[bu[0], CS], w=[mt[0]])
            B.pool(lambda e: e.tensor_tensor(out=mt[1][:], in0=bu[1][:], in1=sn, op=ALU.mult), r=[bu[1], SN], w=[mt[1]])
            B.dve(lambda e: e.tensor_tensor(out=mt[2][:], in0=bu[1][:], in1=cs, op=ALU.mult), r=[bu[1], CS], w=[mt[2]])
            B.pool(lambda e: e.tensor_tensor(out=mt[3][:], in0=bu[0][:], in1=sn, op=ALU.mult), r=[bu[0], SN], w=[mt[3]])
            B.dve(lambda e: e.tensor_tensor(out=z[0][:], in0=mt[0][:], in1=mt[1][:], op=ALU.add), r=[mt[0], mt[1]], w=[z[0]])
            B.dve(lambda e: e.tensor_tensor(out=z[1][:], in0=mt[2][:], in1=mt[3][:], op=ALU.subtract), r=[mt[2], mt[3]], w=[z[1]])
            for k in range(2):
                for q in range(4):
                    s_ = 4 * c + q
                    B.dve(lambda e, k=k, q=q, s_=s_: e.tensor_tensor_scan(out=gs[k][:, q * 128:(q + 1) * 128], data0=prm[:, 3, s_:s_ + 1].broadcast_to([128, 128]),
                                                                          data1=z[k][:, q * 128:(q + 1) * 128], initial=HS[k][:, s_:s_ + 1], op0=ALU.mult, op1=ALU.add),
                          r=[prm, z[k], HS[k]], w=[gs[k]])
            B.dve(lambda e: e.tensor_tensor(out=mt[0][:], in0=gs[0][:], in1=cs, op=ALU.mult), r=[gs[0], CS], w=[mt[0]])
            B.pool(lambda e: e.tensor_tensor(out=mt[1][:], in0=gs[1][:], in1=sn, op=ALU.mult), r=[gs[1], SN], w=[mt[1]])
            B.dve(lambda e: e.tensor_tensor(out=mt[2][:], in0=gs[0][:], in1=sn, op=ALU.mult), r=[gs[0], SN], w=[mt[2]])
            B.pool(lambda e: e.tensor_tensor(out=mt[3][:], in0=gs[1][:], in1=cs, op=ALU.mult), r=[gs[1], CS], w=[mt[3]])
            B.dve(lambda e: e.tensor_tensor(out=hh[0][:], in0=mt[0][:], in1=mt[1][:], op=ALU.subtract), r=[mt[0], mt[1]], w=[hh[0]])
            B.dve(lambda e: e.tensor_tensor(out=hh[1][:], in0=mt[2][:], in1=mt[3][:], op=ALU.add), r=[mt[2], mt[3]], w=[hh[1]])
            for k in range(2):
                B.dve(lambda e, k=k, c=c: e.tensor_copy(out=HS[k][:, 4 * c:4 * c + 4], in_=hh[k][:].rearrange("p (q t) -> p q t", t=128)[:, :, rows - 1]), r=[hh[k]], w=[HS[k]])
                B.act(lambda e, k=k: e.copy(out=hb[k][:], in_=hh[k][:]), r=[hh[k]], w=[hb[k]])
            yb = ps[5 + c % 2]
            for q in range(4):
                for k in range(2):
                    B.pe(lambda e, q=q, k=k, c=c, yb=yb: e.matmul(yb[:, 0:128], lhsT=CTm[k][:, 4 * c + q, :], rhs=hb[k][:, q * 128:(q + 1) * 128],
                                                                 start=(q == 0 and k == 0), stop=(q == 3 and k == 1)), r=[CTm[k], hb[k]], w=[yb])
            B.dve(lambda e, c=c, yb=yb: e.scalar_tensor_tensor(out=yT[:, c, :], in0=uT[:, c, :], scalar=Dp[:, c:c + 1], in1=yb[:, 0:128], op0=ALU.mult, op1=ALU.add),
                  r=[uT, Dp, yb], w=[yT])
        if sample:
            are = prm[:, 5, :].unsqueeze(2).broadcast_to([128, 32, nsq])
            aim = prm[:, 6, :].unsqueeze(2).broadcast_to([128, 32, nsq])
            for t in range(DEC_SEQ):
                bt = [BUs[k][:].rearrange("p s (q t) -> p s q t", t=DEC_SEQ)[:, :, :, t] for k in range(2)]
                B.dve(lambda e: e.tensor_tensor(out=sm_[0][:], in0=Hs[0][:], in1=are, op=ALU.mult), r=[Hs[0], prm], w=[sm_[0]])
                B.dve(lambda e: e.tensor_tensor(out=sm_[1][:], in0=Hs[1][:], in1=aim, op=ALU.mult), r=[Hs[1], prm], w=[sm_[1]])
                B.dve(lambda e: e.tensor_tensor(out=sm_[2][:], in0=Hs[1][:], in1=are, op=ALU.mult), r=[Hs[1], prm], w=[sm_[2]])
                B.dve(lambda e: e.tensor_tensor(out=sm_[3][:], in0=Hs[0][:], in1=aim, op=ALU.mult), r=[Hs[0], prm], w=[sm_[3]])
                B.dve(lambda e: e.tensor_tensor(out=sm_[0][:], in0=sm_[0][:], in1=sm_[1][:], op=ALU.subtract), r=[sm_[0], sm_[1]], w=[sm_[0]])
                B.dve(lambda e: e.tensor_tensor(out=sm_[2][:], in0=sm_[2][:], in1=sm_[3][:], op=ALU.add), r=[sm_[2], sm_[3]], w=[sm_[2]])
                B.dve(lambda e, bt=bt: e.tensor_tensor(out=Hs[0][:], in0=sm_[0][:], in1=bt[0], op=ALU.add), r=[sm_[0], BUs[0]], w=[Hs[0]])
                B.dve(lambda e, bt=bt: e.tensor_tensor(out=Hs[1][:], in0=sm_[2][:], in1=bt[1], op=ALU.add), r=[sm_[2], BUs[1]], w=[Hs[1]])
                for k in range(2):
                    B.dve(lambda e, k=k, t=t: e.tensor_copy(out=Hall[k][:].rearrange("p s (q t) -> p s q t", t=DEC_SEQ)[:, :, :, t], in_=Hs[k][:]), r=[Hs[k]], w=[Hall[k]])
            hbs = [B.sb("hbs", [128, 32, srows], BF16, ss) for _ in range(2)]
            for k in range(2):
                B.act(lambda e, k=k: e.copy(out=hbs[k][:], in_=Hall[k][:]), r=[Hall[k]], w=[hbs[k]])
            for c in range(8):
                yb = ps[5 + c % 2]
                for q in range(4):
                    for k in range(2):
                        B.pe(lambda e, q=q, k=k, c=c, yb=yb: e.matmul(yb[:, 0:N], lhsT=CTm[k][:, 4 * c + q, :], rhs=hbs[k][:, 4 * c + q, :],
                                                                     start=(q == 0 and k == 0), stop=(q == 3 and k == 1)), r=[CTm[k], hbs[k]], w=[yb])
                B.dve(lambda e, c=c, yb=yb: e.scalar_tensor_tensor(out=yT[:, c, 0:N], in0=uT[:, c, 0:N], scalar=Dp[:, c:c + 1], in1=yb[:, 0:N], op0=ALU.mult, op1=ALU.add),
                      r=[uT, Dp, yb], w=[yT])
        yf = yT[:].rearrange("p c t -> p (c t)")
        B.pool(lambda e: e.tensor_tensor(out=gt[:], in0=yf, in1=yf, op=ALU.mult), r=[yT], w=[gt])
        B.dve(lambda e: e.tensor_scalar(out=gt[:], in0=gt[:], scalar1=0.044715, scalar2=1.0, op0=ALU.mult, op1=ALU.add), r=[gt], w=[gt])
        B.dve(lambda e: e.tensor_tensor(out=gt[:], in0=gt[:], in1=yf, op=ALU.mult), r=[gt, yT], w=[gt])
        B.act(lambda e: e.activation(out=gt[:], in_=gt[:], func=AF.Sigmoid, scale=2.0 * math.sqrt(2.0 / math.pi)), r=[gt], w=[gt])
        B.dve(lambda e: e.tensor_tensor(out=zT[:].rearrange("p c t -> p (c t)"), in0=gt[:], in1=yf, op=ALU.mult), r=[gt, yT], w=[zT])
        for hf in range(2):
            B.linear(ps[1 + hf], zT, wg, hf * 512, (hf + 1) * 512, 8)
            B.act(lambda e, hf=hf: e.activation(out=sgt[:, hf * 512:(hf + 1) * 512], in_=ps[1 + hf][:, :], func=AF.Sigmoid), r=[ps[1 + hf]], w=[sgt])
            B.linear(ps[6 + hf], zT, wv, hf * 512, (hf + 1) * 512, 8)
            B.dve(lambda e, hf=hf: e.tensor_tensor(out=hbuf[:, hf * 512:(hf + 1) * 512], in0=ps[6 + hf][:, :], in1=sgt[:, hf * 512:(hf + 1) * 512], op=ALU.mult),
                  r=[ps[6 + hf], sgt], w=[hbuf])
        o = xo[j % 2]
        resid_ln_sb(B, x, hbuf, G, Bt, o, tmp)
        for (r0, r1), ap, tt in dst(j):
            B.dma("sp", ap, o[r0:r1, :], r=[o], w=[tt])
    for k, nm in enumerate(("re_p", "im_p")):
        B.pe(lambda e, k=k: e.transpose(ps[0][0:32, k * 128:(k + 1) * 128], HS[k][:, :], B.ident[:, :]), r=[HS[k], B.ident], w=[ps[0]])
    hso = B.sb("hso", [32, 256], F32, ss)
    B.dve(lambda e: e.tensor_copy(out=hso[:], in_=ps[0][0:32, 0:256]), r=[ps[0]], w=[hso])
    for k, nm in enumerate(("re_p", "im_p")):
        B.dma("sp", d[nm].t[m].rearrange("(s g) p -> s (g p)", g=2), hso[:, k * 128:(k + 1) * 128], r=[hso], w=[d[nm]])
    hout = hin
    for k, nm in enumerate(("re_s", "im_s")):
        for s0 in range(0, 32, 4):
            bank = ps[1 + (s0 // 4) % 2]
            for i in range(4):
                B.pe(lambda e, k=k, s0=s0, i=i, bank=bank: e.transpose(bank[0:nsq, i * 128:(i + 1) * 128], Hs[k][:, s0 + i, :], B.ident[:, :]), r=[Hs[k], B.ident], w=[bank])
            B.act(lambda e, s0=s0, bank=bank: e.copy(out=hout[:, s0 * 128:(s0 + 4) * 128], in_=bank[0:nsq, :]), r=[bank], w=[hout])
        B.dma("sp", d[nm].t[m].rearrange("s g p -> s (g p)"), hout[:, :], r=[hout], w=[d[nm]])
    B.sy.barrier()
    ss.close()


def stage_rwkv(B, li, m, src, dst):
    cfg, d, ps = B.cfg, B.d, B.ps
    ntp, nt, nsq, srows = cfg.ntp, cfg.nt, cfg.nsq, cfg.srows
    Xin, Xint = src.X, src.Xt
    if not hasattr(B, "RWd"):
        B.RWd = B.dram_scr("RWd", [6, 128, D], F32)
        B.YSd = B.dram_scr("YSd", [128, D], F32)
    RWd, YSd = B.RWd, B.YSd
    with ExitStack() as st:
        W = {}
        for nm in ("wr", "wk", "wv", "wo"):
            W[nm] = B.sb(nm, [128, 8, D], BF16, st)
            B.load_w(W[nm], d["rw_" + nm].t[m])
        for nm, n in (("w1", 64), ("a1", 64), ("g1", 128)):
            W[nm] = B.sb(nm, [128, 8, n], BF16, st)
            B.load_w(W[nm], d["rw_" + nm].t[m])
        for nm, k in (("w2", 64), ("a2", 64), ("g2", 128)):
            W[nm] = B.sb(nm, [k, 1, D], BF16, st)
            B.load_w(W[nm], d["rw_" + nm].t[m])
        MU = B.sb("MU", [128, 6, 8], F32, st)
        B.load_T(MU, MU[:, :, :].rearrange("p j c -> p (j c)"), d["rw_mu"].t[m].rearrange("j (c p) -> (j c) p", p=128), 48)
        R_ = {}
        for nm in ("w0", "a0", "k_k", "k_a", "r_k", "lnx_g", "lnx_b"):
            R_[nm] = B.sb("r_" + nm, [128, D], F32, st)
            B.load_bcast(R_[nm], d["rw_" + nm].t[m])
        G = B.sb("G", [128, D], F32, st)
        Bt = B.sb("Bt", [128, D], F32, st)
        B.load_bcast(G, d["ln_g"].t[li, 1])
        B.load_bcast(Bt, d["ln_b"].t[li, 1])
        tri = B.sb("tri", [128, 128], F32, st)
        m2 = B.sb("m2", [128, 256], F32, st)
        sl = B.sb("sl", [128, 128], F32, st)
        B.pool(lambda e: e.memset(tri[:], 1.0), w=[tri])
        B.pool(lambda e: e.affine_select(out=tri[:], in_=tri[:], pattern=[[1, 128]], compare_op=ALU.is_ge, fill=0.0, base=0, channel_multiplier=-1), r=[tri], w=[tri])
        B.pool(lambda e: e.memset(m2[:], 1.0), w=[m2])
        B.pool(lambda e: e.affine_select(out=m2[:, 0:128], in_=m2[:, 0:128], pattern=[[1, 128]], compare_op=ALU.is_gt, fill=0.0, base=0, channel_multiplier=-1), r=[m2], w=[m2])
        B.pool(lambda e: e.affine_select(out=m2[:, 128:256], in_=m2[:, 128:256], pattern=[[1, 128]], compare_op=ALU.is_ge, fill=0.0, base=0, channel_multiplier=-1), r=[m2], w=[m2])
        B.pool(lambda e: e.memset(sl[:], 1.0), w=[sl])
        B.pool(lambda e: e.affine_select(out=sl[:], in_=sl[:], pattern=[[-1, 128]], compare_op=ALU.is_gt, fill=0.0, base=0, channel_multiplier=1), r=[sl], w=[sl])
        vmask = B.sb("vmask", [128, 1], F32, st)
        B.pool(lambda e: e.memset(vmask[:], 1.0), w=[vmask])
        B.pool(lambda e: e.affine_select(out=vmask[:], in_=vmask[:], pattern=[[0, 1]], compare_op=ALU.is_gt, fill=0.0, base=cfg.rows(ntp - 1), channel_multiplier=-1), r=[vmask], w=[vmask])
        ST = B.sb("ST", [64, NH, 64], F32, st)
        STb = B.sb("STb", [64, NH, 64], BF16, st)
        B.dve(lambda e: e.memset(ST[:], 0.0), w=[ST])
        B.dve(lambda e: e.memset(STb[:], 0.0), w=[STb])

        s2 = ExitStack()
        f32 = lambda nm: B.sb(nm, [128, D], F32, s2)
        x, xp, t0, t1 = f32("x"), f32("xp"), f32("t0"), f32("t1")
        r32, k32, v32, a32, ld, kk = f32("r32"), f32("k32"), f32("v32"), f32("a32"), f32("ld"), f32("kk")
        g32 = xp
        xT = B.sb("xT", [128, 8, 128], F32, s2)
        xxT = B.sb("xxT", [128, 8, 128], F32, s2)
        mixT = [B.sb("mixT", [128, 8, 128], BF16, s2) for _ in range(2)]
        lo = B.sb("lo", [128, 128], BF16, s2)
        loT = B.sb("loT", [128, 1, 128], BF16, s2)
        ss16 = B.sb("ss16", [128, 4, NH], F32, s2)
        bfs = {nm: B.sb(nm, [128, D], BF16, s2) for nm in ("rt", "at", "bt", "kt", "vb")}
        tmp = dict(xa=t0, y=t1, junk=kk, st=B.sb("st", [128, 8], F32, s2))
        xo = [B.sb("xo", [128, D], F32, s2)] * 2

        def mix(jx, buf):
            B.pool(lambda e: e.tensor_tensor(out=t0[:].rearrange("p (c t) -> p c t", t=128), in0=xxT[:], in1=MU[:, jx, :].unsqueeze(2).broadcast_to([128, 8, 128]), op=ALU.mult),
                   r=[xxT, MU], w=[t0])
            B.dve(lambda e: e.tensor_tensor(out=buf[:], in0=t0[:].rearrange("p (c t) -> p c t", t=128), in1=xT[:], op=ALU.add), r=[t0, xT], w=[buf])

        def proj_full(jx, wname, out32):
            buf = mixT[jx % 2]
            mix(jx, buf)
            for hf in range(2):
                B.linear(ps[1 + hf], buf, W[wname], hf * 512, (hf + 1) * 512, 8)
                B.act(lambda e, hf=hf: e.copy(out=out32[:, hf * 512:(hf + 1) * 512], in_=ps[1 + hf][:, :]), r=[ps[1 + hf]], w=[out32])

        def proj_lora(jx, w1n, w2n, n1, mid_func, bias_row, out_func, out32, scale=1.0):
            buf = mixT[jx % 2]
            mix(jx, buf)
            B.linear(ps[3], buf, W[w1n], 0, n1, 8)
            B.act(lambda e: e.activation(out=lo[:, 0:n1], in_=ps[3][:, 0:n1], func=mid_func), r=[ps[3]], w=[lo])
            B.transpose_to(lo, 1, loT, loT[0:n1, :, :], ps[0], cw=n1)
            for hf in range(2):
                B.pe(lambda e, hf=hf: e.matmul(ps[1 + hf][:, :], lhsT=loT[0:n1, 0, :], rhs=W[w2n][0:n1, 0, hf * 512:(hf + 1) * 512], start=True, stop=True),
                     r=[loT, W[w2n]], w=[ps[1 + hf]])
                if bias_row is not None:
                    B.dve(lambda e, hf=hf: e.tensor_tensor(out=out32[:, hf * 512:(hf + 1) * 512], in0=ps[1 + hf][:, :], in1=bias_row[:, hf * 512:(hf + 1) * 512], op=ALU.add),
                          r=[ps[1 + hf], bias_row], w=[out32])
                    B.act(lambda e, hf=hf: e.activation(out=out32[:, hf * 512:(hf + 1) * 512], in_=out32[:, hf * 512:(hf + 1) * 512], func=out_func), r=[out32], w=[out32])
                else:
                    B.act(lambda e, hf=hf: e.copy(out=out32[:, hf * 512:(hf + 1) * 512], in_=ps[1 + hf][:, :]), r=[ps[1 + hf]], w=[out32])

        def v3(t_, n=64):
            return t_[:].rearrange("p (h k) -> p h k", k=n)

        def bc16(col_ap):
            return col_ap.unsqueeze(2).broadcast_to([128, NH, 64])

        def front(j):
            rows = cfg.rows(j)
            base = 128 * j
            sample = (j == ntp)
            if rows < 128:
                B.dve(lambda e: e.memset(x[:], 0.0), w=[x])
            B.dve(lambda e: e.memset(xp[:], 0.0), w=[xp])
            B.dma("sp", x[0:rows, :], Xin[base:base + rows, :], r=[Xint[j]], w=[x])
            if not sample:
                if j > 0:
                    B.dma("sp", xp[0:rows, :], Xin[base - 1:base - 1 + rows, :], r=[Xint[j], Xint[j - 1]], w=[xp])
                else:
                    B.dma("sp", xp[1:rows, :], Xin[0:rows - 1, :], r=[Xint[j]], w=[xp])
            else:
                xv = Xin[base:base + rows, :].rearrange("(s t) n -> s t n", t=DEC_SEQ)
                for s_ in range(nsq):
                    B.dma("sp", xp[s_ * DEC_SEQ:s_ * DEC_SEQ + 1, :], d["state_rwkv_shift"].t[m, s_:s_ + 1, :], w=[xp])
                    B.dma("sp", xp[s_ * DEC_SEQ + 1:(s_ + 1) * DEC_SEQ, :], xv[s_, 0:DEC_SEQ - 1, :], r=[Xint[j]], w=[xp])
            B.dve(lambda e: e.tensor_tensor(out=xp[:], in0=xp[:], in1=x[:], op=ALU.subtract), r=[xp, x], w=[xp])
            for src_, dstT in ((x, xT), (xp, xxT)):
                for h2 in range(2):
                    for c in range(4):
                        B.pe(lambda e, h2=h2, c=c, src_=src_: e.transpose(ps[4 + h2][:, c * 128:(c + 1) * 128], src_[:, (h2 * 4 + c) * 128:(h2 * 4 + c + 1) * 128], B.ident[:, :]),
                             r=[src_, B.ident], w=[ps[4 + h2]])
                    B.act(lambda e, h2=h2, dstT=dstT: e.copy(out=dstT[:, h2 * 4:(h2 + 1) * 4, :], in_=ps[4 + h2][:, :].rearrange("p (c t) -> p c t", t=128)), r=[ps[4 + h2]], w=[dstT])
            proj_full(0, "wr", r32)
            proj_lora(1, "w1", "w2", 64, AF.Tanh, R_["w0"], AF.Sigmoid, ld)
            proj_full(2, "wk", k32)
            proj_full(3, "wv", v32)
            proj_lora(4, "a1", "a2", 64, AF.Copy, R_["a0"], AF.Sigmoid, a32)
            proj_lora(5, "g1", "g2", 128, AF.Sigmoid, None, None, g32)
            B.act(lambda e: e.activation(out=ld[:], in_=ld[:], func=AF.Copy, scale=-math.exp(-0.5)), r=[ld], w=[ld])
            if rows < 128 and not sample:
                B.dve(lambda e: e.tensor_scalar(out=ld[:], in0=ld[:], scalar1=vmask[:, 0:1], scalar2=None, op0=ALU.mult), r=[ld, vmask], w=[ld])
            B.dve(lambda e: e.tensor_tensor(out=kk[:], in0=k32[:], in1=R_["k_k"][:], op=ALU.mult), r=[k32, R_["k_k"]], w=[kk])
            B.pool(lambda e: e.tensor_tensor(out=t0[:], in0=kk[:], in1=kk[:], op=ALU.mult), r=[kk], w=[t0])
            B.dve(lambda e: e.tensor_reduce(out=ss16[:, 0, :], in_=v3(t0), axis=AX.X, op=ALU.add), r=[t0], w=[ss16])
            B.dve(lambda e: e.tensor_scalar(out=ss16[:, 0, :], in0=ss16[:, 0, :], scalar1=1e-24, scalar2=None, op0=ALU.max), r=[ss16], w=[ss16])
            B.act(lambda e: e.activation(out=ss16[:, 0, :], in_=ss16[:, 0, :], func=AF.Sqrt), r=[ss16], w=[ss16])
            B.dve(lambda e: e.reciprocal(out=ss16[:, 0, :], in_=ss16[:, 0, :]), r=[ss16], w=[ss16])
            B.dve(lambda e: e.tensor_tensor(out=v3(kk), in0=v3(kk), in1=bc16(ss16[:, 0, :]), op=ALU.mult), r=[kk, ss16], w=[kk])
            B.dve(lambda e: e.scalar_tensor_tensor(out=t0[:], in0=a32[:], scalar=-1.0, in1=R_["k_a"][:], op0=ALU.add, op1=ALU.mult), r=[a32, R_["k_a"]], w=[t0])
            B.dve(lambda e: e.scalar_tensor_tensor(out=k32[:], in0=t0[:], scalar=1.0, in1=k32[:], op0=ALU.add, op1=ALU.mult), r=[t0, k32], w=[k32])
            B.pool(lambda e: e.tensor_tensor(out=t0[:], in0=r32[:], in1=k32[:], op=ALU.mult), r=[r32, k32], w=[t0])
            B.dve(lambda e: e.tensor_tensor(out=t0[:], in0=t0[:], in1=R_["r_k"][:], op=ALU.mult), r=[t0, R_["r_k"]], w=[t0])
            B.dve(lambda e: e.tensor_reduce(out=ss16[:, 1, :], in_=v3(t0), axis=AX.X, op=ALU.add), r=[t0], w=[ss16])
            B.dve(lambda e: e.tensor_tensor(out=v3(t0), in0=v3(v32), in1=bc16(ss16[:, 1, :]), op=ALU.mult), r=[v32, ss16], w=[t0])
            B.pool(lambda e: e.tensor_tensor(out=t1[:], in0=kk[:], in1=a32[:], op=ALU.mult), r=[kk, a32], w=[t1])

        def post(j, y32):
            B.dve(lambda e: e.tensor_reduce(out=ss16[:, 2, :], in_=v3(y32), axis=AX.X, op=ALU.add), r=[y32], w=[ss16])
            B.dve(lambda e: e.tensor_scalar(out=ss16[:, 2, :], in0=ss16[:, 2, :], scalar1=-1.0 / 64, scalar2=None, op0=ALU.mult), r=[ss16], w=[ss16])
            B.dve(lambda e: e.tensor_tensor(out=v3(y32), in0=v3(y32), in1=bc16(ss16[:, 2, :]), op=ALU.add), r=[y32, ss16], w=[y32])
            B.pool(lambda e: e.tensor_tensor(out=t1[:], in0=y32[:], in1=y32[:], op=ALU.mult), r=[y32], w=[t1])
            B.dve(lambda e: e.tensor_reduce(out=ss16[:, 3, :], in_=v3(t1), axis=AX.X, op=ALU.add), r=[t1], w=[ss16])
            B.act(lambda e: e.activation(out=ss16[:, 3, :], in_=ss16[:, 3, :], func=AF.Sqrt, bias=B.eps[:, 2:3], scale=1.0 / 64), r=[ss16, B.eps], w=[ss16])
            B.dve(lambda e: e.reciprocal(out=ss16[:, 3, :], in_=ss16[:, 3, :]), r=[ss16], w=[ss16])
            B.dve(lambda e: e.tensor_tensor(out=v3(y32), in0=v3(y32), in1=bc16(ss16[:, 3, :]), op=ALU.mult), r=[y32, ss16], w=[y32])
            B.dve(lambda e: e.tensor_tensor(out=y32[:], in0=y32[:], in1=R_["lnx_g"][:], op=ALU.mult), r=[y32, R_["lnx_g"]], w=[y32])
            B.pool(lambda e: e.tensor_tensor(out=y32[:], in0=y32[:], in1=R_["lnx_b"][:], op=ALU.add), r=[y32, R_["lnx_b"]], w=[y32])
            B.dve(lambda e: e.tensor_tensor(out=y32[:], in0=y32[:], in1=t0[:], op=ALU.add), r=[y32, t0], w=[y32])
            yg = bfs["rt"]
            B.dve(lambda e: e.tensor_tensor(out=yg[:], in0=y32[:], in1=g32[:], op=ALU.mult), r=[y32, g32], w=[yg])
            buf = mixT[0]
            B.transpose_to(yg, 8, buf, buf[:, :, :], ps[0])
            for hf in range(2):
                B.linear(ps[6 + hf], buf, W["wo"], hf * 512, (hf + 1) * 512, 8)
            o = xo[j % 2]
            B.resid_ln(x, [ps[6], ps[7]], 1.0, G, Bt, o, tmp)
            for (r0, r1), ap, tt in dst(j):
                B.dma("sp", ap, o[r0:r1, :], r=[o], w=[tt])

        rwkv_prompt_loop(B, m, front, post, locals())
        front(ntp)
        B.act(lambda e: e.activation(out=ld[:], in_=ld[:], func=AF.Exp), r=[ld], w=[ld])
        B.dve(lambda e: e.tensor_scalar(out=kk[:], in0=kk[:], scalar1=-1.0, scalar2=None, op0=ALU.mult), r=[kk], w=[kk])
        for i, t_ in enumerate((r32, ld, k32, v32, kk, t1)):
            B.dma("sp", RWd.t[i, 0:srows, :], t_[0:srows, :], r=[t_], w=[RWd])
        rwkv_sample_rec(B, m, RWd, YSd)
        y32 = r32
        B.dma("sp", y32[0:srows, :], YSd.t[0:srows, :], r=[YSd], w=[y32])
        post(ntp, y32)
        B.dma("sp", d["sh_p"].t[m:m + 1, :], Xin[cfg.L - 1:cfg.L, :], r=[Xint[ntp - 1]], w=[d["sh_p"]])
        B.dma("sp", d["sh_s"].t[m], Xin[128 * ntp:128 * ntp + srows, :].rearrange("(s t) n -> s t n", t=DEC_SEQ)[:, DEC_SEQ - 1, :], r=[Xint[ntp]], w=[d["sh_s"]])
        B.sy.barrier()
        s2.close()


def rwkv_prompt_loop(B, m, front, post, L):
    cfg, d, ps = B.cfg, B.d, B.ps
    ntp = cfg.ntp
    r32, k32, v32, ld, kk, t0, t1, a32 = (L[k] for k in ("r32", "k32", "v32", "ld", "kk", "t0", "t1", "a32"))
    bfs, tri, m2, sl, ST, STb = (L[k] for k in ("bfs", "tri", "m2", "sl", "ST", "STb"))
    with ExitStack() as s3:
        HT = B.sb("HT", [64, NH, 4, 128], BF16, s3)
        WC = B.sb("WC", [64, NH], F32, s3)
        AKm = B.sb("AKm", [128, 256], BF16, s3)
        ABm = B.sb("ABm", [128, 256], BF16, s3)
        NX = [B.sb("NX", [128, 256], BF16, s3) for _ in range(2)]
        NT = [B.sb("NT", [128, 128], BF16, s3) for _ in range(2)]
        Zb = B.sb("Zb", [128, 64], BF16, s3)
        Ub = B.sb("Ub", [128, 64], BF16, s3)
        ones1 = B.ones
        for j in range(ntp):
            front(j)
            for hf in range(2):
                B.pe(lambda e, hf=hf: e.matmul(ps[1 + hf][:, :], lhsT=tri[:, :], rhs=ld[:, hf * 512:(hf + 1) * 512], start=True, stop=True), r=[tri, ld], w=[ps[1 + hf]])
            for h in range(NH):
                B.pe(lambda e, h=h: e.matmul(ps[3][0:64, h:h + 1], lhsT=ld[:, h * 64:(h + 1) * 64], rhs=ones1[:, 0:1], start=True, stop=True), r=[ld, ones1], w=[ps[3]])
            B.act(lambda e: e.activation(out=WC[:, :], in_=ps[3][0:64, 0:NH], func=AF.Exp), r=[ps[3]], w=[WC])
            y32 = a32
            for hf in range(2):
                sl_ = slice(hf * 512, (hf + 1) * 512)
                cum = ps[1 + hf]
                B.act(lambda e, sl_=sl_, cum=cum: e.activation(out=y32[:, sl_], in_=cum[:, :], func=AF.Exp), r=[cum], w=[y32])
                B.dve(lambda e, sl_=sl_: e.tensor_tensor(out=bfs["rt"][:, sl_], in0=r32[:, sl_], in1=y32[:, sl_], op=ALU.mult), r=[r32, y32], w=[bfs["rt"]])
                B.act(lambda e, sl_=sl_, cum=cum: e.activation(out=y32[:, sl_], in_=cum[:, :], func=AF.Exp, scale=-1.0), r=[cum], w=[y32])
                B.dve(lambda e, sl_=sl_: e.tensor_tensor(out=bfs["kt"][:, sl_], in0=k32[:, sl_], in1=y32[:, sl_], op=ALU.mult), r=[k32, y32], w=[bfs["kt"]])
                B.pool(lambda e, sl_=sl_: e.tensor_tensor(out=bfs["bt"][:, sl_], in0=t1[:, sl_], in1=y32[:, sl_], op=ALU.mult), r=[t1, y32], w=[bfs["bt"]])
                B.dve(lambda e, sl_=sl_, cum=cum: e.tensor_tensor(out=y32[:, sl_], in0=cum[:, :], in1=ld[:, sl_], op=ALU.subtract), r=[cum, ld], w=[y32])
                B.act(lambda e, sl_=sl_: e.activation(out=y32[:, sl_], in_=y32[:, sl_], func=AF.Exp), r=[y32], w=[y32])
                B.dve(lambda e, sl_=sl_: e.scalar_tensor_tensor(out=bfs["at"][:, sl_], in0=kk[:, sl_], scalar=-1.0, in1=y32[:, sl_], op0=ALU.mult, op1=ALU.mult), r=[kk, y32], w=[bfs["at"]])
            B.pool(lambda e: e.tensor_copy(out=bfs["vb"][:], in_=v32[:]), r=[v32], w=[bfs["vb"]])
            for h in range(NH):
                bi = 4 + h % 2
                pv = B.psb(bi)
                for i, nm in enumerate(("at", "rt", "kt", "bt")):
                    B.pe(lambda e, h=h, i=i, nm=nm, pv=pv: e.transpose(pv[0:64, i * 128:(i + 1) * 128], bfs[nm][:, h * 64:(h + 1) * 64], B.identb[:, :]), r=[bfs[nm], B.identb], w=[ps[bi]])
                B.act(lambda e, h=h, pv=pv: e.copy(out=HT[:, h, :, :], in_=pv[0:64, 0:512].rearrange("p (i t) -> p i t", t=128)), r=[ps[bi]], w=[HT])
            for h in range(NH):
                hs = slice(h * 64, (h + 1) * 64)
                rhsAR = HT[:, h, 0:2, :].rearrange("p i t -> p (i t)")
                B.pe(lambda e: e.matmul(ps[1][:, 0:256], lhsT=HT[:, h, 2, :], rhs=rhsAR, start=True, stop=True), r=[HT], w=[ps[1]])
                B.pe(lambda e: e.matmul(ps[2][:, 0:256], lhsT=HT[:, h, 3, :], rhs=rhsAR, start=True, stop=True), r=[HT], w=[ps[2]])
                B.pe(lambda e: e.matmul(ps[3][:, 0:128], lhsT=HT[:, h, 0, :], rhs=HT[:, h, 3, :], start=True, stop=True), r=[HT], w=[ps[3]])
                B.dve(lambda e: e.tensor_tensor(out=AKm[:], in0=ps[1][:, 0:256], in1=m2[:], op=ALU.mult), r=[ps[1], m2], w=[AKm])
                B.dve(lambda e: e.tensor_tensor(out=ABm[:], in0=ps[2][:, 0:256], in1=m2[:], op=ALU.mult), r=[ps[2], m2], w=[ABm])
                B.dve(lambda e: e.tensor_tensor(out=NT[0][:], in0=ps[3][:, 0:128], in1=sl[:], op=ALU.mult), r=[ps[3], sl], w=[NT[0]])
                B.pool(lambda e: e.tensor_copy(out=NX[0][:, 0:128], in_=ABm[:, 0:128]), r=[ABm], w=[NX[0]])
                B.pool(lambda e: e.tensor_tensor(out=NX[0][:, 128:256], in0=ABm[:, 0:128], in1=B.identb[:, :], op=ALU.add), r=[ABm, B.identb], w=[NX[0]])
                cur = 0
                for lvl in range(7):
                    nx, nt_ = NX[cur], NT[cur]
                    nx2, nt2 = NX[1 - cur], NT[1 - cur]
                    if lvl == 0:
                        B.pe(lambda e, nx=nx, nt_=nt_: e.matmul(ps[1][:, 0:128], lhsT=nt_[:, :], rhs=nx[:, 0:128], start=True, stop=True), r=[nt_, nx], w=[ps[1]])
                        B.pe(lambda e, nx=nx, nt_=nt_: e.matmul(ps[2][:, 0:128], lhsT=nx[:, 0:128], rhs=nt_[:, :], start=True, stop=True), r=[nt_, nx], w=[ps[2]])
                        B.act(lambda e, nx2=nx2: e.copy(out=nx2[:, 0:128], in_=ps[1][:, 0:128]), r=[ps[1]], w=[nx2])
                        B.dve(lambda e, nx=nx, nx2=nx2: e.tensor_copy(out=nx2[:, 128:256], in_=nx[:, 128:256]), r=[nx], w=[nx2])
                        B.act(lambda e, nt2=nt2: e.copy(out=nt2[:, :], in_=ps[2][:, 0:128]), r=[ps[2]], w=[nt2])
                    elif lvl < 6:
                        B.pe(lambda e, nx=nx, nt_=nt_: e.matmul(ps[1][:, 0:256], lhsT=nt_[:, :], rhs=nx[:, :], start=True, stop=True), r=[nt_, nx], w=[ps[1]])
                        B.pe(lambda e, nx=nx, nt_=nt_: e.matmul(ps[2][:, 0:128], lhsT=nx[:, 0:128], rhs=nt_[:, :], start=True, stop=True), r=[nt_, nx], w=[ps[2]])
                        B.act(lambda e, nx2=nx2: e.copy(out=nx2[:, 0:128], in_=ps[1][:, 0:128]), r=[ps[1]], w=[nx2])
                        B.dve(lambda e, nx=nx, nx2=nx2: e.tensor_tensor(out=nx2[:, 128:256], in0=ps[1][:, 128:256], in1=nx[:, 128:256], op=ALU.add), r=[ps[1], nx], w=[nx2])
                        B.act(lambda e, nt2=nt2: e.copy(out=nt2[:, :], in_=ps[2][:, 0:128]), r=[ps[2]], w=[nt2])
                    else:
                        B.pe(lambda e, nx=nx, nt_=nt_: e.matmul(ps[1][:, 0:128], lhsT=nt_[:, :], rhs=nx[:, 128:256], start=True, stop=True), r=[nt_, nx], w=[ps[1]])
                        B.dve(lambda e, nx=nx, nx2=nx2: e.tensor_tensor(out=nx2[:, 128:256], in0=ps[1][:, 0:128], in1=nx[:, 128:256], op=ALU.add), r=[ps[1], nx], w=[nx2])
                    cur = 1 - cur
                XT = NX[cur]
                B.pe(lambda e: e.matmul(ps[3][:, 0:64], lhsT=HT[:, h, 0, :], rhs=STb[:, h, :], start=True, stop=False), r=[HT, STb], w=[ps[3]])
                B.pe(lambda e: e.matmul(ps[3][:, 0:64], lhsT=AKm[:, 0:128], rhs=bfs["vb"][:, hs], start=False, stop=True), r=[AKm, bfs["vb"]], w=[ps[3]])
                B.act(lambda e: e.copy(out=Zb[:, :], in_=ps[3][:, 0:64]), r=[ps[3]], w=[Zb])
                B.pe(lambda e, XT=XT: e.matmul(ps[3][:, 64:128], lhsT=XT[:, 128:256], rhs=Zb[:, :], start=True, stop=True), r=[XT, Zb], w=[ps[3]])
                B.act(lambda e: e.copy(out=Ub[:, :], in_=ps[3][:, 64:128]), r=[ps[3]], w=[Ub])
                yb = ps[6 + (h // 8) % 2]
                yc = slice((h % 8) * 64, (h % 8 + 1) * 64)
                B.pe(lambda e, yb=yb, yc=yc: e.matmul(yb[:, yc], lhsT=HT[:, h, 1, :], rhs=STb[:, h, :], start=True, stop=False), r=[HT, STb], w=[yb])
                B.pe(lambda e, yb=yb, yc=yc: e.matmul(yb[:, yc], lhsT=ABm[:, 128:256], rhs=Ub[:, :], start=False, stop=False), r=[ABm, Ub], w=[yb])
                B.pe(lambda e, yb=yb, yc=yc: e.matmul(yb[:, yc], lhsT=AKm[:, 128:256], rhs=bfs["vb"][:, hs], start=False, stop=True), r=[AKm, bfs["vb"]], w=[yb])
                if h % 8 == 7:
                    B.act(lambda e, yb=yb, h=h: e.copy(out=y32[:, (h - 7) * 64:(h + 1) * 64], in_=yb[:, :]), r=[yb], w=[y32])
                B.pe(lambda e: e.matmul(ps[3][0:64, 128:192], lhsT=bfs["bt"][:, hs], rhs=Ub[:, :], start=True, stop=False), r=[bfs["bt"], Ub], w=[ps[3]])
                B.pe(lambda e: e.matmul(ps[3][0:64, 128:192], lhsT=bfs["kt"][:, hs], rhs=bfs["vb"][:, hs], start=False, stop=True), r=[bfs["kt"], bfs["vb"]], w=[ps[3]])
                B.dve(lambda e: e.tensor_scalar(out=ST[:, h, :], in0=ST[:, h, :], scalar1=WC[:, h:h + 1], scalar2=None, op0=ALU.mult), r=[ST, WC], w=[ST])
                B.dve(lambda e: e.scalar_tensor_tensor(out=ST[:, h, :], in0=ps[3][0:64, 128:192], scalar=WC[:, h:h + 1], in1=ST[:, h, :], op0=ALU.mult, op1=ALU.add), r=[ps[3], WC, ST], w=[ST])
                B.act(lambda e: e.copy(out=STb[:, h, :], in_=ST[:, h, :]), r=[ST], w=[STb])
            post(j, y32)
        so = T(HT.t[:].rearrange("p h i t -> p (h i t)").bitcast(F32)[:, 0:NH * 64].rearrange("p (h k) -> p h k", k=64), "so")
        so.wr, so.rd = HT.wr, HT.rd
        for h0 in range(0, NH, 8):
            bank = ps[1 + (h0 // 8) % 2]
            for i in range(8):
                B.pe(lambda e, h0=h0, i=i, bank=bank: e.transpose(bank[0:64, i * 64:(i + 1) * 64], ST[:, h0 + i, :], B.ident[0:64, 0:64]), r=[ST, B.ident], w=[bank])
            B.act(lambda e, h0=h0, bank=bank: e.copy(out=so[:, h0:h0 + 8, :], in_=bank[0:64, :].rearrange("p (h k) -> p h k", k=64)), r=[bank], w=[so])
        B.dma("sp", d["wkv_p"].t[m].rearrange("h v k -> v h k"), so[:, :, :], r=[so], w=[d["wkv_p"]])
        B.sy.barrier()


def rwkv_sample_rec(B, m, RWd, YSd):
    cfg, d, ps = B.cfg, B.d, B.ps
    nsq, srows = cfg.nsq, cfg.srows
    P = nsq * 8
    VS = 8
    with ExitStack() as s3:
        vec = [B.sb("vec", [P, DEC_SEQ, 128], F32, s3) for _ in range(6)]
        for i in range(6):
            for s_ in range(nsq):
                B.dma("sp", vec[i][s_ * 8:(s_ + 1) * 8, :, :], RWd.t[i, s_ * DEC_SEQ:(s_ + 1) * DEC_SEQ, :].rearrange("t (g c) -> g t c", c=128), r=[RWd], w=[vec[i]])
        Yall = B.sb("Yall", [P, DEC_SEQ, 128], F32, s3)
        S = B.sb("S", [P, VS, 64], F32, s3)
        tmp = B.sb("tmp", [P, VS, 64], F32, s3)
        sa = B.sb("sa", [P, VS], F32, s3)
        wkv_in = d["state_rwkv_wkv"].t[m].rearrange("s (g h2) v k -> (s g) h2 v k", h2=2)
        wkv_out = d["wkv_s"].t[m].rearrange("s (g h2) v k -> (s g) h2 v k", h2=2)
        n = 0
        for h2 in range(2):
            kvec = lambda i, t: vec[i][:, t, h2 * 64:(h2 + 1) * 64].unsqueeze(1).broadcast_to([P, VS, 64])
            for v0 in range(0, 64, VS):
                B.dma("sp", S[:, :, :], wkv_in[:, h2, v0:v0 + VS, :], w=[S])
                for t in range(DEC_SEQ):
                    vv = vec[3][:, t, h2 * 64 + v0:h2 * 64 + v0 + VS]
                    e1, e2 = ("dve", "pool") if n % 2 == 0 else ("pool", "dve")
                    n += 1
                    B.dve(lambda e: e.tensor_tensor(out=tmp[:], in0=S[:], in1=kvec(4, t), op=ALU.mult), r=[S, vec[4]], w=[tmp])
                    B.dve(lambda e: e.tensor_reduce(out=sa[:, :], in_=tmp[:], axis=AX.X, op=ALU.add), r=[tmp], w=[sa])
                    B.dve(lambda e: e.tensor_tensor(out=S[:], in0=S[:], in1=kvec(1, t), op=ALU.mult), r=[S, vec[1]], w=[S])
                    B.dve(lambda e: e.tensor_tensor(out=tmp[:], in0=sa[:, :].unsqueeze(2).broadcast_to([P, VS, 64]), in1=kvec(5, t), op=ALU.mult), r=[sa, vec[5]], w=[tmp])
                    B.dve(lambda e: e.tensor_tensor(out=S[:], in0=S[:], in1=tmp[:], op=ALU.add), r=[S, tmp], w=[S])
                    B.dve(lambda e, vv=vv: e.tensor_tensor(out=tmp[:], in0=vv.unsqueeze(2).broadcast_to([P, VS, 64]), in1=kvec(2, t), op=ALU.mult), r=[vec[3], vec[2]], w=[tmp])
                    B.dve(lambda e: e.tensor_tensor(out=S[:], in0=S[:], in1=tmp[:], op=ALU.add), r=[S, tmp], w=[S])
                    B.dve(lambda e: e.tensor_tensor(out=tmp[:], in0=S[:], in1=kvec(0, t), op=ALU.mult), r=[S, vec[0]], w=[tmp])
                    B.dve(lambda e, t=t: e.tensor_reduce(out=Yall[:, t, h2 * 64 + v0:h2 * 64 + v0 + VS], in_=tmp[:], axis=AX.X, op=ALU.add), r=[tmp], w=[Yall])
                B.dma("sp", wkv_out[:, h2, v0:v0 + VS, :], S[:, :, :], r=[S], w=[d["wkv_s"]])
        for s_ in range(nsq):
            B.dma("sp", YSd.t[s_ * DEC_SEQ:(s_ + 1) * DEC_SEQ, :].rearrange("t (g c) -> g t c", c=128), Yall[s_ * 8:(s_ + 1) * 8, :, :], r=[Yall], w=[YSd])
        B.sy.barrier()
```

```python
from contextlib import ExitStack
import math
import numpy as np
import concourse.bass as bass
import concourse.mybir as mybir
from concourse.bass_utils import run_bass_kernel_spmd

F32 = mybir.dt.float32
BF16 = mybir.dt.bfloat16
I32 = mybir.dt.int32
AF = mybir.ActivationFunctionType
ALU = mybir.AluOpType
AX = mybir.AxisListType


DEBUG_BARRIER = False
DEBUG_TILES = None


class T:
    __slots__ = ("t", "wr", "rd", "name")

    def __init__(self, t, name=""):
        self.t = t
        self.wr = {}
        self.rd = {}
        self.name = name

    def __getitem__(self, k):
        return self.t[k]


class Sync:
    NDMA = 24
    MAXFLY = 16

    def __init__(self, nc, es):
        self.nc = nc
        self.es = es
        self.eng = {"pe": nc.tensor, "act": nc.scalar, "dve": nc.vector, "pool": nc.gpsimd, "sp": nc.sync}
        self.sems = {}
        self.cnt = {}
        self.waited = {}
        for e in ("pe", "act", "dve", "pool"):
            self._mk("c_" + e)
        self.dma_rr = {}
        for q in ("sp", "pool", "act"):
            self.dma_rr[q] = 0
            for i in range(self.NDMA):
                self._mk("d_%s%d" % (q, i))
        self.n_ins = 0
        self.dma_hist = {}

    def _mk(self, name):
        self.sems[name] = self.es.enter_context(self.nc.semaphore(name))
        self.cnt[name] = 0

    def _wait(self, en, evs):
        own = "c_" + en
        e = self.eng[en]
        for s, v in evs.items():
            if s == own and en == "pe":
                continue
            if self.waited.get((en, s), 0) >= v:
                continue
            e.wait_ge(self.sems[s], v)
            self.waited[(en, s)] = v

    def op(self, en, fn, reads=(), writes=(), dma=False):
        evs = {}
        for t in reads:
            for s, v in t.wr.items():
                if evs.get(s, 0) < v:
                    evs[s] = v
        for t in writes:
            for d in (t.wr, t.rd):
                for s, v in d.items():
                    if evs.get(s, 0) < v:
                        evs[s] = v
        if dma:
            e = self.eng[en]
            for s, v in evs.items():
                if self.waited.get((en, s), 0) >= v:
                    continue
                e.wait_ge(self.sems[s], v)
                self.waited[(en, s)] = v
            hist = self.dma_hist.setdefault(en, [])
            if len(hist) >= self.MAXFLY:
                s_old, v_old = hist[-self.MAXFLY]
                if self.waited.get((en, s_old), 0) < v_old:
                    e.wait_ge(self.sems[s_old], v_old)
                    self.waited[(en, s_old)] = v_old
            i = self.dma_rr[en]
            self.dma_rr[en] = (i + 1) % self.NDMA
            sname = "d_%s%d" % (en, i)
            inc = 16
        else:
            self._wait(en, evs)
            sname = "c_" + en
            inc = 1
        ins = fn(self.eng[en])
        self.cnt[sname] += inc
        ins.then_inc(self.sems[sname], inc)
        v = self.cnt[sname]
        if dma:
            self.dma_hist[en].append((sname, v))
            if len(self.dma_hist[en]) > 64:
                del self.dma_hist[en][:32]
        for t in writes:
            t.wr[sname] = v
        for t in reads:
            t.rd[sname] = v
        self.n_ins += 1
        return ins

    def barrier(self):
        snap = dict(self.cnt)
        for en in ("pe", "act", "dve", "pool", "sp"):
            self._wait(en, snap)

    def finish(self):
        self.barrier()


D = 1024
DFF = 2816
NH = 16
QR = 768
KVR = 256
NOPE = 64
ROPE = 32
QK = 96
VD = 64
LN_EPS = 1e-5
RMS_EPS = 1e-6
GN_EPS = 64e-5
N_META = 16
PAGE = 128
DEC_SEQ = 4


class Cfg:
    def __init__(self, ntf=64, nsq=16, npages=64, npool=10240, depth=4, n_cores=8, n_batch=2):
        self.ntf = ntf
        self.L = 128 * ntf + N_META
        self.seq = 128 * ntf
        self.ntp = ntf + 1
        self.nsq = nsq
        self.srows = nsq * DEC_SEQ
        self.nt = self.ntp + 1
        self.npages = npages
        self.past = npages * PAGE
        self.npool = npool
        self.depth = depth
        self.alpha = (2 * depth) ** 0.25
        self.n_mla = (depth + 2) // 3
        self.n_rwkv = (depth + 1) // 3
        self.n_s5 = depth // 3
        self.n_cores = n_cores
        self.n_batch = n_batch

    def rows(self, j):
        if j < self.ntf:
            return 128
        if j == self.ntf:
            return N_META
        return self.srows


class Builder:
    def __init__(self, nc, cfg):
        self.nc = nc
        self.cfg = cfg
        self.es = ExitStack()
        self.sy = Sync(nc, self.es)
        self.din = {}
        self.dout = {}
        self.uid = 0

    def sb(self, name, shape, dt=F32, st=None):
        self.uid += 1
        return T((st or self.es).enter_context(self.nc.sbuf_tensor("%s_%d" % (name, self.uid), list(shape), dt)), name)

    def dram_in(self, name, shape, dt=F32):
        t = T(self.nc.dram_tensor(name, list(shape), dt, kind="ExternalInput").ap(), name)
        self.din[name] = t
        return t

    def dram_out(self, name, shape, dt=F32):
        t = T(self.nc.dram_tensor(name, list(shape), dt, kind="ExternalOutput").ap(), name)
        self.dout[name] = t
        return t

    def dram_scr(self, name, shape, dt=F32):
        return T(self.nc.dram_tensor(name, list(shape), dt, kind="Internal").ap(), name)

    def pe(self, fn, r=(), w=()):
        return self.sy.op("pe", fn, r, w)

    def act(self, fn, r=(), w=()):
        return self.sy.op("act", fn, r, w)

    def dve(self, fn, r=(), w=()):
        return self.sy.op("dve", fn, r, w)

    def pool(self, fn, r=(), w=()):
        return self.sy.op("pool", fn, r, w)

    def dma(self, q, out, in_, r=(), w=()):
        return self.sy.op(q, lambda e: e.dma_start(out=out, in_=in_), r, w, dma=True)

    def setup_common(self):
        nc = self.nc
        self.ps = []
        for i in range(8):
            self.ps.append(T(self.es.enter_context(nc.psum_tensor("psb%d" % i, [128, 512], F32)), "ps%d" % i))
        self.ident = self.sb("ident", [128, 128], F32)
        self.identb = self.sb("identb", [128, 128], BF16)
        self.pool(lambda e: e.memset(self.ident[:], 0.0), w=[self.ident])
        self.pool(lambda e: e.affine_select(out=self.ident[:], in_=self.ident[:], pattern=[[-1, 128]], compare_op=ALU.not_equal,
                                            fill=1.0, base=0, channel_multiplier=1), r=[self.ident], w=[self.ident])
        self.dve(lambda e: e.tensor_copy(out=self.identb[:], in_=self.ident[:]), r=[self.ident], w=[self.identb])
        self.ones = self.sb("ones", [128, 128], F32)
        self.pool(lambda e: e.memset(self.ones[:], 1.0), w=[self.ones])
        self.eps = self.sb("eps", [128, 4], F32)
        for i, v in enumerate((LN_EPS, RMS_EPS, GN_EPS, 0.0)):
            self.dve(lambda e, i=i, v=v: e.memset(self.eps[:, i:i + 1], v), w=[self.eps])

    def psb(self, i):
        return self.ps[i].t[:].bitcast(BF16)

    def load_w(self, dst, src, st_q="pool"):
        K = src.shape[0]
        if K <= 128:
            self.dma(st_q, dst[0:K, 0, :], src, w=[dst])
        else:
            v = src.rearrange("(kc p) n -> p kc n", p=128)
            for kc in range(K // 128):
                self.dma(st_q, dst[:, kc, :], v[:, kc, :], w=[dst])

    def load_bcast(self, dst, src_row):
        n = src_row.shape[-1]
        self.dma("sp", dst[:, 0:n], src_row.rearrange("(o n) -> o n", o=1).broadcast_to([128, n]), w=[dst])

    def load_T(self, dst, dst_ap, src_rows, n):
        with ExitStack() as s1:
            tmp = self.sb("ldT", [n, 128], F32, s1)
            self.dma("sp", tmp[:, :], src_rows, w=[tmp])
            self.pe(lambda e: e.transpose(self.ps[0][:, 0:n], tmp[:, :], self.ident[0:n, 0:n]), r=[tmp, self.ident], w=[self.ps[0]])
            self.dve(lambda e: e.tensor_copy(out=dst_ap, in_=self.ps[0][:, 0:n]), r=[self.ps[0]], w=[dst])
            self.sy.barrier()

    def transpose_to(self, src, nch, dst, dst_ap, bank, dt=BF16, rows=128, evac="act", src_off=0, cw=128):
        per = (1024 if dt == BF16 else 512)
        assert nch * rows <= per
        if dt == BF16:
            pv = self.psb(self.ps.index(bank))
            idt = self.identb
        else:
            pv = bank.t[:]
            idt = self.ident
        for c in range(nch):
            self.pe(lambda e, c=c: e.transpose(pv[0:cw, c * rows:(c + 1) * rows], src[0:rows, src_off + c * cw: src_off + (c + 1) * cw],
                                               idt[0:rows, 0:rows]), r=[src, idt], w=[bank])
        i = pv[0:cw, 0:nch * rows].rearrange("p (c r) -> p c r", c=nch)
        if evac == "act":
            self.act(lambda e: e.copy(out=dst_ap, in_=i), r=[bank], w=[dst])
        else:
            self.dve(lambda e: e.tensor_copy(out=dst_ap, in_=i), r=[bank], w=[dst])

    def linear(self, bank, xT, W, n0, n1, nkc, M=128, kp=128):
        for kc in range(nkc):
            self.pe(lambda e, kc=kc: e.matmul(bank[0:M, 0:n1 - n0], lhsT=xT[0:kp, kc, 0:M], rhs=W[0:kp, kc, n0:n1],
                                              start=(kc == 0), stop=(kc == nkc - 1)), r=[xT, W], w=[bank])

    def rstd_from_ss(self, ss, n, eps_col, out):
        self.act(lambda e: e.activation(out=out[:, 0:1], in_=ss[:, 0:1], func=AF.Sqrt, bias=self.eps[:, eps_col:eps_col + 1], scale=1.0 / n),
                 r=[ss, self.eps], w=[out])
        self.dve(lambda e: e.reciprocal(out=out[:, 0:1], in_=out[:, 0:1]), r=[out], w=[out])

    def resid_ln(self, x, banks, c, G, Bt, out, tmp):
        alpha = self.cfg.alpha
        xa, y, junk, st = tmp["xa"], tmp["y"], tmp["junk"], tmp["st"]
        self.act(lambda e: e.activation(out=xa[:], in_=x[:], func=AF.Copy, scale=alpha), r=[x], w=[xa])
        for h in range(2):
            self.dve(lambda e, h=h: e.scalar_tensor_tensor(out=y[:, h * 512:(h + 1) * 512], in0=banks[h][:, :], scalar=float(c),
                                                         in1=xa[:, h * 512:(h + 1) * 512], op0=ALU.mult, op1=ALU.add,
                                                         accum_out=st[:, h:h + 1]), r=[banks[h], xa], w=[y, st])
        self.ln_core(y, G, Bt, out, junk, st)

    def ln_core(self, y, G, Bt, out, junk, st):
        self.dve(lambda e: e.tensor_scalar(out=st[:, 2:3], in0=st[:, 0:1], scalar1=st[:, 1:2], scalar2=-1.0 / D, op0=ALU.add, op1=ALU.mult),
                 r=[st], w=[st])
        self.act(lambda e: e.activation(out=junk[:], in_=y[:], func=AF.Square, bias=st[:, 2:3], scale=1.0, accum_out=st[:, 3:4]),
                 r=[y, st], w=[junk, st])
        self.rstd_from_ss(_col(st, 3), D, 0, _col(st, 4))
        self.dve(lambda e: e.tensor_scalar(out=junk[:], in0=y[:], scalar1=st[:, 2:3], scalar2=st[:, 4:5], op0=ALU.add, op1=ALU.mult),
                 r=[y, st], w=[junk])
        self.pool(lambda e: e.tensor_tensor(out=junk[:], in0=junk[:], in1=G[:], op=ALU.mult), r=[junk, G], w=[junk])
        self.dve(lambda e: e.tensor_tensor(out=out[:], in0=junk[:], in1=Bt[:], op=ALU.add), r=[junk, Bt], w=[out])


class _col:
    def __init__(self, t, c):
        self._t = t
        self.c = c

    @property
    def wr(self):
        return self._t.wr

    @property
    def rd(self):
        return self._t.rd

    def __getitem__(self, k):
        return self._t.t[:, self.c:self.c + 1]


def x0_src(cfg, d):
    def f(j):
        if j == 0:
            return [((0, N_META), d["meta_tokens"][:, :], d["meta_tokens"]),
                    ((N_META, 128), d["x_prompt"][0:128 - N_META, :], d["x_prompt"])]
        if j < cfg.ntp:
            r = cfg.rows(j)
            return [((0, r), d["x_prompt"][128 * j - N_META:128 * j - N_META + r, :], d["x_prompt"])]
        return [((0, cfg.srows), d["x_sample"][:, :], d["x_sample"])]
    return f


def y_dst(cfg, d):
    def f(j):
        if j == 0:
            return [((N_META, 128), d["y_prompt"][0:128 - N_META, :], d["y_prompt"])]
        if j < cfg.ntp:
            r = cfg.rows(j)
            return [((0, r), d["y_prompt"][128 * j - N_META:128 * j - N_META + r, :], d["y_prompt"])]
        return [((0, cfg.srows), d["y_sample"][:, :], d["y_sample"])]
    return f


def scr_map(cfg, X, Xt):
    def f(j):
        r = cfg.rows(j)
        return [((0, r), X[128 * j:128 * j + r, :], Xt[j])]
    f.X = X
    f.Xt = Xt
    return f


def stage_ffn(B, li, half, src, dst, tiles=None):
    cfg, d = B.cfg, B.din
    with ExitStack() as st:
        w1 = B.sb("w1", [128, 8, DFF], BF16, st)
        w3 = B.sb("w3", [128, 8, DFF], BF16, st)
        w2 = B.sb("w2", [128, 22, D], BF16, st)
        B.load_w(w1, d["ffn_w1"].t[li, half])
        B.load_w(w3, d["ffn_w3"].t[li, half])
        B.load_w(w2, d["ffn_w2"].t[li, half])
        G = B.sb("G", [128, D], F32, st)
        Bt = B.sb("Bt", [128, D], F32, st)
        lni = 0 if half == 0 else 2
        B.load_bcast(G, d["ln_g"].t[li, lni])
        B.load_bcast(Bt, d["ln_b"].t[li, lni])
        xs = [B.sb("xs", [128, D], F32, st) for _ in range(3)]
        xb = B.sb("xb", [128, D], BF16, st)
        xT = B.sb("xT", [128, 8, 128], BF16, st)
        sg = [B.sb("sg", [128, 512], F32, st) for _ in range(2)]
        g = B.sb("g", [128, DFF], BF16, st)
        gT = B.sb("gT", [128, 22, 128], BF16, st)
        tmp = dict(xa=B.sb("xa", [128, D], F32, st), y=B.sb("y", [128, D], F32, st), junk=B.sb("junk", [128, D], F32, st),
                   st=B.sb("st", [128, 8], F32, st))
        xo = [B.sb("xo", [128, D], F32, st) for _ in range(2)]
        for t in xs:
            B.dve(lambda e, t=t: e.memset(t[:], 0.0), w=[t])
        ps = B.ps
        tl = list(range(cfg.nt)) if tiles is None else tiles
        if DEBUG_TILES is not None:
            tl = DEBUG_TILES

        def load(j, buf):
            for (r0, r1), ap, tt in src(j):
                B.dma("sp", buf[r0:r1, :], ap, r=[tt], w=[buf])

        load(tl[0], xs[0])
        for k, j in enumerate(tl):
            x = xs[k % 3]
            if k + 1 < len(tl):
                load(tl[k + 1], xs[(k + 1) % 3])
            B.pool(lambda e: e.tensor_copy(out=xb[:], in_=x[:]), r=[x], w=[xb])
            B.transpose_to(xb, 8, xT, xT[:, :, :], ps[0])
            for gi in range(6):
                n0 = gi * 512
                n1 = min(DFF, n0 + 512)
                b1, b3 = ps[1 + 2 * (gi % 2)], ps[2 + 2 * (gi % 2)]
                B.linear(b1, xT, w1, n0, n1, 8)
                B.linear(b3, xT, w3, n0, n1, 8)
                s_ = sg[gi % 2]
                B.act(lambda e, s_=s_, b1=b1, n=n1 - n0: e.activation(out=s_[:, 0:n], in_=b1[:, 0:n], func=AF.Silu), r=[b1], w=[s_])
                B.dve(lambda e, s_=s_, b3=b3, n0=n0, n1=n1: e.tensor_tensor(out=g[:, n0:n1], in0=s_[:, 0:n1 - n0], in1=b3[:, 0:n1 - n0], op=ALU.mult),
                      r=[s_, b3], w=[g])
            for c0, nch in ((0, 8), (8, 8), (16, 6)):
                B.transpose_to(g, nch, gT, gT[:, c0:c0 + nch, :], ps[5], src_off=c0 * 128, evac="act" if c0 != 8 else "dve")
            for h in range(2):
                for fc in range(22):
                    B.pe(lambda e, h=h, fc=fc: e.matmul(ps[6 + h][:, :], lhsT=gT[:, fc, :], rhs=w2[:, fc, h * 512:(h + 1) * 512],
                                                       start=(fc == 0), stop=(fc == 21)), r=[gT, w2], w=[ps[6 + h]])
            o = xo[k % 2]
            B.resid_ln(x, [ps[6], ps[7]], 0.5, G, Bt, o, tmp)
            for (r0, r1), ap, tt in dst(j):
                B.dma("sp", ap, o[r0:r1, :], r=[o], w=[tt])
            if DEBUG_BARRIER:
                B.sy.barrier()
        B.sy.barrier()


WEIGHT_SHAPES = dict(
    meta_tokens=(N_META, D), ln_g=("depth", 3, D), ln_b=("depth", 3, D),
    ffn_w1=("depth", 2, D, DFF), ffn_w3=("depth", 2, D, DFF), ffn_w2=("depth", 2, DFF, D),
    mla_w_dq=("n_mla", D, QR), mla_q_norm=("n_mla", QR), mla_w_uq=("n_mla", QR, NH * QK),
    mla_w_dkv=("n_mla", D, KVR + ROPE), mla_kv_norm=("n_mla", KVR), mla_w_uk=("n_mla", KVR, NH * NOPE),
    mla_w_uv=("n_mla", KVR, NH * VD), mla_w_o=("n_mla", NH * VD, D),
    rw_mu=("n_rwkv", 6, D), rw_wr=("n_rwkv", D, D), rw_wk=("n_rwkv", D, D), rw_wv=("n_rwkv", D, D),
    rw_w0=("n_rwkv", D), rw_w1=("n_rwkv", D, 64), rw_w2=("n_rwkv", 64, D), rw_a0=("n_rwkv", D),
    rw_a1=("n_rwkv", D, 64), rw_a2=("n_rwkv", 64, D), rw_g1=("n_rwkv", D, 128), rw_g2=("n_rwkv", 128, D),
    rw_k_k=("n_rwkv", D), rw_k_a=("n_rwkv", D), rw_r_k=("n_rwkv", D), rw_lnx_g=("n_rwkv", D), rw_lnx_b=("n_rwkv", D),
    rw_wo=("n_rwkv", D, D),
    s5_lam_re=("n_s5", 64, 64), s5_lam_im=("n_s5", 64, 64), s5_log_dt=("n_s5", 64),
    s5_b_re=("n_s5", 64, 64, 16), s5_b_im=("n_s5", 64, 64, 16), s5_c_re=("n_s5", 64, 16, 64), s5_c_im=("n_s5", 64, 16, 64),
    s5_d=("n_s5", D), s5_wv=("n_s5", D, D), s5_wg=("n_s5", D, D),
)


def _shape(cfg, shp):
    return tuple(getattr(cfg, s) if isinstance(s, str) else s for s in shp)


def io_shapes(cfg):
    ins = dict(
        x_prompt=(cfg.seq, D), x_sample=(cfg.srows, D),
        cache_mla_latent=(cfg.n_mla * cfg.npool * PAGE, KVR), cache_mla_krope=(cfg.n_mla * cfg.npool * PAGE, ROPE),
        state_rwkv_wkv=(cfg.n_rwkv, cfg.nsq, NH, 64, 64), state_rwkv_shift=(cfg.n_rwkv, cfg.nsq, D),
        state_s5_re=(cfg.n_s5, cfg.nsq, 64, 64), state_s5_im=(cfg.n_s5, cfg.nsq, 64, 64),
        page_table=(cfg.nsq, cfg.npages),
    )
    for k, v in WEIGHT_SHAPES.items():
        ins[k] = _shape(cfg, v)
    outs = dict(
        y_prompt=(cfg.seq, D), y_sample=(cfg.srows, D),
        lat_p=(cfg.n_mla, cfg.L, KVR), kr_p=(cfg.n_mla, cfg.L, ROPE), lat_s=(cfg.n_mla, cfg.srows, KVR), kr_s=(cfg.n_mla, cfg.srows, ROPE),
        wkv_p=(cfg.n_rwkv, NH, 64, 64), sh_p=(cfg.n_rwkv, D), wkv_s=(cfg.n_rwkv, cfg.nsq, NH, 64, 64), sh_s=(cfg.n_rwkv, cfg.nsq, D),
        re_p=(cfg.n_s5, 64, 64), im_p=(cfg.n_s5, 64, 64), re_s=(cfg.n_s5, cfg.nsq, 64, 64), im_s=(cfg.n_s5, cfg.nsq, 64, 64),
    )
    return ins, outs


def build_program(cfg, nstages=None):
    nc = bass.Bass("TRN2", target_bir_lowering=False)
    B = Builder(nc, cfg)
    ins, outs = io_shapes(cfg)
    for k, shp in ins.items():
        B.dram_in(k, shp, I32 if k == "page_table" else F32)
    for k, shp in outs.items():
        B.dram_out(k, shp, F32)
    d = dict(B.din)
    d.update(B.dout)
    B.d = d
    B.setup_common()
    XA = B.dram_scr("XA", [cfg.nt * 128, D])
    XB = B.dram_scr("XB", [cfg.nt * 128, D])
    Xs = [XA, XB]
    Xt = [[T(None, "XA%d" % j) for j in range(cfg.nt)], [T(None, "XB%d" % j) for j in range(cfg.nt)]]
    B.X = Xs
    B.Xt = Xt
    total = 3 * cfg.depth
    if nstages is None:
        nstages = total
    for s in range(nstages):
        li, kind = s // 3, s % 3
        src = x0_src(cfg, d) if s == 0 else scr_map(cfg, Xs[(s - 1) % 2].t, Xt[(s - 1) % 2])
        dst = y_dst(cfg, d) if s == nstages - 1 else scr_map(cfg, Xs[s % 2].t, Xt[s % 2])
        if kind == 0:
            stage_ffn(B, li, 0, src, dst)
        elif kind == 2:
            stage_ffn(B, li, 1, src, dst)
        else:
            mk, m = li % 3, li // 3
            if mk == 0:
                stage_mla(B, li, m, src, dst)
            elif mk == 1:
                stage_rwkv(B, li, m, src, dst)
            else:
                stage_s5(B, li, m, src, dst)
    B.sy.finish()
    B.es.close()
    return nc, B


def shard_inputs(cfg, inputs, c):
    b = c % cfg.n_batch
    s0 = c * cfg.nsq
    m = {}
    m["x_prompt"] = np.ascontiguousarray(inputs["x_prompt"][b])
    m["x_sample"] = np.ascontiguousarray(inputs["x_sample"][s0:s0 + cfg.nsq]).reshape(cfg.srows, D)
    m["cache_mla_latent"] = inputs["cache_mla_latent"].reshape(-1, KVR)
    m["cache_mla_krope"] = inputs["cache_mla_krope"].reshape(-1, ROPE)
    m["state_rwkv_wkv"] = np.ascontiguousarray(inputs["state_rwkv_wkv"][:, s0:s0 + cfg.nsq])
    m["state_rwkv_shift"] = np.ascontiguousarray(inputs["state_rwkv_shift"][:, s0:s0 + cfg.nsq])
    m["state_s5_re"] = np.ascontiguousarray(inputs["state_s5_re"][:, s0:s0 + cfg.nsq])
    m["state_s5_im"] = np.ascontiguousarray(inputs["state_s5_im"][:, s0:s0 + cfg.nsq])
    m["page_table"] = np.ascontiguousarray(inputs["page_table"][s0:s0 + cfg.nsq]).astype(np.int32)
    for k, shp in WEIGHT_SHAPES.items():
        m[k] = np.ascontiguousarray(inputs[k]).reshape(_shape(cfg, shp))
    return m


def assemble(cfg, res):
    nb, nco = cfg.n_batch, cfg.n_cores
    def pb(name):
        return [res[b][name] for b in range(nb)]
    def sc(name):
        return [res[c][name] for c in range(nco)]
    y_p = np.stack(pb("y_prompt"), 0)
    y_s = np.concatenate(sc("y_sample"), 0).reshape(nco * cfg.nsq, DEC_SEQ, D)
    lat_p = np.stack(pb("lat_p"), 1)
    kr_p = np.stack(pb("kr_p"), 1)
    lat_s = np.concatenate([r.reshape(cfg.n_mla, cfg.nsq, DEC_SEQ, KVR) for r in sc("lat_s")], 1)
    kr_s = np.concatenate([r.reshape(cfg.n_mla, cfg.nsq, DEC_SEQ, ROPE) for r in sc("kr_s")], 1)
    wkv_p = np.stack(pb("wkv_p"), 1)
    sh_p = np.stack(pb("sh_p"), 1)
    wkv_s = np.concatenate(sc("wkv_s"), 1)
    sh_s = np.concatenate(sc("sh_s"), 1)
    re_p = np.stack(pb("re_p"), 1)
    im_p = np.stack(pb("im_p"), 1)
    re_s = np.concatenate(sc("re_s"), 1)
    im_s = np.concatenate(sc("im_s"), 1)
    return (y_p, y_s, lat_p, kr_p, lat_s, kr_s, wkv_p, sh_p, wkv_s, sh_s, re_p, im_p, re_s, im_s)


def run(cfg, inputs, nstages=None, trace=False):
    nc, B = build_program(cfg, nstages)
    in_maps = [shard_inputs(cfg, inputs, c) for c in range(cfg.n_cores)]
    res = run_bass_kernel_spmd(nc, in_maps, core_ids=list(range(cfg.n_cores)), trace=trace)
    return assemble(cfg, res.results), res


def kernel(**inputs):
    cfg = Cfg()
    out, _ = run(cfg, inputs)
    return tuple(np.ascontiguousarray(o, dtype=np.float32) for o in out)


def range_reduce_sin(B, ang, out, n, tmpf, tmpi):
    C1 = 6.28125
    C2 = 2 * math.pi - C1
    B.dve(lambda e: e.tensor_scalar(out=tmpf[:, 0:n], in0=ang[:, 0:n], scalar1=1.0 / (2 * math.pi), scalar2=None, op0=ALU.mult), r=[ang], w=[tmpf])
    B.dve(lambda e: e.tensor_copy(out=tmpi[:, 0:n], in_=tmpf[:, 0:n]), r=[tmpf], w=[tmpi])
    B.dve(lambda e: e.tensor_copy(out=tmpf[:, 0:n], in_=tmpi[:, 0:n]), r=[tmpi], w=[tmpf])
    B.dve(lambda e: e.scalar_tensor_tensor(out=out[:, 0:n], in0=tmpf[:, 0:n], scalar=-C1, in1=ang[:, 0:n], op0=ALU.mult, op1=ALU.add), r=[tmpf, ang], w=[out])
    B.dve(lambda e: e.scalar_tensor_tensor(out=out[:, 0:n], in0=tmpf[:, 0:n], scalar=-C2, in1=out[:, 0:n], op0=ALU.mult, op1=ALU.add), r=[tmpf, out], w=[out])
    B.dve(lambda e: e.tensor_scalar(out=out[:, 0:n], in0=out[:, 0:n], scalar1=math.pi, scalar2=-math.pi, op0=ALU.min, op1=ALU.max), r=[out], w=[out])
    B.act(lambda e: e.activation(out=out[:, 0:n], in_=out[:, 0:n], func=AF.Sin), r=[out], w=[out])


def setup_rope(B, stk):
    cfg = B.cfg
    nt = cfg.nt
    B.COS = B.sb("COS", [128, nt * 16], F32, stk)
    B.SIN = B.sb("SIN", [128, nt * 16], F32, stk)
    with ExitStack() as st:
        pos = B.sb("pos", [128, nt], F32, st)
        pi_ = B.sb("pi_", [128, 1], I32, st)
        ang = B.sb("ang", [128, nt * 16], F32, st)
        ang2 = B.sb("ang2", [128, nt * 16], F32, st)
        tf = B.sb("tf", [128, nt * 16], F32, st)
        ti = B.sb("ti", [128, nt * 16], I32, st)
        B.pool(lambda e: e.iota(pos[:, 0:cfg.ntp], pattern=[[128, cfg.ntp]], base=0, channel_multiplier=1, allow_small_or_imprecise_dtypes=True), w=[pos])
        B.pool(lambda e: e.iota(pi_[:], pattern=[[0, 1]], base=0, channel_multiplier=1), w=[pi_])
        B.dve(lambda e: e.tensor_single_scalar(out=pi_[:], in_=pi_[:], scalar=3, op=ALU.bitwise_and), r=[pi_], w=[pi_])
        B.dve(lambda e: e.tensor_copy(out=pos[:, cfg.ntp:nt], in_=pi_[:]), r=[pi_], w=[pos])
        B.dve(lambda e: e.tensor_scalar(out=pos[:, cfg.ntp:nt], in0=pos[:, cfg.ntp:nt], scalar1=float(cfg.past), scalar2=None, op0=ALU.add), r=[pos], w=[pos])
        a3 = ang[:].rearrange("p (j f) -> p j f", f=16)
        for f in range(16):
            inv = float(np.float32(1.0) / np.float32(10000.0) ** (np.float32(f) * np.float32(2.0 / ROPE)))
            B.dve(lambda e, f=f, inv=inv: e.tensor_scalar(out=a3[:, :, f], in0=pos[:, :], scalar1=inv, scalar2=None, op0=ALU.mult), r=[pos], w=[ang])
        B.dve(lambda e: e.tensor_scalar(out=ang2[:], in0=ang[:], scalar1=math.pi / 2, scalar2=None, op0=ALU.add), r=[ang], w=[ang2])
        range_reduce_sin(B, ang, B.SIN, nt * 16, tf, ti)
        range_reduce_sin(B, ang2, B.COS, nt * 16, tf, ti)
        B.sy.barrier()


def rope_apply(B, src, src_ap, dst, dst_ap, j, nh, t):
    c = B.COS[:, j * 16:(j + 1) * 16].unsqueeze(1).broadcast_to([128, nh, 16])
    s = B.SIN[:, j * 16:(j + 1) * 16].unsqueeze(1).broadcast_to([128, nh, 16])
    x1, x2 = src_ap[:, :, 0:16], src_ap[:, :, 16:32]
    def v(k):
        return t[k][:, 0:nh * 16].rearrange("p (h f) -> p h f", f=16)
    B.dve(lambda e: e.tensor_tensor(out=v(0), in0=x1, in1=c, op=ALU.mult), r=[src, B.COS], w=[t[0]])
    B.dve(lambda e: e.tensor_tensor(out=v(1), in0=x2, in1=s, op=ALU.mult), r=[src, B.SIN], w=[t[1]])
    B.dve(lambda e: e.tensor_tensor(out=v(2), in0=x1, in1=s, op=ALU.mult), r=[src, B.SIN], w=[t[2]])
    B.dve(lambda e: e.tensor_tensor(out=v(3), in0=x2, in1=c, op=ALU.mult), r=[src, B.COS], w=[t[3]])
    B.dve(lambda e: e.tensor_tensor(out=dst_ap[:, :, 0:16], in0=v(0), in1=v(1), op=ALU.subtract), r=[t[0], t[1]], w=[dst])
    B.dve(lambda e: e.tensor_tensor(out=dst_ap[:, :, 16:32], in0=v(2), in1=v(3), op=ALU.add), r=[t[2], t[3]], w=[dst])


def stage_mla(B, li, m, src, dst):
    cfg, d, ps = B.cfg, B.d, B.ps
    ntp, nt = cfg.ntp, cfg.nt
    NTOK = ntp * 128
    scale = QK ** -0.5
    nblk = (ntp + 3) // 4
    if not hasattr(B, "QTd"):
        B.QTd = B.dram_scr("QTd", [NH, QK, nblk * 512], BF16)
        B.OTd = B.dram_scr("OTd", [NH, VD, nblk * 512 + 128], BF16)
        B.CNd = B.dram_scr("CNd", [128, KVR + ROPE], F32)
    QTd, OTd, CNd = B.QTd, B.OTd, B.CNd
    with ExitStack() as st:
        w_uk = B.sb("w_uk", [128, 2, NH * NOPE], BF16, st)
        w_uv = B.sb("w_uv", [128, 2, NH * VD], BF16, st)
        B.load_w(w_uk, d["mla_w_uk"].t[m])
        B.load_w(w_uv, d["mla_w_uv"].t[m])
        qs_b = B.sb("qs_b", [128, NH, QK], BF16, st)
        sA = ExitStack()
        setup_rope(B, sA)
        w_dq = B.sb("w_dq", [128, 8, QR], BF16, sA)
        w_uq = B.sb("w_uq", [128, 6, NH * QK], BF16, sA)
        w_dkv = B.sb("w_dkv", [128, 8, KVR + ROPE], BF16, sA)
        B.load_w(w_dq, d["mla_w_dq"].t[m])
        B.load_w(w_uq, d["mla_w_uq"].t[m])
        B.load_w(w_dkv, d["mla_w_dkv"].t[m])
        Gq = B.sb("Gq", [128, QR], F32, sA)
        Gkv = B.sb("Gkv", [128, KVR], F32, sA)
        B.load_bcast(Gq, d["mla_q_norm"].t[m])
        B.load_bcast(Gkv, d["mla_kv_norm"].t[m])
        wukx = B.sb("wukx", [128, 2, NH, QK], BF16, sA)
        B.pool(lambda e: e.memset(wukx[:], 0.0), w=[wukx])
        B.pool(lambda e: e.tensor_copy(out=wukx[:, :, :, 0:NOPE], in_=w_uk[:, :, :].rearrange("p k (h n) -> p k h n", n=NOPE)), r=[w_uk], w=[wukx])
        sel = B.sb("sel", [32, QK], BF16, sA)
        B.pool(lambda e: e.memset(sel[:], 0.0), w=[sel])
        B.pool(lambda e: e.tensor_copy(out=sel[:, NOPE:QK], in_=B.identb[0:32, 0:32]), r=[B.identb], w=[sel])
        cT = B.sb("cT", [128, 2, NTOK], BF16, sA)
        kpeT = B.sb("kpeT", [32, NTOK], BF16, sA)
        qmax = B.sb("qmax", [128, NH], F32, sA)
        kmax = B.sb("kmax", [128, NH], F32, sA)
        B.dve(lambda e: e.memset(qmax[:], 0.0), w=[qmax])
        B.dve(lambda e: e.memset(kmax[:], 0.0), w=[kmax])
        st2 = ExitStack()
        xs = [B.sb("xs", [128, D], F32, st2) for _ in range(2)]
        xb = B.sb("xb", [128, D], BF16, st2)
        xT = B.sb("xT", [128, 8, 128], BF16, st2)
        stq = B.sb("stq", [128, 8], F32, st2)
        cqn = B.sb("cqn", [128, QR], BF16, st2)
        cqT = B.sb("cqT", [128, 6, 128], BF16, st2)
        qf = B.sb("qf", [128, NH * QK], F32, st2)
        qb = B.sb("qb", [128, NH, QK], BF16, st2)
        sq = B.sb("sq", [128, NH * QK], F32, st2)
        ss16 = B.sb("ss16", [128, NH], F32, st2)
        rt = [B.sb("rt", [128, NH * 16], F32, st2) for _ in range(4)]
        QTs = [B.sb("QTs", [QK, NH, 512], BF16, st2) for _ in range(2)]
        c32 = [B.sb("c32", [128, KVR + ROPE], F32, st2) for _ in range(2)]
        cb = B.sb("cb", [128, KVR + ROPE], BF16, st2)
        for t_ in xs + QTs:
            B.dve(lambda e, t_=t_: e.memset(t_[:], 0.0), w=[t_])
        qf3 = qf[:].rearrange("p (h q) -> p h q", q=QK)

        def load(j, buf):
            for (r0, r1), ap, tt in src(j):
                B.dma("sp", buf[r0:r1, :], ap, r=[tt], w=[buf])

        load(0, xs[0])
        for j in range(nt):
            x = xs[j % 2]
            rows = cfg.rows(j)
            sample = (j == ntp)
            B.pool(lambda e: e.tensor_copy(out=xb[:], in_=x[:]), r=[x], w=[xb])
            B.transpose_to(xb, 8, xT, xT[:, :, :], ps[0])
            if j + 1 < nt:
                load(j + 1, xs[(j + 1) % 2])
            B.linear(ps[1], xT, w_dq, 0, 512, 8)
            B.linear(ps[2], xT, w_dq, 512, QR, 8)
            B.act(lambda e: e.activation(out=sq[:, 0:512], in_=ps[1][:, 0:512], func=AF.Square, accum_out=stq[:, 0:1]), r=[ps[1]], w=[sq, stq])
            B.act(lambda e: e.activation(out=sq[:, 512:QR], in_=ps[2][:, 0:QR - 512], func=AF.Square, accum_out=stq[:, 1:2]), r=[ps[2]], w=[sq, stq])
            B.dve(lambda e: e.tensor_tensor(out=stq[:, 2:3], in0=stq[:, 0:1], in1=stq[:, 1:2], op=ALU.add), r=[stq], w=[stq])
            B.rstd_from_ss(_col(stq, 2), QR, 1, _col(stq, 3))
            B.dve(lambda e: e.scalar_tensor_tensor(out=cqn[:, 0:512], in0=ps[1][:, 0:512], scalar=stq[:, 3:4], in1=Gq[:, 0:512], op0=ALU.mult, op1=ALU.mult),
                  r=[ps[1], stq, Gq], w=[cqn])
            B.dve(lambda e: e.scalar_tensor_tensor(out=cqn[:, 512:QR], in0=ps[2][:, 0:QR - 512], scalar=stq[:, 3:4], in1=Gq[:, 512:QR], op0=ALU.mult, op1=ALU.mult),
                  r=[ps[2], stq, Gq], w=[cqn])
            B.transpose_to(cqn, 6, cqT, cqT[:, :, :], ps[0])
            for k3 in range(3):
                B.linear(ps[3 + k3], cqT, w_uq, 512 * k3, 512 * (k3 + 1), 6)
                B.act(lambda e, k3=k3: e.copy(out=qf[:, 512 * k3:512 * (k3 + 1)], in_=ps[3 + k3][:, :]), r=[ps[3 + k3]], w=[qf])
            qdst = qs_b if sample else qb
            B.dve(lambda e: e.tensor_copy(out=qdst[:, :, 0:NOPE], in_=qf3[:, :, 0:NOPE]), r=[qf], w=[qdst])
            rope_apply(B, qf, qf3[:, :, NOPE:QK], qdst, qdst[:, :, NOPE:QK], j, NH, rt)
            if not sample:
                B.pool(lambda e: e.tensor_tensor(out=sq[:], in0=qf[:], in1=qf[:], op=ALU.mult), r=[qf], w=[sq])
                B.dve(lambda e: e.tensor_reduce(out=ss16[:], in_=sq[:].rearrange("p (h q) -> p h q", q=QK), axis=AX.X, op=ALU.add), r=[sq], w=[ss16])
                B.dve(lambda e: e.tensor_tensor(out=qmax[:], in0=qmax[:], in1=ss16[:], op=ALU.max), r=[qmax, ss16], w=[qmax])
                QT_ = QTs[(j // 4) % 2]
                for hb in range(2):
                    bank = ps[6 + hb]
                    pv = B.psb(6 + hb)
                    for hh in range(8):
                        h = hb * 8 + hh
                        B.pe(lambda e, h=h, hh=hh, pv=pv: e.transpose(pv[0:QK, hh * 128:(hh + 1) * 128], qb[:, h, :], B.identb[:, :]), r=[qb, B.identb], w=[bank])
                    B.act(lambda e, hb=hb, pv=pv: e.copy(out=QT_[:, hb * 8:(hb + 1) * 8, (j % 4) * 128:(j % 4 + 1) * 128],
                                                         in_=pv[0:QK, 0:1024].rearrange("p (h t) -> p h t", t=128)), r=[bank], w=[QT_])
                if j % 4 == 3 or j == ntp - 1:
                    blk = j // 4
                    B.dma("sp", QTd.t.rearrange("h r t -> r h t")[:, :, blk * 512:(blk + 1) * 512], QT_[:, :, :], r=[QT_], w=[QTd])
            B.linear(ps[1], xT, w_dkv, 0, KVR + ROPE, 8)
            B.act(lambda e: e.activation(out=sq[:, 0:KVR], in_=ps[1][:, 0:KVR], func=AF.Square, accum_out=stq[:, 4:5]), r=[ps[1]], w=[sq, stq])
            B.rstd_from_ss(_col(stq, 4), KVR, 1, _col(stq, 5))
            c_ = c32[j % 2]
            B.dve(lambda e: e.scalar_tensor_tensor(out=c_[:, 0:KVR], in0=ps[1][:, 0:KVR], scalar=stq[:, 5:6], in1=Gkv[:, :], op0=ALU.mult, op1=ALU.mult),
                  r=[ps[1], stq, Gkv], w=[c_])
            rope_apply(B, ps[1], ps[1][:, KVR:KVR + ROPE].rearrange("p (o f) -> p o f", o=1), c_,
                       c_[:, KVR:KVR + ROPE].rearrange("p (o f) -> p o f", o=1), j, 1, rt)
            B.pool(lambda e: e.tensor_copy(out=cb[:], in_=c_[:]), r=[c_], w=[cb])
            if sample:
                B.dma("sp", d["lat_s"].t[m, :, :], c_[0:rows, 0:KVR], r=[c_], w=[d["lat_s"]])
                B.dma("sp", d["kr_s"].t[m, :, :], c_[0:rows, KVR:KVR + ROPE], r=[c_], w=[d["kr_s"]])
                B.dma("sp", CNd.t[0:rows, :], c_[0:rows, :], r=[c_], w=[CNd])
            else:
                B.dma("sp", d["lat_p"].t[m, 128 * j:128 * j + rows, :], c_[0:rows, 0:KVR], r=[c_], w=[d["lat_p"]])
                B.dma("sp", d["kr_p"].t[m, 128 * j:128 * j + rows, :], c_[0:rows, KVR:KVR + ROPE], r=[c_], w=[d["kr_p"]])
                B.transpose_to(cb, 2, cT, cT[:, :, j * 128:(j + 1) * 128], ps[0])
                pv0 = B.psb(0)
                B.pe(lambda e: e.transpose(pv0[0:ROPE, 0:128], cb[:, KVR:KVR + ROPE], B.identb[:, :]), r=[cb, B.identb], w=[ps[0]])
                B.act(lambda e: e.copy(out=kpeT[:, j * 128:(j + 1) * 128], in_=pv0[0:ROPE, 0:128]), r=[ps[0]], w=[kpeT])
                for k2 in range(2):
                    for kc in range(2):
                        B.pe(lambda e, k2=k2, kc=kc: e.matmul(ps[3 + k2][:, :], lhsT=cT[:, kc, j * 128:(j + 1) * 128], rhs=w_uk[:, kc, 512 * k2:512 * (k2 + 1)],
                                                             start=(kc == 0), stop=(kc == 1)), r=[cT, w_uk], w=[ps[3 + k2]])
                    B.act(lambda e, k2=k2: e.activation(out=sq[:, 512 * k2:512 * (k2 + 1)], in_=ps[3 + k2][:, :], func=AF.Square), r=[ps[3 + k2]], w=[sq])
                B.dve(lambda e: e.tensor_reduce(out=ss16[:], in_=sq[:, 0:NH * NOPE].rearrange("p (h q) -> p h q", q=NOPE), axis=AX.X, op=ALU.add), r=[sq], w=[ss16])
                B.act(lambda e: e.activation(out=sq[:, 1024:1024 + ROPE], in_=c_[:, KVR:KVR + ROPE], func=AF.Square, accum_out=stq[:, 6:7]), r=[c_], w=[sq, stq])
                B.dve(lambda e: e.tensor_scalar(out=ss16[:], in0=ss16[:], scalar1=stq[:, 6:7], scalar2=None, op0=ALU.add), r=[ss16, stq], w=[ss16])
                B.dve(lambda e: e.tensor_tensor(out=kmax[:], in0=kmax[:], in1=ss16[:], op=ALU.max), r=[kmax, ss16], w=[kmax])
        B.sy.barrier()
        st2.close()
        mla_prompt_attention(B, st, m, cT, kpeT, wukx, sel, w_uv, qmax, kmax, scale)
        sA.close()
        mla_sample_attention(B, st, m, qs_b, w_uk, w_uv, scale)
    mla_outproj(B, li, m, src, dst)


def bcast_partition_max(B, src, dst, bank, tmp):
    B.pe(lambda e: e.transpose(bank[0:NH, 0:128], src[:, 0:NH], B.ident[:, :]), r=[src, B.ident], w=[bank])
    B.dve(lambda e: e.tensor_reduce(out=tmp[0:NH, 0:1], in_=bank[0:NH, 0:128], axis=AX.X, op=ALU.max), r=[bank], w=[tmp])
    B.dve(lambda e: e.tensor_scalar(out=tmp[0:NH, 1:1 + NH], in0=B.ident[0:NH, 0:NH], scalar1=tmp[0:NH, 0:1], scalar2=None, op0=ALU.mult), r=[tmp, B.ident], w=[tmp])
    B.pe(lambda e: e.matmul(bank[:, 256:256 + NH], lhsT=B.ones[0:NH, :], rhs=tmp[0:NH, 1:1 + NH], start=True, stop=True), r=[B.ones, tmp], w=[bank])
    B.dve(lambda e: e.tensor_copy(out=dst[:, 0:NH], in_=bank[:, 256:256 + NH]), r=[bank], w=[dst])


def mla_prompt_attention(B, st, m, cT, kpeT, wukx, sel, w_uv, qmax, kmax, scale):
    cfg, d, ps = B.cfg, B.d, B.ps
    ntp = cfg.ntp
    NTOK = ntp * 128
    nblk = (ntp + 3) // 4
    QTd, OTd = B.QTd, B.OTd
    with ExitStack() as s2:
        tmpm = B.sb("tmpm", [128, 1 + NH], F32, s2)
        MQ = B.sb("MQ", [128, NH], F32, s2)
        MK = B.sb("MK", [128, NH], F32, s2)
        negM = B.sb("negM", [128, NH], F32, s2)
        bcast_partition_max(B, qmax, MQ, ps[0], tmpm)
        bcast_partition_max(B, kmax, MK, ps[0], tmpm)
        B.dve(lambda e: e.tensor_tensor(out=negM[:], in0=MQ[:], in1=MK[:], op=ALU.mult), r=[MQ, MK], w=[negM])
        B.act(lambda e: e.activation(out=negM[:], in_=negM[:], func=AF.Sqrt), r=[negM], w=[negM])
        B.dve(lambda e: e.tensor_scalar(out=negM[:], in0=negM[:], scalar1=-scale, scalar2=None, op0=ALU.mult), r=[negM], w=[negM])
        masks = []
        mf = B.sb("mf", [128, 512], F32, s2)
        for i in range(4):
            mk = B.sb("mask", [128, 512], BF16, s2)
            B.pool(lambda e: e.memset(mf[:], 1.0), w=[mf])
            B.pool(lambda e, i=i: e.affine_select(out=mf[:], in_=mf[:], pattern=[[1, 512]], compare_op=ALU.is_ge, fill=0.0, base=-128 * i, channel_multiplier=-1),
                   r=[mf], w=[mf])
            B.pool(lambda e, mk=mk: e.tensor_copy(out=mk[:], in_=mf[:]), r=[mf], w=[mk])
            masks.append(mk)
        KTh = [B.sb("KTh", [QK, NTOK], BF16, s2) for _ in range(2)]
        Vh = [B.sb("Vh", [128, ntp, VD + 1], BF16, s2) for _ in range(2)]
        for v_ in Vh:
            B.pool(lambda e, v_=v_: e.memset(v_[:], 1.0), w=[v_])
        QTq = [B.sb("QTq", [QK, 512], BF16, s2) for _ in range(2)]
        PT = [B.sb("PT", [128, 512], BF16, s2) for _ in range(3)]
        Osb = [B.sb("Osb", [VD + 1, 512], F32, s2) for _ in range(2)]
        rl = B.sb("rl", [VD, 512], F32, s2)
        On = [B.sb("On", [VD, 512], BF16, s2) for _ in range(2)]
        onesr = B.sb("onesr", [VD + 1, VD], F32, s2)
        B.pool(lambda e: e.memset(onesr[:], 1.0), w=[onesr])
        nstep = 0
        nq_i = 0
        for h in range(NH):
            KT, V = KTh[h % 2], Vh[h % 2]
            for b in range(nblk):
                n = min(512, NTOK - b * 512)
                bank = ps[1 + b % 2]
                for kc in range(2):
                    B.pe(lambda e, kc=kc, b=b, n=n, bank=bank: e.matmul(bank[0:QK, 0:n], lhsT=wukx[:, kc, h, :], rhs=cT[:, kc, b * 512:b * 512 + n],
                                                                        start=(kc == 0), stop=False), r=[wukx, cT], w=[bank])
                B.pe(lambda e, b=b, n=n, bank=bank: e.matmul(bank[0:QK, 0:n], lhsT=sel[:, :], rhs=kpeT[:, b * 512:b * 512 + n], start=False, stop=True),
                     r=[sel, kpeT], w=[bank])
                B.dve(lambda e, b=b, n=n, bank=bank: e.tensor_copy(out=KT[:, b * 512:b * 512 + n], in_=bank[0:QK, 0:n]), r=[bank], w=[KT])
            for t0 in range(0, ntp, 8):
                nt8 = min(8, ntp - t0)
                bank = ps[3]
                for i in range(nt8):
                    for kc in range(2):
                        B.pe(lambda e, i=i, kc=kc, t0=t0, bank=bank: e.matmul(bank[:, i * VD:(i + 1) * VD], lhsT=cT[:, kc, (t0 + i) * 128:(t0 + i + 1) * 128],
                                                                              rhs=w_uv[:, kc, h * VD:(h + 1) * VD], start=(kc == 0), stop=(kc == 1)),
                             r=[cT, w_uv], w=[bank])
                B.dve(lambda e, t0=t0, nt8=nt8, bank=bank: e.tensor_copy(out=V[:, t0:t0 + nt8, 0:VD], in_=bank[:, 0:nt8 * VD].rearrange("p (t v) -> p t v", v=VD)),
                      r=[bank], w=[V])
            for b in range(nblk):
                tiles = list(range(4 * b, min(4 * b + 4, ntp)))
                nq = 128 * len(tiles)
                Q = QTq[nq_i % 2]
                Ob = ps[6 + nq_i % 2]
                B.dma("sp", Q[:, 0:nq], QTd.t[h, :, b * 512:b * 512 + nq], r=[QTd], w=[Q])
                last_kt = tiles[-1]
                for kt in range(last_kt + 1):
                    kr = cfg.rows(kt)
                    Sb = ps[4 + nstep % 2]
                    P = PT[nstep % 3]
                    nstep += 1
                    B.pe(lambda e, kt=kt, kr=kr, Sb=Sb: e.matmul(Sb[0:kr, 0:nq], lhsT=KT[:, kt * 128:kt * 128 + kr], rhs=Q[:, 0:nq], start=True, stop=True),
                         r=[KT, Q], w=[Sb])
                    B.act(lambda e, kr=kr, Sb=Sb, P=P: e.activation(out=P[0:kr, 0:nq], in_=Sb[0:kr, 0:nq], func=AF.Exp, bias=negM[0:kr, h:h + 1], scale=scale),
                          r=[Sb, negM], w=[P])
                    if kt >= 4 * b:
                        mk = masks[kt - 4 * b]
                        B.pool(lambda e, kr=kr, P=P, mk=mk: e.tensor_tensor(out=P[0:kr, 0:nq], in0=P[0:kr, 0:nq], in1=mk[0:kr, 0:nq], op=ALU.mult), r=[P, mk], w=[P])
                    B.pe(lambda e, kt=kt, kr=kr, P=P: e.matmul(Ob[0:VD + 1, 0:nq], lhsT=V[0:kr, kt, :], rhs=P[0:kr, 0:nq], start=(kt == 0), stop=(kt == last_kt)),
                         r=[V, P], w=[Ob])
                O_ = Osb[nq_i % 2]
                On_ = On[nq_i % 2]
                B.act(lambda e: e.copy(out=O_[:, 0:nq], in_=Ob[0:VD + 1, 0:nq]), r=[Ob], w=[O_])
                B.pe(lambda e: e.matmul(ps[0][0:VD, 0:nq], lhsT=onesr[VD:VD + 1, :], rhs=O_[VD:VD + 1, 0:nq], start=True, stop=True), r=[onesr, O_], w=[ps[0]])
                B.dve(lambda e: e.reciprocal(out=rl[:, 0:nq], in_=ps[0][0:VD, 0:nq]), r=[ps[0]], w=[rl])
                B.dve(lambda e: e.tensor_tensor(out=On_[:, 0:nq], in0=O_[0:VD, 0:nq], in1=rl[:, 0:nq], op=ALU.mult), r=[O_, rl], w=[On_])
                B.dma("sp", OTd.t[h, :, b * 512:b * 512 + nq], On_[:, 0:nq], r=[On_], w=[OTd])
                nq_i += 1
        B.sy.barrier()


def mla_sample_attention(B, st, m, qs_b, w_uk, w_uv, scale):
    cfg, d, ps = B.cfg, B.d, B.ps
    nsq, srows, npg = cfg.nsq, cfg.srows, cfg.npages
    NK = npg * 128 + DEC_SEQ
    R = KVR + ROPE
    OTd, CNd = B.OTd, B.CNd
    nblk = (cfg.ntp + 3) // 4
    with ExitStack() as s2:
        wukT = B.sb("wukT", [NOPE, NH, KVR], BF16, s2)
        for h in range(NH):
            pv = B.psb(h % 2)
            for kc in range(2):
                B.pe(lambda e, h=h, kc=kc, pv=pv: e.transpose(pv[0:NOPE, kc * 128:(kc + 1) * 128], w_uk[:, kc, h * NOPE:(h + 1) * NOPE], B.identb[:, :]),
                     r=[w_uk, B.identb], w=[ps[h % 2]])
            B.act(lambda e, h=h, pv=pv: e.copy(out=wukT[:, h, :], in_=pv[0:NOPE, 0:KVR]), r=[ps[h % 2]], w=[wukT])
        qnT = B.sb("qnT", [NOPE, NH, srows], BF16, s2)
        pv = B.psb(2)
        for h in range(NH):
            B.pe(lambda e, h=h: e.transpose(pv[0:NOPE, h * srows:(h + 1) * srows], qs_b[0:srows, h, 0:NOPE], B.identb[0:srows, 0:srows]),
                 r=[qs_b, B.identb], w=[ps[2]])
        B.act(lambda e: e.copy(out=qnT[:, :, :], in_=pv[0:NOPE, 0:NH * srows].rearrange("p (h t) -> p h t", t=srows)), r=[ps[2]], w=[qnT])
        QL = B.sb("QL", [srows, NH, R], BF16, s2)
        for h in range(NH):
            bank = ps[3 + h % 2]
            B.pe(lambda e, h=h, bank=bank: e.matmul(bank[0:srows, 0:KVR], lhsT=qnT[:, h, :], rhs=wukT[:, h, :], start=True, stop=True), r=[qnT, wukT], w=[bank])
            B.act(lambda e, h=h, bank=bank: e.copy(out=QL[:, h, 0:KVR], in_=bank[0:srows, 0:KVR]), r=[bank], w=[QL])
        B.dve(lambda e: e.tensor_copy(out=QL[:, :, KVR:R], in_=qs_b[0:srows, :, NOPE:QK]), r=[qs_b], w=[QL])
        QLT = B.sb("QLT", [128, 3, srows, NH], BF16, s2)
        for c, (c0, cw) in enumerate(((0, 128), (128, 128), (256, 32))):
            for h0 in range(0, NH, 8):
                bank = ps[5 + (c + h0 // 8) % 2]
                pv = B.psb(5 + (c + h0 // 8) % 2)
                for hh in range(8):
                    h = h0 + hh
                    B.pe(lambda e, h=h, hh=hh, c0=c0, cw=cw, pv=pv: e.transpose(pv[0:cw, hh * srows:(hh + 1) * srows], QL[:, h, c0:c0 + cw], B.identb[0:srows, 0:srows]),
                         r=[QL, B.identb], w=[bank])
                B.act(lambda e, c=c, h0=h0, cw=cw, pv=pv: e.copy(out=QLT[0:cw, c, :, h0:h0 + 8].rearrange("p r h -> p h r"),
                                                              in_=pv[0:cw, 0:8 * srows].rearrange("p (h r) -> p h r", r=srows)), r=[bank], w=[QLT])
        ptb = B.sb("ptb", [128, nsq * npg], I32, s2)
        idx = B.sb("idx", [128, nsq * npg], I32, s2)
        pidx = B.sb("pidx", [128, 1], I32, s2)
        B.dma("sp", ptb[:, :], d["page_table"].t.rearrange("s g -> (s g)").rearrange("(o n) -> o n", o=1).broadcast_to([128, nsq * npg]), w=[ptb])
        B.pool(lambda e: e.iota(pidx[:], pattern=[[0, 1]], base=m * cfg.npool * 128, channel_multiplier=1), w=[pidx])
        B.pool(lambda e: e.tensor_scalar(out=idx[:], in0=ptb[:], scalar1=128, scalar2=None, op0=ALU.mult), r=[ptb], w=[idx])
        B.pool(lambda e: e.tensor_tensor(out=idx[:], in0=idx[:], in1=pidx[:].broadcast_to([128, nsq * npg]), op=ALU.add), r=[idx, pidx], w=[idx])
        mskf = B.sb("mskf", [64, DEC_SEQ], F32, s2)
        B.pool(lambda e: e.memset(mskf[:], 1.0), w=[mskf])
        B.pool(lambda e: e.affine_select(out=mskf[:], in_=mskf[:], pattern=[[-NH, DEC_SEQ]], compare_op=ALU.is_ge, fill=0.0, base=0, channel_multiplier=1),
               r=[mskf], w=[mskf])
        CP = [B.sb("CP", [128, npg + 1, R], BF16, s2) for _ in range(2)]
        CTs = [B.sb("CTs", [128, 3, 128], BF16, s2) for _ in range(3)]
        S_all = B.sb("S_all", [64, NK], F32, s2)
        Pb = B.sb("Pb", [64, NK], BF16, s2)
        Pn = B.sb("Pn", [64, DEC_SEQ], F32, s2)
        PTs = B.sb("PTs", [128, npg + 1, 64], BF16, s2)
        sm = B.sb("sm", [64, 8], F32, s2)
        OLs = B.sb("OLs", [64, KVR], BF16, s2)
        OLT = B.sb("OLT", [128, 2, NH, srows], BF16, s2)
        lat, kr_ = d["cache_mla_latent"], d["cache_mla_krope"]
        nct = 0
        for s in range(nsq):
            C = CP[s % 2]
            for g in range(npg):
                B.sy.op("pool", lambda e, g=g: e.indirect_dma_start(out=C[:, g, 0:KVR], out_offset=None, in_=lat.t,
                        in_offset=bass.IndirectOffsetOnAxis(ap=idx[:, s * npg + g:s * npg + g + 1], axis=0)), [idx, lat], [C], dma=True)
                B.sy.op("pool", lambda e, g=g: e.indirect_dma_start(out=C[:, g, KVR:R], out_offset=None, in_=kr_.t,
                        in_offset=bass.IndirectOffsetOnAxis(ap=idx[:, s * npg + g:s * npg + g + 1], axis=0)), [idx, kr_], [C], dma=True)
            B.dma("pool", C[0:DEC_SEQ, npg, :], CNd.t[s * DEC_SEQ:(s + 1) * DEC_SEQ, :], r=[CNd], w=[C])
            qcols = lambda c, kw: QLT[0:kw, c, s * DEC_SEQ:(s + 1) * DEC_SEQ, :].rearrange("p t h -> p (t h)")
            for g in range(npg + 1):
                kr = 128 if g < npg else DEC_SEQ
                CT_ = CTs[nct % 3]
                nct += 1
                bi = 1 + (g % 2)
                pv = B.psb(bi)
                for c, (c0, cw) in enumerate(((0, 128), (128, 128), (256, 32))):
                    B.pe(lambda e, g=g, kr=kr, c=c, c0=c0, cw=cw, pv=pv: e.transpose(pv[0:cw, c * 128:c * 128 + kr], C[0:kr, g, c0:c0 + cw], B.identb[0:kr, 0:kr]),
                         r=[C, B.identb], w=[ps[bi]])
                B.act(lambda e, kr=kr, CT_=CT_, pv=pv: e.copy(out=CT_[:, :, 0:kr], in_=pv[:, 0:384].rearrange("p (c k) -> p c k", k=128)[:, :, 0:kr]), r=[ps[bi]], w=[CT_])
                sb_i = 3 + (g // 4) % 2
                off = (g % 4) * 128
                for c, (c0, cw) in enumerate(((0, 128), (128, 128), (256, 32))):
                    B.pe(lambda e, c=c, cw=cw, kr=kr, CT_=CT_, sb_i=sb_i, off=off: e.matmul(ps[sb_i][0:64, off:off + kr], lhsT=qcols(c, cw), rhs=CT_[0:cw, c, 0:kr],
                                                                                        start=(c == 0), stop=(c == 2)), r=[QLT, CT_], w=[ps[sb_i]])
                if g % 4 == 3 or g == npg:
                    g0 = (g // 4) * 4
                    n = off + kr
                    B.dve(lambda e, g0=g0, n=n, sb_i=sb_i: e.tensor_scalar(out=S_all[:, g0 * 128:g0 * 128 + n], in0=ps[sb_i][0:64, 0:n], scalar1=scale, scalar2=None, op0=ALU.mult),
                          r=[ps[sb_i]], w=[S_all])
            B.dve(lambda e: e.tensor_reduce(out=sm[:, 0:1], in_=S_all[:, 0:NK], axis=AX.X, op=ALU.max), r=[S_all], w=[sm])
            B.dve(lambda e: e.tensor_scalar(out=sm[:, 1:2], in0=sm[:, 0:1], scalar1=-1.0, scalar2=None, op0=ALU.mult), r=[sm], w=[sm])
            B.act(lambda e: e.activation(out=Pb[:, 0:npg * 128], in_=S_all[:, 0:npg * 128], func=AF.Exp, bias=sm[:, 1:2], scale=1.0, accum_out=sm[:, 2:3]),
                  r=[S_all, sm], w=[Pb, sm])
            B.act(lambda e: e.activation(out=Pn[:, :], in_=S_all[:, npg * 128:NK], func=AF.Exp, bias=sm[:, 1:2], scale=1.0), r=[S_all, sm], w=[Pn])
            B.dve(lambda e: e.tensor_tensor(out=Pn[:, :], in0=Pn[:, :], in1=mskf[:, :], op=ALU.mult), r=[Pn, mskf], w=[Pn])
            B.dve(lambda e: e.tensor_reduce(out=sm[:, 3:4], in_=Pn[:, :], axis=AX.X, op=ALU.add), r=[Pn], w=[sm])
            B.dve(lambda e: e.tensor_copy(out=Pb[:, npg * 128:NK], in_=Pn[:, :]), r=[Pn], w=[Pb])
            B.dve(lambda e: e.tensor_tensor(out=sm[:, 4:5], in0=sm[:, 2:3], in1=sm[:, 3:4], op=ALU.add), r=[sm], w=[sm])
            B.dve(lambda e: e.reciprocal(out=sm[:, 5:6], in_=sm[:, 4:5]), r=[sm], w=[sm])
            for g0 in range(0, npg + 1, 8):
                n8 = min(8, npg + 1 - g0)
                bi = 5 + (g0 // 8) % 2
                pv = B.psb(bi)
                for i in range(n8):
                    g = g0 + i
                    kr = 128 if g < npg else DEC_SEQ
                    B.pe(lambda e, g=g, i=i, kr=kr, pv=pv: e.transpose(pv[0:kr, i * 64:(i + 1) * 64], Pb[:, g * 128:g * 128 + kr], B.identb[0:64, 0:64]),
                         r=[Pb, B.identb], w=[ps[bi]])
                nfull = n8 if g0 + n8 <= npg else n8 - 1
                if nfull:
                    B.act(lambda e, g0=g0, nfull=nfull, pv=pv: e.copy(out=PTs[:, g0:g0 + nfull, :], in_=pv[:, 0:nfull * 64].rearrange("p (g q) -> p g q", q=64)), r=[ps[bi]], w=[PTs])
                if nfull != n8:
                    B.act(lambda e, pv=pv, nfull=nfull: e.copy(out=PTs[0:DEC_SEQ, npg, :], in_=pv[0:DEC_SEQ, nfull * 64:(nfull + 1) * 64]), r=[ps[bi]], w=[PTs])
            for g in range(npg + 1):
                kr = 128 if g < npg else DEC_SEQ
                B.pe(lambda e, g=g, kr=kr: e.matmul(ps[7][0:64, 0:KVR], lhsT=PTs[0:kr, g, :], rhs=C[0:kr, g, 0:KVR], start=(g == 0), stop=(g == npg)), r=[PTs, C], w=[ps[7]])
            B.dve(lambda e: e.tensor_scalar(out=OLs[:, :], in0=ps[7][0:64, 0:KVR], scalar1=sm[:, 5:6], scalar2=None, op0=ALU.mult), r=[ps[7], sm], w=[OLs])
            pv = B.psb(0)
            for kc in range(2):
                B.pe(lambda e, kc=kc: e.transpose(pv[:, kc * 64:(kc + 1) * 64], OLs[:, kc * 128:(kc + 1) * 128], B.identb[0:64, 0:64]), r=[OLs, B.identb], w=[ps[0]])
            B.act(lambda e: e.copy(out=OLT[:, :, :, s * DEC_SEQ:(s + 1) * DEC_SEQ].rearrange("p k h t -> p k t h"),
                                   in_=pv[:, 0:128].rearrange("p (k t h) -> p k t h", k=2, t=DEC_SEQ)), r=[ps[0]], w=[OLT])
        OTs = B.sb("OTs", [VD, NH, srows], BF16, s2)
        for h in range(NH):
            bank = ps[1 + h % 2]
            for kc in range(2):
                B.pe(lambda e, h=h, kc=kc, bank=bank: e.matmul(bank[0:VD, 0:srows], lhsT=w_uv[:, kc, h * VD:(h + 1) * VD], rhs=OLT[:, kc, h, :], start=(kc == 0), stop=(kc == 1)),
                     r=[w_uv, OLT], w=[bank])
            B.act(lambda e, h=h, bank=bank: e.copy(out=OTs[:, h, :], in_=bank[0:VD, 0:srows]), r=[bank], w=[OTs])
        B.dma("sp", OTd.t.rearrange("h v t -> v h t")[:, :, nblk * 512:nblk * 512 + srows], OTs[:, :, :], r=[OTs], w=[OTd])
        B.sy.barrier()


def mla_outproj(B, li, m, src, dst):
    cfg, d, ps = B.cfg, B.d, B.ps
    ntp, nt = cfg.ntp, cfg.nt
    nblk = (ntp + 3) // 4
    OTd = B.OTd
    with ExitStack() as st:
        w_o = B.sb("w_o", [VD, NH, D], BF16, st)
        wv = d["mla_w_o"].t[m].rearrange("(h v) n -> v h n", v=VD)
        for h in range(NH):
            B.dma("pool", w_o[:, h, :], wv[:, h, :], w=[w_o])
        G = B.sb("G", [128, D], F32, st)
        Bt = B.sb("Bt", [128, D], F32, st)
        B.load_bcast(G, d["ln_g"].t[li, 1])
        B.load_bcast(Bt, d["ln_b"].t[li, 1])
        xs = [B.sb("xs", [128, D], F32, st) for _ in range(2)]
        OTg = [B.sb("OTg", [VD, NH, 512], BF16, st) for _ in range(2)]
        tmp = dict(xa=B.sb("xa", [128, D], F32, st), y=B.sb("y", [128, D], F32, st), junk=B.sb("junk", [128, D], F32, st),
                   st=B.sb("st", [128, 8], F32, st))
        xo = [B.sb("xo", [128, D], F32, st) for _ in range(2)]
        for t_ in xs:
            B.dve(lambda e, t_=t_: e.memset(t_[:], 0.0), w=[t_])
        for j in range(nt):
            x = xs[j % 2]
            for (r0, r1), ap, tt in src(j):
                B.dma("sp", x[r0:r1, :], ap, r=[tt], w=[x])
            if j < ntp:
                blk, off = j // 4, (j % 4) * 128
                OT = OTg[blk % 2]
                if j % 4 == 0:
                    n = min(512, ntp * 128 - blk * 512)
                    B.dma("sp", OT[:, :, 0:n], OTd.t.rearrange("h v t -> v h t")[:, :, blk * 512:blk * 512 + n], r=[OTd], w=[OT])
                M = 128
            else:
                OT = OTg[(nblk) % 2]
                off, M = 0, cfg.srows
                B.dma("sp", OT[:, :, 0:M], OTd.t.rearrange("h v t -> v h t")[:, :, nblk * 512:nblk * 512 + M], r=[OTd], w=[OT])
            for hf in range(2):
                for h in range(NH):
                    B.pe(lambda e, h=h, hf=hf, OT=OT, off=off, M=M: e.matmul(ps[6 + hf][0:M, :], lhsT=OT[:, h, off:off + M], rhs=w_o[:, h, hf * 512:(hf + 1) * 512],
                                                                           start=(h == 0), stop=(h == NH - 1)), r=[OT, w_o], w=[ps[6 + hf]])
            o = xo[j % 2]
            B.resid_ln(x, [ps[6], ps[7]], 1.0, G, Bt, o, tmp)
            for (r0, r1), ap, tt in dst(j):
                B.dma("sp", ap, o[r0:r1, :], r=[o], w=[tt])
        B.sy.barrier()


def resid_ln_sb(B, x, h, G, Bt, out, tmp):
    y, junk, st = tmp["y"], tmp["junk"], tmp["st"]
    B.dve(lambda e: e.memset(st[:, 1:2], 0.0), w=[st])
    B.dve(lambda e: e.scalar_tensor_tensor(out=y[:], in0=x[:], scalar=float(B.cfg.alpha), in1=h[:], op0=ALU.mult, op1=ALU.add, accum_out=st[:, 0:1]),
          r=[x, h], w=[y, st])
    B.ln_core(y, G, Bt, out, junk, st)


def stage_s5(B, li, m, src, dst):
    cfg, d, ps = B.cfg, B.d, B.ps
    ntp, nt, nsq, srows = cfg.ntp, cfg.nt, cfg.nsq, cfg.srows
    with ExitStack() as st:
        BbT = [B.sb("BbT", [128, 32, 128], BF16, st) for _ in range(2)]
        CTm = [B.sb("CTm", [128, 32, 128], BF16, st) for _ in range(2)]
        prm = B.sb("prm", [128, 12, 32], F32, st)
        Dp = B.sb("Dp", [128, 8], F32, st)
        wv = B.sb("wv", [128, 8, D], BF16, st)
        wg = B.sb("wg", [128, 8, D], BF16, st)
        B.load_w(wv, d["s5_wv"].t[m])
        B.load_w(wg, d["s5_wg"].t[m])
        G = B.sb("G", [128, D], F32, st)
        Bt = B.sb("Bt", [128, D], F32, st)
        B.load_bcast(G, d["ln_g"].t[li, 1])
        B.load_bcast(Bt, d["ln_b"].t[li, 1])
        B.load_T(Dp, Dp[:, :], d["s5_d"].t[m].rearrange("(c p) -> c p", p=128), 8)
        P = lambda k: prm[:, k, :]
        bufs = s5_alloc(B, st)
        sp = ExitStack()
        CS = B.sb("CS", [128, 32 * 128], F32, sp)
        SN = B.sb("SN", [128, 32 * 128], F32, sp)
        with ExitStack() as s1:
            raw = B.sb("raw", [32, 3, 128], F32, s1)
            ldt = B.sb("ldt", [32, 2], F32, s1)
            B.dma("sp", raw[:, 0, :], d["s5_lam_re"].t[m].rearrange("(s g) p -> s (g p)", g=2), w=[raw])
            B.dma("sp", raw[:, 1, :], d["s5_lam_im"].t[m].rearrange("(s g) p -> s (g p)", g=2), w=[raw])
            B.dma("sp", ldt[:, :], d["s5_log_dt"].t[m].rearrange("(s g) -> s g", g=2), w=[ldt])
            B.dve(lambda e: e.tensor_copy(out=raw[:, 2, :].rearrange("s (g p) -> s g p", g=2), in_=ldt[:, :].unsqueeze(2).broadcast_to([32, 2, 64])), r=[ldt], w=[raw])
            for k in range(3):
                B.pe(lambda e, k=k: e.transpose(ps[0][:, k * 32:(k + 1) * 32], raw[:, k, :], B.ident[0:32, 0:32]), r=[raw, B.ident], w=[ps[0]])
            B.dve(lambda e: e.tensor_copy(out=prm[:, 0:3, :], in_=ps[0][:, 0:96].rearrange("p (k s) -> p k s", k=3)), r=[ps[0]], w=[prm])
            B.act(lambda e: e.activation(out=P(2), in_=P(2), func=AF.Exp), r=[prm], w=[prm])
            B.dve(lambda e: e.tensor_tensor(out=P(9), in0=P(0), in1=P(2), op=ALU.mult), r=[prm], w=[prm])
            B.act(lambda e: e.activation(out=P(3), in_=P(9), func=AF.Exp), r=[prm], w=[prm])
            B.dve(lambda e: e.tensor_tensor(out=P(4), in0=P(1), in1=P(2), op=ALU.mult), r=[prm], w=[prm])
            tf = B.sb("tf", [128, 4096], F32, s1)
            ti = B.sb("ti", [128, 4096], I32, s1)
            a32 = B.sb("a32", [128, 4, 32], F32, s1)
            B.dve(lambda e: e.tensor_copy(out=a32[:, 0, :], in_=P(4)), r=[prm], w=[a32])
            B.dve(lambda e: e.tensor_scalar(out=a32[:, 1, :], in0=P(4), scalar1=math.pi / 2, scalar2=None, op0=ALU.add), r=[prm], w=[a32])
            a32f = T(a32.t[:].rearrange("p k s -> p (k s)"), "a32f")
            a32f.wr, a32f.rd = a32.wr, a32.rd
            sc_ = B.sb("sc_", [128, 64], F32, s1)
            range_reduce_sin_signed(B, a32f, sc_, 64, tf, ti)
            B.dve(lambda e: e.tensor_tensor(out=P(6), in0=P(3), in1=sc_[:, 0:32], op=ALU.mult), r=[prm, sc_], w=[prm])
            B.dve(lambda e: e.tensor_tensor(out=P(5), in0=P(3), in1=sc_[:, 32:64], op=ALU.mult), r=[prm, sc_], w=[prm])
            B.dve(lambda e: e.tensor_tensor(out=P(9), in0=P(0), in1=P(0), op=ALU.mult), r=[prm], w=[prm])
            B.dve(lambda e: e.tensor_tensor(out=P(10), in0=P(1), in1=P(1), op=ALU.mult), r=[prm], w=[prm])
            B.dve(lambda e: e.tensor_tensor(out=P(9), in0=P(9), in1=P(10), op=ALU.add), r=[prm], w=[prm])
            B.dve(lambda e: e.reciprocal(out=P(9), in_=P(9)), r=[prm], w=[prm])
            B.dve(lambda e: e.tensor_scalar(out=P(10), in0=P(5), scalar1=-1.0, scalar2=None, op0=ALU.add), r=[prm], w=[prm])
            B.dve(lambda e: e.tensor_tensor(out=P(7), in0=P(10), in1=P(0), op=ALU.mult), r=[prm], w=[prm])
            B.dve(lambda e: e.tensor_tensor(out=P(11), in0=P(6), in1=P(1), op=ALU.mult), r=[prm], w=[prm])
            B.dve(lambda e: e.tensor_tensor(out=P(7), in0=P(7), in1=P(11), op=ALU.add), r=[prm], w=[prm])
            B.dve(lambda e: e.tensor_tensor(out=P(7), in0=P(7), in1=P(9), op=ALU.mult), r=[prm], w=[prm])
            B.dve(lambda e: e.tensor_tensor(out=P(8), in0=P(6), in1=P(0), op=ALU.mult), r=[prm], w=[prm])
            B.dve(lambda e: e.tensor_tensor(out=P(11), in0=P(10), in1=P(1), op=ALU.mult), r=[prm], w=[prm])
            B.dve(lambda e: e.tensor_tensor(out=P(8), in0=P(8), in1=P(11), op=ALU.subtract), r=[prm], w=[prm])
            B.dve(lambda e: e.tensor_tensor(out=P(8), in0=P(8), in1=P(9), op=ALU.mult), r=[prm], w=[prm])
            t1 = B.sb("t1", [128, 128], F32, s1)
            B.pool(lambda e: e.iota(t1[:], pattern=[[1, 128]], base=1, channel_multiplier=0, allow_small_or_imprecise_dtypes=True), w=[t1])
            ang = B.sb("ang", [128, 4096], F32, s1)
            for s_ in range(32):
                B.dve(lambda e, s_=s_: e.tensor_scalar(out=ang[:, s_ * 128:(s_ + 1) * 128], in0=t1[:], scalar1=prm[:, 4, s_:s_ + 1], scalar2=None, op0=ALU.mult), r=[t1, prm], w=[ang])
            range_reduce_sin_signed(B, ang, SN, 4096, tf, ti)
            B.dve(lambda e: e.tensor_scalar(out=ang[:], in0=ang[:], scalar1=math.pi / 2, scalar2=None, op0=ALU.add), r=[ang], w=[ang])
            range_reduce_sin_signed(B, ang, CS, 4096, tf, ti)
            B.sy.barrier()
        with ExitStack() as s1:
            Braw = [B.sb("Braw", [128, 32, 16], F32, s1) for _ in range(2)]
            bb = [B.sb("bb", [128, 32, 16], F32, s1) for _ in range(2)]
            tb = B.sb("tb", [128, 32, 16], F32, s1)
            for k, nm in enumerate(("s5_b_re", "s5_b_im")):
                src_v = d[nm].t[m].rearrange("(s g) p i -> g p s i", g=2)
                for g2 in range(2):
                    B.dma("sp", Braw[k][g2 * 64:(g2 + 1) * 64, :, :], src_v[g2], w=[Braw[k]])
            cre = prm[:, 7, :].unsqueeze(2).broadcast_to([128, 32, 16])
            cim = prm[:, 8, :].unsqueeze(2).broadcast_to([128, 32, 16])
            B.dve(lambda e: e.tensor_tensor(out=bb[0][:], in0=Braw[0][:], in1=cre, op=ALU.mult), r=[Braw[0], prm], w=[bb[0]])
            B.dve(lambda e: e.tensor_tensor(out=tb[:], in0=Braw[1][:], in1=cim, op=ALU.mult), r=[Braw[1], prm], w=[tb])
            B.dve(lambda e: e.tensor_tensor(out=bb[0][:], in0=bb[0][:], in1=tb[:], op=ALU.subtract), r=[bb[0], tb], w=[bb[0]])
            B.dve(lambda e: e.tensor_tensor(out=bb[1][:], in0=Braw[1][:], in1=cre, op=ALU.mult), r=[Braw[1], prm], w=[bb[1]])
            B.dve(lambda e: e.tensor_tensor(out=tb[:], in0=Braw[0][:], in1=cim, op=ALU.mult), r=[Braw[0], prm], w=[tb])
            B.dve(lambda e: e.tensor_tensor(out=bb[1][:], in0=bb[1][:], in1=tb[:], op=ALU.add), r=[bb[1], tb], w=[bb[1]])
            Ep = B.sb("Ep", [128, 32, 128], F32, s1)
            for k in range(2):
                B.pool(lambda e: e.memset(Ep[:], 0.0), w=[Ep])
                E4 = Ep[:].rearrange("p (c q) n -> p c q n", q=4)
                b4 = bb[k][:].rearrange("p (c q) i -> p c q i", q=4)
                for g2 in range(2):
                    for q in range(4):
                        B.pool(lambda e, g2=g2, q=q, E4=E4, b4=b4: e.tensor_copy(out=E4[g2 * 64:(g2 + 1) * 64, :, q, q * 32 + g2 * 16:q * 32 + g2 * 16 + 16],
                                                                                 in_=b4[g2 * 64:(g2 + 1) * 64, :, q, :]), r=[bb[k]], w=[Ep])
                for s0 in range(0, 32, 4):
                    bank = ps[1 + (s0 // 4) % 2]
                    for i in range(4):
                        B.pe(lambda e, s0=s0, i=i, bank=bank: e.transpose(bank[:, i * 128:(i + 1) * 128], Ep[:, s0 + i, :], B.ident[:, :]), r=[Ep, B.ident], w=[bank])
                    B.act(lambda e, s0=s0, k=k, bank=bank: e.copy(out=BbT[k][:, s0:s0 + 4, :], in_=bank[:, :].rearrange("p (s n) -> p s n", n=128)), r=[bank], w=[BbT[k]])
            B.sy.barrier()
        with ExitStack() as s1:
            selc = B.sb("selc", [16, 8, 128], BF16, s1)
            B.pool(lambda e: e.memset(selc[:], 0.0), w=[selc])
            for q in range(4):
                for g2 in range(2):
                    B.pool(lambda e, q=q, g2=g2: e.tensor_copy(out=selc[:, q * 2 + g2, q * 32 + g2 * 16:q * 32 + g2 * 16 + 16], in_=B.identb[0:16, 0:16]), r=[B.identb], w=[selc])
            F = [B.sb("F", [16, 32, 128], BF16, s1) for _ in range(2)]
            for k, nm in enumerate(("s5_c_re", "s5_c_im")):
                src_v = d[nm].t[m].rearrange("(s g) o p -> g o s p", g=2)
                for g2 in range(2):
                    B.pool(lambda e, g2=g2: e.memset(F[g2][:], 0.0), w=[F[g2]])
                    B.dma("pool", F[g2][:, :, g2 * 64:(g2 + 1) * 64], src_v[g2], w=[F[g2]])
                for s0 in range(0, 32, 4):
                    bank = ps[3 + (s0 // 4) % 2]
                    for i in range(4):
                        s_ = s0 + i
                        q = s_ % 4
                        for g2 in range(2):
                            B.pe(lambda e, s_=s_, i=i, g2=g2, q=q, bank=bank: e.matmul(bank[:, i * 128:(i + 1) * 128], lhsT=F[g2][:, s_, :], rhs=selc[:, q * 2 + g2, :],
                                                                                   start=(g2 == 0), stop=(g2 == 1)), r=[F[g2], selc], w=[bank])
                    B.act(lambda e, s0=s0, k=k, bank=bank: e.activation(out=CTm[k][:, s0:s0 + 4, :], in_=bank[:, :].rearrange("p (s n) -> p s n", n=128), func=AF.Copy,
                                                                        scale=(1.0 if k == 0 else -1.0)), r=[bank], w=[CTm[k]])
            B.sy.barrier()
        s5_body(B, st, sp, bufs, li, m, src, dst, BbT, CTm, CS, SN, prm, Dp, wv, wg, G, Bt)


def range_reduce_sin_signed(B, ang, out, n, tmpf, tmpi):
    range_reduce_sin(B, ang, out, n, tmpf, tmpi)


def s5_alloc(B, st):
    bufs = {}
    bufs["xs"] = [B.sb("xs", [128, D], F32, st) for _ in range(2)]
    bufs["uT"] = B.sb("uT", [128, 8, 128], F32, st)
    bufs["uTb"] = B.sb("uTb", [128, 8, 128], BF16, st)
    bufs["HS"] = [B.sb("HS", [128, 32], F32, st) for _ in range(2)]
    bufs["yT"] = B.sb("yT", [128, 8, 128], F32, st)
    bufs["gt"] = B.sb("gt", [128, D], F32, st)
    bufs["zT"] = B.sb("zT", [128, 8, 128], BF16, st)
    bufs["sgt"] = B.sb("sgt", [128, D], F32, st)
    bufs["hbuf"] = B.sb("hbuf", [128, D], F32, st)
    bufs["y"] = B.sb("y", [128, D], F32, st)
    bufs["st"] = B.sb("st", [128, 8], F32, st)
    bufs["xo"] = [B.sb("xo", [128, D], F32, st) for _ in range(2)]
    return bufs


def s5_body(B, st, sp, bufs, li, m, src, dst, BbT, CTm, CS, SN, prm, Dp, wv, wg, G, Bt):
    cfg, d, ps = B.cfg, B.d, B.ps
    ntp, nt, nsq, srows = cfg.ntp, cfg.nt, cfg.nsq, cfg.srows
    xs, uT, uTb, HS, yT, gt, zT, sgt, hbuf, xo = (bufs[k] for k in ("xs", "uT", "uTb", "HS", "yT", "gt", "zT", "sgt", "hbuf", "xo"))
    bu = [B.sb("bu", [128, 512], F32, sp) for _ in range(2)]
    mt = [B.sb("mt", [128, 512], F32, sp) for _ in range(4)]
    z = [B.sb("z", [128, 512], F32, sp) for _ in range(2)]
    gs = [B.sb("gs", [128, 512], F32, sp) for _ in range(2)]
    hh = [B.sb("hh", [128, 512], F32, sp) for _ in range(2)]
    hb = [B.sb("hb", [128, 512], BF16, sp) for _ in range(2)]
    tmp = dict(y=bufs["y"], junk=gt, st=bufs["st"])
    for t_ in xs + HS:
        B.dve(lambda e, t_=t_: e.memset(t_[:], 0.0), w=[t_])
    ss = ExitStack()
    Hs = Hall = BUs = sm_ = None

    def load(j, buf):
        for (r0, r1), ap, tt in src(j):
            B.dma("sp", buf[r0:r1, :], ap, r=[tt], w=[buf])

    load(0, xs[0])
    for j in range(nt):
        x = xs[j % 2]
        rows = cfg.rows(j)
        sample = (j == ntp)
        N = srows if sample else 128
        if j + 1 < nt:
            load(j + 1, xs[(j + 1) % 2])
        if sample:
            B.sy.barrier()
            sp.close()
            Hs = [B.sb("Hs", [128, 32, nsq], F32, ss) for _ in range(2)]
            Hall = [B.sb("Hall", [128, 32, srows], F32, ss) for _ in range(2)]
            BUs = [B.sb("BUs", [128, 32, srows], F32, ss) for _ in range(2)]
            sm_ = [B.sb("sm_", [128, 32, nsq], F32, ss) for _ in range(4)]
            hin = B.sb("hin", [nsq, 4096], F32, ss)
            for k, nm in enumerate(("state_s5_re", "state_s5_im")):
                B.dma("sp", hin[:, :], d[nm].t[m].rearrange("s g p -> s (g p)"), w=[hin])
                for s_ in range(32):
                    B.pe(lambda e, s_=s_: e.transpose(ps[0][:, s_ * nsq:(s_ + 1) * nsq], hin[:, s_ * 128:(s_ + 1) * 128], B.ident[0:nsq, 0:nsq]), r=[hin, B.ident], w=[ps[0]])
                B.dve(lambda e, k=k: e.tensor_copy(out=Hs[k][:, :, :], in_=ps[0][:, 0:32 * nsq].rearrange("p (s q) -> p s q", q=nsq)), r=[ps[0]], w=[Hs[k]])
        for h2 in range(2):
            for c in range(4):
                B.pe(lambda e, h2=h2, c=c: e.transpose(ps[1 + h2][:, c * 128:(c + 1) * 128], x[:, (h2 * 4 + c) * 128:(h2 * 4 + c + 1) * 128], B.ident[:, :]), r=[x, B.ident], w=[ps[1 + h2]])
            B.act(lambda e, h2=h2: e.copy(out=uT[:, h2 * 4:(h2 + 1) * 4, :], in_=ps[1 + h2][:, :].rearrange("p (c t) -> p c t", t=128)), r=[ps[1 + h2]], w=[uT])
        B.pool(lambda e: e.tensor_copy(out=uTb[:], in_=uT[:]), r=[uT], w=[uTb])
        for c in range(8):
            for k in range(2):
                bank = ps[3 + k]
                for q in range(4):
                    B.pe(lambda e, k=k, q=q, c=c, bank=bank: e.matmul(bank[:, q * 128:q * 128 + N], lhsT=BbT[k][:, 4 * c + q, :], rhs=uTb[:, c, 0:N], start=True, stop=True),
                         r=[BbT[k], uTb], w=[bank])
            if sample:
                for k in range(2):
                    B.act(lambda e, k=k, c=c: e.copy(out=BUs[k][:, 4 * c:4 * c + 4, :], in_=ps[3 + k][:, :].rearrange("p (q t) -> p q t", t=128)[:, :, 0:N]), r=[ps[3 + k]], w=[BUs[k]])
                continue
            for k in range(2):
                B.act(lambda e, k=k: e.copy(out=bu[k][:], in_=ps[3 + k][:, :]), r=[ps[3 + k]], w=[bu[k]])
            cs = CS[:, c * 512:(c + 1) * 512]
            sn = SN[:, c * 512:(c + 1) * 512]
            B.dve(lambda e: e.tensor_tensor(out=mt[0][:], in0=bu[0][:], in1=cs, op=ALU.mult), r=[bu[0], CS], w=[mt[0]])
            B.pool(lambda e: e.tensor_tensor(out=mt[1][:], in0=bu[1][:], in1=sn, op=ALU.mult), r=[bu[1], SN], w=[mt[1]])
            B.dve(lambda e: e.tensor_tensor(out=mt[2][:], in0=bu[1][:], in1=cs, op=ALU.mult), r=[bu[1], CS], w=[mt[2]])
            B.pool(lambda e: e.tensor_tensor(out=mt[3][:], in0=bu[0][:], in1=sn, op=ALU.mult), r=[bu[0], SN], w=[mt[3]])
            B.dve(lambda e: e.tensor_tensor(out=z[0][:], in0=mt[0][:], in1=mt[1][:], op=ALU.add), r=[mt[0], mt[1]], w=[z[0]])
            B.dve(lambda e: e.tensor_tensor(out=z[1][:], in0=mt[2][:], in1=mt[3][:], op=ALU.subtract), r=[mt[2], mt[3]], w=[z[1]])
            for k in range(2):
                for q in range(4):
                    s_ = 4 * c + q
                    B.dve(lambda e, k=k, q=q, s_=s_: e.tensor_tensor_scan(out=gs[k][:, q * 128:(q + 1) * 128], data0=prm[:, 3, s_:s_ + 1].broadcast_to([128, 128]),
                                                                          data1=z[k][:, q * 128:(q + 1) * 128], initial=HS[k][:, s_:s_ + 1], op0=ALU.mult, op1=ALU.add),
                          r=[prm, z[k], HS[k]], w=[gs[k]])
            B.dve(lambda e: e.tensor_tensor(out=mt[0][:], in0=gs[0][:], in1=cs, op=ALU.mult), r=[gs[0], CS], w=[mt[0]])
            B.pool(lambda e: e.tensor_tensor(out=mt[1][:], in0=gs[1][:], in1=sn, op=ALU.mult), r=[gs[1], SN], w=[mt[1]])
            B.dve(lambda e: e.tensor_tensor(out=mt[2][:], in0=gs[0][:], in1=sn, op=ALU.mult), r=[gs[0], SN], w=[mt[2]])
            B.pool(lambda e: e.tensor_tensor(out=mt[3][:], in0=gs[1][:], in1=cs, op=ALU.mult), r=[gs[1], CS], w=[mt[3]])
            B.dve(lambda e: e.tensor_tensor(out=hh[0][:], in0=mt[0][:], in1=mt[1][:], op=ALU.subtract), r=[mt[0], mt[1]], w=[hh[0]])
            B.dve(lambda e: e.tensor_tensor(out=hh[1][:], in0=mt[2][:], in1=mt[3][:], op=ALU.add), r=[mt[2], mt[3]], w=[hh[1]])
            for k in range(2):
                B.dve(lambda e, k=k, c=c: e.tensor_copy(out=HS[k][:, 4 * c:4 * c + 4], in_=hh[k][:].rearrange("p (q t) -> p q t", t=128)[:, :, rows - 1]), r=[hh[k]], w=[HS[k]])
                B.act(lambda e, k=k: e.copy(out=hb[k][:], in_=hh[k][:]), r=[hh[k]], w=[hb[k]])
            yb = ps[5 + c % 2]
            for q in range(4):
                for k in range(2):
                    B.pe(lambda e, q=q, k=k, c=c, yb=yb: e.matmul(yb[:, 0:128], lhsT=CTm[k][:, 4 * c + q, :], rhs=hb[k][:, q * 128:(q + 1) * 128],
                                                                 start=(q == 0 and k == 0), stop=(q == 3 and k == 1)), r=[CTm[k], hb[k]], w=[yb])
            B.dve(lambda e, c=c, yb=yb: e.scalar_tensor_tensor(out=yT[:, c, :], in0=uT[:, c, :], scalar=Dp[:, c:c + 1], in1=yb[:, 0:128], op0=ALU.mult, op1=ALU.add),
                  r=[uT, Dp, yb], w=[yT])
        if sample:
            are = prm[:, 5, :].unsqueeze(2).broadcast_to([128, 32, nsq])
            aim = prm[:, 6, :].unsqueeze(2).broadcast_to([128, 32, nsq])
            for t in range(DEC_SEQ):
                bt = [BUs[k][:].rearrange("p s (q t) -> p s q t", t=DEC_SEQ)[:, :, :, t] for k in range(2)]
                B.dve(lambda e: e.tensor_tensor(out=sm_[0][:], in0=Hs[0][:], in1=are, op=ALU.mult), r=[Hs[0], prm], w=[sm_[0]])
                B.dve(lambda e: e.tensor_tensor(out=sm_[1][:], in0=Hs[1][:], in1=aim, op=ALU.mult), r=[Hs[1], prm], w=[sm_[1]])
                B.dve(lambda e: e.tensor_tensor(out=sm_[2][:], in0=Hs[1][:], in1=are, op=ALU.mult), r=[Hs[1], prm], w=[sm_[2]])
                B.dve(lambda e: e.tensor_tensor(out=sm_[3][:], in0=Hs[0][:], in1=aim, op=ALU.mult), r=[Hs[0], prm], w=[sm_[3]])
                B.dve(lambda e: e.tensor_tensor(out=sm_[0][:], in0=sm_[0][:], in1=sm_[1][:], op=ALU.subtract), r=[sm_[0], sm_[1]], w=[sm_[0]])
                B.dve(lambda e: e.tensor_tensor(out=sm_[2][:], in0=sm_[2][:], in1=sm_[3][:], op=ALU.add), r=[sm_[2], sm_[3]], w=[sm_[2]])
                B.dve(lambda e, bt=bt: e.tensor_tensor(out=Hs[0][:], in0=sm_[0][:], in1=bt[0], op=ALU.add), r=[sm_[0], BUs[0]], w=[Hs[0]])
                B.dve(lambda e, bt=bt: e.tensor_tensor(out=Hs[1][:], in0=sm_[2][:], in1=bt[1], op=ALU.add), r=[sm_[2], BUs[1]], w=[Hs[1]])
                for k in range(2):
                    B.dve(lambda e, k=k, t=t: e.tensor_copy(out=Hall[k][:].rearrange("p s (q t) -> p s q t", t=DEC_SEQ)[:, :, :, t], in_=Hs[k][:]), r=[Hs[k]], w=[Hall[k]])
            hbs = [B.sb("hbs", [128, 32, srows], BF16, ss) for _ in range(2)]
            for k in range(2):
                B.act(lambda e, k=k: e.copy(out=hbs[k][:], in_=Hall[k][:]), r=[Hall[k]], w=[hbs[k]])
            for c in range(8):
                yb = ps[5 + c % 2]
                for q in range(4):
                    for k in range(2):
                        B.pe(lambda e, q=q, k=k, c=c, yb=yb: e.matmul(yb[:, 0:N], lhsT=CTm[k][:, 4 * c + q, :], rhs=hbs[k][:, 4 * c + q, :],
                                                                     start=(q == 0 and k == 0), stop=(q == 3 and k == 1)), r=[CTm[k], hbs[k]], w=[yb])
                B.dve(lambda e, c=c, yb=yb: e.scalar_tensor_tensor(out=yT[:, c, 0:N], in0=uT[:, c, 0:N], scalar=Dp[:, c:c + 1], in1=yb[:, 0:N], op0=ALU.mult, op1=ALU.add),
                      r=[uT, Dp, yb], w=[yT])
        yf = yT[:].rearrange("p c t -> p (c t)")
        B.pool(lambda e: e.tensor_tensor(out=gt[:], in0=yf, in1=yf, op=ALU.mult), r=[yT], w=[gt])
        B.dve(lambda e: e.tensor_scalar(out=gt[:], in0=gt[:], scalar1=0.044715, scalar2=1.0, op0=ALU.mult, op1=ALU.add), r=[gt], w=[gt])
        B.dve(lambda e: e.tensor_tensor(out=gt[:], in0=gt[:], in1=yf, op=ALU.mult), r=[gt, yT], w=[gt])
        B.act(lambda e: e.activation(out=gt[:], in_=gt[:], func=AF.Sigmoid, scale=2.0 * math.sqrt(2.0 / math.pi)), r=[gt], w=[gt])
        B.dve(lambda e: e.tensor_tensor(out=zT[:].rearrange("p c t -> p (c t)"), in0=gt[:], in1=yf, op=ALU.mult), r=[gt, yT], w=[zT])
        for hf in range(2):
            B.linear(ps[1 + hf], zT, wg, hf * 512, (hf + 1) * 512, 8)
            B.act(lambda e, hf=hf: e.activation(out=sgt[:, hf * 512:(hf + 1) * 512], in_=ps[1 + hf][:, :], func=AF.Sigmoid), r=[ps[1 + hf]], w=[sgt])
            B.linear(ps[6 + hf], zT, wv, hf * 512, (hf + 1) * 512, 8)
            B.dve(lambda e, hf=hf: e.tensor_tensor(out=hbuf[:, hf * 512:(hf + 1) * 512], in0=ps[6 + hf][:, :], in1=sgt[:, hf * 512:(hf + 1) * 512], op=ALU.mult),
                  r=[ps[6 + hf], sgt], w=[hbuf])
        o = xo[j % 2]
        resid_ln_sb(B, x, hbuf, G, Bt, o, tmp)
        for (r0, r1), ap, tt in dst(j):
            B.dma("sp", ap, o[r0:r1, :], r=[o], w=[tt])
    for k, nm in enumerate(("re_p", "im_p")):
        B.pe(lambda e, k=k: e.transpose(ps[0][0:32, k * 128:(k + 1) * 128], HS[k][:, :], B.ident[:, :]), r=[HS[k], B.ident], w=[ps[0]])
    hso = B.sb("hso", [32, 256], F32, ss)
    B.dve(lambda e: e.tensor_copy(out=hso[:], in_=ps[0][0:32, 0:256]), r=[ps[0]], w=[hso])
    for k, nm in enumerate(("re_p", "im_p")):
        B.dma("sp", d[nm].t[m].rearrange("(s g) p -> s (g p)", g=2), hso[:, k * 128:(k + 1) * 128], r=[hso], w=[d[nm]])
    hout = hin
    for k, nm in enumerate(("re_s", "im_s")):
        for s0 in range(0, 32, 4):
            bank = ps[1 + (s0 // 4) % 2]
            for i in range(4):
                B.pe(lambda e, k=k, s0=s0, i=i, bank=bank: e.transpose(bank[0:nsq, i * 128:(i + 1) * 128], Hs[k][:, s0 + i, :], B.ident[:, :]), r=[Hs[k], B.ident], w=[bank])
            B.act(lambda e, s0=s0, bank=bank: e.copy(out=hout[:, s0 * 128:(s0 + 4) * 128], in_=bank[0:nsq, :]), r=[bank], w=[hout])
        B.dma("sp", d[nm].t[m].rearrange("s g p -> s (g p)"), hout[:, :], r=[hout], w=[d[nm]])
    B.sy.barrier()
    ss.close()


def stage_rwkv(B, li, m, src, dst):
    cfg, d, ps = B.cfg, B.d, B.ps
    ntp, nt, nsq, srows = cfg.ntp, cfg.nt, cfg.nsq, cfg.srows
    Xin, Xint = src.X, src.Xt
    if not hasattr(B, "RWd"):
        B.RWd = B.dram_scr("RWd", [6, 128, D], F32)
        B.YSd = B.dram_scr("YSd", [128, D], F32)
    RWd, YSd = B.RWd, B.YSd
    with ExitStack() as st:
        W = {}
        for nm in ("wr", "wk", "wv", "wo"):
            W[nm] = B.sb(nm, [128, 8, D], BF16, st)
            B.load_w(W[nm], d["rw_" + nm].t[m])
        for nm, n in (("w1", 64), ("a1", 64), ("g1", 128)):
            W[nm] = B.sb(nm, [128, 8, n], BF16, st)
            B.load_w(W[nm], d["rw_" + nm].t[m])
        for nm, k in (("w2", 64), ("a2", 64), ("g2", 128)):
            W[nm] = B.sb(nm, [k, 1, D], BF16, st)
            B.load_w(W[nm], d["rw_" + nm].t[m])
        MU = B.sb("MU", [128, 6, 8], F32, st)
        B.load_T(MU, MU[:, :, :].rearrange("p j c -> p (j c)"), d["rw_mu"].t[m].rearrange("j (c p) -> (j c) p", p=128), 48)
        R_ = {}
        for nm in ("w0", "a0", "k_k", "k_a", "r_k", "lnx_g", "lnx_b"):
            R_[nm] = B.sb("r_" + nm, [128, D], F32, st)
            B.load_bcast(R_[nm], d["rw_" + nm].t[m])
        G = B.sb("G", [128, D], F32, st)
        Bt = B.sb("Bt", [128, D], F32, st)
        B.load_bcast(G, d["ln_g"].t[li, 1])
        B.load_bcast(Bt, d["ln_b"].t[li, 1])
        tri = B.sb("tri", [128, 128], F32, st)
        m2 = B.sb("m2", [128, 256], F32, st)
        sl = B.sb("sl", [128, 128], F32, st)
        B.pool(lambda e: e.memset(tri[:], 1.0), w=[tri])
        B.pool(lambda e: e.affine_select(out=tri[:], in_=tri[:], pattern=[[1, 128]], compare_op=ALU.is_ge, fill=0.0, base=0, channel_multiplier=-1), r=[tri], w=[tri])
        B.pool(lambda e: e.memset(m2[:], 1.0), w=[m2])
        B.pool(lambda e: e.affine_select(out=m2[:, 0:128], in_=m2[:, 0:128], pattern=[[1, 128]], compare_op=ALU.is_gt, fill=0.0, base=0, channel_multiplier=-1), r=[m2], w=[m2])
        B.pool(lambda e: e.affine_select(out=m2[:, 128:256], in_=m2[:, 128:256], pattern=[[1, 128]], compare_op=ALU.is_ge, fill=0.0, base=0, channel_multiplier=-1), r=[m2], w=[m2])
        B.pool(lambda e: e.memset(sl[:], 1.0), w=[sl])
        B.pool(lambda e: e.affine_select(out=sl[:], in_=sl[:], pattern=[[-1, 128]], compare_op=ALU.is_gt, fill=0.0, base=0, channel_multiplier=1), r=[sl], w=[sl])
        vmask = B.sb("vmask", [128, 1], F32, st)
        B.pool(lambda e: e.memset(vmask[:], 1.0), w=[vmask])
        B.pool(lambda e: e.affine_select(out=vmask[:], in_=vmask[:], pattern=[[0, 1]], compare_op=ALU.is_gt, fill=0.0, base=cfg.rows(ntp - 1), channel_multiplier=-1), r=[vmask], w=[vmask])
        ST = B.sb("ST", [64, NH, 64], F32, st)
        STb = B.sb("STb", [64, NH, 64], BF16, st)
        B.dve(lambda e: e.memset(ST[:], 0.0), w=[ST])
        B.dve(lambda e: e.memset(STb[:], 0.0), w=[STb])

        s2 = ExitStack()
        f32 = lambda nm: B.sb(nm, [128, D], F32, s2)
        x, xp, t0, t1 = f32("x"), f32("xp"), f32("t0"), f32("t1")
        r32, k32, v32, a32, ld, kk = f32("r32"), f32("k32"), f32("v32"), f32("a32"), f32("ld"), f32("kk")
        g32 = xp
        xT = B.sb("xT", [128, 8, 128], F32, s2)
        xxT = B.sb("xxT", [128, 8, 128], F32, s2)
        mixT = [B.sb("mixT", [128, 8, 128], BF16, s2) for _ in range(2)]
        lo = B.sb("lo", [128, 128], BF16, s2)
        loT = B.sb("loT", [128, 1, 128], BF16, s2)
        ss16 = B.sb("ss16", [128, 4, NH], F32, s2)
        bfs = {nm: B.sb(nm, [128, D], BF16, s2) for nm in ("rt", "at", "bt", "kt", "vb")}
        tmp = dict(xa=t0, y=t1, junk=kk, st=B.sb("st", [128, 8], F32, s2))
        xo = [B.sb("xo", [128, D], F32, s2)] * 2

        def mix(jx, buf):
            B.pool(lambda e: e.tensor_tensor(out=t0[:].rearrange("p (c t) -> p c t", t=128), in0=xxT[:], in1=MU[:, jx, :].unsqueeze(2).broadcast_to([128, 8, 128]), op=ALU.mult),
                   r=[xxT, MU], w=[t0])
            B.dve(lambda e: e.tensor_tensor(out=buf[:], in0=t0[:].rearrange("p (c t) -> p c t", t=128), in1=xT[:], op=ALU.add), r=[t0, xT], w=[buf])

        def proj_full(jx, wname, out32):
            buf = mixT[jx % 2]
            mix(jx, buf)
            for hf in range(2):
                B.linear(ps[1 + hf], buf, W[wname], hf * 512, (hf + 1) * 512, 8)
                B.act(lambda e, hf=hf: e.copy(out=out32[:, hf * 512:(hf + 1) * 512], in_=ps[1 + hf][:, :]), r=[ps[1 + hf]], w=[out32])

        def proj_lora(jx, w1n, w2n, n1, mid_func, bias_row, out_func, out32, scale=1.0):
            buf = mixT[jx % 2]
            mix(jx, buf)
            B.linear(ps[3], buf, W[w1n], 0, n1, 8)
            B.act(lambda e: e.activation(out=lo[:, 0:n1], in_=ps[3][:, 0:n1], func=mid_func), r=[ps[3]], w=[lo])
            B.transpose_to(lo, 1, loT, loT[0:n1, :, :], ps[0], cw=n1)
            for hf in range(2):
                B.pe(lambda e, hf=hf: e.matmul(ps[1 + hf][:, :], lhsT=loT[0:n1, 0, :], rhs=W[w2n][0:n1, 0, hf * 512:(hf + 1) * 512], start=True, stop=True),
                     r=[loT, W[w2n]], w=[ps[1 + hf]])
                if bias_row is not None:
                    B.dve(lambda e, hf=hf: e.tensor_tensor(out=out32[:, hf * 512:(hf + 1) * 512], in0=ps[1 + hf][:, :], in1=bias_row[:, hf * 512:(hf + 1) * 512], op=ALU.add),
                          r=[ps[1 + hf], bias_row], w=[out32])
                    B.act(lambda e, hf=hf: e.activation(out=out32[:, hf * 512:(hf + 1) * 512], in_=out32[:, hf * 512:(hf + 1) * 512], func=out_func), r=[out32], w=[out32])
                else:
                    B.act(lambda e, hf=hf: e.copy(out=out32[:, hf * 512:(hf + 1) * 512], in_=ps[1 + hf][:, :]), r=[ps[1 + hf]], w=[out32])

        def v3(t_, n=64):
            return t_[:].rearrange("p (h k) -> p h k", k=n)

        def bc16(col_ap):
            return col_ap.unsqueeze(2).broadcast_to([128, NH, 64])

        def front(j):
            rows = cfg.rows(j)
            base = 128 * j
            sample = (j == ntp)
            if rows < 128:
                B.dve(lambda e: e.memset(x[:], 0.0), w=[x])
            B.dve(lambda e: e.memset(xp[:], 0.0), w=[xp])
            B.dma("sp", x[0:rows, :], Xin[base:base + rows, :], r=[Xint[j]], w=[x])
            if not sample:
                if j > 0:
                    B.dma("sp", xp[0:rows, :], Xin[base - 1:base - 1 + rows, :], r=[Xint[j], Xint[j - 1]], w=[xp])
                else:
                    B.dma("sp", xp[1:rows, :], Xin[0:rows - 1, :], r=[Xint[j]], w=[xp])
            else:
                xv = Xin[base:base + rows, :].rearrange("(s t) n -> s t n", t=DEC_SEQ)
                for s_ in range(nsq):
                    B.dma("sp", xp[s_ * DEC_SEQ:s_ * DEC_SEQ + 1, :], d["state_rwkv_shift"].t[m, s_:s_ + 1, :], w=[xp])
                    B.dma("sp", xp[s_ * DEC_SEQ + 1:(s_ + 1) * DEC_SEQ, :], xv[s_, 0:DEC_SEQ - 1, :], r=[Xint[j]], w=[xp])
            B.dve(lambda e: e.tensor_tensor(out=xp[:], in0=xp[:], in1=x[:], op=ALU.subtract), r=[xp, x], w=[xp])
            for src_, dstT in ((x, xT), (xp, xxT)):
                for h2 in range(2):
                    for c in range(4):
                        B.pe(lambda e, h2=h2, c=c, src_=src_: e.transpose(ps[4 + h2][:, c * 128:(c + 1) * 128], src_[:, (h2 * 4 + c) * 128:(h2 * 4 + c + 1) * 128], B.ident[:, :]),
                             r=[src_, B.ident], w=[ps[4 + h2]])
                    B.act(lambda e, h2=h2, dstT=dstT: e.copy(out=dstT[:, h2 * 4:(h2 + 1) * 4, :], in_=ps[4 + h2][:, :].rearrange("p (c t) -> p c t", t=128)), r=[ps[4 + h2]], w=[dstT])
            proj_full(0, "wr", r32)
            proj_lora(1, "w1", "w2", 64, AF.Tanh, R_["w0"], AF.Sigmoid, ld)
            proj_full(2, "wk", k32)
            proj_full(3, "wv", v32)
            proj_lora(4, "a1", "a2", 64, AF.Copy, R_["a0"], AF.Sigmoid, a32)
            proj_lora(5, "g1", "g2", 128, AF.Sigmoid, None, None, g32)
            B.act(lambda e: e.activation(out=ld[:], in_=ld[:], func=AF.Copy, scale=-math.exp(-0.5)), r=[ld], w=[ld])
            if rows < 128 and not sample:
                B.dve(lambda e: e.tensor_scalar(out=ld[:], in0=ld[:], scalar1=vmask[:, 0:1], scalar2=None, op0=ALU.mult), r=[ld, vmask], w=[ld])
            B.dve(lambda e: e.tensor_tensor(out=kk[:], in0=k32[:], in1=R_["k_k"][:], op=ALU.mult), r=[k32, R_["k_k"]], w=[kk])
            B.pool(lambda e: e.tensor_tensor(out=t0[:], in0=kk[:], in1=kk[:], op=ALU.mult), r=[kk], w=[t0])
            B.dve(lambda e: e.tensor_reduce(out=ss16[:, 0, :], in_=v3(t0), axis=AX.X, op=ALU.add), r=[t0], w=[ss16])
            B.dve(lambda e: e.tensor_scalar(out=ss16[:, 0, :], in0=ss16[:, 0, :], scalar1=1e-24, scalar2=None, op0=ALU.max), r=[ss16], w=[ss16])
            B.act(lambda e: e.activation(out=ss16[:, 0, :], in_=ss16[:, 0, :], func=AF.Sqrt), r=[ss16], w=[ss16])
            B.dve(lambda e: e.reciprocal(out=ss16[:, 0, :], in_=ss16[:, 0, :]), r=[ss16], w=[ss16])
            B.dve(lambda e: e.tensor_tensor(out=v3(kk), in0=v3(kk), in1=bc16(ss16[:, 0, :]), op=ALU.mult), r=[kk, ss16], w=[kk])
            B.dve(lambda e: e.scalar_tensor_tensor(out=t0[:], in0=a32[:], scalar=-1.0, in1=R_["k_a"][:], op0=ALU.add, op1=ALU.mult), r=[a32, R_["k_a"]], w=[t0])
            B.dve(lambda e: e.scalar_tensor_tensor(out=k32[:], in0=t0[:], scalar=1.0, in1=k32[:], op0=ALU.add, op1=ALU.mult), r=[t0, k32], w=[k32])
            B.pool(lambda e: e.tensor_tensor(out=t0[:], in0=r32[:], in1=k32[:], op=ALU.mult), r=[r32, k32], w=[t0])
            B.dve(lambda e: e.tensor_tensor(out=t0[:], in0=t0[:], in1=R_["r_k"][:], op=ALU.mult), r=[t0, R_["r_k"]], w=[t0])
            B.dve(lambda e: e.tensor_reduce(out=ss16[:, 1, :], in_=v3(t0), axis=AX.X, op=ALU.add), r=[t0], w=[ss16])
            B.dve(lambda e: e.tensor_tensor(out=v3(t0), in0=v3(v32), in1=bc16(ss16[:, 1, :]), op=ALU.mult), r=[v32, ss16], w=[t0])
            B.pool(lambda e: e.tensor_tensor(out=t1[:], in0=kk[:], in1=a32[:], op=ALU.mult), r=[kk, a32], w=[t1])

        def post(j, y32):
            B.dve(lambda e: e.tensor_reduce(out=ss16[:, 2, :], in_=v3(y32), axis=AX.X, op=ALU.add), r=[y32], w=[ss16])
            B.dve(lambda e: e.tensor_scalar(out=ss16[:, 2, :], in0=ss16[:, 2, :], scalar1=-1.0 / 64, scalar2=None, op0=ALU.mult), r=[ss16], w=[ss16])
            B.dve(lambda e: e.tensor_tensor(out=v3(y32), in0=v3(y32), in1=bc16(ss16[:, 2, :]), op=ALU.add), r=[y32, ss16], w=[y32])
            B.pool(lambda e: e.tensor_tensor(out=t1[:], in0=y32[:], in1=y32[:], op=ALU.mult), r=[y32], w=[t1])
            B.dve(lambda e: e.tensor_reduce(out=ss16[:, 3, :], in_=v3(t1), axis=AX.X, op=ALU.add), r=[t1], w=[ss16])
            B.act(lambda e: e.activation(out=ss16[:, 3, :], in_=ss16[:, 3, :], func=AF.Sqrt, bias=B.eps[:, 2:3], scale=1.0 / 64), r=[ss16, B.eps], w=[ss16])
            B.dve(lambda e: e.reciprocal(out=ss16[:, 3, :], in_=ss16[:, 3, :]), r=[ss16], w=[ss16])
            B.dve(lambda e: e.tensor_tensor(out=v3(y32), in0=v3(y32), in1=bc16(ss16[:, 3, :]), op=ALU.mult), r=[y32, ss16], w=[y32])
            B.dve(lambda e: e.tensor_tensor(out=y32[:], in0=y32[:], in1=R_["lnx_g"][:], op=ALU.mult), r=[y32, R_["lnx_g"]], w=[y32])
            B.pool(lambda e: e.tensor_tensor(out=y32[:], in0=y32[:], in1=R_["lnx_b"][:], op=ALU.add), r=[y32, R_["lnx_b"]], w=[y32])
            B.dve(lambda e: e.tensor_tensor(out=y32[:], in0=y32[:], in1=t0[:], op=ALU.add), r=[y32, t0], w=[y32])
            yg = bfs["rt"]
            B.dve(lambda e: e.tensor_tensor(out=yg[:], in0=y32[:], in1=g32[:], op=ALU.mult), r=[y32, g32], w=[yg])
            buf = mixT[0]
            B.transpose_to(yg, 8, buf, buf[:, :, :], ps[0])
            for hf in range(2):
                B.linear(ps[6 + hf], buf, W["wo"], hf * 512, (hf + 1) * 512, 8)
            o = xo[j % 2]
            B.resid_ln(x, [ps[6], ps[7]], 1.0, G, Bt, o, tmp)
            for (r0, r1), ap, tt in dst(j):
                B.dma("sp", ap, o[r0:r1, :], r=[o], w=[tt])

        rwkv_prompt_loop(B, m, front, post, locals())
        front(ntp)
        B.act(lambda e: e.activation(out=ld[:], in_=ld[:], func=AF.Exp), r=[ld], w=[ld])
        B.dve(lambda e: e.tensor_scalar(out=kk[:], in0=kk[:], scalar1=-1.0, scalar2=None, op0=ALU.mult), r=[kk], w=[kk])
        for i, t_ in enumerate((r32, ld, k32, v32, kk, t1)):
            B.dma("sp", RWd.t[i, 0:srows, :], t_[0:srows, :], r=[t_], w=[RWd])
        rwkv_sample_rec(B, m, RWd, YSd)
        y32 = r32
        B.dma("sp", y32[0:srows, :], YSd.t[0:srows, :], r=[YSd], w=[y32])
        post(ntp, y32)
        B.dma("sp", d["sh_p"].t[m:m + 1, :], Xin[cfg.L - 1:cfg.L, :], r=[Xint[ntp - 1]], w=[d["sh_p"]])
        B.dma("sp", d["sh_s"].t[m], Xin[128 * ntp:128 * ntp + srows, :].rearrange("(s t) n -> s t n", t=DEC_SEQ)[:, DEC_SEQ - 1, :], r=[Xint[ntp]], w=[d["sh_s"]])
        B.sy.barrier()
        s2.close()


def rwkv_prompt_loop(B, m, front, post, L):
    cfg, d, ps = B.cfg, B.d, B.ps
    ntp = cfg.ntp
    r32, k32, v32, ld, kk, t0, t1, a32 = (L[k] for k in ("r32", "k32", "v32", "ld", "kk", "t0", "t1", "a32"))
    bfs, tri, m2, sl, ST, STb = (L[k] for k in ("bfs", "tri", "m2", "sl", "ST", "STb"))
    with ExitStack() as s3:
        HT = B.sb("HT", [64, NH, 4, 128], BF16, s3)
        WC = B.sb("WC", [64, NH], F32, s3)
        HB = []
        for par in range(2):
            HB.append((B.sb("AKm", [128, 256], BF16, s3), B.sb("ABm", [128, 256], BF16, s3),
                       [B.sb("NX", [128, 256], BF16, s3) for _ in range(2)], [B.sb("NT", [128, 128], BF16, s3) for _ in range(2)],
                       B.sb("Zb", [128, 64], BF16, s3), B.sb("Ub", [128, 64], BF16, s3)))
        ones1 = B.ones
        for j in range(ntp):
            front(j)
            for hf in range(2):
                B.pe(lambda e, hf=hf: e.matmul(ps[1 + hf][:, :], lhsT=tri[:, :], rhs=ld[:, hf * 512:(hf + 1) * 512], start=True, stop=True), r=[tri, ld], w=[ps[1 + hf]])
            for h in range(NH):
                B.pe(lambda e, h=h: e.matmul(ps[3][0:64, h:h + 1], lhsT=ld[:, h * 64:(h + 1) * 64], rhs=ones1[:, 0:1], start=True, stop=True), r=[ld, ones1], w=[ps[3]])
            B.act(lambda e: e.activation(out=WC[:, :], in_=ps[3][0:64, 0:NH], func=AF.Exp), r=[ps[3]], w=[WC])
            y32 = a32
            for hf in range(2):
                sl_ = slice(hf * 512, (hf + 1) * 512)
                cum = ps[1 + hf]
                B.act(lambda e, sl_=sl_, cum=cum: e.activation(out=y32[:, sl_], in_=cum[:, :], func=AF.Exp), r=[cum], w=[y32])
                B.dve(lambda e, sl_=sl_: e.tensor_tensor(out=bfs["rt"][:, sl_], in0=r32[:, sl_], in1=y32[:, sl_], op=ALU.mult), r=[r32, y32], w=[bfs["rt"]])
                B.act(lambda e, sl_=sl_, cum=cum: e.activation(out=y32[:, sl_], in_=cum[:, :], func=AF.Exp, scale=-1.0), r=[cum], w=[y32])
                B.dve(lambda e, sl_=sl_: e.tensor_tensor(out=bfs["kt"][:, sl_], in0=k32[:, sl_], in1=y32[:, sl_], op=ALU.mult), r=[k32, y32], w=[bfs["kt"]])
                B.pool(lambda e, sl_=sl_: e.tensor_tensor(out=bfs["bt"][:, sl_], in0=t1[:, sl_], in1=y32[:, sl_], op=ALU.mult), r=[t1, y32], w=[bfs["bt"]])
                B.dve(lambda e, sl_=sl_, cum=cum: e.tensor_tensor(out=y32[:, sl_], in0=cum[:, :], in1=ld[:, sl_], op=ALU.subtract), r=[cum, ld], w=[y32])
                B.act(lambda e, sl_=sl_: e.activation(out=y32[:, sl_], in_=y32[:, sl_], func=AF.Exp), r=[y32], w=[y32])
                B.dve(lambda e, sl_=sl_: e.scalar_tensor_tensor(out=bfs["at"][:, sl_], in0=kk[:, sl_], scalar=-1.0, in1=y32[:, sl_], op0=ALU.mult, op1=ALU.mult), r=[kk, y32], w=[bfs["at"]])
            B.pool(lambda e: e.tensor_copy(out=bfs["vb"][:], in_=v32[:]), r=[v32], w=[bfs["vb"]])
            for h in range(NH):
                bi = 4 + h % 2
                pv = B.psb(bi)
                for i, nm in enumerate(("at", "rt", "kt", "bt")):
                    B.pe(lambda e, h=h, i=i, nm=nm, pv=pv: e.transpose(pv[0:64, i * 128:(i + 1) * 128], bfs[nm][:, h * 64:(h + 1) * 64], B.identb[:, :]), r=[bfs[nm], B.identb], w=[ps[bi]])
                B.act(lambda e, h=h, pv=pv: e.copy(out=HT[:, h, :, :], in_=pv[0:64, 0:512].rearrange("p (i t) -> p i t", t=128)), r=[ps[bi]], w=[HT])
            def head_steps(h, par):
                bA, bB, bM = (ps[1], ps[2], ps[3]) if par == 0 else (ps[4], ps[5], ps[0])
                AKm, ABm, NX, NT, Zb, Ub = HB[par]
                hs = slice(h * 64, (h + 1) * 64)
                rhsAR = HT[:, h, 0:2, :].rearrange("p i t -> p (i t)")
                B.pe(lambda e: e.matmul(bA[:, 0:256], lhsT=HT[:, h, 2, :], rhs=rhsAR, start=True, stop=True), r=[HT], w=[bA])
                B.pe(lambda e: e.matmul(bB[:, 0:256], lhsT=HT[:, h, 3, :], rhs=rhsAR, start=True, stop=True), r=[HT], w=[bB])
                B.pe(lambda e: e.matmul(bM[:, 0:128], lhsT=HT[:, h, 0, :], rhs=HT[:, h, 3, :], start=True, stop=True), r=[HT], w=[bM])
                yield
                B.dve(lambda e: e.tensor_tensor(out=AKm[:], in0=bA[:, 0:256], in1=m2[:], op=ALU.mult), r=[bA, m2], w=[AKm])
                B.dve(lambda e: e.tensor_tensor(out=ABm[:], in0=bB[:, 0:256], in1=m2[:], op=ALU.mult), r=[bB, m2], w=[ABm])
                B.dve(lambda e: e.tensor_tensor(out=NT[0][:], in0=bM[:, 0:128], in1=sl[:], op=ALU.mult), r=[bM, sl], w=[NT[0]])
                yield
                B.pool(lambda e: e.tensor_copy(out=NX[0][:, 0:128], in_=ABm[:, 0:128]), r=[ABm], w=[NX[0]])
                B.pool(lambda e: e.tensor_tensor(out=NX[0][:, 128:256], in0=ABm[:, 0:128], in1=B.identb[:, :], op=ALU.add), r=[ABm, B.identb], w=[NX[0]])
                yield
                cur = 0
                for lvl in range(7):
                    nx, nt_ = NX[cur], NT[cur]
                    nx2, nt2 = NX[1 - cur], NT[1 - cur]
                    if lvl == 0:
                        B.pe(lambda e: e.matmul(bA[:, 0:128], lhsT=nt_[:, :], rhs=nx[:, 0:128], start=True, stop=True), r=[nt_, nx], w=[bA])
                        B.pe(lambda e: e.matmul(bB[:, 0:128], lhsT=nx[:, 0:128], rhs=nt_[:, :], start=True, stop=True), r=[nt_, nx], w=[bB])
                        yield
                        B.act(lambda e: e.copy(out=nx2[:, 0:128], in_=bA[:, 0:128]), r=[bA], w=[nx2])
                        B.dve(lambda e: e.tensor_copy(out=nx2[:, 128:256], in_=nx[:, 128:256]), r=[nx], w=[nx2])
                        B.act(lambda e: e.copy(out=nt2[:, :], in_=bB[:, 0:128]), r=[bB], w=[nt2])
                    elif lvl < 6:
                        B.pe(lambda e: e.matmul(bA[:, 0:256], lhsT=nt_[:, :], rhs=nx[:, :], start=True, stop=True), r=[nt_, nx], w=[bA])
                        B.pe(lambda e: e.matmul(bB[:, 0:128], lhsT=nx[:, 0:128], rhs=nt_[:, :], start=True, stop=True), r=[nt_, nx], w=[bB])
                        yield
                        B.act(lambda e: e.copy(out=nx2[:, 0:128], in_=bA[:, 0:128]), r=[bA], w=[nx2])
                        B.dve(lambda e: e.tensor_tensor(out=nx2[:, 128:256], in0=bA[:, 128:256], in1=nx[:, 128:256], op=ALU.add), r=[bA, nx], w=[nx2])
                        B.act(lambda e: e.copy(out=nt2[:, :], in_=bB[:, 0:128]), r=[bB], w=[nt2])
                    else:
                        B.pe(lambda e: e.matmul(bA[:, 0:128], lhsT=nt_[:, :], rhs=nx[:, 128:256], start=True, stop=True), r=[nt_, nx], w=[bA])
                        yield
                        B.dve(lambda e: e.tensor_tensor(out=nx2[:, 128:256], in0=bA[:, 0:128], in1=nx[:, 128:256], op=ALU.add), r=[bA, nx], w=[nx2])
                    cur = 1 - cur
                    yield
                XT = NX[cur]
                B.pe(lambda e: e.matmul(bM[:, 0:64], lhsT=HT[:, h, 0, :], rhs=STb[:, h, :], start=True, stop=False), r=[HT, STb], w=[bM])
                B.pe(lambda e: e.matmul(bM[:, 0:64], lhsT=AKm[:, 0:128], rhs=bfs["vb"][:, hs], start=False, stop=True), r=[AKm, bfs["vb"]], w=[bM])
                yield
                B.act(lambda e: e.copy(out=Zb[:, :], in_=bM[:, 0:64]), r=[bM], w=[Zb])
                yield
                B.pe(lambda e: e.matmul(bM[:, 64:128], lhsT=XT[:, 128:256], rhs=Zb[:, :], start=True, stop=True), r=[XT, Zb], w=[bM])
                yield
                B.act(lambda e: e.copy(out=Ub[:, :], in_=bM[:, 64:128]), r=[bM], w=[Ub])
                yield
                yb = ps[6 + (h // 8) % 2]
                yc = slice((h % 8) * 64, (h % 8 + 1) * 64)
                B.pe(lambda e: e.matmul(yb[:, yc], lhsT=HT[:, h, 1, :], rhs=STb[:, h, :], start=True, stop=False), r=[HT, STb], w=[yb])
                B.pe(lambda e: e.matmul(yb[:, yc], lhsT=ABm[:, 128:256], rhs=Ub[:, :], start=False, stop=False), r=[ABm, Ub], w=[yb])
                B.pe(lambda e: e.matmul(yb[:, yc], lhsT=AKm[:, 128:256], rhs=bfs["vb"][:, hs], start=False, stop=True), r=[AKm, bfs["vb"]], w=[yb])
                B.pe(lambda e: e.matmul(bM[0:64, 128:192], lhsT=bfs["bt"][:, hs], rhs=Ub[:, :], start=True, stop=False), r=[bfs["bt"], Ub], w=[bM])
                B.pe(lambda e: e.matmul(bM[0:64, 128:192], lhsT=bfs["kt"][:, hs], rhs=bfs["vb"][:, hs], start=False, stop=True), r=[bfs["kt"], bfs["vb"]], w=[bM])
                yield
                B.dve(lambda e: e.tensor_scalar(out=ST[:, h, :], in0=ST[:, h, :], scalar1=WC[:, h:h + 1], scalar2=None, op0=ALU.mult), r=[ST, WC], w=[ST])
                B.dve(lambda e: e.scalar_tensor_tensor(out=ST[:, h, :], in0=bM[0:64, 128:192], scalar=WC[:, h:h + 1], in1=ST[:, h, :], op0=ALU.mult, op1=ALU.add), r=[bM, WC, ST], w=[ST])
                B.act(lambda e: e.copy(out=STb[:, h, :], in_=ST[:, h, :]), r=[ST], w=[STb])
                yield

            for h0 in range(0, NH, 2):
                gens = [head_steps(h0, 0), head_steps(h0 + 1, 1)]
                alive = [True, True]
                while any(alive):
                    for gi, g_ in enumerate(gens):
                        if alive[gi]:
                            try:
                                next(g_)
                            except StopIteration:
                                alive[gi] = False
                if h0 % 8 == 6:
                    yb = ps[6 + (h0 // 8) % 2]
                    B.act(lambda e, yb=yb, h0=h0: e.copy(out=y32[:, (h0 - 6) * 64:(h0 + 2) * 64], in_=yb[:, :]), r=[yb], w=[y32])
            post(j, y32)
        so = T(HT.t[:].rearrange("p h i t -> p (h i t)").bitcast(F32)[:, 0:NH * 64].rearrange("p (h k) -> p h k", k=64), "so")
        so.wr, so.rd = HT.wr, HT.rd
        for h0 in range(0, NH, 8):
            bank = ps[1 + (h0 // 8) % 2]
            for i in range(8):
                B.pe(lambda e, h0=h0, i=i, bank=bank: e.transpose(bank[0:64, i * 64:(i + 1) * 64], ST[:, h0 + i, :], B.ident[0:64, 0:64]), r=[ST, B.ident], w=[bank])
            B.act(lambda e, h0=h0, bank=bank: e.copy(out=so[:, h0:h0 + 8, :], in_=bank[0:64, :].rearrange("p (h k) -> p h k", k=64)), r=[bank], w=[so])
        B.dma("sp", d["wkv_p"].t[m].rearrange("h v k -> v h k"), so[:, :, :], r=[so], w=[d["wkv_p"]])
        B.sy.barrier()


def rwkv_sample_rec(B, m, RWd, YSd):
    cfg, d, ps = B.cfg, B.d, B.ps
    nsq, srows = cfg.nsq, cfg.srows
    P = nsq * 8
    VS = 8
    with ExitStack() as s3:
        vec = [B.sb("vec", [P, DEC_SEQ, 128], F32, s3) for _ in range(6)]
        for i in range(6):
            for s_ in range(nsq):
                B.dma("sp", vec[i][s_ * 8:(s_ + 1) * 8, :, :], RWd.t[i, s_ * DEC_SEQ:(s_ + 1) * DEC_SEQ, :].rearrange("t (g c) -> g t c", c=128), r=[RWd], w=[vec[i]])
        Yall = B.sb("Yall", [P, DEC_SEQ, 128], F32, s3)
        S = B.sb("S", [P, VS, 64], F32, s3)
        tmp = B.sb("tmp", [P, VS, 64], F32, s3)
        sa = B.sb("sa", [P, VS], F32, s3)
        wkv_in = d["state_rwkv_wkv"].t[m].rearrange("s (g h2) v k -> (s g) h2 v k", h2=2)
        wkv_out = d["wkv_s"].t[m].rearrange("s (g h2) v k -> (s g) h2 v k", h2=2)
        n = 0
        for h2 in range(2):
            kvec = lambda i, t: vec[i][:, t, h2 * 64:(h2 + 1) * 64].unsqueeze(1).broadcast_to([P, VS, 64])
            for v0 in range(0, 64, VS):
                B.dma("sp", S[:, :, :], wkv_in[:, h2, v0:v0 + VS, :], w=[S])
                for t in range(DEC_SEQ):
                    vv = vec[3][:, t, h2 * 64 + v0:h2 * 64 + v0 + VS]
                    e1, e2 = ("dve", "pool") if n % 2 == 0 else ("pool", "dve")
                    n += 1
                    B.dve(lambda e: e.tensor_tensor(out=tmp[:], in0=S[:], in1=kvec(4, t), op=ALU.mult), r=[S, vec[4]], w=[tmp])
                    B.dve(lambda e: e.tensor_reduce(out=sa[:, :], in_=tmp[:], axis=AX.X, op=ALU.add), r=[tmp], w=[sa])
                    B.dve(lambda e: e.tensor_tensor(out=S[:], in0=S[:], in1=kvec(1, t), op=ALU.mult), r=[S, vec[1]], w=[S])
                    B.dve(lambda e: e.tensor_tensor(out=tmp[:], in0=sa[:, :].unsqueeze(2).broadcast_to([P, VS, 64]), in1=kvec(5, t), op=ALU.mult), r=[sa, vec[5]], w=[tmp])
                    B.dve(lambda e: e.tensor_tensor(out=S[:], in0=S[:], in1=tmp[:], op=ALU.add), r=[S, tmp], w=[S])
                    B.dve(lambda e, vv=vv: e.tensor_tensor(out=tmp[:], in0=vv.unsqueeze(2).broadcast_to([P, VS, 64]), in1=kvec(2, t), op=ALU.mult), r=[vec[3], vec[2]], w=[tmp])
                    B.dve(lambda e: e.tensor_tensor(out=S[:], in0=S[:], in1=tmp[:], op=ALU.add), r=[S, tmp], w=[S])
                    B.dve(lambda e: e.tensor_tensor(out=tmp[:], in0=S[:], in1=kvec(0, t), op=ALU.mult), r=[S, vec[0]], w=[tmp])
                    B.dve(lambda e, t=t: e.tensor_reduce(out=Yall[:, t, h2 * 64 + v0:h2 * 64 + v0 + VS], in_=tmp[:], axis=AX.X, op=ALU.add), r=[tmp], w=[Yall])
                B.dma("sp", wkv_out[:, h2, v0:v0 + VS, :], S[:, :, :], r=[S], w=[d["wkv_s"]])
        for s_ in range(nsq):
            B.dma("sp", YSd.t[s_ * DEC_SEQ:(s_ + 1) * DEC_SEQ, :].rearrange("t (g c) -> g t c", c=128), Yall[s_ * 8:(s_ + 1) * 8, :, :], r=[Yall], w=[YSd])
        B.sy.barrier()
```

```python
from contextlib import ExitStack
import math
import numpy as np
import concourse.bass as bass
import concourse.mybir as mybir
from concourse.bass_utils import run_bass_kernel_spmd

F32 = mybir.dt.float32
BF16 = mybir.dt.bfloat16
I32 = mybir.dt.int32
AF = mybir.ActivationFunctionType
ALU = mybir.AluOpType
AX = mybir.AxisListType


DEBUG_BARRIER = False
DEBUG_TILES = None


class T:
    __slots__ = ("t", "wr", "rd", "name")

    def __init__(self, t, name=""):
        self.t = t
        self.wr = {}
        self.rd = {}
        self.name = name

    def __getitem__(self, k):
        return self.t[k]


class Sync:
    NDMA = 24
    MAXFLY = 16

    def __init__(self, nc, es):
        self.nc = nc
        self.es = es
        self.eng = {"pe": nc.tensor, "act": nc.scalar, "dve": nc.vector, "pool": nc.gpsimd, "sp": nc.sync}
        self.sems = {}
        self.cnt = {}
        self.waited = {}
        for e in ("pe", "act", "dve", "pool"):
            self._mk("c_" + e)
        self.dma_rr = {}
        for q in ("sp", "pool", "act"):
            self.dma_rr[q] = 0
            for i in range(self.NDMA):
                self._mk("d_%s%d" % (q, i))
        self.n_ins = 0
        self.dma_hist = {}

    def _mk(self, name):
        self.sems[name] = self.es.enter_context(self.nc.semaphore(name))
        self.cnt[name] = 0

    def _wait(self, en, evs):
        own = "c_" + en
        e = self.eng[en]
        for s, v in evs.items():
            if s == own and en == "pe":
                continue
            if self.waited.get((en, s), 0) >= v:
                continue
            e.wait_ge(self.sems[s], v)
            self.waited[(en, s)] = v

    def op(self, en, fn, reads=(), writes=(), dma=False):
        evs = {}
        for t in reads:
            for s, v in t.wr.items():
                if evs.get(s, 0) < v:
                    evs[s] = v
        for t in writes:
            for d in (t.wr, t.rd):
                for s, v in d.items():
                    if evs.get(s, 0) < v:
                        evs[s] = v
        if dma:
            e = self.eng[en]
            for s, v in evs.items():
                if self.waited.get((en, s), 0) >= v:
                    continue
                e.wait_ge(self.sems[s], v)
                self.waited[(en, s)] = v
            hist = self.dma_hist.setdefault(en, [])
            if len(hist) >= self.MAXFLY:
                s_old, v_old = hist[-self.MAXFLY]
                if self.waited.get((en, s_old), 0) < v_old:
                    e.wait_ge(self.sems[s_old], v_old)
                    self.waited[(en, s_old)] = v_old
            i = self.dma_rr[en]
            self.dma_rr[en] = (i + 1) % self.NDMA
            sname = "d_%s%d" % (en, i)
            inc = 16
        else:
            self._wait(en, evs)
            sname = "c_" + en
            inc = 1
        ins = fn(self.eng[en])
        self.cnt[sname] += inc
        ins.then_inc(self.sems[sname], inc)
        v = self.cnt[sname]
        if dma:
            self.dma_hist[en].append((sname, v))
            if len(self.dma_hist[en]) > 64:
                del self.dma_hist[en][:32]
        for t in writes:
            t.wr[sname] = v
        for t in reads:
            t.rd[sname] = v
        self.n_ins += 1
        return ins

    def barrier(self):
        snap = dict(self.cnt)
        for en in ("pe", "act", "dve", "pool", "sp"):
            self._wait(en, snap)

    def finish(self):
        self.barrier()


D = 1024
DFF = 2816
NH = 16
QR = 768
KVR = 256
NOPE = 64
ROPE = 32
QK = 96
VD = 64
LN_EPS = 1e-5
RMS_EPS = 1e-6
GN_EPS = 64e-5
N_META = 16
PAGE = 128
DEC_SEQ = 4


class Cfg:
    def __init__(self, ntf=64, nsq=16, npages=64, npool=10240, depth=4, n_cores=8, n_batch=2):
        self.ntf = ntf
        self.L = 128 * ntf + N_META
        self.seq = 128 * ntf
        self.ntp = ntf + 1
        self.nsq = nsq
        self.srows = nsq * DEC_SEQ
        self.nt = self.ntp + 1
        self.npages = npages
        self.past = npages * PAGE
        self.npool = npool
        self.depth = depth
        self.alpha = (2 * depth) ** 0.25
        self.n_mla = (depth + 2) // 3
        self.n_rwkv = (depth + 1) // 3
        self.n_s5 = depth // 3
        self.n_cores = n_cores
        self.n_batch = n_batch

    def rows(self, j):
        if j < self.ntf:
            return 128
        if j == self.ntf:
            return N_META
        return self.srows


class Builder:
    def __init__(self, nc, cfg):
        self.nc = nc
        self.cfg = cfg
        self.es = ExitStack()
        self.sy = Sync(nc, self.es)
        self.din = {}
        self.dout = {}
        self.uid = 0

    def sb(self, name, shape, dt=F32, st=None):
        self.uid += 1
        return T((st or self.es).enter_context(self.nc.sbuf_tensor("%s_%d" % (name, self.uid), list(shape), dt)), name)

    def dram_in(self, name, shape, dt=F32):
        t = T(self.nc.dram_tensor(name, list(shape), dt, kind="ExternalInput").ap(), name)
        self.din[name] = t
        return t

    def dram_out(self, name, shape, dt=F32):
        t = T(self.nc.dram_tensor(name, list(shape), dt, kind="ExternalOutput").ap(), name)
        self.dout[name] = t
        return t

    def dram_scr(self, name, shape, dt=F32):
        return T(self.nc.dram_tensor(name, list(shape), dt, kind="Internal").ap(), name)

    def pe(self, fn, r=(), w=()):
        return self.sy.op("pe", fn, r, w)

    def act(self, fn, r=(), w=()):
        return self.sy.op("act", fn, r, w)

    def dve(self, fn, r=(), w=()):
        return self.sy.op("dve", fn, r, w)

    def pool(self, fn, r=(), w=()):
        return self.sy.op("pool", fn, r, w)

    def dma(self, q, out, in_, r=(), w=()):
        return self.sy.op(q, lambda e: e.dma_start(out=out, in_=in_), r, w, dma=True)

    def setup_common(self):
        nc = self.nc
        self.ps = []
        for i in range(8):
            self.ps.append(T(self.es.enter_context(nc.psum_tensor("psb%d" % i, [128, 512], F32)), "ps%d" % i))
        self.ident = self.sb("ident", [128, 128], F32)
        self.identb = self.sb("identb", [128, 128], BF16)
        self.pool(lambda e: e.memset(self.ident[:], 0.0), w=[self.ident])
        self.pool(lambda e: e.affine_select(out=self.ident[:], in_=self.ident[:], pattern=[[-1, 128]], compare_op=ALU.not_equal,
                                            fill=1.0, base=0, channel_multiplier=1), r=[self.ident], w=[self.ident])
        self.dve(lambda e: e.tensor_copy(out=self.identb[:], in_=self.ident[:]), r=[self.ident], w=[self.identb])
        self.ones = self.sb("ones", [128, 128], F32)
        self.pool(lambda e: e.memset(self.ones[:], 1.0), w=[self.ones])
        self.eps = self.sb("eps", [128, 4], F32)
        for i, v in enumerate((LN_EPS, RMS_EPS, GN_EPS, 0.0)):
            self.dve(lambda e, i=i, v=v: e.memset(self.eps[:, i:i + 1], v), w=[self.eps])

    def psb(self, i):
        return self.ps[i].t[:].bitcast(BF16)

    def load_w(self, dst, src, st_q="pool"):
        K = src.shape[0]
        if K <= 128:
            self.dma(st_q, dst[0:K, 0, :], src, w=[dst])
        else:
            v = src.rearrange("(kc p) n -> p kc n", p=128)
            for kc in range(K // 128):
                self.dma(st_q, dst[:, kc, :], v[:, kc, :], w=[dst])

    def load_bcast(self, dst, src_row):
        n = src_row.shape[-1]
        self.dma("sp", dst[:, 0:n], src_row.rearrange("(o n) -> o n", o=1).broadcast_to([128, n]), w=[dst])

    def load_T(self, dst, dst_ap, src_rows, n):
        with ExitStack() as s1:
            tmp = self.sb("ldT", [n, 128], F32, s1)
            self.dma("sp", tmp[:, :], src_rows, w=[tmp])
            self.pe(lambda e: e.transpose(self.ps[0][:, 0:n], tmp[:, :], self.ident[0:n, 0:n]), r=[tmp, self.ident], w=[self.ps[0]])
            self.dve(lambda e: e.tensor_copy(out=dst_ap, in_=self.ps[0][:, 0:n]), r=[self.ps[0]], w=[dst])
            self.sy.barrier()

    def transpose_to(self, src, nch, dst, dst_ap, bank, dt=BF16, rows=128, evac="act", src_off=0, cw=128):
        per = (1024 if dt == BF16 else 512)
        assert nch * rows <= per
        if dt == BF16:
            pv = self.psb(self.ps.index(bank))
            idt = self.identb
        else:
            pv = bank.t[:]
            idt = self.ident
        for c in range(nch):
            self.pe(lambda e, c=c: e.transpose(pv[0:cw, c * rows:(c + 1) * rows], src[0:rows, src_off + c * cw: src_off + (c + 1) * cw],
                                               idt[0:rows, 0:rows]), r=[src, idt], w=[bank])
        i = pv[0:cw, 0:nch * rows].rearrange("p (c r) -> p c r", c=nch)
        if evac == "act":
            self.act(lambda e: e.copy(out=dst_ap, in_=i), r=[bank], w=[dst])
        else:
            self.dve(lambda e: e.tensor_copy(out=dst_ap, in_=i), r=[bank], w=[dst])

    def linear(self, bank, xT, W, n0, n1, nkc, M=128, kp=128):
        for kc in range(nkc):
            self.pe(lambda e, kc=kc: e.matmul(bank[0:M, 0:n1 - n0], lhsT=xT[0:kp, kc, 0:M], rhs=W[0:kp, kc, n0:n1],
                                              start=(kc == 0), stop=(kc == nkc - 1)), r=[xT, W], w=[bank])

    def rstd_from_ss(self, ss, n, eps_col, out):
        self.act(lambda e: e.activation(out=out[:, 0:1], in_=ss[:, 0:1], func=AF.Sqrt, bias=self.eps[:, eps_col:eps_col + 1], scale=1.0 / n),
                 r=[ss, self.eps], w=[out])
        self.dve(lambda e: e.reciprocal(out=out[:, 0:1], in_=out[:, 0:1]), r=[out], w=[out])

    def resid_ln(self, x, banks, c, G, Bt, out, tmp):
        alpha = self.cfg.alpha
        xa, y, junk, st = tmp["xa"], tmp["y"], tmp["junk"], tmp["st"]
        self.act(lambda e: e.activation(out=xa[:], in_=x[:], func=AF.Copy, scale=alpha), r=[x], w=[xa])
        for h in range(2):
            self.dve(lambda e, h=h: e.scalar_tensor_tensor(out=y[:, h * 512:(h + 1) * 512], in0=banks[h][:, :], scalar=float(c),
                                                         in1=xa[:, h * 512:(h + 1) * 512], op0=ALU.mult, op1=ALU.add,
                                                         accum_out=st[:, h:h + 1]), r=[banks[h], xa], w=[y, st])
        self.ln_core(y, G, Bt, out, junk, st)

    def ln_core(self, y, G, Bt, out, junk, st):
        self.dve(lambda e: e.tensor_scalar(out=st[:, 2:3], in0=st[:, 0:1], scalar1=st[:, 1:2], scalar2=-1.0 / D, op0=ALU.add, op1=ALU.mult),
                 r=[st], w=[st])
        self.act(lambda e: e.activation(out=junk[:], in_=y[:], func=AF.Square, bias=st[:, 2:3], scale=1.0, accum_out=st[:, 3:4]),
                 r=[y, st], w=[junk, st])
        self.rstd_from_ss(_col(st, 3), D, 0, _col(st, 4))
        self.dve(lambda e: e.tensor_scalar(out=junk[:], in0=y[:], scalar1=st[:, 2:3], scalar2=st[:, 4:5], op0=ALU.add, op1=ALU.mult),
                 r=[y, st], w=[junk])
        self.pool(lambda e: e.tensor_tensor(out=junk[:], in0=junk[:], in1=G[:], op=ALU.mult), r=[junk, G], w=[junk])
        self.dve(lambda e: e.tensor_tensor(out=out[:], in0=junk[:], in1=Bt[:], op=ALU.add), r=[junk, Bt], w=[out])


class _col:
    def __init__(self, t, c):
        self._t = t
        self.c = c

    @property
    def wr(self):
        return self._t.wr

    @property
    def rd(self):
        return self._t.rd

    def __getitem__(self, k):
        return self._t.t[:, self.c:self.c + 1]


def x0_src(cfg, d):
    def f(j):
        if j == 0:
            return [((0, N_META), d["meta_tokens"][:, :], d["meta_tokens"]),
                    ((N_META, 128), d["x_prompt"][0:128 - N_META, :], d["x_prompt"])]
        if j < cfg.ntp:
            r = cfg.rows(j)
            return [((0, r), d["x_prompt"][128 * j - N_META:128 * j - N_META + r, :], d["x_prompt"])]
        return [((0, cfg.srows), d["x_sample"][:, :], d["x_sample"])]
    return f


def y_dst(cfg, d):
    def f(j):
        if j == 0:
            return [((N_META, 128), d["y_prompt"][0:128 - N_META, :], d["y_prompt"])]
        if j < cfg.ntp:
            r = cfg.rows(j)
            return [((0, r), d["y_prompt"][128 * j - N_META:128 * j - N_META + r, :], d["y_prompt"])]
        return [((0, cfg.srows), d["y_sample"][:, :], d["y_sample"])]
    return f


def scr_map(cfg, X, Xt):
    def f(j):
        r = cfg.rows(j)
        return [((0, r), X[128 * j:128 * j + r, :], Xt[j])]
    f.X = X
    f.Xt = Xt
    return f


def stage_ffn(B, li, half, src, dst, tiles=None):
    cfg, d = B.cfg, B.din
    with ExitStack() as st:
        w1 = B.sb("w1", [128, 8, DFF], BF16, st)
        w3 = B.sb("w3", [128, 8, DFF], BF16, st)
        w2 = B.sb("w2", [128, 22, D], BF16, st)
        B.load_w(w1, d["ffn_w1"].t[li, half])
        B.load_w(w3, d["ffn_w3"].t[li, half])
        B.load_w(w2, d["ffn_w2"].t[li, half])
        G = B.sb("G", [128, D], F32, st)
        Bt = B.sb("Bt", [128, D], F32, st)
        lni = 0 if half == 0 else 2
        B.load_bcast(G, d["ln_g"].t[li, lni])
        B.load_bcast(Bt, d["ln_b"].t[li, lni])
        xs = [B.sb("xs", [128, D], F32, st) for _ in range(3)]
        xb = B.sb("xb", [128, D], BF16, st)
        xT = B.sb("xT", [128, 8, 128], BF16, st)
        sg = [B.sb("sg", [128, 512], F32, st) for _ in range(2)]
        g = B.sb("g", [128, DFF], BF16, st)
        gT = B.sb("gT", [128, 22, 128], BF16, st)
        tmp = dict(xa=B.sb("xa", [128, D], F32, st), y=B.sb("y", [128, D], F32, st), junk=B.sb("junk", [128, D], F32, st),
                   st=B.sb("st", [128, 8], F32, st))
        xo = [B.sb("xo", [128, D], F32, st) for _ in range(2)]
        for t in xs:
            B.dve(lambda e, t=t: e.memset(t[:], 0.0), w=[t])
        ps = B.ps
        tl = list(range(cfg.nt)) if tiles is None else tiles
        if DEBUG_TILES is not None:
            tl = DEBUG_TILES

        def load(j, buf):
            for (r0, r1), ap, tt in src(j):
                B.dma("sp", buf[r0:r1, :], ap, r=[tt], w=[buf])

        load(tl[0], xs[0])
        for k, j in enumerate(tl):
            x = xs[k % 3]
            if k + 1 < len(tl):
                load(tl[k + 1], xs[(k + 1) % 3])
            B.pool(lambda e: e.tensor_copy(out=xb[:], in_=x[:]), r=[x], w=[xb])
            B.transpose_to(xb, 8, xT, xT[:, :, :], ps[0])
            for gi in range(6):
                n0 = gi * 512
                n1 = min(DFF, n0 + 512)
                b1, b3 = ps[1 + 2 * (gi % 2)], ps[2 + 2 * (gi % 2)]
                B.linear(b1, xT, w1, n0, n1, 8)
                B.linear(b3, xT, w3, n0, n1, 8)
                s_ = sg[gi % 2]
                B.act(lambda e, s_=s_, b1=b1, n=n1 - n0: e.activation(out=s_[:, 0:n], in_=b1[:, 0:n], func=AF.Silu), r=[b1], w=[s_])
                B.dve(lambda e, s_=s_, b3=b3, n0=n0, n1=n1: e.tensor_tensor(out=g[:, n0:n1], in0=s_[:, 0:n1 - n0], in1=b3[:, 0:n1 - n0], op=ALU.mult),
                      r=[s_, b3], w=[g])
            for c0, nch in ((0, 8), (8, 8), (16, 6)):
                B.transpose_to(g, nch, gT, gT[:, c0:c0 + nch, :], ps[5], src_off=c0 * 128, evac="act" if c0 != 8 else "dve")
            for h in range(2):
                for fc in range(22):
                    B.pe(lambda e, h=h, fc=fc: e.matmul(ps[6 + h][:, :], lhsT=gT[:, fc, :], rhs=w2[:, fc, h * 512:(h + 1) * 512],
                                                       start=(fc == 0), stop=(fc == 21)), r=[gT, w2], w=[ps[6 + h]])
            o = xo[k % 2]
            B.resid_ln(x, [ps[6], ps[7]], 0.5, G, Bt, o, tmp)
            for (r0, r1), ap, tt in dst(j):
                B.dma("sp", ap, o[r0:r1, :], r=[o], w=[tt])
            if DEBUG_BARRIER:
                B.sy.barrier()
        B.sy.barrier()


WEIGHT_SHAPES = dict(
    meta_tokens=(N_META, D), ln_g=("depth", 3, D), ln_b=("depth", 3, D),
    ffn_w1=("depth", 2, D, DFF), ffn_w3=("depth", 2, D, DFF), ffn_w2=("depth", 2, DFF, D),
    mla_w_dq=("n_mla", D, QR), mla_q_norm=("n_mla", QR), mla_w_uq=("n_mla", QR, NH * QK),
    mla_w_dkv=("n_mla", D, KVR + ROPE), mla_kv_norm=("n_mla", KVR), mla_w_uk=("n_mla", KVR, NH * NOPE),
    mla_w_uv=("n_mla", KVR, NH * VD), mla_w_o=("n_mla", NH * VD, D),
    rw_mu=("n_rwkv", 6, D), rw_wr=("n_rwkv", D, D), rw_wk=("n_rwkv", D, D), rw_wv=("n_rwkv", D, D),
    rw_w0=("n_rwkv", D), rw_w1=("n_rwkv", D, 64), rw_w2=("n_rwkv", 64, D), rw_a0=("n_rwkv", D),
    rw_a1=("n_rwkv", D, 64), rw_a2=("n_rwkv", 64, D), rw_g1=("n_rwkv", D, 128), rw_g2=("n_rwkv", 128, D),
    rw_k_k=("n_rwkv", D), rw_k_a=("n_rwkv", D), rw_r_k=("n_rwkv", D), rw_lnx_g=("n_rwkv", D), rw_lnx_b=("n_rwkv", D),
    rw_wo=("n_rwkv", D, D),
    s5_lam_re=("n_s5", 64, 64), s5_lam_im=("n_s5", 64, 64), s5_log_dt=("n_s5", 64),
    s5_b_re=("n_s5", 64, 64, 16), s5_b_im=("n_s5", 64, 64, 16), s5_c_re=("n_s5", 64, 16, 64), s5_c_im=("n_s5", 64, 16, 64),
    s5_d=("n_s5", D), s5_wv=("n_s5", D, D), s5_wg=("n_s5", D, D),
)


def _shape(cfg, shp):
    return tuple(getattr(cfg, s) if isinstance(s, str) else s for s in shp)


def io_shapes(cfg):
    ins = dict(
        x_prompt=(cfg.seq, D), x_sample=(cfg.srows, D),
        cache_mla_latent=(cfg.n_mla * cfg.npool * PAGE, KVR), cache_mla_krope=(cfg.n_mla * cfg.npool * PAGE, ROPE),
        state_rwkv_wkv=(cfg.n_rwkv, cfg.nsq, NH, 64, 64), state_rwkv_shift=(cfg.n_rwkv, cfg.nsq, D),
        state_s5_re=(cfg.n_s5, cfg.nsq, 64, 64), state_s5_im=(cfg.n_s5, cfg.nsq, 64, 64),
        page_table=(cfg.nsq, cfg.npages),
    )
    for k, v in WEIGHT_SHAPES.items():
        ins[k] = _shape(cfg, v)
    outs = dict(
        y_prompt=(cfg.seq, D), y_sample=(cfg.srows, D),
        lat_p=(cfg.n_mla, cfg.L, KVR), kr_p=(cfg.n_mla, cfg.L, ROPE), lat_s=(cfg.n_mla, cfg.srows, KVR), kr_s=(cfg.n_mla, cfg.srows, ROPE),
        wkv_p=(cfg.n_rwkv, NH, 64, 64), sh_p=(cfg.n_rwkv, D), wkv_s=(cfg.n_rwkv, cfg.nsq, NH, 64, 64), sh_s=(cfg.n_rwkv, cfg.nsq, D),
        re_p=(cfg.n_s5, 64, 64), im_p=(cfg.n_s5, 64, 64), re_s=(cfg.n_s5, cfg.nsq, 64, 64), im_s=(cfg.n_s5, cfg.nsq, 64, 64),
    )
    return ins, outs


def build_program(cfg, nstages=None):
    nc = bass.Bass("TRN2", target_bir_lowering=False)
    B = Builder(nc, cfg)
    ins, outs = io_shapes(cfg)
    for k, shp in ins.items():
        B.dram_in(k, shp, I32 if k == "page_table" else F32)
    for k, shp in outs.items():
        B.dram_out(k, shp, F32)
    d = dict(B.din)
    d.update(B.dout)
    B.d = d
    B.setup_common()
    XA = B.dram_scr("XA", [cfg.nt * 128, D])
    XB = B.dram_scr("XB", [cfg.nt * 128, D])
    Xs = [XA, XB]
    Xt = [[T(None, "XA%d" % j) for j in range(cfg.nt)], [T(None, "XB%d" % j) for j in range(cfg.nt)]]
    B.X = Xs
    B.Xt = Xt
    total = 3 * cfg.depth
    if nstages is None:
        nstages = total
    for s in range(nstages):
        li, kind = s // 3, s % 3
        src = x0_src(cfg, d) if s == 0 else scr_map(cfg, Xs[(s - 1) % 2].t, Xt[(s - 1) % 2])
        dst = y_dst(cfg, d) if s == nstages - 1 else scr_map(cfg, Xs[s % 2].t, Xt[s % 2])
        if kind == 0:
            stage_ffn(B, li, 0, src, dst)
        elif kind == 2:
            stage_ffn(B, li, 1, src, dst)
        else:
            mk, m = li % 3, li // 3
            if mk == 0:
                stage_mla(B, li, m, src, dst)
            elif mk == 1:
                stage_rwkv(B, li, m, src, dst)
            else:
                stage_s5(B, li, m, src, dst)
    B.sy.finish()
    B.es.close()
    return nc, B


def shard_inputs(cfg, inputs, c):
    b = c % cfg.n_batch
    s0 = c * cfg.nsq
    m = {}
    m["x_prompt"] = np.ascontiguousarray(inputs["x_prompt"][b])
    m["x_sample"] = np.ascontiguousarray(inputs["x_sample"][s0:s0 + cfg.nsq]).reshape(cfg.srows, D)
    m["cache_mla_latent"] = inputs["cache_mla_latent"].reshape(-1, KVR)
    m["cache_mla_krope"] = inputs["cache_mla_krope"].reshape(-1, ROPE)
    m["state_rwkv_wkv"] = np.ascontiguousarray(inputs["state_rwkv_wkv"][:, s0:s0 + cfg.nsq])
    m["state_rwkv_shift"] = np.ascontiguousarray(inputs["state_rwkv_shift"][:, s0:s0 + cfg.nsq])
    m["state_s5_re"] = np.ascontiguousarray(inputs["state_s5_re"][:, s0:s0 + cfg.nsq])
    m["state_s5_im"] = np.ascontiguousarray(inputs["state_s5_im"][:, s0:s0 + cfg.nsq])
    m["page_table"] = np.ascontiguousarray(inputs["page_table"][s0:s0 + cfg.nsq]).astype(np.int32)
    for k, shp in WEIGHT_SHAPES.items():
        m[k] = np.ascontiguousarray(inputs[k]).reshape(_shape(cfg, shp))
    return m


def assemble(cfg, res):
    nb, nco = cfg.n_batch, cfg.n_cores
    def pb(name):
        return [res[b][name] for b in range(nb)]
    def sc(name):
        return [res[c][name] for c in range(nco)]
    y_p = np.stack(pb("y_prompt"), 0)
    y_s = np.concatenate(sc("y_sample"), 0).reshape(nco * cfg.nsq, DEC_SEQ, D)
    lat_p = np.stack(pb("lat_p"), 1)
    kr_p = np.stack(pb("kr_p"), 1)
    lat_s = np.concatenate([r.reshape(cfg.n_mla, cfg.nsq, DEC_SEQ, KVR) for r in sc("lat_s")], 1)
    kr_s = np.concatenate([r.reshape(cfg.n_mla, cfg.nsq, DEC_SEQ, ROPE) for r in sc("kr_s")], 1)
    wkv_p = np.stack(pb("wkv_p"), 1)
    sh_p = np.stack(pb("sh_p"), 1)
    wkv_s = np.concatenate(sc("wkv_s"), 1)
    sh_s = np.concatenate(sc("sh_s"), 1)
    re_p = np.stack(pb("re_p"), 1)
    im_p = np.stack(pb("im_p"), 1)
    re_s = np.concatenate(sc("re_s"), 1)
    im_s = np.concatenate(sc("im_s"), 1)
    return (y_p, y_s, lat_p, kr_p, lat_s, kr_s, wkv_p, sh_p, wkv_s, sh_s, re_p, im_p, re_s, im_s)


def run(cfg, inputs, nstages=None, trace=False):
    nc, B = build_program(cfg, nstages)
    in_maps = [shard_inputs(cfg, inputs, c) for c in range(cfg.n_cores)]
    res = run_bass_kernel_spmd(nc, in_maps, core_ids=list(range(cfg.n_cores)), trace=trace)
    return assemble(cfg, res.results), res


def kernel(**inputs):
    cfg = Cfg()
    out, _ = run(cfg, inputs)
    return tuple(np.ascontiguousarray(o, dtype=np.float32) for o in out)


def range_reduce_sin(B, ang, out, n, tmpf, tmpi):
    C1 = 6.28125
    C2 = 2 * math.pi - C1
    B.dve(lambda e: e.tensor_scalar(out=tmpf[:, 0:n], in0=ang[:, 0:n], scalar1=1.0 / (2 * math.pi), scalar2=None, op0=ALU.mult), r=[ang], w=[tmpf])
    B.dve(lambda e: e.tensor_copy(out=tmpi[:, 0:n], in_=tmpf[:, 0:n]), r=[tmpf], w=[tmpi])
    B.dve(lambda e: e.tensor_copy(out=tmpf[:, 0:n], in_=tmpi[:, 0:n]), r=[tmpi], w=[tmpf])
    B.dve(lambda e: e.scalar_tensor_tensor(out=out[:, 0:n], in0=tmpf[:, 0:n], scalar=-C1, in1=ang[:, 0:n], op0=ALU.mult, op1=ALU.add), r=[tmpf, ang], w=[out])
    B.dve(lambda e: e.scalar_tensor_tensor(out=out[:, 0:n], in0=tmpf[:, 0:n], scalar=-C2, in1=out[:, 0:n], op0=ALU.mult, op1=ALU.add), r=[tmpf, out], w=[out])
    B.dve(lambda e: e.tensor_scalar(out=out[:, 0:n], in0=out[:, 0:n], scalar1=math.pi, scalar2=-math.pi, op0=ALU.min, op1=ALU.max), r=[out], w=[out])
    B.act(lambda e: e.activation(out=out[:, 0:n], in_=out[:, 0:n], func=AF.Sin), r=[out], w=[out])


def setup_rope(B, stk):
    cfg = B.cfg
    nt = cfg.nt
    B.COS = B.sb("COS", [128, nt * 16], F32, stk)
    B.SIN = B.sb("SIN", [128, nt * 16], F32, stk)
    with ExitStack() as st:
        pos = B.sb("pos", [128, nt], F32, st)
        pi_ = B.sb("pi_", [128, 1], I32, st)
        ang = B.sb("ang", [128, nt * 16], F32, st)
        ang2 = B.sb("ang2", [128, nt * 16], F32, st)
        tf = B.sb("tf", [128, nt * 16], F32, st)
        ti = B.sb("ti", [128, nt * 16], I32, st)
        B.pool(lambda e: e.iota(pos[:, 0:cfg.ntp], pattern=[[128, cfg.ntp]], base=0, channel_multiplier=1, allow_small_or_imprecise_dtypes=True), w=[pos])
        B.pool(lambda e: e.iota(pi_[:], pattern=[[0, 1]], base=0, channel_multiplier=1), w=[pi_])
        B.dve(lambda e: e.tensor_single_scalar(out=pi_[:], in_=pi_[:], scalar=3, op=ALU.bitwise_and), r=[pi_], w=[pi_])
        B.dve(lambda e: e.tensor_copy(out=pos[:, cfg.ntp:nt], in_=pi_[:]), r=[pi_], w=[pos])
        B.dve(lambda e: e.tensor_scalar(out=pos[:, cfg.ntp:nt], in0=pos[:, cfg.ntp:nt], scalar1=float(cfg.past), scalar2=None, op0=ALU.add), r=[pos], w=[pos])
        a3 = ang[:].rearrange("p (j f) -> p j f", f=16)
        for f in range(16):
            inv = float(np.float32(1.0) / np.float32(10000.0) ** (np.float32(f) * np.float32(2.0 / ROPE)))
            B.dve(lambda e, f=f, inv=inv: e.tensor_scalar(out=a3[:, :, f], in0=pos[:, :], scalar1=inv, scalar2=None, op0=ALU.mult), r=[pos], w=[ang])
        B.dve(lambda e: e.tensor_scalar(out=ang2[:], in0=ang[:], scalar1=math.pi / 2, scalar2=None, op0=ALU.add), r=[ang], w=[ang2])
        range_reduce_sin(B, ang, B.SIN, nt * 16, tf, ti)
        range_reduce_sin(B, ang2, B.COS, nt * 16, tf, ti)
        B.sy.barrier()


def rope_apply(B, src, src_ap, dst, dst_ap, j, nh, t):
    c = B.COS[:, j * 16:(j + 1) * 16].unsqueeze(1).broadcast_to([128, nh, 16])
    s = B.SIN[:, j * 16:(j + 1) * 16].unsqueeze(1).broadcast_to([128, nh, 16])
    x1, x2 = src_ap[:, :, 0:16], src_ap[:, :, 16:32]
    def v(k):
        return t[k][:, 0:nh * 16].rearrange("p (h f) -> p h f", f=16)
    B.dve(lambda e: e.tensor_tensor(out=v(0), in0=x1, in1=c, op=ALU.mult), r=[src, B.COS], w=[t[0]])
    B.dve(lambda e: e.tensor_tensor(out=v(1), in0=x2, in1=s, op=ALU.mult), r=[src, B.SIN], w=[t[1]])
    B.dve(lambda e: e.tensor_tensor(out=v(2), in0=x1, in1=s, op=ALU.mult), r=[src, B.SIN], w=[t[2]])
    B.dve(lambda e: e.tensor_tensor(out=v(3), in0=x2, in1=c, op=ALU.mult), r=[src, B.COS], w=[t[3]])
    B.dve(lambda e: e.tensor_tensor(out=dst_ap[:, :, 0:16], in0=v(0), in1=v(1), op=ALU.subtract), r=[t[0], t[1]], w=[dst])
    B.dve(lambda e: e.tensor_tensor(out=dst_ap[:, :, 16:32], in0=v(2), in1=v(3), op=ALU.add), r=[t[2], t[3]], w=[dst])


def stage_mla(B, li, m, src, dst):
    cfg, d, ps = B.cfg, B.d, B.ps
    ntp, nt = cfg.ntp, cfg.nt
    NTOK = ntp * 128
    scale = QK ** -0.5
    nblk = (ntp + 3) // 4
    if not hasattr(B, "QTd"):
        B.QTd = B.dram_scr("QTd", [NH, QK, nblk * 512], BF16)
        B.OTd = B.dram_scr("OTd", [NH, VD, nblk * 512 + 128], BF16)
        B.CNd = B.dram_scr("CNd", [128, KVR + ROPE], F32)
    QTd, OTd, CNd = B.QTd, B.OTd, B.CNd
    with ExitStack() as st:
        w_uk = B.sb("w_uk", [128, 2, NH * NOPE], BF16, st)
        w_uv = B.sb("w_uv", [128, 2, NH * VD], BF16, st)
        B.load_w(w_uk, d["mla_w_uk"].t[m])
        B.load_w(w_uv, d["mla_w_uv"].t[m])
        qs_b = B.sb("qs_b", [128, NH, QK], BF16, st)
        sA = ExitStack()
        setup_rope(B, sA)
        w_dq = B.sb("w_dq", [128, 8, QR], BF16, sA)
        w_uq = B.sb("w_uq", [128, 6, NH * QK], BF16, sA)
        w_dkv = B.sb("w_dkv", [128, 8, KVR + ROPE], BF16, sA)
        B.load_w(w_dq, d["mla_w_dq"].t[m])
        B.load_w(w_uq, d["mla_w_uq"].t[m])
        B.load_w(w_dkv, d["mla_w_dkv"].t[m])
        Gq = B.sb("Gq", [128, QR], F32, sA)
        Gkv = B.sb("Gkv", [128, KVR], F32, sA)
        B.load_bcast(Gq, d["mla_q_norm"].t[m])
        B.load_bcast(Gkv, d["mla_kv_norm"].t[m])
        wukx = B.sb("wukx", [128, 2, NH, QK], BF16, sA)
        B.pool(lambda e: e.memset(wukx[:], 0.0), w=[wukx])
        B.pool(lambda e: e.tensor_copy(out=wukx[:, :, :, 0:NOPE], in_=w_uk[:, :, :].rearrange("p k (h n) -> p k h n", n=NOPE)), r=[w_uk], w=[wukx])
        sel = B.sb("sel", [32, QK], BF16, sA)
        B.pool(lambda e: e.memset(sel[:], 0.0), w=[sel])
        B.pool(lambda e: e.tensor_copy(out=sel[:, NOPE:QK], in_=B.identb[0:32, 0:32]), r=[B.identb], w=[sel])
        cT = B.sb("cT", [128, 2, NTOK], BF16, sA)
        kpeT = B.sb("kpeT", [32, NTOK], BF16, sA)
        qmax = B.sb("qmax", [128, NH], F32, sA)
        kmax = B.sb("kmax", [128, NH], F32, sA)
        B.dve(lambda e: e.memset(qmax[:], 0.0), w=[qmax])
        B.dve(lambda e: e.memset(kmax[:], 0.0), w=[kmax])
        st2 = ExitStack()
        xs = [B.sb("xs", [128, D], F32, st2) for _ in range(2)]
        xb = B.sb("xb", [128, D], BF16, st2)
        xT = B.sb("xT", [128, 8, 128], BF16, st2)
        stq = B.sb("stq", [128, 8], F32, st2)
        cqn = B.sb("cqn", [128, QR], BF16, st2)
        cqT = B.sb("cqT", [128, 6, 128], BF16, st2)
        qf = B.sb("qf", [128, NH * QK], F32, st2)
        qb = B.sb("qb", [128, NH, QK], BF16, st2)
        sq = B.sb("sq", [128, NH * QK], F32, st2)
        ss16 = B.sb("ss16", [128, NH], F32, st2)
        ss16k = B.sb("ss16k", [128, NH], F32, st2)
        stk = B.sb("stk", [128, 4], F32, st2)
        sq2 = B.sb("sq2", [128, NH * NOPE + ROPE], F32, st2)
        rtk = [B.sb("rtk", [128, 16], F32, st2) for _ in range(4)]
        rt = [B.sb("rt", [128, NH * 16], F32, st2) for _ in range(4)]
        QTs = [B.sb("QTs", [QK, NH, 512], BF16, st2) for _ in range(2)]
        c32 = [B.sb("c32", [128, KVR + ROPE], F32, st2) for _ in range(2)]
        cb = B.sb("cb", [128, KVR + ROPE], BF16, st2)
        for t_ in xs + QTs:
            B.dve(lambda e, t_=t_: e.memset(t_[:], 0.0), w=[t_])
        qf3 = qf[:].rearrange("p (h q) -> p h q", q=QK)

        def load(j, buf):
            for (r0, r1), ap, tt in src(j):
                B.dma("sp", buf[r0:r1, :], ap, r=[tt], w=[buf])

        load(0, xs[0])
        for j in range(nt):
            x = xs[j % 2]
            rows = cfg.rows(j)
            sample = (j == ntp)
            B.pool(lambda e: e.tensor_copy(out=xb[:], in_=x[:]), r=[x], w=[xb])
            B.transpose_to(xb, 8, xT, xT[:, :, :], ps[0])
            if j + 1 < nt:
                load(j + 1, xs[(j + 1) % 2])
            def q_side():
                B.linear(ps[1], xT, w_dq, 0, 512, 8)
                B.linear(ps[2], xT, w_dq, 512, QR, 8)
                yield
                B.act(lambda e: e.activation(out=sq[:, 0:512], in_=ps[1][:, 0:512], func=AF.Square, accum_out=stq[:, 0:1]), r=[ps[1]], w=[sq, stq])
                B.act(lambda e: e.activation(out=sq[:, 512:QR], in_=ps[2][:, 0:QR - 512], func=AF.Square, accum_out=stq[:, 1:2]), r=[ps[2]], w=[sq, stq])
                yield
                B.dve(lambda e: e.tensor_tensor(out=stq[:, 2:3], in0=stq[:, 0:1], in1=stq[:, 1:2], op=ALU.add), r=[stq], w=[stq])
                yield
                B.act(lambda e: e.activation(out=stq[:, 3:4], in_=stq[:, 2:3], func=AF.Sqrt, bias=B.eps[:, 1:2], scale=1.0 / QR), r=[stq, B.eps], w=[stq])
                yield
                B.dve(lambda e: e.reciprocal(out=stq[:, 3:4], in_=stq[:, 3:4]), r=[stq], w=[stq])
                B.dve(lambda e: e.scalar_tensor_tensor(out=cqn[:, 0:512], in0=ps[1][:, 0:512], scalar=stq[:, 3:4], in1=Gq[:, 0:512], op0=ALU.mult, op1=ALU.mult),
                      r=[ps[1], stq, Gq], w=[cqn])
                B.dve(lambda e: e.scalar_tensor_tensor(out=cqn[:, 512:QR], in0=ps[2][:, 0:QR - 512], scalar=stq[:, 3:4], in1=Gq[:, 512:QR], op0=ALU.mult, op1=ALU.mult),
                      r=[ps[2], stq, Gq], w=[cqn])
                yield
                B.transpose_to(cqn, 6, cqT, cqT[:, :, :], ps[0])
                yield
                for k3 in range(3):
                    bq = ps[3 + k3 % 2]
                    B.linear(bq, cqT, w_uq, 512 * k3, 512 * (k3 + 1), 6)
                    yield
                    B.act(lambda e, k3=k3, bq=bq: e.copy(out=qf[:, 512 * k3:512 * (k3 + 1)], in_=bq[:, :]), r=[bq], w=[qf])
                    yield
                qdst = qs_b if sample else qb
                B.dve(lambda e: e.tensor_copy(out=qdst[:, :, 0:NOPE], in_=qf3[:, :, 0:NOPE]), r=[qf], w=[qdst])
                rope_apply(B, qf, qf3[:, :, NOPE:QK], qdst, qdst[:, :, NOPE:QK], j, NH, rt)
                yield
                if not sample:
                    B.pool(lambda e: e.tensor_tensor(out=sq[:], in0=qf[:], in1=qf[:], op=ALU.mult), r=[qf], w=[sq])
                    yield
                    B.dve(lambda e: e.tensor_reduce(out=ss16[:], in_=sq[:].rearrange("p (h q) -> p h q", q=QK), axis=AX.X, op=ALU.add), r=[sq], w=[ss16])
                    B.dve(lambda e: e.tensor_tensor(out=qmax[:], in0=qmax[:], in1=ss16[:], op=ALU.max), r=[qmax, ss16], w=[qmax])
                    QT_ = QTs[(j // 4) % 2]
                    bank = ps[5]
                    pv = B.psb(5)
                    for hb in range(2):
                        for hh in range(8):
                            h = hb * 8 + hh
                            B.pe(lambda e, h=h, hh=hh: e.transpose(pv[0:QK, hh * 128:(hh + 1) * 128], qb[:, h, :], B.identb[:, :]), r=[qb, B.identb], w=[bank])
                        yield
                        B.act(lambda e, hb=hb: e.copy(out=QT_[:, hb * 8:(hb + 1) * 8, (j % 4) * 128:(j % 4 + 1) * 128],
                                                      in_=pv[0:QK, 0:1024].rearrange("p (h t) -> p h t", t=128)), r=[bank], w=[QT_])
                        yield
                    if j % 4 == 3 or j == ntp - 1:
                        blk = j // 4
                        B.dma("sp", QTd.t.rearrange("h r t -> r h t")[:, :, blk * 512:(blk + 1) * 512], QT_[:, :, :], r=[QT_], w=[QTd])
                yield

            def k_side():
                bk, bt_ = ps[6], ps[7]
                B.linear(bk, xT, w_dkv, 0, KVR + ROPE, 8)
                yield
                B.act(lambda e: e.activation(out=sq2[:, 0:KVR], in_=bk[:, 0:KVR], func=AF.Square, accum_out=stk[:, 0:1]), r=[bk], w=[sq2, stk])
                yield
                B.act(lambda e: e.activation(out=stk[:, 1:2], in_=stk[:, 0:1], func=AF.Sqrt, bias=B.eps[:, 1:2], scale=1.0 / KVR), r=[stk, B.eps], w=[stk])
                yield
                B.dve(lambda e: e.reciprocal(out=stk[:, 1:2], in_=stk[:, 1:2]), r=[stk], w=[stk])
                c_ = c32[j % 2]
                B.dve(lambda e: e.scalar_tensor_tensor(out=c_[:, 0:KVR], in0=bk[:, 0:KVR], scalar=stk[:, 1:2], in1=Gkv[:, :], op0=ALU.mult, op1=ALU.mult),
                      r=[bk, stk, Gkv], w=[c_])
                rope_apply(B, bk, bk[:, KVR:KVR + ROPE].rearrange("p (o f) -> p o f", o=1), c_,
                           c_[:, KVR:KVR + ROPE].rearrange("p (o f) -> p o f", o=1), j, 1, rtk)
                yield
                B.pool(lambda e: e.tensor_copy(out=cb[:], in_=c_[:]), r=[c_], w=[cb])
                if sample:
                    B.dma("sp", d["lat_s"].t[m, :, :], c_[0:rows, 0:KVR], r=[c_], w=[d["lat_s"]])
                    B.dma("sp", d["kr_s"].t[m, :, :], c_[0:rows, KVR:KVR + ROPE], r=[c_], w=[d["kr_s"]])
                    B.dma("sp", CNd.t[0:rows, :], c_[0:rows, :], r=[c_], w=[CNd])
                    yield
                    return
                B.dma("sp", d["lat_p"].t[m, 128 * j:128 * j + rows, :], c_[0:rows, 0:KVR], r=[c_], w=[d["lat_p"]])
                B.dma("sp", d["kr_p"].t[m, 128 * j:128 * j + rows, :], c_[0:rows, KVR:KVR + ROPE], r=[c_], w=[d["kr_p"]])
                yield
                pv7 = B.psb(7)
                for c in range(2):
                    B.pe(lambda e, c=c: e.transpose(pv7[:, c * 128:(c + 1) * 128], cb[:, c * 128:(c + 1) * 128], B.identb[:, :]), r=[cb, B.identb], w=[bt_])
                B.pe(lambda e: e.transpose(pv7[0:ROPE, 256:384], cb[:, KVR:KVR + ROPE], B.identb[:, :]), r=[cb, B.identb], w=[bt_])
                yield
                B.act(lambda e: e.copy(out=cT[:, :, j * 128:(j + 1) * 128], in_=pv7[:, 0:256].rearrange("p (c t) -> p c t", t=128)), r=[bt_], w=[cT])
                B.act(lambda e: e.copy(out=kpeT[:, j * 128:(j + 1) * 128], in_=pv7[0:ROPE, 256:384]), r=[bt_], w=[kpeT])
                yield
                for k2 in range(2):
                    for kc in range(2):
                        B.pe(lambda e, k2=k2, kc=kc: e.matmul(bk[:, :], lhsT=cT[:, kc, j * 128:(j + 1) * 128], rhs=w_uk[:, kc, 512 * k2:512 * (k2 + 1)],
                                                             start=(kc == 0), stop=(kc == 1)), r=[cT, w_uk], w=[bk])
                    yield
                    B.act(lambda e, k2=k2: e.activation(out=sq2[:, 512 * k2:512 * (k2 + 1)], in_=bk[:, :], func=AF.Square), r=[bk], w=[sq2])
                    yield
                B.dve(lambda e: e.tensor_reduce(out=ss16k[:], in_=sq2[:, 0:NH * NOPE].rearrange("p (h q) -> p h q", q=NOPE), axis=AX.X, op=ALU.add), r=[sq2], w=[ss16k])
                B.act(lambda e: e.activation(out=sq2[:, 1024:1024 + ROPE], in_=c_[:, KVR:KVR + ROPE], func=AF.Square, accum_out=stk[:, 2:3]), r=[c_], w=[sq2, stk])
                yield
                B.dve(lambda e: e.tensor_scalar(out=ss16k[:], in0=ss16k[:], scalar1=stk[:, 2:3], scalar2=None, op0=ALU.add), r=[ss16k, stk], w=[ss16k])
                B.dve(lambda e: e.tensor_tensor(out=kmax[:], in0=kmax[:], in1=ss16k[:], op=ALU.max), r=[kmax, ss16k], w=[kmax])
                yield

            gens = [q_side(), k_side()]
            alive = [True, True]
            while any(alive):
                for gi, g_ in enumerate(gens):
                    if alive[gi]:
                        try:
                            next(g_)
                        except StopIteration:
                            alive[gi] = False
        B.sy.barrier()
        st2.close()
        mla_prompt_attention(B, st, m, cT, kpeT, wukx, sel, w_uv, qmax, kmax, scale)
        sA.close()
        mla_sample_attention(B, st, m, qs_b, w_uk, w_uv, scale)
    mla_outproj(B, li, m, src, dst)


def bcast_partition_max(B, src, dst, bank, tmp):
    B.pe(lambda e: e.transpose(bank[0:NH, 0:128], src[:, 0:NH], B.ident[:, :]), r=[src, B.ident], w=[bank])
    B.dve(lambda e: e.tensor_reduce(out=tmp[0:NH, 0:1], in_=bank[0:NH, 0:128], axis=AX.X, op=ALU.max), r=[bank], w=[tmp])
    B.dve(lambda e: e.tensor_scalar(out=tmp[0:NH, 1:1 + NH], in0=B.ident[0:NH, 0:NH], scalar1=tmp[0:NH, 0:1], scalar2=None, op0=ALU.mult), r=[tmp, B.ident], w=[tmp])
    B.pe(lambda e: e.matmul(bank[:, 256:256 + NH], lhsT=B.ones[0:NH, :], rhs=tmp[0:NH, 1:1 + NH], start=True, stop=True), r=[B.ones, tmp], w=[bank])
    B.dve(lambda e: e.tensor_copy(out=dst[:, 0:NH], in_=bank[:, 256:256 + NH]), r=[bank], w=[dst])


def mla_prompt_attention(B, st, m, cT, kpeT, wukx, sel, w_uv, qmax, kmax, scale):
    cfg, d, ps = B.cfg, B.d, B.ps
    ntp = cfg.ntp
    NTOK = ntp * 128
    nblk = (ntp + 3) // 4
    QTd, OTd = B.QTd, B.OTd
    with ExitStack() as s2:
        tmpm = B.sb("tmpm", [128, 1 + NH], F32, s2)
        MQ = B.sb("MQ", [128, NH], F32, s2)
        MK = B.sb("MK", [128, NH], F32, s2)
        negM = B.sb("negM", [128, NH], F32, s2)
        bcast_partition_max(B, qmax, MQ, ps[0], tmpm)
        bcast_partition_max(B, kmax, MK, ps[0], tmpm)
        B.dve(lambda e: e.tensor_tensor(out=negM[:], in0=MQ[:], in1=MK[:], op=ALU.mult), r=[MQ, MK], w=[negM])
        B.act(lambda e: e.activation(out=negM[:], in_=negM[:], func=AF.Sqrt), r=[negM], w=[negM])
        B.dve(lambda e: e.tensor_scalar(out=negM[:], in0=negM[:], scalar1=-scale, scalar2=None, op0=ALU.mult), r=[negM], w=[negM])
        masks = []
        mf = B.sb("mf", [128, 512], F32, s2)
        for i in range(4):
            mk = B.sb("mask", [128, 512], BF16, s2)
            B.pool(lambda e: e.memset(mf[:], 1.0), w=[mf])
            B.pool(lambda e, i=i: e.affine_select(out=mf[:], in_=mf[:], pattern=[[1, 512]], compare_op=ALU.is_ge, fill=0.0, base=-128 * i, channel_multiplier=-1),
                   r=[mf], w=[mf])
            B.pool(lambda e, mk=mk: e.tensor_copy(out=mk[:], in_=mf[:]), r=[mf], w=[mk])
            masks.append(mk)
        KTh = [B.sb("KTh", [QK, NTOK], BF16, s2) for _ in range(2)]
        Vh = [B.sb("Vh", [128, ntp, VD + 1], BF16, s2) for _ in range(2)]
        for v_ in Vh:
            B.pool(lambda e, v_=v_: e.memset(v_[:], 1.0), w=[v_])
        QTq = [B.sb("QTq", [QK, 512], BF16, s2) for _ in range(2)]
        PT = [B.sb("PT", [128, 512], BF16, s2) for _ in range(3)]
        Osb = [B.sb("Osb", [VD + 1, 512], F32, s2) for _ in range(2)]
        rl = B.sb("rl", [VD, 512], F32, s2)
        On = [B.sb("On", [VD, 512], BF16, s2) for _ in range(2)]
        onesr = B.sb("onesr", [VD + 1, VD], F32, s2)
        B.pool(lambda e: e.memset(onesr[:], 1.0), w=[onesr])
        nstep = 0
        nq_i = 0
        for h in range(NH):
            KT, V = KTh[h % 2], Vh[h % 2]
            for b in range(nblk):
                n = min(512, NTOK - b * 512)
                bank = ps[1 + b % 2]
                for kc in range(2):
                    B.pe(lambda e, kc=kc, b=b, n=n, bank=bank: e.matmul(bank[0:QK, 0:n], lhsT=wukx[:, kc, h, :], rhs=cT[:, kc, b * 512:b * 512 + n],
                                                                        start=(kc == 0), stop=False), r=[wukx, cT], w=[bank])
                B.pe(lambda e, b=b, n=n, bank=bank: e.matmul(bank[0:QK, 0:n], lhsT=sel[:, :], rhs=kpeT[:, b * 512:b * 512 + n], start=False, stop=True),
                     r=[sel, kpeT], w=[bank])
                B.dve(lambda e, b=b, n=n, bank=bank: e.tensor_copy(out=KT[:, b * 512:b * 512 + n], in_=bank[0:QK, 0:n]), r=[bank], w=[KT])
            for t0 in range(0, ntp, 8):
                nt8 = min(8, ntp - t0)
                bank = ps[3]
                for i in range(nt8):
                    for kc in range(2):
                        B.pe(lambda e, i=i, kc=kc, t0=t0, bank=bank: e.matmul(bank[:, i * VD:(i + 1) * VD], lhsT=cT[:, kc, (t0 + i) * 128:(t0 + i + 1) * 128],
                                                                              rhs=w_uv[:, kc, h * VD:(h + 1) * VD], start=(kc == 0), stop=(kc == 1)),
                             r=[cT, w_uv], w=[bank])
                B.dve(lambda e, t0=t0, nt8=nt8, bank=bank: e.tensor_copy(out=V[:, t0:t0 + nt8, 0:VD], in_=bank[:, 0:nt8 * VD].rearrange("p (t v) -> p t v", v=VD)),
                      r=[bank], w=[V])
            for b in range(nblk):
                tiles = list(range(4 * b, min(4 * b + 4, ntp)))
                nq = 128 * len(tiles)
                Q = QTq[nq_i % 2]
                Ob = ps[6 + nq_i % 2]
                B.dma("sp", Q[:, 0:nq], QTd.t[h, :, b * 512:b * 512 + nq], r=[QTd], w=[Q])
                last_kt = tiles[-1]
                for kt in range(last_kt + 1):
                    kr = cfg.rows(kt)
                    Sb = ps[4 + nstep % 2]
                    P = PT[nstep % 3]
                    nstep += 1
                    B.pe(lambda e, kt=kt, kr=kr, Sb=Sb: e.matmul(Sb[0:kr, 0:nq], lhsT=KT[:, kt * 128:kt * 128 + kr], rhs=Q[:, 0:nq], start=True, stop=True),
                         r=[KT, Q], w=[Sb])
                    B.act(lambda e, kr=kr, Sb=Sb, P=P: e.activation(out=P[0:kr, 0:nq], in_=Sb[0:kr, 0:nq], func=AF.Exp, bias=negM[0:kr, h:h + 1], scale=scale),
                          r=[Sb, negM], w=[P])
                    if kt >= 4 * b:
                        mk = masks[kt - 4 * b]
                        B.dve(lambda e, kr=kr, P=P, mk=mk: e.tensor_tensor(out=P[0:kr, 0:nq], in0=P[0:kr, 0:nq], in1=mk[0:kr, 0:nq], op=ALU.mult), r=[P, mk], w=[P])
                    B.pe(lambda e, kt=kt, kr=kr, P=P: e.matmul(Ob[0:VD + 1, 0:nq], lhsT=V[0:kr, kt, :], rhs=P[0:kr, 0:nq], start=(kt == 0), stop=(kt == last_kt)),
                         r=[V, P], w=[Ob])
                O_ = Osb[nq_i % 2]
                On_ = On[nq_i % 2]
                B.act(lambda e: e.copy(out=O_[:, 0:nq], in_=Ob[0:VD + 1, 0:nq]), r=[Ob], w=[O_])
                B.pe(lambda e: e.matmul(ps[0][0:VD, 0:nq], lhsT=onesr[VD:VD + 1, :], rhs=O_[VD:VD + 1, 0:nq], start=True, stop=True), r=[onesr, O_], w=[ps[0]])
                B.dve(lambda e: e.reciprocal(out=rl[:, 0:nq], in_=ps[0][0:VD, 0:nq]), r=[ps[0]], w=[rl])
                B.dve(lambda e: e.tensor_tensor(out=On_[:, 0:nq], in0=O_[0:VD, 0:nq], in1=rl[:, 0:nq], op=ALU.mult), r=[O_, rl], w=[On_])
                B.dma("sp", OTd.t[h, :, b * 512:b * 512 + nq], On_[:, 0:nq], r=[On_], w=[OTd])
                nq_i += 1
        B.sy.barrier()


def mla_sample_attention(B, st, m, qs_b, w_uk, w_uv, scale):
    cfg, d, ps = B.cfg, B.d, B.ps
    nsq, srows, npg = cfg.nsq, cfg.srows, cfg.npages
    NK = npg * 128 + DEC_SEQ
    R = KVR + ROPE
    OTd, CNd = B.OTd, B.CNd
    nblk = (cfg.ntp + 3) // 4
    with ExitStack() as s2:
        wukT = B.sb("wukT", [NOPE, NH, KVR], BF16, s2)
        for h in range(NH):
            pv = B.psb(h % 2)
            for kc in range(2):
                B.pe(lambda e, h=h, kc=kc, pv=pv: e.transpose(pv[0:NOPE, kc * 128:(kc + 1) * 128], w_uk[:, kc, h * NOPE:(h + 1) * NOPE], B.identb[:, :]),
                     r=[w_uk, B.identb], w=[ps[h % 2]])
            B.act(lambda e, h=h, pv=pv: e.copy(out=wukT[:, h, :], in_=pv[0:NOPE, 0:KVR]), r=[ps[h % 2]], w=[wukT])
        qnT = B.sb("qnT", [NOPE, NH, srows], BF16, s2)
        pv = B.psb(2)
        for h in range(NH):
            B.pe(lambda e, h=h: e.transpose(pv[0:NOPE, h * srows:(h + 1) * srows], qs_b[0:srows, h, 0:NOPE], B.identb[0:srows, 0:srows]),
                 r=[qs_b, B.identb], w=[ps[2]])
        B.act(lambda e: e.copy(out=qnT[:, :, :], in_=pv[0:NOPE, 0:NH * srows].rearrange("p (h t) -> p h t", t=srows)), r=[ps[2]], w=[qnT])
        QL = B.sb("QL", [srows, NH, R], BF16, s2)
        for h in range(NH):
            bank = ps[3 + h % 2]
            B.pe(lambda e, h=h, bank=bank: e.matmul(bank[0:srows, 0:KVR], lhsT=qnT[:, h, :], rhs=wukT[:, h, :], start=True, stop=True), r=[qnT, wukT], w=[bank])
            B.act(lambda e, h=h, bank=bank: e.copy(out=QL[:, h, 0:KVR], in_=bank[0:srows, 0:KVR]), r=[bank], w=[QL])
        B.dve(lambda e: e.tensor_copy(out=QL[:, :, KVR:R], in_=qs_b[0:srows, :, NOPE:QK]), r=[qs_b], w=[QL])
        QLT = B.sb("QLT", [128, 3, srows, NH], BF16, s2)
        for c, (c0, cw) in enumerate(((0, 128), (128, 128), (256, 32))):
            for h0 in range(0, NH, 8):
                bank = ps[5 + (c + h0 // 8) % 2]
                pv = B.psb(5 + (c + h0 // 8) % 2)
                for hh in range(8):
                    h = h0 + hh
                    B.pe(lambda e, h=h, hh=hh, c0=c0, cw=cw, pv=pv: e.transpose(pv[0:cw, hh * srows:(hh + 1) * srows], QL[:, h, c0:c0 + cw], B.identb[0:srows, 0:srows]),
                         r=[QL, B.identb], w=[bank])
                B.act(lambda e, c=c, h0=h0, cw=cw, pv=pv: e.copy(out=QLT[0:cw, c, :, h0:h0 + 8].rearrange("p r h -> p h r"),
                                                              in_=pv[0:cw, 0:8 * srows].rearrange("p (h r) -> p h r", r=srows)), r=[bank], w=[QLT])
        ptb = B.sb("ptb", [128, nsq * npg], I32, s2)
        idx = B.sb("idx", [128, nsq * npg], I32, s2)
        pidx = B.sb("pidx", [128, 1], I32, s2)
        B.dma("sp", ptb[:, :], d["page_table"].t.rearrange("s g -> (s g)").rearrange("(o n) -> o n", o=1).broadcast_to([128, nsq * npg]), w=[ptb])
        B.pool(lambda e: e.iota(pidx[:], pattern=[[0, 1]], base=m * cfg.npool * 128, channel_multiplier=1), w=[pidx])
        B.pool(lambda e: e.tensor_scalar(out=idx[:], in0=ptb[:], scalar1=128, scalar2=None, op0=ALU.mult), r=[ptb], w=[idx])
        B.pool(lambda e: e.tensor_tensor(out=idx[:], in0=idx[:], in1=pidx[:].broadcast_to([128, nsq * npg]), op=ALU.add), r=[idx, pidx], w=[idx])
        mskf = B.sb("mskf", [64, DEC_SEQ], F32, s2)
        B.pool(lambda e: e.memset(mskf[:], 1.0), w=[mskf])
        B.pool(lambda e: e.affine_select(out=mskf[:], in_=mskf[:], pattern=[[-NH, DEC_SEQ]], compare_op=ALU.is_ge, fill=0.0, base=0, channel_multiplier=1),
               r=[mskf], w=[mskf])
        CP = [B.sb("CP", [128, npg + 1, R], BF16, s2) for _ in range(2)]
        CTs = [B.sb("CTs", [128, 3, 128], BF16, s2) for _ in range(3)]
        S_all = B.sb("S_all", [64, NK], F32, s2)
        Pb = B.sb("Pb", [64, NK], BF16, s2)
        Pn = B.sb("Pn", [64, DEC_SEQ], F32, s2)
        PTs = B.sb("PTs", [128, npg + 1, 64], BF16, s2)
        sm = B.sb("sm", [64, 8], F32, s2)
        OLs = B.sb("OLs", [64, KVR], BF16, s2)
        OLT = B.sb("OLT", [128, 2, NH, srows], BF16, s2)
        lat, kr_ = d["cache_mla_latent"], d["cache_mla_krope"]
        nct = 0
        for s in range(nsq):
            C = CP[s % 2]
            for g in range(npg):
                B.sy.op("pool", lambda e, g=g: e.indirect_dma_start(out=C[:, g, 0:KVR], out_offset=None, in_=lat.t,
                        in_offset=bass.IndirectOffsetOnAxis(ap=idx[:, s * npg + g:s * npg + g + 1], axis=0)), [idx, lat], [C], dma=True)
                B.sy.op("pool", lambda e, g=g: e.indirect_dma_start(out=C[:, g, KVR:R], out_offset=None, in_=kr_.t,
                        in_offset=bass.IndirectOffsetOnAxis(ap=idx[:, s * npg + g:s * npg + g + 1], axis=0)), [idx, kr_], [C], dma=True)
            B.dma("pool", C[0:DEC_SEQ, npg, :], CNd.t[s * DEC_SEQ:(s + 1) * DEC_SEQ, :], r=[CNd], w=[C])
            qcols = lambda c, kw: QLT[0:kw, c, s * DEC_SEQ:(s + 1) * DEC_SEQ, :].rearrange("p t h -> p (t h)")
            for g in range(npg + 1):
                kr = 128 if g < npg else DEC_SEQ
                CT_ = CTs[nct % 3]
                nct += 1
                bi = 1 + (g % 2)
                pv = B.psb(bi)
                for c, (c0, cw) in enumerate(((0, 128), (128, 128), (256, 32))):
                    B.pe(lambda e, g=g, kr=kr, c=c, c0=c0, cw=cw, pv=pv: e.transpose(pv[0:cw, c * 128:c * 128 + kr], C[0:kr, g, c0:c0 + cw], B.identb[0:kr, 0:kr]),
                         r=[C, B.identb], w=[ps[bi]])
                B.act(lambda e, kr=kr, CT_=CT_, pv=pv: e.copy(out=CT_[:, :, 0:kr], in_=pv[:, 0:384].rearrange("p (c k) -> p c k", k=128)[:, :, 0:kr]), r=[ps[bi]], w=[CT_])
                sb_i = 3 + (g // 4) % 2
                off = (g % 4) * 128
                for c, (c0, cw) in enumerate(((0, 128), (128, 128), (256, 32))):
                    B.pe(lambda e, c=c, cw=cw, kr=kr, CT_=CT_, sb_i=sb_i, off=off: e.matmul(ps[sb_i][0:64, off:off + kr], lhsT=qcols(c, cw), rhs=CT_[0:cw, c, 0:kr],
                                                                                        start=(c == 0), stop=(c == 2)), r=[QLT, CT_], w=[ps[sb_i]])
                if g % 4 == 3 or g == npg:
                    g0 = (g // 4) * 4
                    n = off + kr
                    B.dve(lambda e, g0=g0, n=n, sb_i=sb_i: e.tensor_scalar(out=S_all[:, g0 * 128:g0 * 128 + n], in0=ps[sb_i][0:64, 0:n], scalar1=scale, scalar2=None, op0=ALU.mult),
                          r=[ps[sb_i]], w=[S_all])
            B.dve(lambda e: e.tensor_reduce(out=sm[:, 0:1], in_=S_all[:, 0:NK], axis=AX.X, op=ALU.max), r=[S_all], w=[sm])
            B.dve(lambda e: e.tensor_scalar(out=sm[:, 1:2], in0=sm[:, 0:1], scalar1=-1.0, scalar2=None, op0=ALU.mult), r=[sm], w=[sm])
            B.act(lambda e: e.activation(out=Pb[:, 0:npg * 128], in_=S_all[:, 0:npg * 128], func=AF.Exp, bias=sm[:, 1:2], scale=1.0, accum_out=sm[:, 2:3]),
                  r=[S_all, sm], w=[Pb, sm])
            B.act(lambda e: e.activation(out=Pn[:, :], in_=S_all[:, npg * 128:NK], func=AF.Exp, bias=sm[:, 1:2], scale=1.0), r=[S_all, sm], w=[Pn])
            B.dve(lambda e: e.tensor_tensor(out=Pn[:, :], in0=Pn[:, :], in1=mskf[:, :], op=ALU.mult), r=[Pn, mskf], w=[Pn])
            B.dve(lambda e: e.tensor_reduce(out=sm[:, 3:4], in_=Pn[:, :], axis=AX.X, op=ALU.add), r=[Pn], w=[sm])
            B.dve(lambda e: e.tensor_copy(out=Pb[:, npg * 128:NK], in_=Pn[:, :]), r=[Pn], w=[Pb])
            B.dve(lambda e: e.tensor_tensor(out=sm[:, 4:5], in0=sm[:, 2:3], in1=sm[:, 3:4], op=ALU.add), r=[sm], w=[sm])
            B.dve(lambda e: e.reciprocal(out=sm[:, 5:6], in_=sm[:, 4:5]), r=[sm], w=[sm])
            for g0 in range(0, npg + 1, 8):
                n8 = min(8, npg + 1 - g0)
                bi = 5 + (g0 // 8) % 2
                pv = B.psb(bi)
                for i in range(n8):
                    g = g0 + i
                    kr = 128 if g < npg else DEC_SEQ
                    B.pe(lambda e, g=g, i=i, kr=kr, pv=pv: e.transpose(pv[0:kr, i * 64:(i + 1) * 64], Pb[:, g * 128:g * 128 + kr], B.identb[0:64, 0:64]),
                         r=[Pb, B.identb], w=[ps[bi]])
                nfull = n8 if g0 + n8 <= npg else n8 - 1
                if nfull:
                    B.act(lambda e, g0=g0, nfull=nfull, pv=pv: e.copy(out=PTs[:, g0:g0 + nfull, :], in_=pv[:, 0:nfull * 64].rearrange("p (g q) -> p g q", q=64)), r=[ps[bi]], w=[PTs])
                if nfull != n8:
                    B.act(lambda e, pv=pv, nfull=nfull: e.copy(out=PTs[0:DEC_SEQ, npg, :], in_=pv[0:DEC_SEQ, nfull * 64:(nfull + 1) * 64]), r=[ps[bi]], w=[PTs])
            for g in range(npg + 1):
                kr = 128 if g < npg else DEC_SEQ
                B.pe(lambda e, g=g, kr=kr: e.matmul(ps[7][0:64, 0:KVR], lhsT=PTs[0:kr, g, :], rhs=C[0:kr, g, 0:KVR], start=(g == 0), stop=(g == npg)), r=[PTs, C], w=[ps[7]])
            B.dve(lambda e: e.tensor_scalar(out=OLs[:, :], in0=ps[7][0:64, 0:KVR], scalar1=sm[:, 5:6], scalar2=None, op0=ALU.mult), r=[ps[7], sm], w=[OLs])
            pv = B.psb(0)
            for kc in range(2):
                B.pe(lambda e, kc=kc: e.transpose(pv[:, kc * 64:(kc + 1) * 64], OLs[:, kc * 128:(kc + 1) * 128], B.identb[0:64, 0:64]), r=[OLs, B.identb], w=[ps[0]])
            B.act(lambda e: e.copy(out=OLT[:, :, :, s * DEC_SEQ:(s + 1) * DEC_SEQ].rearrange("p k h t -> p k t h"),
                                   in_=pv[:, 0:128].rearrange("p (k t h) -> p k t h", k=2, t=DEC_SEQ)), r=[ps[0]], w=[OLT])
        OTs = B.sb("OTs", [VD, NH, srows], BF16, s2)
        for h in range(NH):
            bank = ps[1 + h % 2]
            for kc in range(2):
                B.pe(lambda e, h=h, kc=kc, bank=bank: e.matmul(bank[0:VD, 0:srows], lhsT=w_uv[:, kc, h * VD:(h + 1) * VD], rhs=OLT[:, kc, h, :], start=(kc == 0), stop=(kc == 1)),
                     r=[w_uv, OLT], w=[bank])
            B.act(lambda e, h=h, bank=bank: e.copy(out=OTs[:, h, :], in_=bank[0:VD, 0:srows]), r=[bank], w=[OTs])
        B.dma("sp", OTd.t.rearrange("h v t -> v h t")[:, :, nblk * 512:nblk * 512 + srows], OTs[:, :, :], r=[OTs], w=[OTd])
        B.sy.barrier()


def mla_outproj(B, li, m, src, dst):
    cfg, d, ps = B.cfg, B.d, B.ps
    ntp, nt = cfg.ntp, cfg.nt
    nblk = (ntp + 3) // 4
    OTd = B.OTd
    with ExitStack() as st:
        w_o = B.sb("w_o", [VD, NH, D], BF16, st)
        wv = d["mla_w_o"].t[m].rearrange("(h v) n -> v h n", v=VD)
        for h in range(NH):
            B.dma("pool", w_o[:, h, :], wv[:, h, :], w=[w_o])
        G = B.sb("G", [128, D], F32, st)
        Bt = B.sb("Bt", [128, D], F32, st)
        B.load_bcast(G, d["ln_g"].t[li, 1])
        B.load_bcast(Bt, d["ln_b"].t[li, 1])
        xs = [B.sb("xs", [128, D], F32, st) for _ in range(2)]
        OTg = [B.sb("OTg", [VD, NH, 512], BF16, st) for _ in range(2)]
        tmp = dict(xa=B.sb("xa", [128, D], F32, st), y=B.sb("y", [128, D], F32, st), junk=B.sb("junk", [128, D], F32, st),
                   st=B.sb("st", [128, 8], F32, st))
        xo = [B.sb("xo", [128, D], F32, st) for _ in range(2)]
        for t_ in xs:
            B.dve(lambda e, t_=t_: e.memset(t_[:], 0.0), w=[t_])
        for j in range(nt):
            x = xs[j % 2]
            for (r0, r1), ap, tt in src(j):
                B.dma("sp", x[r0:r1, :], ap, r=[tt], w=[x])
            if j < ntp:
                blk, off = j // 4, (j % 4) * 128
                OT = OTg[blk % 2]
                if j % 4 == 0:
                    n = min(512, ntp * 128 - blk * 512)
                    B.dma("sp", OT[:, :, 0:n], OTd.t.rearrange("h v t -> v h t")[:, :, blk * 512:blk * 512 + n], r=[OTd], w=[OT])
                M = 128
            else:
                OT = OTg[(nblk) % 2]
                off, M = 0, cfg.srows
                B.dma("sp", OT[:, :, 0:M], OTd.t.rearrange("h v t -> v h t")[:, :, nblk * 512:nblk * 512 + M], r=[OTd], w=[OT])
            for hf in range(2):
                for h in range(NH):
                    B.pe(lambda e, h=h, hf=hf, OT=OT, off=off, M=M: e.matmul(ps[6 + hf][0:M, :], lhsT=OT[:, h, off:off + M], rhs=w_o[:, h, hf * 512:(hf + 1) * 512],
                                                                           start=(h == 0), stop=(h == NH - 1)), r=[OT, w_o], w=[ps[6 + hf]])
            o = xo[j % 2]
            B.resid_ln(x, [ps[6], ps[7]], 1.0, G, Bt, o, tmp)
            for (r0, r1), ap, tt in dst(j):
                B.dma("sp", ap, o[r0:r1, :], r=[o], w=[tt])
        B.sy.barrier()


def resid_ln_sb(B, x, h, G, Bt, out, tmp):
    y, junk, st = tmp["y"], tmp["junk"], tmp["st"]
    B.dve(lambda e: e.memset(st[:, 1:2], 0.0), w=[st])
    B.dve(lambda e: e.scalar_tensor_tensor(out=y[:], in0=x[:], scalar=float(B.cfg.alpha), in1=h[:], op0=ALU.mult, op1=ALU.add, accum_out=st[:, 0:1]),
          r=[x, h], w=[y, st])
    B.ln_core(y, G, Bt, out, junk, st)


def stage_s5(B, li, m, src, dst):
    cfg, d, ps = B.cfg, B.d, B.ps
    ntp, nt, nsq, srows = cfg.ntp, cfg.nt, cfg.nsq, cfg.srows
    with ExitStack() as st:
        BbT = [B.sb("BbT", [128, 32, 128], BF16, st) for _ in range(2)]
        CTm = [B.sb("CTm", [128, 32, 128], BF16, st) for _ in range(2)]
        prm = B.sb("prm", [128, 12, 32], F32, st)
        Dp = B.sb("Dp", [128, 8], F32, st)
        wv = B.sb("wv", [128, 8, D], BF16, st)
        wg = B.sb("wg", [128, 8, D], BF16, st)
        B.load_w(wv, d["s5_wv"].t[m])
        B.load_w(wg, d["s5_wg"].t[m])
        G = B.sb("G", [128, D], F32, st)
        Bt = B.sb("Bt", [128, D], F32, st)
        B.load_bcast(G, d["ln_g"].t[li, 1])
        B.load_bcast(Bt, d["ln_b"].t[li, 1])
        B.load_T(Dp, Dp[:, :], d["s5_d"].t[m].rearrange("(c p) -> c p", p=128), 8)
        P = lambda k: prm[:, k, :]
        bufs = s5_alloc(B, st)
        sp = ExitStack()
        CS = B.sb("CS", [128, 32 * 128], F32, sp)
        SN = B.sb("SN", [128, 32 * 128], F32, sp)
        with ExitStack() as s1:
            raw = B.sb("raw", [32, 3, 128], F32, s1)
            ldt = B.sb("ldt", [32, 2], F32, s1)
            B.dma("sp", raw[:, 0, :], d["s5_lam_re"].t[m].rearrange("(s g) p -> s (g p)", g=2), w=[raw])
            B.dma("sp", raw[:, 1, :], d["s5_lam_im"].t[m].rearrange("(s g) p -> s (g p)", g=2), w=[raw])
            B.dma("sp", ldt[:, :], d["s5_log_dt"].t[m].rearrange("(s g) -> s g", g=2), w=[ldt])
            B.dve(lambda e: e.tensor_copy(out=raw[:, 2, :].rearrange("s (g p) -> s g p", g=2), in_=ldt[:, :].unsqueeze(2).broadcast_to([32, 2, 64])), r=[ldt], w=[raw])
            for k in range(3):
                B.pe(lambda e, k=k: e.transpose(ps[0][:, k * 32:(k + 1) * 32], raw[:, k, :], B.ident[0:32, 0:32]), r=[raw, B.ident], w=[ps[0]])
            B.dve(lambda e: e.tensor_copy(out=prm[:, 0:3, :], in_=ps[0][:, 0:96].rearrange("p (k s) -> p k s", k=3)), r=[ps[0]], w=[prm])
            B.act(lambda e: e.activation(out=P(2), in_=P(2), func=AF.Exp), r=[prm], w=[prm])
            B.dve(lambda e: e.tensor_tensor(out=P(9), in0=P(0), in1=P(2), op=ALU.mult), r=[prm], w=[prm])
            B.act(lambda e: e.activation(out=P(3), in_=P(9), func=AF.Exp), r=[prm], w=[prm])
            B.dve(lambda e: e.tensor_tensor(out=P(4), in0=P(1), in1=P(2), op=ALU.mult), r=[prm], w=[prm])
            tf = B.sb("tf", [128, 4096], F32, s1)
            ti = B.sb("ti", [128, 4096], I32, s1)
            a32 = B.sb("a32", [128, 4, 32], F32, s1)
            B.dve(lambda e: e.tensor_copy(out=a32[:, 0, :], in_=P(4)), r=[prm], w=[a32])
            B.dve(lambda e: e.tensor_scalar(out=a32[:, 1, :], in0=P(4), scalar1=math.pi / 2, scalar2=None, op0=ALU.add), r=[prm], w=[a32])
            a32f = T(a32.t[:].rearrange("p k s -> p (k s)"), "a32f")
            a32f.wr, a32f.rd = a32.wr, a32.rd
            sc_ = B.sb("sc_", [128, 64], F32, s1)
            range_reduce_sin_signed(B, a32f, sc_, 64, tf, ti)
            B.dve(lambda e: e.tensor_tensor(out=P(6), in0=P(3), in1=sc_[:, 0:32], op=ALU.mult), r=[prm, sc_], w=[prm])
            B.dve(lambda e: e.tensor_tensor(out=P(5), in0=P(3), in1=sc_[:, 32:64], op=ALU.mult), r=[prm, sc_], w=[prm])
            B.dve(lambda e: e.tensor_tensor(out=P(9), in0=P(0), in1=P(0), op=ALU.mult), r=[prm], w=[prm])
            B.dve(lambda e: e.tensor_tensor(out=P(10), in0=P(1), in1=P(1), op=ALU.mult), r=[prm], w=[prm])
            B.dve(lambda e: e.tensor_tensor(out=P(9), in0=P(9), in1=P(10), op=ALU.add), r=[prm], w=[prm])
            B.dve(lambda e: e.reciprocal(out=P(9), in_=P(9)), r=[prm], w=[prm])
            B.dve(lambda e: e.tensor_scalar(out=P(10), in0=P(5), scalar1=-1.0, scalar2=None, op0=ALU.add), r=[prm], w=[prm])
            B.dve(lambda e: e.tensor_tensor(out=P(7), in0=P(10), in1=P(0), op=ALU.mult), r=[prm], w=[prm])
            B.dve(lambda e: e.tensor_tensor(out=P(11), in0=P(6), in1=P(1), op=ALU.mult), r=[prm], w=[prm])
            B.dve(lambda e: e.tensor_tensor(out=P(7), in0=P(7), in1=P(11), op=ALU.add), r=[prm], w=[prm])
            B.dve(lambda e: e.tensor_tensor(out=P(7), in0=P(7), in1=P(9), op=ALU.mult), r=[prm], w=[prm])
            B.dve(lambda e: e.tensor_tensor(out=P(8), in0=P(6), in1=P(0), op=ALU.mult), r=[prm], w=[prm])
            B.dve(lambda e: e.tensor_tensor(out=P(11), in0=P(10), in1=P(1), op=ALU.mult), r=[prm], w=[prm])
            B.dve(lambda e: e.tensor_tensor(out=P(8), in0=P(8), in1=P(11), op=ALU.subtract), r=[prm], w=[prm])
            B.dve(lambda e: e.tensor_tensor(out=P(8), in0=P(8), in1=P(9), op=ALU.mult), r=[prm], w=[prm])
            t1 = B.sb("t1", [128, 128], F32, s1)
            B.pool(lambda e: e.iota(t1[:], pattern=[[1, 128]], base=1, channel_multiplier=0, allow_small_or_imprecise_dtypes=True), w=[t1])
            ang = B.sb("ang", [128, 4096], F32, s1)
            for s_ in range(32):
                B.dve(lambda e, s_=s_: e.tensor_scalar(out=ang[:, s_ * 128:(s_ + 1) * 128], in0=t1[:], scalar1=prm[:, 4, s_:s_ + 1], scalar2=None, op0=ALU.mult), r=[t1, prm], w=[ang])
            range_reduce_sin_signed(B, ang, SN, 4096, tf, ti)
            B.dve(lambda e: e.tensor_scalar(out=ang[:], in0=ang[:], scalar1=math.pi / 2, scalar2=None, op0=ALU.add), r=[ang], w=[ang])
            range_reduce_sin_signed(B, ang, CS, 4096, tf, ti)
            B.sy.barrier()
        with ExitStack() as s1:
            Braw = [B.sb("Braw", [128, 32, 16], F32, s1) for _ in range(2)]
            bb = [B.sb("bb", [128, 32, 16], F32, s1) for _ in range(2)]
            tb = B.sb("tb", [128, 32, 16], F32, s1)
            for k, nm in enumerate(("s5_b_re", "s5_b_im")):
                src_v = d[nm].t[m].rearrange("(s g) p i -> g p s i", g=2)
                for g2 in range(2):
                    B.dma("sp", Braw[k][g2 * 64:(g2 + 1) * 64, :, :], src_v[g2], w=[Braw[k]])
            cre = prm[:, 7, :].unsqueeze(2).broadcast_to([128, 32, 16])
            cim = prm[:, 8, :].unsqueeze(2).broadcast_to([128, 32, 16])
            B.dve(lambda e: e.tensor_tensor(out=bb[0][:], in0=Braw[0][:], in1=cre, op=ALU.mult), r=[Braw[0], prm], w=[bb[0]])
            B.dve(lambda e: e.tensor_tensor(out=tb[:], in0=Braw[1][:], in1=cim, op=ALU.mult), r=[Braw[1], prm], w=[tb])
            B.dve(lambda e: e.tensor_tensor(out=bb[0][:], in0=bb[0][:], in1=tb[:], op=ALU.subtract), r=[bb[0], tb], w=[bb[0]])
            B.dve(lambda e: e.tensor_tensor(out=bb[1][:], in0=Braw[1][:], in1=cre, op=ALU.mult), r=[Braw[1], prm], w=[bb[1]])
            B.dve(lambda e: e.tensor_tensor(out=tb[:], in0=Braw[0][:], in1=cim, op=ALU.mult), r=[Braw[0], prm], w=[tb])
            B.dve(lambda e: e.tensor_tensor(out=bb[1][:], in0=bb[1][:], in1=tb[:], op=ALU.add), r=[bb[1], tb], w=[bb[1]])
            Ep = B.sb("Ep", [128, 32, 128], F32, s1)
            for k in range(2):
                B.pool(lambda e: e.memset(Ep[:], 0.0), w=[Ep])
                E4 = Ep[:].rearrange("p (c q) n -> p c q n", q=4)
                b4 = bb[k][:].rearrange("p (c q) i -> p c q i", q=4)
                for g2 in range(2):
                    for q in range(4):
                        B.pool(lambda e, g2=g2, q=q, E4=E4, b4=b4: e.tensor_copy(out=E4[g2 * 64:(g2 + 1) * 64, :, q, q * 32 + g2 * 16:q * 32 + g2 * 16 + 16],
                                                                                 in_=b4[g2 * 64:(g2 + 1) * 64, :, q, :]), r=[bb[k]], w=[Ep])
                for s0 in range(0, 32, 4):
                    bank = ps[1 + (s0 // 4) % 2]
                    for i in range(4):
                        B.pe(lambda e, s0=s0, i=i, bank=bank: e.transpose(bank[:, i * 128:(i + 1) * 128], Ep[:, s0 + i, :], B.ident[:, :]), r=[Ep, B.ident], w=[bank])
                    B.act(lambda e, s0=s0, k=k, bank=bank: e.copy(out=BbT[k][:, s0:s0 + 4, :], in_=bank[:, :].rearrange("p (s n) -> p s n", n=128)), r=[bank], w=[BbT[k]])
            B.sy.barrier()
        with ExitStack() as s1:
            selc = B.sb("selc", [16, 8, 128], BF16, s1)
            B.pool(lambda e: e.memset(selc[:], 0.0), w=[selc])
            for q in range(4):
                for g2 in range(2):
                    B.pool(lambda e, q=q, g2=g2: e.tensor_copy(out=selc[:, q * 2 + g2, q * 32 + g2 * 16:q * 32 + g2 * 16 + 16], in_=B.identb[0:16, 0:16]), r=[B.identb], w=[selc])
            F = [B.sb("F", [16, 32, 128], BF16, s1) for _ in range(2)]
            for k, nm in enumerate(("s5_c_re", "s5_c_im")):
                src_v = d[nm].t[m].rearrange("(s g) o p -> g o s p", g=2)
                for g2 in range(2):
                    B.pool(lambda e, g2=g2: e.memset(F[g2][:], 0.0), w=[F[g2]])
                    B.dma("pool", F[g2][:, :, g2 * 64:(g2 + 1) * 64], src_v[g2], w=[F[g2]])
                for s0 in range(0, 32, 4):
                    bank = ps[3 + (s0 // 4) % 2]
                    for i in range(4):
                        s_ = s0 + i
                        q = s_ % 4
                        for g2 in range(2):
                            B.pe(lambda e, s_=s_, i=i, g2=g2, q=q, bank=bank: e.matmul(bank[:, i * 128:(i + 1) * 128], lhsT=F[g2][:, s_, :], rhs=selc[:, q * 2 + g2, :],
                                                                                   start=(g2 == 0), stop=(g2 == 1)), r=[F[g2], selc], w=[bank])
                    B.act(lambda e, s0=s0, k=k, bank=bank: e.activation(out=CTm[k][:, s0:s0 + 4, :], in_=bank[:, :].rearrange("p (s n) -> p s n", n=128), func=AF.Copy,
                                                                        scale=(1.0 if k == 0 else -1.0)), r=[bank], w=[CTm[k]])
            B.sy.barrier()
        s5_body(B, st, sp, bufs, li, m, src, dst, BbT, CTm, CS, SN, prm, Dp, wv, wg, G, Bt)


def range_reduce_sin_signed(B, ang, out, n, tmpf, tmpi):
    range_reduce_sin(B, ang, out, n, tmpf, tmpi)


def s5_alloc(B, st):
    bufs = {}
    bufs["xs"] = [B.sb("xs", [128, D], F32, st) for _ in range(2)]
    bufs["uT"] = B.sb("uT", [128, 8, 128], F32, st)
    bufs["uTb"] = B.sb("uTb", [128, 8, 128], BF16, st)
    bufs["HS"] = [B.sb("HS", [128, 32], F32, st) for _ in range(2)]
    bufs["yT"] = B.sb("yT", [128, 8, 128], F32, st)
    bufs["gt"] = B.sb("gt", [128, D], F32, st)
    bufs["zT"] = B.sb("zT", [128, 8, 128], BF16, st)
    bufs["sgt"] = B.sb("sgt", [128, D], F32, st)
    bufs["hbuf"] = B.sb("hbuf", [128, D], F32, st)
    bufs["y"] = B.sb("y", [128, D], F32, st)
    bufs["st"] = B.sb("st", [128, 8], F32, st)
    bufs["xo"] = [B.sb("xo", [128, D], F32, st) for _ in range(2)]
    return bufs


def s5_body(B, st, sp, bufs, li, m, src, dst, BbT, CTm, CS, SN, prm, Dp, wv, wg, G, Bt):
    cfg, d, ps = B.cfg, B.d, B.ps
    ntp, nt, nsq, srows = cfg.ntp, cfg.nt, cfg.nsq, cfg.srows
    xs, uT, uTb, HS, yT, gt, zT, sgt, hbuf, xo = (bufs[k] for k in ("xs", "uT", "uTb", "HS", "yT", "gt", "zT", "sgt", "hbuf", "xo"))
    bu = [B.sb("bu", [128, 512], F32, sp) for _ in range(2)]
    mt = [B.sb("mt", [128, 512], F32, sp) for _ in range(4)]
    z = [B.sb("z", [128, 512], F32, sp) for _ in range(2)]
    gs = [B.sb("gs", [128, 512], F32, sp) for _ in range(2)]
    hh = [B.sb("hh", [128, 512], F32, sp) for _ in range(2)]
    hb = [B.sb("hb", [128, 512], BF16, sp) for _ in range(2)]
    tmp = dict(y=bufs["y"], junk=gt, st=bufs["st"])
    for t_ in xs + HS:
        B.dve(lambda e, t_=t_: e.memset(t_[:], 0.0), w=[t_])
    ss = ExitStack()
    Hs = Hall = BUs = sm_ = None

    def load(j, buf):
        for (r0, r1), ap, tt in src(j):
            B.dma("sp", buf[r0:r1, :], ap, r=[tt], w=[buf])

    load(0, xs[0])
    for j in range(nt):
        x = xs[j % 2]
        rows = cfg.rows(j)
        sample = (j == ntp)
        N = srows if sample else 128
        if j + 1 < nt:
            load(j + 1, xs[(j + 1) % 2])
        if sample:
            B.sy.barrier()
            sp.close()
            Hs = [B.sb("Hs", [128, 32, nsq], F32, ss) for _ in range(2)]
            Hall = [B.sb("Hall", [128, 32, srows], F32, ss) for _ in range(2)]
            BUs = [B.sb("BUs", [128, 32, srows], F32, ss) for _ in range(2)]
            sm_ = [B.sb("sm_", [128, 32, nsq], F32, ss) for _ in range(4)]
            hin = B.sb("hin", [nsq, 4096], F32, ss)
            for k, nm in enumerate(("state_s5_re", "state_s5_im")):
                B.dma("sp", hin[:, :], d[nm].t[m].rearrange("s g p -> s (g p)"), w=[hin])
                for s_ in range(32):
                    B.pe(lambda e, s_=s_: e.transpose(ps[0][:, s_ * nsq:(s_ + 1) * nsq], hin[:, s_ * 128:(s_ + 1) * 128], B.ident[0:nsq, 0:nsq]), r=[hin, B.ident], w=[ps[0]])
                B.dve(lambda e, k=k: e.tensor_copy(out=Hs[k][:, :, :], in_=ps[0][:, 0:32 * nsq].rearrange("p (s q) -> p s q", q=nsq)), r=[ps[0]], w=[Hs[k]])
        for h2 in range(2):
            for c in range(4):
                B.pe(lambda e, h2=h2, c=c: e.transpose(ps[1 + h2][:, c * 128:(c + 1) * 128], x[:, (h2 * 4 + c) * 128:(h2 * 4 + c + 1) * 128], B.ident[:, :]), r=[x, B.ident], w=[ps[1 + h2]])
            B.act(lambda e, h2=h2: e.copy(out=uT[:, h2 * 4:(h2 + 1) * 4, :], in_=ps[1 + h2][:, :].rearrange("p (c t) -> p c t", t=128)), r=[ps[1 + h2]], w=[uT])
        B.pool(lambda e: e.tensor_copy(out=uTb[:], in_=uT[:]), r=[uT], w=[uTb])
        for c in range(8):
            for k in range(2):
                bank = ps[3 + k]
                for q in range(4):
                    B.pe(lambda e, k=k, q=q, c=c, bank=bank: e.matmul(bank[:, q * 128:q * 128 + N], lhsT=BbT[k][:, 4 * c + q, :], rhs=uTb[:, c, 0:N], start=True, stop=True),
                         r=[BbT[k], uTb], w=[bank])
            if sample:
                for k in range(2):
                    B.act(lambda e, k=k, c=c: e.copy(out=BUs[k][:, 4 * c:4 * c + 4, :], in_=ps[3 + k][:, :].rearrange("p (q t) -> p q t", t=128)[:, :, 0:N]), r=[ps[3 + k]], w=[BUs[k]])
                continue
            for k in range(2):
                B.act(lambda e, k=k: e.copy(out=bu[k][:], in_=ps[3 + k][:, :]), r=[ps[3 + k]], w=[bu[k]])
            cs = CS[:, c * 512:(c + 1) * 512]
            sn = SN[:, c * 512:(c + 1) * 512]
            B.dve(lambda e: e.tensor_tensor(out=mt[0][:], in0=bu[0][:], in1=cs, op=ALU.mult), r=[bu[0], CS], w=[mt[0]])
            B.pool(lambda e: e.tensor_tensor(out=mt[1][:], in0=bu[1][:], in1=sn, op=ALU.mult), r=[bu[1], SN], w=[mt[1]])
            B.dve(lambda e: e.tensor_tensor(out=mt[2][:], in0=bu[1][:], in1=cs, op=ALU.mult), r=[bu[1], CS], w=[mt[2]])
            B.pool(lambda e: e.tensor_tensor(out=mt[3][:], in0=bu[0][:], in1=sn, op=ALU.mult), r=[bu[0], SN], w=[mt[3]])
            B.dve(lambda e: e.tensor_tensor(out=z[0][:], in0=mt[0][:], in1=mt[1][:], op=ALU.add), r=[mt[0], mt[1]], w=[z[0]])
            B.dve(lambda e: e.tensor_tensor(out=z[1][:], in0=mt[2][:], in1=mt[3][:], op=ALU.subtract), r=[mt[2], mt[3]], w=[z[1]])
            for k in range(2):
                for q in range(4):
                    s_ = 4 * c + q
                    B.dve(lambda e, k=k, q=q, s_=s_: e.tensor_tensor_scan(out=gs[k][:, q * 128:(q + 1) * 128], data0=prm[:, 3, s_:s_ + 1].broadcast_to([128, 128]),
                                                                          data1=z[k][:, q * 128:(q + 1) * 128], initial=HS[k][:, s_:s_ + 1], op0=ALU.mult, op1=ALU.add),
                          r=[prm, z[k], HS[k]], w=[gs[k]])
            B.dve(lambda e: e.tensor_tensor(out=mt[0][:], in0=gs[0][:], in1=cs, op=ALU.mult), r=[gs[0], CS], w=[mt[0]])
            B.pool(lambda e: e.tensor_tensor(out=mt[1][:], in0=gs[1][:], in1=sn, op=ALU.mult), r=[gs[1], SN], w=[mt[1]])
            B.dve(lambda e: e.tensor_tensor(out=mt[2][:], in0=gs[0][:], in1=sn, op=ALU.mult), r=[gs[0], SN], w=[mt[2]])
            B.pool(lambda e: e.tensor_tensor(out=mt[3][:], in0=gs[1][:], in1=cs, op=ALU.mult), r=[gs[1], CS], w=[mt[3]])
            B.dve(lambda e: e.tensor_tensor(out=hh[0][:], in0=mt[0][:], in1=mt[1][:], op=ALU.subtract), r=[mt[0], mt[1]], w=[hh[0]])
            B.dve(lambda e: e.tensor_tensor(out=hh[1][:], in0=mt[2][:], in1=mt[3][:], op=ALU.add), r=[mt[2], mt[3]], w=[hh[1]])
            for k in range(2):
                B.dve(lambda e, k=k, c=c: e.tensor_copy(out=HS[k][:, 4 * c:4 * c + 4], in_=hh[k][:].rearrange("p (q t) -> p q t", t=128)[:, :, rows - 1]), r=[hh[k]], w=[HS[k]])
                B.act(lambda e, k=k: e.copy(out=hb[k][:], in_=hh[k][:]), r=[hh[k]], w=[hb[k]])
            yb = ps[5 + c % 2]
            for q in range(4):
                for k in range(2):
                    B.pe(lambda e, q=q, k=k, c=c, yb=yb: e.matmul(yb[:, 0:128], lhsT=CTm[k][:, 4 * c + q, :], rhs=hb[k][:, q * 128:(q + 1) * 128],
                                                                 start=(q == 0 and k == 0), stop=(q == 3 and k == 1)), r=[CTm[k], hb[k]], w=[yb])
            B.dve(lambda e, c=c, yb=yb: e.scalar_tensor_tensor(out=yT[:, c, :], in0=uT[:, c, :], scalar=Dp[:, c:c + 1], in1=yb[:, 0:128], op0=ALU.mult, op1=ALU.add),
                  r=[uT, Dp, yb], w=[yT])
        if sample:
            are = prm[:, 5, :].unsqueeze(2).broadcast_to([128, 32, nsq])
            aim = prm[:, 6, :].unsqueeze(2).broadcast_to([128, 32, nsq])
            for t in range(DEC_SEQ):
                bt = [BUs[k][:].rearrange("p s (q t) -> p s q t", t=DEC_SEQ)[:, :, :, t] for k in range(2)]
                B.dve(lambda e: e.tensor_tensor(out=sm_[0][:], in0=Hs[0][:], in1=are, op=ALU.mult), r=[Hs[0], prm], w=[sm_[0]])
                B.dve(lambda e: e.tensor_tensor(out=sm_[1][:], in0=Hs[1][:], in1=aim, op=ALU.mult), r=[Hs[1], prm], w=[sm_[1]])
                B.dve(lambda e: e.tensor_tensor(out=sm_[2][:], in0=Hs[1][:], in1=are, op=ALU.mult), r=[Hs[1], prm], w=[sm_[2]])
                B.dve(lambda e: e.tensor_tensor(out=sm_[3][:], in0=Hs[0][:], in1=aim, op=ALU.mult), r=[Hs[0], prm], w=[sm_[3]])
                B.dve(lambda e: e.tensor_tensor(out=sm_[0][:], in0=sm_[0][:], in1=sm_[1][:], op=ALU.subtract), r=[sm_[0], sm_[1]], w=[sm_[0]])
                B.dve(lambda e: e.tensor_tensor(out=sm_[2][:], in0=sm_[2][:], in1=sm_[3][:], op=ALU.add), r=[sm_[2], sm_[3]], w=[sm_[2]])
                B.dve(lambda e, bt=bt: e.tensor_tensor(out=Hs[0][:], in0=sm_[0][:], in1=bt[0], op=ALU.add), r=[sm_[0], BUs[0]], w=[Hs[0]])
                B.dve(lambda e, bt=bt: e.tensor_tensor(out=Hs[1][:], in0=sm_[2][:], in1=bt[1], op=ALU.add), r=[sm_[2], BUs[1]], w=[Hs[1]])
                for k in range(2):
                    B.dve(lambda e, k=k, t=t: e.tensor_copy(out=Hall[k][:].rearrange("p s (q t) -> p s q t", t=DEC_SEQ)[:, :, :, t], in_=Hs[k][:]), r=[Hs[k]], w=[Hall[k]])
            hbs = [B.sb("hbs", [128, 32, srows], BF16, ss) for _ in range(2)]
            for k in range(2):
                B.act(lambda e, k=k: e.copy(out=hbs[k][:], in_=Hall[k][:]), r=[Hall[k]], w=[hbs[k]])
            for c in range(8):
                yb = ps[5 + c % 2]
                for q in range(4):
                    for k in range(2):
                        B.pe(lambda e, q=q, k=k, c=c, yb=yb: e.matmul(yb[:, 0:N], lhsT=CTm[k][:, 4 * c + q, :], rhs=hbs[k][:, 4 * c + q, :],
                                                                     start=(q == 0 and k == 0), stop=(q == 3 and k == 1)), r=[CTm[k], hbs[k]], w=[yb])
                B.dve(lambda e, c=c, yb=yb: e.scalar_tensor_tensor(out=yT[:, c, 0:N], in0=uT[:, c, 0:N], scalar=Dp[:, c:c + 1], in1=yb[:, 0:N], op0=ALU.mult, op1=ALU.add),
                      r=[uT, Dp, yb], w=[yT])
        yf = yT[:].rearrange("p c t -> p (c t)")
        B.pool(lambda e: e.tensor_tensor(out=gt[:], in0=yf, in1=yf, op=ALU.mult), r=[yT], w=[gt])
        B.dve(lambda e: e.tensor_scalar(out=gt[:], in0=gt[:], scalar1=0.044715, scalar2=1.0, op0=ALU.mult, op1=ALU.add), r=[gt], w=[gt])
        B.dve(lambda e: e.tensor_tensor(out=gt[:], in0=gt[:], in1=yf, op=ALU.mult), r=[gt, yT], w=[gt])
        B.act(lambda e: e.activation(out=gt[:], in_=gt[:], func=AF.Sigmoid, scale=2.0 * math.sqrt(2.0 / math.pi)), r=[gt], w=[gt])
        B.dve(lambda e: e.tensor_tensor(out=zT[:].rearrange("p c t -> p (c t)"), in0=gt[:], in1=yf, op=ALU.mult), r=[gt, yT], w=[zT])
        for hf in range(2):
            B.linear(ps[1 + hf], zT, wg, hf * 512, (hf + 1) * 512, 8)
            B.act(lambda e, hf=hf: e.activation(out=sgt[:, hf * 512:(hf + 1) * 512], in_=ps[1 + hf][:, :], func=AF.Sigmoid), r=[ps[1 + hf]], w=[sgt])
            B.linear(ps[6 + hf], zT, wv, hf * 512, (hf + 1) * 512, 8)
            B.dve(lambda e, hf=hf: e.tensor_tensor(out=hbuf[:, hf * 512:(hf + 1) * 512], in0=ps[6 + hf][:, :], in1=sgt[:, hf * 512:(hf + 1) * 512], op=ALU.mult),
                  r=[ps[6 + hf], sgt], w=[hbuf])
        o = xo[j % 2]
        resid_ln_sb(B, x, hbuf, G, Bt, o, tmp)
        for (r0, r1), ap, tt in dst(j):
            B.dma("sp", ap, o[r0:r1, :], r=[o], w=[tt])
    for k, nm in enumerate(("re_p", "im_p")):
        B.pe(lambda e, k=k: e.transpose(ps[0][0:32, k * 128:(k + 1) * 128], HS[k][:, :], B.ident[:, :]), r=[HS[k], B.ident], w=[ps[0]])
    hso = B.sb("hso", [32, 256], F32, ss)
    B.dve(lambda e: e.tensor_copy(out=hso[:], in_=ps[0][0:32, 0:256]), r=[ps[0]], w=[hso])
    for k, nm in enumerate(("re_p", "im_p")):
        B.dma("sp", d[nm].t[m].rearrange("(s g) p -> s (g p)", g=2), hso[:, k * 128:(k + 1) * 128], r=[hso], w=[d[nm]])
    hout = hin
    for k, nm in enumerate(("re_s", "im_s")):
        for s0 in range(0, 32, 4):
            bank = ps[1 + (s0 // 4) % 2]
            for i in range(4):
                B.pe(lambda e, k=k, s0=s0, i=i, bank=bank: e.transpose(bank[0:nsq, i * 128:(i + 1) * 128], Hs[k][:, s0 + i, :], B.ident[:, :]), r=[Hs[k], B.ident], w=[bank])
            B.act(lambda e, s0=s0, bank=bank: e.copy(out=hout[:, s0 * 128:(s0 + 4) * 128], in_=bank[0:nsq, :]), r=[bank], w=[hout])
        B.dma("sp", d[nm].t[m].rearrange("s g p -> s (g p)"), hout[:, :], r=[hout], w=[d[nm]])
    B.sy.barrier()
    ss.close()


def stage_rwkv(B, li, m, src, dst):
    cfg, d, ps = B.cfg, B.d, B.ps
    ntp, nt, nsq, srows = cfg.ntp, cfg.nt, cfg.nsq, cfg.srows
    Xin, Xint = src.X, src.Xt
    if not hasattr(B, "RWd"):
        B.RWd = B.dram_scr("RWd", [6, 128, D], F32)
        B.YSd = B.dram_scr("YSd", [128, D], F32)
    RWd, YSd = B.RWd, B.YSd
    with ExitStack() as st:
        W = {}
        for nm in ("wr", "wk", "wv", "wo"):
            W[nm] = B.sb(nm, [128, 8, D], BF16, st)
            B.load_w(W[nm], d["rw_" + nm].t[m])
        for nm, n in (("w1", 64), ("a1", 64), ("g1", 128)):
            W[nm] = B.sb(nm, [128, 8, n], BF16, st)
            B.load_w(W[nm], d["rw_" + nm].t[m])
        for nm, k in (("w2", 64), ("a2", 64), ("g2", 128)):
            W[nm] = B.sb(nm, [k, 1, D], BF16, st)
            B.load_w(W[nm], d["rw_" + nm].t[m])
        MU = B.sb("MU", [128, 6, 8], F32, st)
        B.load_T(MU, MU[:, :, :].rearrange("p j c -> p (j c)"), d["rw_mu"].t[m].rearrange("j (c p) -> (j c) p", p=128), 48)
        R_ = {}
        for nm in ("w0", "a0", "k_k", "k_a", "r_k", "lnx_g", "lnx_b"):
            R_[nm] = B.sb("r_" + nm, [128, D], F32, st)
            B.load_bcast(R_[nm], d["rw_" + nm].t[m])
        G = B.sb("G", [128, D], F32, st)
        Bt = B.sb("Bt", [128, D], F32, st)
        B.load_bcast(G, d["ln_g"].t[li, 1])
        B.load_bcast(Bt, d["ln_b"].t[li, 1])
        tri = B.sb("tri", [128, 128], F32, st)
        m2 = B.sb("m2", [128, 256], F32, st)
        sl = B.sb("sl", [128, 128], F32, st)
        B.pool(lambda e: e.memset(tri[:], 1.0), w=[tri])
        B.pool(lambda e: e.affine_select(out=tri[:], in_=tri[:], pattern=[[1, 128]], compare_op=ALU.is_ge, fill=0.0, base=0, channel_multiplier=-1), r=[tri], w=[tri])
        B.pool(lambda e: e.memset(m2[:], 1.0), w=[m2])
        B.pool(lambda e: e.affine_select(out=m2[:, 0:128], in_=m2[:, 0:128], pattern=[[1, 128]], compare_op=ALU.is_gt, fill=0.0, base=0, channel_multiplier=-1), r=[m2], w=[m2])
        B.pool(lambda e: e.affine_select(out=m2[:, 128:256], in_=m2[:, 128:256], pattern=[[1, 128]], compare_op=ALU.is_ge, fill=0.0, base=0, channel_multiplier=-1), r=[m2], w=[m2])
        B.pool(lambda e: e.memset(sl[:], 1.0), w=[sl])
        B.pool(lambda e: e.affine_select(out=sl[:], in_=sl[:], pattern=[[-1, 128]], compare_op=ALU.is_gt, fill=0.0, base=0, channel_multiplier=1), r=[sl], w=[sl])
        vmask = B.sb("vmask", [128, 1], F32, st)
        B.pool(lambda e: e.memset(vmask[:], 1.0), w=[vmask])
        B.pool(lambda e: e.affine_select(out=vmask[:], in_=vmask[:], pattern=[[0, 1]], compare_op=ALU.is_gt, fill=0.0, base=cfg.rows(ntp - 1), channel_multiplier=-1), r=[vmask], w=[vmask])
        ST = B.sb("ST", [64, NH, 64], F32, st)
        STb = B.sb("STb", [64, NH, 64], BF16, st)
        B.dve(lambda e: e.memset(ST[:], 0.0), w=[ST])
        B.dve(lambda e: e.memset(STb[:], 0.0), w=[STb])

        s2 = ExitStack()
        f32 = lambda nm: B.sb(nm, [128, D], F32, s2)
        x, xp, t0, t1 = f32("x"), f32("xp"), f32("t0"), f32("t1")
        r32, k32, v32, a32, ld, kk = f32("r32"), f32("k32"), f32("v32"), f32("a32"), f32("ld"), f32("kk")
        g32 = xp
        xT = B.sb("xT", [128, 8, 128], F32, s2)
        xxT = B.sb("xxT", [128, 8, 128], F32, s2)
        mixT = [B.sb("mixT", [128, 8, 128], BF16, s2) for _ in range(2)]
        lo = B.sb("lo", [128, 128], BF16, s2)
        loT = B.sb("loT", [128, 1, 128], BF16, s2)
        ss16 = B.sb("ss16", [128, 4, NH], F32, s2)
        bfs = {nm: B.sb(nm, [128, D], BF16, s2) for nm in ("rt", "at", "bt", "kt", "vb")}
        tmp = dict(xa=t0, y=t1, junk=kk, st=B.sb("st", [128, 8], F32, s2))
        xo = [B.sb("xo", [128, D], F32, s2)] * 2

        def mix(jx, buf):
            B.pool(lambda e: e.tensor_tensor(out=t0[:].rearrange("p (c t) -> p c t", t=128), in0=xxT[:], in1=MU[:, jx, :].unsqueeze(2).broadcast_to([128, 8, 128]), op=ALU.mult),
                   r=[xxT, MU], w=[t0])
            B.dve(lambda e: e.tensor_tensor(out=buf[:], in0=t0[:].rearrange("p (c t) -> p c t", t=128), in1=xT[:], op=ALU.add), r=[t0, xT], w=[buf])

        def proj_full(jx, wname, out32):
            buf = mixT[jx % 2]
            mix(jx, buf)
            for hf in range(2):
                B.linear(ps[1 + hf], buf, W[wname], hf * 512, (hf + 1) * 512, 8)
                B.act(lambda e, hf=hf: e.copy(out=out32[:, hf * 512:(hf + 1) * 512], in_=ps[1 + hf][:, :]), r=[ps[1 + hf]], w=[out32])

        def proj_lora(jx, w1n, w2n, n1, mid_func, bias_row, out_func, out32, scale=1.0):
            buf = mixT[jx % 2]
            mix(jx, buf)
            B.linear(ps[3], buf, W[w1n], 0, n1, 8)
            B.act(lambda e: e.activation(out=lo[:, 0:n1], in_=ps[3][:, 0:n1], func=mid_func), r=[ps[3]], w=[lo])
            B.transpose_to(lo, 1, loT, loT[0:n1, :, :], ps[0], cw=n1)
            for hf in range(2):
                B.pe(lambda e, hf=hf: e.matmul(ps[1 + hf][:, :], lhsT=loT[0:n1, 0, :], rhs=W[w2n][0:n1, 0, hf * 512:(hf + 1) * 512], start=True, stop=True),
                     r=[loT, W[w2n]], w=[ps[1 + hf]])
                if bias_row is not None:
                    B.dve(lambda e, hf=hf: e.tensor_tensor(out=out32[:, hf * 512:(hf + 1) * 512], in0=ps[1 + hf][:, :], in1=bias_row[:, hf * 512:(hf + 1) * 512], op=ALU.add),
                          r=[ps[1 + hf], bias_row], w=[out32])
                    B.act(lambda e, hf=hf: e.activation(out=out32[:, hf * 512:(hf + 1) * 512], in_=out32[:, hf * 512:(hf + 1) * 512], func=out_func), r=[out32], w=[out32])
                else:
                    B.act(lambda e, hf=hf: e.copy(out=out32[:, hf * 512:(hf + 1) * 512], in_=ps[1 + hf][:, :]), r=[ps[1 + hf]], w=[out32])

        def v3(t_, n=64):
            return t_[:].rearrange("p (h k) -> p h k", k=n)

        def bc16(col_ap):
            return col_ap.unsqueeze(2).broadcast_to([128, NH, 64])

        def front(j):
            rows = cfg.rows(j)
            base = 128 * j
            sample = (j == ntp)
            if rows < 128:
                B.dve(lambda e: e.memset(x[:], 0.0), w=[x])
            B.dve(lambda e: e.memset(xp[:], 0.0), w=[xp])
            B.dma("sp", x[0:rows, :], Xin[base:base + rows, :], r=[Xint[j]], w=[x])
            if not sample:
                if j > 0:
                    B.dma("sp", xp[0:rows, :], Xin[base - 1:base - 1 + rows, :], r=[Xint[j], Xint[j - 1]], w=[xp])
                else:
                    B.dma("sp", xp[1:rows, :], Xin[0:rows - 1, :], r=[Xint[j]], w=[xp])
            else:
                xv = Xin[base:base + rows, :].rearrange("(s t) n -> s t n", t=DEC_SEQ)
                for s_ in range(nsq):
                    B.dma("sp", xp[s_ * DEC_SEQ:s_ * DEC_SEQ + 1, :], d["state_rwkv_shift"].t[m, s_:s_ + 1, :], w=[xp])
                    B.dma("sp", xp[s_ * DEC_SEQ + 1:(s_ + 1) * DEC_SEQ, :], xv[s_, 0:DEC_SEQ - 1, :], r=[Xint[j]], w=[xp])
            B.dve(lambda e: e.tensor_tensor(out=xp[:], in0=xp[:], in1=x[:], op=ALU.subtract), r=[xp, x], w=[xp])
            for src_, dstT in ((x, xT), (xp, xxT)):
                for h2 in range(2):
                    for c in range(4):
                        B.pe(lambda e, h2=h2, c=c, src_=src_: e.transpose(ps[4 + h2][:, c * 128:(c + 1) * 128], src_[:, (h2 * 4 + c) * 128:(h2 * 4 + c + 1) * 128], B.ident[:, :]),
                             r=[src_, B.ident], w=[ps[4 + h2]])
                    B.act(lambda e, h2=h2, dstT=dstT: e.copy(out=dstT[:, h2 * 4:(h2 + 1) * 4, :], in_=ps[4 + h2][:, :].rearrange("p (c t) -> p c t", t=128)), r=[ps[4 + h2]], w=[dstT])
            proj_full(0, "wr", r32)
            proj_lora(1, "w1", "w2", 64, AF.Tanh, R_["w0"], AF.Sigmoid, ld)
            proj_full(2, "wk", k32)
            proj_full(3, "wv", v32)
            proj_lora(4, "a1", "a2", 64, AF.Copy, R_["a0"], AF.Sigmoid, a32)
            proj_lora(5, "g1", "g2", 128, AF.Sigmoid, None, None, g32)
            B.act(lambda e: e.activation(out=ld[:], in_=ld[:], func=AF.Copy, scale=-math.exp(-0.5)), r=[ld], w=[ld])
            if rows < 128 and not sample:
                B.dve(lambda e: e.tensor_scalar(out=ld[:], in0=ld[:], scalar1=vmask[:, 0:1], scalar2=None, op0=ALU.mult), r=[ld, vmask], w=[ld])
            B.dve(lambda e: e.tensor_tensor(out=kk[:], in0=k32[:], in1=R_["k_k"][:], op=ALU.mult), r=[k32, R_["k_k"]], w=[kk])
            B.pool(lambda e: e.tensor_tensor(out=t0[:], in0=kk[:], in1=kk[:], op=ALU.mult), r=[kk], w=[t0])
            B.dve(lambda e: e.tensor_reduce(out=ss16[:, 0, :], in_=v3(t0), axis=AX.X, op=ALU.add), r=[t0], w=[ss16])
            B.dve(lambda e: e.tensor_scalar(out=ss16[:, 0, :], in0=ss16[:, 0, :], scalar1=1e-24, scalar2=None, op0=ALU.max), r=[ss16], w=[ss16])
            B.act(lambda e: e.activation(out=ss16[:, 0, :], in_=ss16[:, 0, :], func=AF.Sqrt), r=[ss16], w=[ss16])
            B.dve(lambda e: e.reciprocal(out=ss16[:, 0, :], in_=ss16[:, 0, :]), r=[ss16], w=[ss16])
            B.dve(lambda e: e.tensor_tensor(out=v3(kk), in0=v3(kk), in1=bc16(ss16[:, 0, :]), op=ALU.mult), r=[kk, ss16], w=[kk])
            B.dve(lambda e: e.scalar_tensor_tensor(out=t0[:], in0=a32[:], scalar=-1.0, in1=R_["k_a"][:], op0=ALU.add, op1=ALU.mult), r=[a32, R_["k_a"]], w=[t0])
            B.dve(lambda e: e.scalar_tensor_tensor(out=k32[:], in0=t0[:], scalar=1.0, in1=k32[:], op0=ALU.add, op1=ALU.mult), r=[t0, k32], w=[k32])
            B.pool(lambda e: e.tensor_tensor(out=t0[:], in0=r32[:], in1=k32[:], op=ALU.mult), r=[r32, k32], w=[t0])
            B.dve(lambda e: e.tensor_tensor(out=t0[:], in0=t0[:], in1=R_["r_k"][:], op=ALU.mult), r=[t0, R_["r_k"]], w=[t0])
            B.dve(lambda e: e.tensor_reduce(out=ss16[:, 1, :], in_=v3(t0), axis=AX.X, op=ALU.add), r=[t0], w=[ss16])
            B.dve(lambda e: e.tensor_tensor(out=v3(t0), in0=v3(v32), in1=bc16(ss16[:, 1, :]), op=ALU.mult), r=[v32, ss16], w=[t0])
            B.pool(lambda e: e.tensor_tensor(out=t1[:], in0=kk[:], in1=a32[:], op=ALU.mult), r=[kk, a32], w=[t1])

        def post(j, y32):
            B.dve(lambda e: e.tensor_reduce(out=ss16[:, 2, :], in_=v3(y32), axis=AX.X, op=ALU.add), r=[y32], w=[ss16])
            B.dve(lambda e: e.tensor_scalar(out=ss16[:, 2, :], in0=ss16[:, 2, :], scalar1=-1.0 / 64, scalar2=None, op0=ALU.mult), r=[ss16], w=[ss16])
            B.dve(lambda e: e.tensor_tensor(out=v3(y32), in0=v3(y32), in1=bc16(ss16[:, 2, :]), op=ALU.add), r=[y32, ss16], w=[y32])
            B.pool(lambda e: e.tensor_tensor(out=t1[:], in0=y32[:], in1=y32[:], op=ALU.mult), r=[y32], w=[t1])
            B.dve(lambda e: e.tensor_reduce(out=ss16[:, 3, :], in_=v3(t1), axis=AX.X, op=ALU.add), r=[t1], w=[ss16])
            B.act(lambda e: e.activation(out=ss16[:, 3, :], in_=ss16[:, 3, :], func=AF.Sqrt, bias=B.eps[:, 2:3], scale=1.0 / 64), r=[ss16, B.eps], w=[ss16])
            B.dve(lambda e: e.reciprocal(out=ss16[:, 3, :], in_=ss16[:, 3, :]), r=[ss16], w=[ss16])
            B.dve(lambda e: e.tensor_tensor(out=v3(y32), in0=v3(y32), in1=bc16(ss16[:, 3, :]), op=ALU.mult), r=[y32, ss16], w=[y32])
            B.dve(lambda e: e.tensor_tensor(out=y32[:], in0=y32[:], in1=R_["lnx_g"][:], op=ALU.mult), r=[y32, R_["lnx_g"]], w=[y32])
            B.pool(lambda e: e.tensor_tensor(out=y32[:], in0=y32[:], in1=R_["lnx_b"][:], op=ALU.add), r=[y32, R_["lnx_b"]], w=[y32])
            B.dve(lambda e: e.tensor_tensor(out=y32[:], in0=y32[:], in1=t0[:], op=ALU.add), r=[y32, t0], w=[y32])
            yg = bfs["rt"]
            B.dve(lambda e: e.tensor_tensor(out=yg[:], in0=y32[:], in1=g32[:], op=ALU.mult), r=[y32, g32], w=[yg])
            buf = mixT[0]
            B.transpose_to(yg, 8, buf, buf[:, :, :], ps[0])
            for hf in range(2):
                B.linear(ps[6 + hf], buf, W["wo"], hf * 512, (hf + 1) * 512, 8)
            o = xo[j % 2]
            B.resid_ln(x, [ps[6], ps[7]], 1.0, G, Bt, o, tmp)
            for (r0, r1), ap, tt in dst(j):
                B.dma("sp", ap, o[r0:r1, :], r=[o], w=[tt])

        rwkv_prompt_loop(B, m, front, post, locals())
        front(ntp)
        B.act(lambda e: e.activation(out=ld[:], in_=ld[:], func=AF.Exp), r=[ld], w=[ld])
        B.dve(lambda e: e.tensor_scalar(out=kk[:], in0=kk[:], scalar1=-1.0, scalar2=None, op0=ALU.mult), r=[kk], w=[kk])
        for i, t_ in enumerate((r32, ld, k32, v32, kk, t1)):
            B.dma("sp", RWd.t[i, 0:srows, :], t_[0:srows, :], r=[t_], w=[RWd])
        rwkv_sample_rec(B, m, RWd, YSd)
        y32 = r32
        B.dma("sp", y32[0:srows, :], YSd.t[0:srows, :], r=[YSd], w=[y32])
        post(ntp, y32)
        B.dma("sp", d["sh_p"].t[m:m + 1, :], Xin[cfg.L - 1:cfg.L, :], r=[Xint[ntp - 1]], w=[d["sh_p"]])
        B.dma("sp", d["sh_s"].t[m], Xin[128 * ntp:128 * ntp + srows, :].rearrange("(s t) n -> s t n", t=DEC_SEQ)[:, DEC_SEQ - 1, :], r=[Xint[ntp]], w=[d["sh_s"]])
        B.sy.barrier()
        s2.close()


def rwkv_prompt_loop(B, m, front, post, L):
    cfg, d, ps = B.cfg, B.d, B.ps
    ntp = cfg.ntp
    r32, k32, v32, ld, kk, t0, t1, a32 = (L[k] for k in ("r32", "k32", "v32", "ld", "kk", "t0", "t1", "a32"))
    bfs, tri, m2, sl, ST, STb = (L[k] for k in ("bfs", "tri", "m2", "sl", "ST", "STb"))
    with ExitStack() as s3:
        HT = B.sb("HT", [64, NH, 4, 128], BF16, s3)
        WC = B.sb("WC", [64, NH], F32, s3)
        HB = []
        for par in range(2):
            HB.append((B.sb("AKm", [128, 256], BF16, s3), B.sb("ABm", [128, 256], BF16, s3),
                       [B.sb("NX", [128, 256], BF16, s3) for _ in range(2)], [B.sb("NT", [128, 128], BF16, s3) for _ in range(2)],
                       B.sb("Zb", [128, 64], BF16, s3), B.sb("Ub", [128, 64], BF16, s3)))
        ones1 = B.ones
        for j in range(ntp):
            front(j)
            for hf in range(2):
                B.pe(lambda e, hf=hf: e.matmul(ps[1 + hf][:, :], lhsT=tri[:, :], rhs=ld[:, hf * 512:(hf + 1) * 512], start=True, stop=True), r=[tri, ld], w=[ps[1 + hf]])
            for h in range(NH):
                B.pe(lambda e, h=h: e.matmul(ps[3][0:64, h:h + 1], lhsT=ld[:, h * 64:(h + 1) * 64], rhs=ones1[:, 0:1], start=True, stop=True), r=[ld, ones1], w=[ps[3]])
            B.act(lambda e: e.activation(out=WC[:, :], in_=ps[3][0:64, 0:NH], func=AF.Exp), r=[ps[3]], w=[WC])
            y32 = a32
            for hf in range(2):
                sl_ = slice(hf * 512, (hf + 1) * 512)
                cum = ps[1 + hf]
                B.act(lambda e, sl_=sl_, cum=cum: e.activation(out=y32[:, sl_], in_=cum[:, :], func=AF.Exp), r=[cum], w=[y32])
                B.dve(lambda e, sl_=sl_: e.tensor_tensor(out=bfs["rt"][:, sl_], in0=r32[:, sl_], in1=y32[:, sl_], op=ALU.mult), r=[r32, y32], w=[bfs["rt"]])
                B.act(lambda e, sl_=sl_, cum=cum: e.activation(out=y32[:, sl_], in_=cum[:, :], func=AF.Exp, scale=-1.0), r=[cum], w=[y32])
                B.dve(lambda e, sl_=sl_: e.tensor_tensor(out=bfs["kt"][:, sl_], in0=k32[:, sl_], in1=y32[:, sl_], op=ALU.mult), r=[k32, y32], w=[bfs["kt"]])
                B.pool(lambda e, sl_=sl_: e.tensor_tensor(out=bfs["bt"][:, sl_], in0=t1[:, sl_], in1=y32[:, sl_], op=ALU.mult), r=[t1, y32], w=[bfs["bt"]])
                B.dve(lambda e, sl_=sl_, cum=cum: e.tensor_tensor(out=y32[:, sl_], in0=cum[:, :], in1=ld[:, sl_], op=ALU.subtract), r=[cum, ld], w=[y32])
                B.act(lambda e, sl_=sl_: e.activation(out=y32[:, sl_], in_=y32[:, sl_], func=AF.Exp), r=[y32], w=[y32])
                B.dve(lambda e, sl_=sl_: e.scalar_tensor_tensor(out=bfs["at"][:, sl_], in0=kk[:, sl_], scalar=-1.0, in1=y32[:, sl_], op0=ALU.mult, op1=ALU.mult), r=[kk, y32], w=[bfs["at"]])
            B.pool(lambda e: e.tensor_copy(out=bfs["vb"][:], in_=v32[:]), r=[v32], w=[bfs["vb"]])
            for h in range(NH):
                bi = 4 + h % 2
                pv = B.psb(bi)
                for i, nm in enumerate(("at", "rt", "kt", "bt")):
                    B.pe(lambda e, h=h, i=i, nm=nm, pv=pv: e.transpose(pv[0:64, i * 128:(i + 1) * 128], bfs[nm][:, h * 64:(h + 1) * 64], B.identb[:, :]), r=[bfs[nm], B.identb], w=[ps[bi]])
                B.act(lambda e, h=h, pv=pv: e.copy(out=HT[:, h, :, :], in_=pv[0:64, 0:512].rearrange("p (i t) -> p i t", t=128)), r=[ps[bi]], w=[HT])
            def head_steps(h, par):
                bA, bB, bM = (ps[1], ps[2], ps[3]) if par == 0 else (ps[4], ps[5], ps[0])
                AKm, ABm, NX, NT, Zb, Ub = HB[par]
                hs = slice(h * 64, (h + 1) * 64)
                rhsAR = HT[:, h, 0:2, :].rearrange("p i t -> p (i t)")
                B.pe(lambda e: e.matmul(bA[:, 0:256], lhsT=HT[:, h, 2, :], rhs=rhsAR, start=True, stop=True), r=[HT], w=[bA])
                B.pe(lambda e: e.matmul(bB[:, 0:256], lhsT=HT[:, h, 3, :], rhs=rhsAR, start=True, stop=True), r=[HT], w=[bB])
                B.pe(lambda e: e.matmul(bM[:, 0:128], lhsT=HT[:, h, 0, :], rhs=HT[:, h, 3, :], start=True, stop=True), r=[HT], w=[bM])
                yield
                B.dve(lambda e: e.tensor_tensor(out=AKm[:], in0=bA[:, 0:256], in1=m2[:], op=ALU.mult), r=[bA, m2], w=[AKm])
                B.dve(lambda e: e.tensor_tensor(out=ABm[:], in0=bB[:, 0:256], in1=m2[:], op=ALU.mult), r=[bB, m2], w=[ABm])
                B.dve(lambda e: e.tensor_tensor(out=NT[0][:], in0=bM[:, 0:128], in1=sl[:], op=ALU.mult), r=[bM, sl], w=[NT[0]])
                yield
                B.pool(lambda e: e.tensor_copy(out=NX[0][:, 0:128], in_=ABm[:, 0:128]), r=[ABm], w=[NX[0]])
                B.pool(lambda e: e.tensor_tensor(out=NX[0][:, 128:256], in0=ABm[:, 0:128], in1=B.identb[:, :], op=ALU.add), r=[ABm, B.identb], w=[NX[0]])
                yield
                cur = 0
                for lvl in range(7):
                    nx, nt_ = NX[cur], NT[cur]
                    nx2, nt2 = NX[1 - cur], NT[1 - cur]
                    if lvl == 0:
                        B.pe(lambda e: e.matmul(bA[:, 0:128], lhsT=nt_[:, :], rhs=nx[:, 0:128], start=True, stop=True), r=[nt_, nx], w=[bA])
                        B.pe(lambda e: e.matmul(bB[:, 0:128], lhsT=nx[:, 0:128], rhs=nt_[:, :], start=True, stop=True), r=[nt_, nx], w=[bB])
                        yield
                        B.act(lambda e: e.copy(out=nx2[:, 0:128], in_=bA[:, 0:128]), r=[bA], w=[nx2])
                        B.dve(lambda e: e.tensor_copy(out=nx2[:, 128:256], in_=nx[:, 128:256]), r=[nx], w=[nx2])
                        B.act(lambda e: e.copy(out=nt2[:, :], in_=bB[:, 0:128]), r=[bB], w=[nt2])
                    elif lvl < 6:
                        B.pe(lambda e: e.matmul(bA[:, 0:256], lhsT=nt_[:, :], rhs=nx[:, :], start=True, stop=True), r=[nt_, nx], w=[bA])
                        B.pe(lambda e: e.matmul(bB[:, 0:128], lhsT=nx[:, 0:128], rhs=nt_[:, :], start=True, stop=True), r=[nt_, nx], w=[bB])
                        yield
                        B.act(lambda e: e.copy(out=nx2[:, 0:128], in_=bA[:, 0:128]), r=[bA], w=[nx2])
                        B.dve(lambda e: e.tensor_tensor(out=nx2[:, 128:256], in0=bA[:, 128:256], in1=nx[:, 128:256], op=ALU.add), r=[bA, nx], w=[nx2])
                        B.act(lambda e: e.copy(out=nt2[:, :], in_=bB[:, 0:128]), r=[bB], w=[nt2])
                    else:
                        B.pe(lambda e: e.matmul(bA[:, 0:128], lhsT=nt_[:, :], rhs=nx[:, 128:256], start=True, stop=True), r=[nt_, nx], w=[bA])
                        yield
                        B.dve(lambda e: e.tensor_tensor(out=nx2[:, 128:256], in0=bA[:, 0:128], in1=nx[:, 128:256], op=ALU.add), r=[bA, nx], w=[nx2])
                    cur = 1 - cur
                    yield
                XT = NX[cur]
                B.pe(lambda e: e.matmul(bM[:, 0:64], lhsT=HT[:, h, 0, :], rhs=STb[:, h, :], start=True, stop=False), r=[HT, STb], w=[bM])
                B.pe(lambda e: e.matmul(bM[:, 0:64], lhsT=AKm[:, 0:128], rhs=bfs["vb"][:, hs], start=False, stop=True), r=[AKm, bfs["vb"]], w=[bM])
                yield
                B.act(lambda e: e.copy(out=Zb[:, :], in_=bM[:, 0:64]), r=[bM], w=[Zb])
                yield
                B.pe(lambda e: e.matmul(bM[:, 64:128], lhsT=XT[:, 128:256], rhs=Zb[:, :], start=True, stop=True), r=[XT, Zb], w=[bM])
                yield
                B.act(lambda e: e.copy(out=Ub[:, :], in_=bM[:, 64:128]), r=[bM], w=[Ub])
                yield
                yb = ps[6 + (h // 8) % 2]
                yc = slice((h % 8) * 64, (h % 8 + 1) * 64)
                B.pe(lambda e: e.matmul(yb[:, yc], lhsT=HT[:, h, 1, :], rhs=STb[:, h, :], start=True, stop=False), r=[HT, STb], w=[yb])
                B.pe(lambda e: e.matmul(yb[:, yc], lhsT=ABm[:, 128:256], rhs=Ub[:, :], start=False, stop=False), r=[ABm, Ub], w=[yb])
                B.pe(lambda e: e.matmul(yb[:, yc], lhsT=AKm[:, 128:256], rhs=bfs["vb"][:, hs], start=False, stop=True), r=[AKm, bfs["vb"]], w=[yb])
                B.pe(lambda e: e.matmul(bM[0:64, 128:192], lhsT=bfs["bt"][:, hs], rhs=Ub[:, :], start=True, stop=False), r=[bfs["bt"], Ub], w=[bM])
                B.pe(lambda e: e.matmul(bM[0:64, 128:192], lhsT=bfs["kt"][:, hs], rhs=bfs["vb"][:, hs], start=False, stop=True), r=[bfs["kt"], bfs["vb"]], w=[bM])
                yield
                B.dve(lambda e: e.tensor_scalar(out=ST[:, h, :], in0=ST[:, h, :], scalar1=WC[:, h:h + 1], scalar2=None, op0=ALU.mult), r=[ST, WC], w=[ST])
                B.dve(lambda e: e.scalar_tensor_tensor(out=ST[:, h, :], in0=bM[0:64, 128:192], scalar=WC[:, h:h + 1], in1=ST[:, h, :], op0=ALU.mult, op1=ALU.add), r=[bM, WC, ST], w=[ST])
                B.act(lambda e: e.copy(out=STb[:, h, :], in_=ST[:, h, :]), r=[ST], w=[STb])
                yield

            for h0 in range(0, NH, 2):
                gens = [head_steps(h0, 0), head_steps(h0 + 1, 1)]
                alive = [True, True]
                while any(alive):
                    for gi, g_ in enumerate(gens):
                        if alive[gi]:
                            try:
                                next(g_)
                            except StopIteration:
                                alive[gi] = False
                if h0 % 8 == 6:
                    yb = ps[6 + (h0 // 8) % 2]
                    B.act(lambda e, yb=yb, h0=h0: e.copy(out=y32[:, (h0 - 6) * 64:(h0 + 2) * 64], in_=yb[:, :]), r=[yb], w=[y32])
            post(j, y32)
        so = T(HT.t[:].rearrange("p h i t -> p (h i t)").bitcast(F32)[:, 0:NH * 64].rearrange("p (h k) -> p h k", k=64), "so")
        so.wr, so.rd = HT.wr, HT.rd
        for h0 in range(0, NH, 8):
            bank = ps[1 + (h0 // 8) % 2]
            for i in range(8):
                B.pe(lambda e, h0=h0, i=i, bank=bank: e.transpose(bank[0:64, i * 64:(i + 1) * 64], ST[:, h0 + i, :], B.ident[0:64, 0:64]), r=[ST, B.ident], w=[bank])
            B.act(lambda e, h0=h0, bank=bank: e.copy(out=so[:, h0:h0 + 8, :], in_=bank[0:64, :].rearrange("p (h k) -> p h k", k=64)), r=[bank], w=[so])
        B.dma("sp", d["wkv_p"].t[m].rearrange("h v k -> v h k"), so[:, :, :], r=[so], w=[d["wkv_p"]])
        B.sy.barrier()


def rwkv_sample_rec(B, m, RWd, YSd):
    cfg, d, ps = B.cfg, B.d, B.ps
    nsq, srows = cfg.nsq, cfg.srows
    P = nsq * 8
    VS = 8
    with ExitStack() as s3:
        vec = [B.sb("vec", [P, DEC_SEQ, 128], F32, s3) for _ in range(6)]
        for i in range(6):
            for s_ in range(nsq):
                B.dma("sp", vec[i][s_ * 8:(s_ + 1) * 8, :, :], RWd.t[i, s_ * DEC_SEQ:(s_ + 1) * DEC_SEQ, :].rearrange("t (g c) -> g t c", c=128), r=[RWd], w=[vec[i]])
        Yall = B.sb("Yall", [P, DEC_SEQ, 128], F32, s3)
        S = B.sb("S", [P, VS, 64], F32, s3)
        tmp = B.sb("tmp", [P, VS, 64], F32, s3)
        sa = B.sb("sa", [P, VS], F32, s3)
        wkv_in = d["state_rwkv_wkv"].t[m].rearrange("s (g h2) v k -> (s g) h2 v k", h2=2)
        wkv_out = d["wkv_s"].t[m].rearrange("s (g h2) v k -> (s g) h2 v k", h2=2)
        n = 0
        for h2 in range(2):
            kvec = lambda i, t: vec[i][:, t, h2 * 64:(h2 + 1) * 64].unsqueeze(1).broadcast_to([P, VS, 64])
            for v0 in range(0, 64, VS):
                B.dma("sp", S[:, :, :], wkv_in[:, h2, v0:v0 + VS, :], w=[S])
                for t in range(DEC_SEQ):
                    vv = vec[3][:, t, h2 * 64 + v0:h2 * 64 + v0 + VS]
                    e1, e2 = ("dve", "pool") if n % 2 == 0 else ("pool", "dve")
                    n += 1
                    B.dve(lambda e: e.tensor_tensor(out=tmp[:], in0=S[:], in1=kvec(4, t), op=ALU.mult), r=[S, vec[4]], w=[tmp])
                    B.dve(lambda e: e.tensor_reduce(out=sa[:, :], in_=tmp[:], axis=AX.X, op=ALU.add), r=[tmp], w=[sa])
                    B.dve(lambda e: e.tensor_tensor(out=S[:], in0=S[:], in1=kvec(1, t), op=ALU.mult), r=[S, vec[1]], w=[S])
                    B.dve(lambda e: e.tensor_tensor(out=tmp[:], in0=sa[:, :].unsqueeze(2).broadcast_to([P, VS, 64]), in1=kvec(5, t), op=ALU.mult), r=[sa, vec[5]], w=[tmp])
                    B.dve(lambda e: e.tensor_tensor(out=S[:], in0=S[:], in1=tmp[:], op=ALU.add), r=[S, tmp], w=[S])
                    B.dve(lambda e, vv=vv: e.tensor_tensor(out=tmp[:], in0=vv.unsqueeze(2).broadcast_to([P, VS, 64]), in1=kvec(2, t), op=ALU.mult), r=[vec[3], vec[2]], w=[tmp])
                    B.dve(lambda e: e.tensor_tensor(out=S[:], in0=S[:], in1=tmp[:], op=ALU.add), r=[S, tmp], w=[S])
                    B.dve(lambda e: e.tensor_tensor(out=tmp[:], in0=S[:], in1=kvec(0, t), op=ALU.mult), r=[S, vec[0]], w=[tmp])
                    B.dve(lambda e, t=t: e.tensor_reduce(out=Yall[:, t, h2 * 64 + v0:h2 * 64 + v0 + VS], in_=tmp[:], axis=AX.X, op=ALU.add), r=[tmp], w=[Yall])
                B.dma("sp", wkv_out[:, h2, v0:v0 + VS, :], S[:, :, :], r=[S], w=[d["wkv_s"]])
        for s_ in range(nsq):
            B.dma("sp", YSd.t[s_ * DEC_SEQ:(s_ + 1) * DEC_SEQ, :].rearrange("t (g c) -> g t c", c=128), Yall[s_ * 8:(s_ + 1) * 8, :, :], r=[Yall], w=[YSd])
        B.sy.barrier()
```

```python
from contextlib import ExitStack
import math
import numpy as np
import concourse.bass as bass
import concourse.mybir as mybir
from concourse.bass_utils import run_bass_kernel_spmd

F32 = mybir.dt.float32
BF16 = mybir.dt.bfloat16
I32 = mybir.dt.int32
AF = mybir.ActivationFunctionType
ALU = mybir.AluOpType
AX = mybir.AxisListType


DEBUG_BARRIER = False
DEBUG_TILES = None


class T:
    __slots__ = ("t", "wr", "rd", "name")

    def __init__(self, t, name=""):
        self.t = t
        self.wr = {}
        self.rd = {}
        self.name = name

    def __getitem__(self, k):
        return self.t[k]


class Sync:
    NDMA = 24
    MAXFLY = 16

    def __init__(self, nc, es):
        self.nc = nc
        self.es = es
        self.eng = {"pe": nc.tensor, "act": nc.scalar, "dve": nc.vector, "pool": nc.gpsimd, "sp": nc.sync}
        self.sems = {}
        self.cnt = {}
        self.waited = {}
        for e in ("pe", "act", "dve", "pool"):
            self._mk("c_" + e)
        self.dma_rr = {}
        for q in ("sp", "pool", "act"):
            self.dma_rr[q] = 0
            for i in range(self.NDMA):
                self._mk("d_%s%d" % (q, i))
        self.n_ins = 0
        self.dma_hist = {}

    def _mk(self, name):
        self.sems[name] = self.es.enter_context(self.nc.semaphore(name))
        self.cnt[name] = 0

    def _wait(self, en, evs):
        own = "c_" + en
        e = self.eng[en]
        for s, v in evs.items():
            if s == own and en == "pe":
                continue
            if self.waited.get((en, s), 0) >= v:
                continue
            e.wait_ge(self.sems[s], v)
            self.waited[(en, s)] = v

    def op(self, en, fn, reads=(), writes=(), dma=False):
        evs = {}
        for t in reads:
            for s, v in t.wr.items():
                if evs.get(s, 0) < v:
                    evs[s] = v
        for t in writes:
            for d in (t.wr, t.rd):
                for s, v in d.items():
                    if evs.get(s, 0) < v:
                        evs[s] = v
        if dma:
            e = self.eng[en]
            for s, v in evs.items():
                if self.waited.get((en, s), 0) >= v:
                    continue
                e.wait_ge(self.sems[s], v)
                self.waited[(en, s)] = v
            hist = self.dma_hist.setdefault(en, [])
            if len(hist) >= self.MAXFLY:
                s_old, v_old = hist[-self.MAXFLY]
                if self.waited.get((en, s_old), 0) < v_old:
                    e.wait_ge(self.sems[s_old], v_old)
                    self.waited[(en, s_old)] = v_old
            i = self.dma_rr[en]
            self.dma_rr[en] = (i + 1) % self.NDMA
            sname = "d_%s%d" % (en, i)
            inc = 16
        else:
            self._wait(en, evs)
            sname = "c_" + en
            inc = 1
        ins = fn(self.eng[en])
        self.cnt[sname] += inc
        ins.then_inc(self.sems[sname], inc)
        v = self.cnt[sname]
        if dma:
            self.dma_hist[en].append((sname, v))
            if len(self.dma_hist[en]) > 64:
                del self.dma_hist[en][:32]
        for t in writes:
            t.wr[sname] = v
        for t in reads:
            t.rd[sname] = v
        self.n_ins += 1
        return ins

    def barrier(self):
        snap = dict(self.cnt)
        for en in ("pe", "act", "dve", "pool", "sp"):
            self._wait(en, snap)

    def finish(self):
        self.barrier()


D = 1024
DFF = 2816
NH = 16
QR = 768
KVR = 256
NOPE = 64
ROPE = 32
QK = 96
VD = 64
LN_EPS = 1e-5
RMS_EPS = 1e-6
GN_EPS = 64e-5
N_META = 16
PAGE = 128
DEC_SEQ = 4


class Cfg:
    def __init__(self, ntf=64, nsq=16, npages=64, npool=10240, depth=4, n_cores=8, n_batch=2):
        self.ntf = ntf
        self.L = 128 * ntf + N_META
        self.seq = 128 * ntf
        self.ntp = ntf + 1
        self.nsq = nsq
        self.srows = nsq * DEC_SEQ
        self.nt = self.ntp + 1
        self.npages = npages
        self.past = npages * PAGE
        self.npool = npool
        self.depth = depth
        self.alpha = (2 * depth) ** 0.25
        self.n_mla = (depth + 2) // 3
        self.n_rwkv = (depth + 1) // 3
        self.n_s5 = depth // 3
        self.n_cores = n_cores
        self.n_batch = n_batch

    def rows(self, j):
        if j < self.ntf:
            return 128
        if j == self.ntf:
            return N_META
        return self.srows


class Builder:
    def __init__(self, nc, cfg):
        self.nc = nc
        self.cfg = cfg
        self.es = ExitStack()
        self.sy = Sync(nc, self.es)
        self.din = {}
        self.dout = {}
        self.uid = 0

    def sb(self, name, shape, dt=F32, st=None):
        self.uid += 1
        return T((st or self.es).enter_context(self.nc.sbuf_tensor("%s_%d" % (name, self.uid), list(shape), dt)), name)

    def dram_in(self, name, shape, dt=F32):
        t = T(self.nc.dram_tensor(name, list(shape), dt, kind="ExternalInput").ap(), name)
        self.din[name] = t
        return t

    def dram_out(self, name, shape, dt=F32):
        t = T(self.nc.dram_tensor(name, list(shape), dt, kind="ExternalOutput").ap(), name)
        self.dout[name] = t
        return t

    def dram_scr(self, name, shape, dt=F32):
        return T(self.nc.dram_tensor(name, list(shape), dt, kind="Internal").ap(), name)

    def pe(self, fn, r=(), w=()):
        return self.sy.op("pe", fn, r, w)

    def act(self, fn, r=(), w=()):
        return self.sy.op("act", fn, r, w)

    def dve(self, fn, r=(), w=()):
        return self.sy.op("dve", fn, r, w)

    def pool(self, fn, r=(), w=()):
        return self.sy.op("pool", fn, r, w)

    def dma(self, q, out, in_, r=(), w=()):
        return self.sy.op(q, lambda e: e.dma_start(out=out, in_=in_), r, w, dma=True)

    def setup_common(self):
        nc = self.nc
        self.ps = []
        for i in range(8):
            self.ps.append(T(self.es.enter_context(nc.psum_tensor("psb%d" % i, [128, 512], F32)), "ps%d" % i))
        self.ident = self.sb("ident", [128, 128], F32)
        self.identb = self.sb("identb", [128, 128], BF16)
        self.pool(lambda e: e.memset(self.ident[:], 0.0), w=[self.ident])
        self.pool(lambda e: e.affine_select(out=self.ident[:], in_=self.ident[:], pattern=[[-1, 128]], compare_op=ALU.not_equal,
                                            fill=1.0, base=0, channel_multiplier=1), r=[self.ident], w=[self.ident])
        self.dve(lambda e: e.tensor_copy(out=self.identb[:], in_=self.ident[:]), r=[self.ident], w=[self.identb])
        self.ones = self.sb("ones", [128, 128], F32)
        self.pool(lambda e: e.memset(self.ones[:], 1.0), w=[self.ones])
        self.eps = self.sb("eps", [128, 4], F32)
        for i, v in enumerate((LN_EPS, RMS_EPS, GN_EPS, 0.0)):
            self.dve(lambda e, i=i, v=v: e.memset(self.eps[:, i:i + 1], v), w=[self.eps])

    def psb(self, i):
        return self.ps[i].t[:].bitcast(BF16)

    def load_w(self, dst, src, st_q="pool"):
        K = src.shape[0]
        if K <= 128:
            self.dma(st_q, dst[0:K, 0, :], src, w=[dst])
        else:
            v = src.rearrange("(kc p) n -> p kc n", p=128)
            for kc in range(K // 128):
                self.dma(st_q, dst[:, kc, :], v[:, kc, :], w=[dst])

    def load_bcast(self, dst, src_row):
        n = src_row.shape[-1]
        self.dma("sp", dst[:, 0:n], src_row.rearrange("(o n) -> o n", o=1).broadcast_to([128, n]), w=[dst])

    def load_T(self, dst, dst_ap, src_rows, n):
        with ExitStack() as s1:
            tmp = self.sb("ldT", [n, 128], F32, s1)
            self.dma("sp", tmp[:, :], src_rows, w=[tmp])
            self.pe(lambda e: e.transpose(self.ps[0][:, 0:n], tmp[:, :], self.ident[0:n, 0:n]), r=[tmp, self.ident], w=[self.ps[0]])
            self.dve(lambda e: e.tensor_copy(out=dst_ap, in_=self.ps[0][:, 0:n]), r=[self.ps[0]], w=[dst])
            self.sy.barrier()

    def transpose_to(self, src, nch, dst, dst_ap, bank, dt=BF16, rows=128, evac="act", src_off=0, cw=128):
        per = (1024 if dt == BF16 else 512)
        assert nch * rows <= per
        if dt == BF16:
            pv = self.psb(self.ps.index(bank))
            idt = self.identb
        else:
            pv = bank.t[:]
            idt = self.ident
        for c in range(nch):
            self.pe(lambda e, c=c: e.transpose(pv[0:cw, c * rows:(c + 1) * rows], src[0:rows, src_off + c * cw: src_off + (c + 1) * cw],
                                               idt[0:rows, 0:rows]), r=[src, idt], w=[bank])
        i = pv[0:cw, 0:nch * rows].rearrange("p (c r) -> p c r", c=nch)
        if evac == "act":
            self.act(lambda e: e.copy(out=dst_ap, in_=i), r=[bank], w=[dst])
        else:
            self.dve(lambda e: e.tensor_copy(out=dst_ap, in_=i), r=[bank], w=[dst])

    def linear(self, bank, xT, W, n0, n1, nkc, M=128, kp=128):
        for kc in range(nkc):
            self.pe(lambda e, kc=kc: e.matmul(bank[0:M, 0:n1 - n0], lhsT=xT[0:kp, kc, 0:M], rhs=W[0:kp, kc, n0:n1],
                                              start=(kc == 0), stop=(kc == nkc - 1)), r=[xT, W], w=[bank])

    def rstd_from_ss(self, ss, n, eps_col, out):
        self.act(lambda e: e.activation(out=out[:, 0:1], in_=ss[:, 0:1], func=AF.Sqrt, bias=self.eps[:, eps_col:eps_col + 1], scale=1.0 / n),
                 r=[ss, self.eps], w=[out])
        self.dve(lambda e: e.reciprocal(out=out[:, 0:1], in_=out[:, 0:1]), r=[out], w=[out])

    def resid_ln(self, x, banks, c, G, Bt, out, tmp):
        alpha = self.cfg.alpha
        xa, y, junk, st = tmp["xa"], tmp["y"], tmp["junk"], tmp["st"]
        self.act(lambda e: e.activation(out=xa[:], in_=x[:], func=AF.Copy, scale=alpha), r=[x], w=[xa])
        for h in range(2):
            self.dve(lambda e, h=h: e.scalar_tensor_tensor(out=y[:, h * 512:(h + 1) * 512], in0=banks[h][:, :], scalar=float(c),
                                                         in1=xa[:, h * 512:(h + 1) * 512], op0=ALU.mult, op1=ALU.add,
                                                         accum_out=st[:, h:h + 1]), r=[banks[h], xa], w=[y, st])
        self.ln_core(y, G, Bt, out, junk, st)

    def ln_core(self, y, G, Bt, out, junk, st):
        self.dve(lambda e: e.tensor_scalar(out=st[:, 2:3], in0=st[:, 0:1], scalar1=st[:, 1:2], scalar2=-1.0 / D, op0=ALU.add, op1=ALU.mult),
                 r=[st], w=[st])
        self.act(lambda e: e.activation(out=junk[:], in_=y[:], func=AF.Square, bias=st[:, 2:3], scale=1.0, accum_out=st[:, 3:4]),
                 r=[y, st], w=[junk, st])
        self.rstd_from_ss(_col(st, 3), D, 0, _col(st, 4))
        self.dve(lambda e: e.tensor_scalar(out=junk[:], in0=y[:], scalar1=st[:, 2:3], scalar2=st[:, 4:5], op0=ALU.add, op1=ALU.mult),
                 r=[y, st], w=[junk])
        self.pool(lambda e: e.tensor_tensor(out=junk[:], in0=junk[:], in1=G[:], op=ALU.mult), r=[junk, G], w=[junk])
        self.dve(lambda e: e.tensor_tensor(out=out[:], in0=junk[:], in1=Bt[:], op=ALU.add), r=[junk, Bt], w=[out])


class _col:
    def __init__(self, t, c):
        self._t = t
        self.c = c

    @property
    def wr(self):
        return self._t.wr

    @property
    def rd(self):
        return self._t.rd

    def __getitem__(self, k):
        return self._t.t[:, self.c:self.c + 1]


def x0_src(cfg, d):
    def f(j):
        if j == 0:
            return [((0, N_META), d["meta_tokens"][:, :], d["meta_tokens"]),
                    ((N_META, 128), d["x_prompt"][0:128 - N_META, :], d["x_prompt"])]
        if j < cfg.ntp:
            r = cfg.rows(j)
            return [((0, r), d["x_prompt"][128 * j - N_META:128 * j - N_META + r, :], d["x_prompt"])]
        return [((0, cfg.srows), d["x_sample"][:, :], d["x_sample"])]
    return f


def y_dst(cfg, d):
    def f(j):
        if j == 0:
            return [((N_META, 128), d["y_prompt"][0:128 - N_META, :], d["y_prompt"])]
        if j < cfg.ntp:
            r = cfg.rows(j)
            return [((0, r), d["y_prompt"][128 * j - N_META:128 * j - N_META + r, :], d["y_prompt"])]
        return [((0, cfg.srows), d["y_sample"][:, :], d["y_sample"])]
    return f


def scr_map(cfg, X, Xt):
    def f(j):
        r = cfg.rows(j)
        return [((0, r), X[128 * j:128 * j + r, :], Xt[j])]
    f.X = X
    f.Xt = Xt
    return f


def stage_ffn(B, li, half, src, dst, tiles=None):
    cfg, d = B.cfg, B.din
    with ExitStack() as st:
        w1 = B.sb("w1", [128, 8, DFF], BF16, st)
        w3 = B.sb("w3", [128, 8, DFF], BF16, st)
        w2 = B.sb("w2", [128, 22, D], BF16, st)
        B.load_w(w1, d["ffn_w1"].t[li, half])
        B.load_w(w3, d["ffn_w3"].t[li, half])
        B.load_w(w2, d["ffn_w2"].t[li, half])
        G = B.sb("G", [128, D], F32, st)
        Bt = B.sb("Bt", [128, D], F32, st)
        lni = 0 if half == 0 else 2
        B.load_bcast(G, d["ln_g"].t[li, lni])
        B.load_bcast(Bt, d["ln_b"].t[li, lni])
        xs = [B.sb("xs", [128, D], F32, st) for _ in range(3)]
        xb = B.sb("xb", [128, D], BF16, st)
        xT = B.sb("xT", [128, 8, 128], BF16, st)
        sg = [B.sb("sg", [128, 512], F32, st) for _ in range(2)]
        g = B.sb("g", [128, DFF], BF16, st)
        gT = B.sb("gT", [128, 22, 128], BF16, st)
        tmp = dict(xa=B.sb("xa", [128, D], F32, st), y=B.sb("y", [128, D], F32, st), junk=B.sb("junk", [128, D], F32, st),
                   st=B.sb("st", [128, 8], F32, st))
        xo = [B.sb("xo", [128, D], F32, st) for _ in range(2)]
        for t in xs:
            B.dve(lambda e, t=t: e.memset(t[:], 0.0), w=[t])
        ps = B.ps
        tl = list(range(cfg.nt)) if tiles is None else tiles
        if DEBUG_TILES is not None:
            tl = DEBUG_TILES

        def load(j, buf):
            for (r0, r1), ap, tt in src(j):
                B.dma("sp", buf[r0:r1, :], ap, r=[tt], w=[buf])

        load(tl[0], xs[0])
        for k, j in enumerate(tl):
            x = xs[k % 3]
            if k + 1 < len(tl):
                load(tl[k + 1], xs[(k + 1) % 3])
            B.pool(lambda e: e.tensor_copy(out=xb[:], in_=x[:]), r=[x], w=[xb])
            B.transpose_to(xb, 8, xT, xT[:, :, :], ps[0])
            for gi in range(6):
                n0 = gi * 512
                n1 = min(DFF, n0 + 512)
                b1, b3 = ps[1 + 2 * (gi % 2)], ps[2 + 2 * (gi % 2)]
                B.linear(b1, xT, w1, n0, n1, 8)
                B.linear(b3, xT, w3, n0, n1, 8)
                s_ = sg[gi % 2]
                B.act(lambda e, s_=s_, b1=b1, n=n1 - n0: e.activation(out=s_[:, 0:n], in_=b1[:, 0:n], func=AF.Silu), r=[b1], w=[s_])
                B.dve(lambda e, s_=s_, b3=b3, n0=n0, n1=n1: e.tensor_tensor(out=g[:, n0:n1], in0=s_[:, 0:n1 - n0], in1=b3[:, 0:n1 - n0], op=ALU.mult),
                      r=[s_, b3], w=[g])
            for c0, nch in ((0, 8), (8, 8), (16, 6)):
                B.transpose_to(g, nch, gT, gT[:, c0:c0 + nch, :], ps[5], src_off=c0 * 128, evac="act" if c0 != 8 else "dve")
            for h in range(2):
                for fc in range(22):
                    B.pe(lambda e, h=h, fc=fc: e.matmul(ps[6 + h][:, :], lhsT=gT[:, fc, :], rhs=w2[:, fc, h * 512:(h + 1) * 512],
                                                       start=(fc == 0), stop=(fc == 21)), r=[gT, w2], w=[ps[6 + h]])
            o = xo[k % 2]
            B.resid_ln(x, [ps[6], ps[7]], 0.5, G, Bt, o, tmp)
            for (r0, r1), ap, tt in dst(j):
                B.dma("sp", ap, o[r0:r1, :], r=[o], w=[tt])
            if DEBUG_BARRIER:
                B.sy.barrier()
        B.sy.barrier()


WEIGHT_SHAPES = dict(
    meta_tokens=(N_META, D), ln_g=("depth", 3, D), ln_b=("depth", 3, D),
    ffn_w1=("depth", 2, D, DFF), ffn_w3=("depth", 2, D, DFF), ffn_w2=("depth", 2, DFF, D),
    mla_w_dq=("n_mla", D, QR), mla_q_norm=("n_mla", QR), mla_w_uq=("n_mla", QR, NH * QK),
    mla_w_dkv=("n_mla", D, KVR + ROPE), mla_kv_norm=("n_mla", KVR), mla_w_uk=("n_mla", KVR, NH * NOPE),
    mla_w_uv=("n_mla", KVR, NH * VD), mla_w_o=("n_mla", NH * VD, D),
    rw_mu=("n_rwkv", 6, D), rw_wr=("n_rwkv", D, D), rw_wk=("n_rwkv", D, D), rw_wv=("n_rwkv", D, D),
    rw_w0=("n_rwkv", D), rw_w1=("n_rwkv", D, 64), rw_w2=("n_rwkv", 64, D), rw_a0=("n_rwkv", D),
    rw_a1=("n_rwkv", D, 64), rw_a2=("n_rwkv", 64, D), rw_g1=("n_rwkv", D, 128), rw_g2=("n_rwkv", 128, D),
    rw_k_k=("n_rwkv", D), rw_k_a=("n_rwkv", D), rw_r_k=("n_rwkv", D), rw_lnx_g=("n_rwkv", D), rw_lnx_b=("n_rwkv", D),
    rw_wo=("n_rwkv", D, D),
    s5_lam_re=("n_s5", 64, 64), s5_lam_im=("n_s5", 64, 64), s5_log_dt=("n_s5", 64),
    s5_b_re=("n_s5", 64, 64, 16), s5_b_im=("n_s5", 64, 64, 16), s5_c_re=("n_s5", 64, 16, 64), s5_c_im=("n_s5", 64, 16, 64),
    s5_d=("n_s5", D), s5_wv=("n_s5", D, D), s5_wg=("n_s5", D, D),
)


def _shape(cfg, shp):
    return tuple(getattr(cfg, s) if isinstance(s, str) else s for s in shp)


def io_shapes(cfg):
    ins = dict(
        x_prompt=(cfg.seq, D), x_sample=(cfg.srows, D),
        cache_mla_latent=(cfg.n_mla * cfg.npool * PAGE, KVR), cache_mla_krope=(cfg.n_mla * cfg.npool * PAGE, ROPE),
        state_rwkv_wkv=(cfg.n_rwkv, cfg.nsq, NH, 64, 64), state_rwkv_shift=(cfg.n_rwkv, cfg.nsq, D),
        state_s5_re=(cfg.n_s5, cfg.nsq, 64, 64), state_s5_im=(cfg.n_s5, cfg.nsq, 64, 64),
        page_table=(cfg.nsq, cfg.npages),
    )
    for k, v in WEIGHT_SHAPES.items():
        ins[k] = _shape(cfg, v)
    outs = dict(
        y_prompt=(cfg.seq, D), y_sample=(cfg.srows, D),
        lat_p=(cfg.n_mla, cfg.L, KVR), kr_p=(cfg.n_mla, cfg.L, ROPE), lat_s=(cfg.n_mla, cfg.srows, KVR), kr_s=(cfg.n_mla, cfg.srows, ROPE),
        wkv_p=(cfg.n_rwkv, NH, 64, 64), sh_p=(cfg.n_rwkv, D), wkv_s=(cfg.n_rwkv, cfg.nsq, NH, 64, 64), sh_s=(cfg.n_rwkv, cfg.nsq, D),
        re_p=(cfg.n_s5, 64, 64), im_p=(cfg.n_s5, 64, 64), re_s=(cfg.n_s5, cfg.nsq, 64, 64), im_s=(cfg.n_s5, cfg.nsq, 64, 64),
    )
    return ins, outs


def build_program(cfg, nstages=None):
    nc = bass.Bass("TRN2", target_bir_lowering=False)
    B = Builder(nc, cfg)
    ins, outs = io_shapes(cfg)
    for k, shp in ins.items():
        B.dram_in(k, shp, I32 if k == "page_table" else F32)
    for k, shp in outs.items():
        B.dram_out(k, shp, F32)
    d = dict(B.din)
    d.update(B.dout)
    B.d = d
    B.setup_common()
    XA = B.dram_scr("XA", [cfg.nt * 128, D])
    XB = B.dram_scr("XB", [cfg.nt * 128, D])
    Xs = [XA, XB]
    Xt = [[T(None, "XA%d" % j) for j in range(cfg.nt)], [T(None, "XB%d" % j) for j in range(cfg.nt)]]
    B.X = Xs
    B.Xt = Xt
    total = 3 * cfg.depth
    if nstages is None:
        nstages = total
    for s in range(nstages):
        li, kind = s // 3, s % 3
        src = x0_src(cfg, d) if s == 0 else scr_map(cfg, Xs[(s - 1) % 2].t, Xt[(s - 1) % 2])
        dst = y_dst(cfg, d) if s == nstages - 1 else scr_map(cfg, Xs[s % 2].t, Xt[s % 2])
        if kind == 0:
            stage_ffn(B, li, 0, src, dst)
        elif kind == 2:
            stage_ffn(B, li, 1, src, dst)
        else:
            mk, m = li % 3, li // 3
            if mk == 0:
                stage_mla(B, li, m, src, dst)
            elif mk == 1:
                stage_rwkv(B, li, m, src, dst)
            else:
                stage_s5(B, li, m, src, dst)
    B.sy.finish()
    B.es.close()
    return nc, B


def shard_inputs(cfg, inputs, c):
    b = c % cfg.n_batch
    s0 = c * cfg.nsq
    m = {}
    m["x_prompt"] = np.ascontiguousarray(inputs["x_prompt"][b])
    m["x_sample"] = np.ascontiguousarray(inputs["x_sample"][s0:s0 + cfg.nsq]).reshape(cfg.srows, D)
    m["cache_mla_latent"] = inputs["cache_mla_latent"].reshape(-1, KVR)
    m["cache_mla_krope"] = inputs["cache_mla_krope"].reshape(-1, ROPE)
    m["state_rwkv_wkv"] = np.ascontiguousarray(inputs["state_rwkv_wkv"][:, s0:s0 + cfg.nsq])
    m["state_rwkv_shift"] = np.ascontiguousarray(inputs["state_rwkv_shift"][:, s0:s0 + cfg.nsq])
    m["state_s5_re"] = np.ascontiguousarray(inputs["state_s5_re"][:, s0:s0 + cfg.nsq])
    m["state_s5_im"] = np.ascontiguousarray(inputs["state_s5_im"][:, s0:s0 + cfg.nsq])
    m["page_table"] = np.ascontiguousarray(inputs["page_table"][s0:s0 + cfg.nsq]).astype(np.int32)
    for k, shp in WEIGHT_SHAPES.items():
        m[k] = np.ascontiguousarray(inputs[k]).reshape(_shape(cfg, shp))
    return m


def assemble(cfg, res):
    nb, nco = cfg.n_batch, cfg.n_cores
    def pb(name):
        return [res[b][name] for b in range(nb)]
    def sc(name):
        return [res[c][name] for c in range(nco)]
    y_p = np.stack(pb("y_prompt"), 0)
    y_s = np.concatenate(sc("y_sample"), 0).reshape(nco * cfg.nsq, DEC_SEQ, D)
    lat_p = np.stack(pb("lat_p"), 1)
    kr_p = np.stack(pb("kr_p"), 1)
    lat_s = np.concatenate([r.reshape(cfg.n_mla, cfg.nsq, DEC_SEQ, KVR) for r in sc("lat_s")], 1)
    kr_s = np.concatenate([r.reshape(cfg.n_mla, cfg.nsq, DEC_SEQ, ROPE) for r in sc("kr_s")], 1)
    wkv_p = np.stack(pb("wkv_p"), 1)
    sh_p = np.stack(pb("sh_p"), 1)
    wkv_s = np.concatenate(sc("wkv_s"), 1)
    sh_s = np.concatenate(sc("sh_s"), 1)
    re_p = np.stack(pb("re_p"), 1)
    im_p = np.stack(pb("im_p"), 1)
    re_s = np.concatenate(sc("re_s"), 1)
    im_s = np.concatenate(sc("im_s"), 1)
    return (y_p, y_s, lat_p, kr_p, lat_s, kr_s, wkv_p, sh_p, wkv_s, sh_s, re_p, im_p, re_s, im_s)


def run(cfg, inputs, nstages=None, trace=False):
    nc, B = build_program(cfg, nstages)
    in_maps = [shard_inputs(cfg, inputs, c) for c in range(cfg.n_cores)]
    res = run_bass_kernel_spmd(nc, in_maps, core_ids=list(range(cfg.n_cores)), trace=trace)
    return assemble(cfg, res.results), res


def kernel(**inputs):
    cfg = Cfg()
    out, _ = run(cfg, inputs)
    return tuple(np.ascontiguousarray(o, dtype=np.float32) for o in out)


def range_reduce_sin(B, ang, out, n, tmpf, tmpi):
    C1 = 6.28125
    C2 = 2 * math.pi - C1
    B.dve(lambda e: e.tensor_scalar(out=tmpf[:, 0:n], in0=ang[:, 0:n], scalar1=1.0 / (2 * math.pi), scalar2=None, op0=ALU.mult), r=[ang], w=[tmpf])
    B.dve(lambda e: e.tensor_copy(out=tmpi[:, 0:n], in_=tmpf[:, 0:n]), r=[tmpf], w=[tmpi])
    B.dve(lambda e: e.tensor_copy(out=tmpf[:, 0:n], in_=tmpi[:, 0:n]), r=[tmpi], w=[tmpf])
    B.dve(lambda e: e.scalar_tensor_tensor(out=out[:, 0:n], in0=tmpf[:, 0:n], scalar=-C1, in1=ang[:, 0:n], op0=ALU.mult, op1=ALU.add), r=[tmpf, ang], w=[out])
    B.dve(lambda e: e.scalar_tensor_tensor(out=out[:, 0:n], in0=tmpf[:, 0:n], scalar=-C2, in1=out[:, 0:n], op0=ALU.mult, op1=ALU.add), r=[tmpf, out], w=[out])
    B.dve(lambda e: e.tensor_scalar(out=out[:, 0:n], in0=out[:, 0:n], scalar1=math.pi, scalar2=-math.pi, op0=ALU.min, op1=ALU.max), r=[out], w=[out])
    B.act(lambda e: e.activation(out=out[:, 0:n], in_=out[:, 0:n], func=AF.Sin), r=[out], w=[out])


def setup_rope(B, stk):
    cfg = B.cfg
    nt = cfg.nt
    B.COS = B.sb("COS", [128, nt * 16], F32, stk)
    B.SIN = B.sb("SIN", [128, nt * 16], F32, stk)
    with ExitStack() as st:
        pos = B.sb("pos", [128, nt], F32, st)
        pi_ = B.sb("pi_", [128, 1], I32, st)
        ang = B.sb("ang", [128, nt * 16], F32, st)
        ang2 = B.sb("ang2", [128, nt * 16], F32, st)
        tf = B.sb("tf", [128, nt * 16], F32, st)
        ti = B.sb("ti", [128, nt * 16], I32, st)
        B.pool(lambda e: e.iota(pos[:, 0:cfg.ntp], pattern=[[128, cfg.ntp]], base=0, channel_multiplier=1, allow_small_or_imprecise_dtypes=True), w=[pos])
        B.pool(lambda e: e.iota(pi_[:], pattern=[[0, 1]], base=0, channel_multiplier=1), w=[pi_])
        B.dve(lambda e: e.tensor_single_scalar(out=pi_[:], in_=pi_[:], scalar=3, op=ALU.bitwise_and), r=[pi_], w=[pi_])
        B.dve(lambda e: e.tensor_copy(out=pos[:, cfg.ntp:nt], in_=pi_[:]), r=[pi_], w=[pos])
        B.dve(lambda e: e.tensor_scalar(out=pos[:, cfg.ntp:nt], in0=pos[:, cfg.ntp:nt], scalar1=float(cfg.past), scalar2=None, op0=ALU.add), r=[pos], w=[pos])
        a3 = ang[:].rearrange("p (j f) -> p j f", f=16)
        for f in range(16):
            inv = float(np.float32(1.0) / np.float32(10000.0) ** (np.float32(f) * np.float32(2.0 / ROPE)))
            B.dve(lambda e, f=f, inv=inv: e.tensor_scalar(out=a3[:, :, f], in0=pos[:, :], scalar1=inv, scalar2=None, op0=ALU.mult), r=[pos], w=[ang])
        B.dve(lambda e: e.tensor_scalar(out=ang2[:], in0=ang[:], scalar1=math.pi / 2, scalar2=None, op0=ALU.add), r=[ang], w=[ang2])
        range_reduce_sin(B, ang, B.SIN, nt * 16, tf, ti)
        range_reduce_sin(B, ang2, B.COS, nt * 16, tf, ti)
        B.sy.barrier()


def rope_apply(B, src, src_ap, dst, dst_ap, j, nh, t):
    c = B.COS[:, j * 16:(j + 1) * 16].unsqueeze(1).broadcast_to([128, nh, 16])
    s = B.SIN[:, j * 16:(j + 1) * 16].unsqueeze(1).broadcast_to([128, nh, 16])
    x1, x2 = src_ap[:, :, 0:16], src_ap[:, :, 16:32]
    def v(k):
        return t[k][:, 0:nh * 16].rearrange("p (h f) -> p h f", f=16)
    B.dve(lambda e: e.tensor_tensor(out=v(0), in0=x1, in1=c, op=ALU.mult), r=[src, B.COS], w=[t[0]])
    B.dve(lambda e: e.tensor_tensor(out=v(1), in0=x2, in1=s, op=ALU.mult), r=[src, B.SIN], w=[t[1]])
    B.dve(lambda e: e.tensor_tensor(out=v(2), in0=x1, in1=s, op=ALU.mult), r=[src, B.SIN], w=[t[2]])
    B.dve(lambda e: e.tensor_tensor(out=v(3), in0=x2, in1=c, op=ALU.mult), r=[src, B.COS], w=[t[3]])
    B.dve(lambda e: e.tensor_tensor(out=dst_ap[:, :, 0:16], in0=v(0), in1=v(1), op=ALU.subtract), r=[t[0], t[1]], w=[dst])
    B.dve(lambda e: e.tensor_tensor(out=dst_ap[:, :, 16:32], in0=v(2), in1=v(3), op=ALU.add), r=[t[2], t[3]], w=[dst])


def stage_mla(B, li, m, src, dst):
    cfg, d, ps = B.cfg, B.d, B.ps
    ntp, nt = cfg.ntp, cfg.nt
    NTOK = ntp * 128
    scale = QK ** -0.5
    nblk = (ntp + 3) // 4
    if not hasattr(B, "QTd"):
        B.QTd = B.dram_scr("QTd", [NH, QK, nblk * 512], BF16)
        B.OTd = B.dram_scr("OTd", [NH, VD, nblk * 512 + 128], BF16)
        B.CNd = B.dram_scr("CNd", [128, KVR + ROPE], F32)
    QTd, OTd, CNd = B.QTd, B.OTd, B.CNd
    with ExitStack() as st:
        w_uk = B.sb("w_uk", [128, 2, NH * NOPE], BF16, st)
        w_uv = B.sb("w_uv", [128, 2, NH * VD], BF16, st)
        B.load_w(w_uk, d["mla_w_uk"].t[m])
        B.load_w(w_uv, d["mla_w_uv"].t[m])
        qs_b = B.sb("qs_b", [128, NH, QK], BF16, st)
        sA = ExitStack()
        setup_rope(B, sA)
        w_dq = B.sb("w_dq", [128, 8, QR], BF16, sA)
        w_uq = B.sb("w_uq", [128, 6, NH * QK], BF16, sA)
        w_dkv = B.sb("w_dkv", [128, 8, KVR + ROPE], BF16, sA)
        B.load_w(w_dq, d["mla_w_dq"].t[m])
        B.load_w(w_uq, d["mla_w_uq"].t[m])
        B.load_w(w_dkv, d["mla_w_dkv"].t[m])
        Gq = B.sb("Gq", [128, QR], F32, sA)
        Gkv = B.sb("Gkv", [128, KVR], F32, sA)
        B.load_bcast(Gq, d["mla_q_norm"].t[m])
        B.load_bcast(Gkv, d["mla_kv_norm"].t[m])
        wukx = B.sb("wukx", [128, 2, NH, QK], BF16, sA)
        B.pool(lambda e: e.memset(wukx[:], 0.0), w=[wukx])
        B.pool(lambda e: e.tensor_copy(out=wukx[:, :, :, 0:NOPE], in_=w_uk[:, :, :].rearrange("p k (h n) -> p k h n", n=NOPE)), r=[w_uk], w=[wukx])
        sel = B.sb("sel", [32, QK], BF16, sA)
        B.pool(lambda e: e.memset(sel[:], 0.0), w=[sel])
        B.pool(lambda e: e.tensor_copy(out=sel[:, NOPE:QK], in_=B.identb[0:32, 0:32]), r=[B.identb], w=[sel])
        cT = B.sb("cT", [128, 2, NTOK], BF16, sA)
        kpeT = B.sb("kpeT", [32, NTOK], BF16, sA)
        qmax = B.sb("qmax", [128, NH], F32, sA)
        kmax = B.sb("kmax", [128, NH], F32, sA)
        B.dve(lambda e: e.memset(qmax[:], 0.0), w=[qmax])
        B.dve(lambda e: e.memset(kmax[:], 0.0), w=[kmax])
        st2 = ExitStack()
        xs = [B.sb("xs", [128, D], F32, st2) for _ in range(2)]
        xb = B.sb("xb", [128, D], BF16, st2)
        xT = B.sb("xT", [128, 8, 128], BF16, st2)
        stq = B.sb("stq", [128, 8], F32, st2)
        cqn = B.sb("cqn", [128, QR], BF16, st2)
        cqT = B.sb("cqT", [128, 6, 128], BF16, st2)
        qf = B.sb("qf", [128, NH * QK], F32, st2)
        qb = B.sb("qb", [128, NH, QK], BF16, st2)
        sq = B.sb("sq", [128, NH * QK], F32, st2)
        ss16 = B.sb("ss16", [128, NH], F32, st2)
        ss16k = B.sb("ss16k", [128, NH], F32, st2)
        stk = B.sb("stk", [128, 4], F32, st2)
        sq2 = B.sb("sq2", [128, NH * NOPE + ROPE], F32, st2)
        rtk = [B.sb("rtk", [128, 16], F32, st2) for _ in range(4)]
        rt = [B.sb("rt", [128, NH * 16], F32, st2) for _ in range(4)]
        QTs = [B.sb("QTs", [QK, NH, 512], BF16, st2) for _ in range(2)]
        c32 = [B.sb("c32", [128, KVR + ROPE], F32, st2) for _ in range(2)]
        cb = B.sb("cb", [128, KVR + ROPE], BF16, st2)
        for t_ in xs + QTs:
            B.dve(lambda e, t_=t_: e.memset(t_[:], 0.0), w=[t_])
        qf3 = qf[:].rearrange("p (h q) -> p h q", q=QK)

        def load(j, buf):
            for (r0, r1), ap, tt in src(j):
                B.dma("sp", buf[r0:r1, :], ap, r=[tt], w=[buf])

        load(0, xs[0])
        for j in range(nt):
            x = xs[j % 2]
            rows = cfg.rows(j)
            sample = (j == ntp)
            B.pool(lambda e: e.tensor_copy(out=xb[:], in_=x[:]), r=[x], w=[xb])
            B.transpose_to(xb, 8, xT, xT[:, :, :], ps[0])
            if j + 1 < nt:
                load(j + 1, xs[(j + 1) % 2])
            def q_side():
                B.linear(ps[1], xT, w_dq, 0, 512, 8)
                B.linear(ps[2], xT, w_dq, 512, QR, 8)
                yield
                B.act(lambda e: e.activation(out=sq[:, 0:512], in_=ps[1][:, 0:512], func=AF.Square, accum_out=stq[:, 0:1]), r=[ps[1]], w=[sq, stq])
                B.act(lambda e: e.activation(out=sq[:, 512:QR], in_=ps[2][:, 0:QR - 512], func=AF.Square, accum_out=stq[:, 1:2]), r=[ps[2]], w=[sq, stq])
                yield
                B.dve(lambda e: e.tensor_tensor(out=stq[:, 2:3], in0=stq[:, 0:1], in1=stq[:, 1:2], op=ALU.add), r=[stq], w=[stq])
                yield
                B.act(lambda e: e.activation(out=stq[:, 3:4], in_=stq[:, 2:3], func=AF.Sqrt, bias=B.eps[:, 1:2], scale=1.0 / QR), r=[stq, B.eps], w=[stq])
                yield
                B.dve(lambda e: e.reciprocal(out=stq[:, 3:4], in_=stq[:, 3:4]), r=[stq], w=[stq])
                B.dve(lambda e: e.scalar_tensor_tensor(out=cqn[:, 0:512], in0=ps[1][:, 0:512], scalar=stq[:, 3:4], in1=Gq[:, 0:512], op0=ALU.mult, op1=ALU.mult),
                      r=[ps[1], stq, Gq], w=[cqn])
                B.dve(lambda e: e.scalar_tensor_tensor(out=cqn[:, 512:QR], in0=ps[2][:, 0:QR - 512], scalar=stq[:, 3:4], in1=Gq[:, 512:QR], op0=ALU.mult, op1=ALU.mult),
                      r=[ps[2], stq, Gq], w=[cqn])
                yield
                B.transpose_to(cqn, 6, cqT, cqT[:, :, :], ps[0])
                yield
                for k3 in range(3):
                    bq = ps[3 + k3 % 2]
                    B.linear(bq, cqT, w_uq, 512 * k3, 512 * (k3 + 1), 6)
                    yield
                    B.act(lambda e, k3=k3, bq=bq: e.copy(out=qf[:, 512 * k3:512 * (k3 + 1)], in_=bq[:, :]), r=[bq], w=[qf])
                    yield
                qdst = qs_b if sample else qb
                B.dve(lambda e: e.tensor_copy(out=qdst[:, :, 0:NOPE], in_=qf3[:, :, 0:NOPE]), r=[qf], w=[qdst])
                rope_apply(B, qf, qf3[:, :, NOPE:QK], qdst, qdst[:, :, NOPE:QK], j, NH, rt)
                yield
                if not sample:
                    B.pool(lambda e: e.tensor_tensor(out=sq[:], in0=qf[:], in1=qf[:], op=ALU.mult), r=[qf], w=[sq])
                    yield
                    B.dve(lambda e: e.tensor_reduce(out=ss16[:], in_=sq[:].rearrange("p (h q) -> p h q", q=QK), axis=AX.X, op=ALU.add), r=[sq], w=[ss16])
                    B.dve(lambda e: e.tensor_tensor(out=qmax[:], in0=qmax[:], in1=ss16[:], op=ALU.max), r=[qmax, ss16], w=[qmax])
                    QT_ = QTs[(j // 4) % 2]
                    bank = ps[5]
                    pv = B.psb(5)
                    for hb in range(2):
                        for hh in range(8):
                            h = hb * 8 + hh
                            B.pe(lambda e, h=h, hh=hh: e.transpose(pv[0:QK, hh * 128:(hh + 1) * 128], qb[:, h, :], B.identb[:, :]), r=[qb, B.identb], w=[bank])
                        yield
                        B.act(lambda e, hb=hb: e.copy(out=QT_[:, hb * 8:(hb + 1) * 8, (j % 4) * 128:(j % 4 + 1) * 128],
                                                      in_=pv[0:QK, 0:1024].rearrange("p (h t) -> p h t", t=128)), r=[bank], w=[QT_])
                        yield
                    if j % 4 == 3 or j == ntp - 1:
                        blk = j // 4
                        B.dma("sp", QTd.t.rearrange("h r t -> r h t")[:, :, blk * 512:(blk + 1) * 512], QT_[:, :, :], r=[QT_], w=[QTd])
                yield

            def k_side():
                bk, bt_ = ps[6], ps[7]
                B.linear(bk, xT, w_dkv, 0, KVR + ROPE, 8)
                yield
                B.act(lambda e: e.activation(out=sq2[:, 0:KVR], in_=bk[:, 0:KVR], func=AF.Square, accum_out=stk[:, 0:1]), r=[bk], w=[sq2, stk])
                yield
                B.act(lambda e: e.activation(out=stk[:, 1:2], in_=stk[:, 0:1], func=AF.Sqrt, bias=B.eps[:, 1:2], scale=1.0 / KVR), r=[stk, B.eps], w=[stk])
                yield
                B.dve(lambda e: e.reciprocal(out=stk[:, 1:2], in_=stk[:, 1:2]), r=[stk], w=[stk])
                c_ = c32[j % 2]
                B.dve(lambda e: e.scalar_tensor_tensor(out=c_[:, 0:KVR], in0=bk[:, 0:KVR], scalar=stk[:, 1:2], in1=Gkv[:, :], op0=ALU.mult, op1=ALU.mult),
                      r=[bk, stk, Gkv], w=[c_])
                rope_apply(B, bk, bk[:, KVR:KVR + ROPE].rearrange("p (o f) -> p o f", o=1), c_,
                           c_[:, KVR:KVR + ROPE].rearrange("p (o f) -> p o f", o=1), j, 1, rtk)
                yield
                B.pool(lambda e: e.tensor_copy(out=cb[:], in_=c_[:]), r=[c_], w=[cb])
                if sample:
                    B.dma("sp", d["lat_s"].t[m, :, :], c_[0:rows, 0:KVR], r=[c_], w=[d["lat_s"]])
                    B.dma("sp", d["kr_s"].t[m, :, :], c_[0:rows, KVR:KVR + ROPE], r=[c_], w=[d["kr_s"]])
                    B.dma("sp", CNd.t[0:rows, :], c_[0:rows, :], r=[c_], w=[CNd])
                    yield
                    return
                B.dma("sp", d["lat_p"].t[m, 128 * j:128 * j + rows, :], c_[0:rows, 0:KVR], r=[c_], w=[d["lat_p"]])
                B.dma("sp", d["kr_p"].t[m, 128 * j:128 * j + rows, :], c_[0:rows, KVR:KVR + ROPE], r=[c_], w=[d["kr_p"]])
                yield
                pv7 = B.psb(7)
                for c in range(2):
                    B.pe(lambda e, c=c: e.transpose(pv7[:, c * 128:(c + 1) * 128], cb[:, c * 128:(c + 1) * 128], B.identb[:, :]), r=[cb, B.identb], w=[bt_])
                B.pe(lambda e: e.transpose(pv7[0:ROPE, 256:384], cb[:, KVR:KVR + ROPE], B.identb[:, :]), r=[cb, B.identb], w=[bt_])
                yield
                B.act(lambda e: e.copy(out=cT[:, :, j * 128:(j + 1) * 128], in_=pv7[:, 0:256].rearrange("p (c t) -> p c t", t=128)), r=[bt_], w=[cT])
                B.act(lambda e: e.copy(out=kpeT[:, j * 128:(j + 1) * 128], in_=pv7[0:ROPE, 256:384]), r=[bt_], w=[kpeT])
                yield
                for k2 in range(2):
                    for kc in range(2):
                        B.pe(lambda e, k2=k2, kc=kc: e.matmul(bk[:, :], lhsT=cT[:, kc, j * 128:(j + 1) * 128], rhs=w_uk[:, kc, 512 * k2:512 * (k2 + 1)],
                                                             start=(kc == 0), stop=(kc == 1)), r=[cT, w_uk], w=[bk])
                    yield
                    B.act(lambda e, k2=k2: e.activation(out=sq2[:, 512 * k2:512 * (k2 + 1)], in_=bk[:, :], func=AF.Square), r=[bk], w=[sq2])
                    yield
                B.dve(lambda e: e.tensor_reduce(out=ss16k[:], in_=sq2[:, 0:NH * NOPE].rearrange("p (h q) -> p h q", q=NOPE), axis=AX.X, op=ALU.add), r=[sq2], w=[ss16k])
                B.act(lambda e: e.activation(out=sq2[:, 1024:1024 + ROPE], in_=c_[:, KVR:KVR + ROPE], func=AF.Square, accum_out=stk[:, 2:3]), r=[c_], w=[sq2, stk])
                yield
                B.dve(lambda e: e.tensor_scalar(out=ss16k[:], in0=ss16k[:], scalar1=stk[:, 2:3], scalar2=None, op0=ALU.add), r=[ss16k, stk], w=[ss16k])
                B.dve(lambda e: e.tensor_tensor(out=kmax[:], in0=kmax[:], in1=ss16k[:], op=ALU.max), r=[kmax, ss16k], w=[kmax])
                yield

            gens = [q_side(), k_side()]
            alive = [True, True]
            while any(alive):
                for gi, g_ in enumerate(gens):
                    if alive[gi]:
                        try:
                            next(g_)
                        except StopIteration:
                            alive[gi] = False
        B.sy.barrier()
        st2.close()
        mla_prompt_attention(B, st, m, cT, kpeT, wukx, sel, w_uv, qmax, kmax, scale)
        sA.close()
        mla_sample_attention(B, st, m, qs_b, w_uk, w_uv, scale)
    mla_outproj(B, li, m, src, dst)


def bcast_partition_max(B, src, dst, bank, tmp):
    B.pe(lambda e: e.transpose(bank[0:NH, 0:128], src[:, 0:NH], B.ident[:, :]), r=[src, B.ident], w=[bank])
    B.dve(lambda e: e.tensor_reduce(out=tmp[0:NH, 0:1], in_=bank[0:NH, 0:128], axis=AX.X, op=ALU.max), r=[bank], w=[tmp])
    B.dve(lambda e: e.tensor_scalar(out=tmp[0:NH, 1:1 + NH], in0=B.ident[0:NH, 0:NH], scalar1=tmp[0:NH, 0:1], scalar2=None, op0=ALU.mult), r=[tmp, B.ident], w=[tmp])
    B.pe(lambda e: e.matmul(bank[:, 256:256 + NH], lhsT=B.ones[0:NH, :], rhs=tmp[0:NH, 1:1 + NH], start=True, stop=True), r=[B.ones, tmp], w=[bank])
    B.dve(lambda e: e.tensor_copy(out=dst[:, 0:NH], in_=bank[:, 256:256 + NH]), r=[bank], w=[dst])


def mla_prompt_attention(B, st, m, cT, kpeT, wukx, sel, w_uv, qmax, kmax, scale):
    cfg, d, ps = B.cfg, B.d, B.ps
    ntp = cfg.ntp
    NTOK = ntp * 128
    nblk = (ntp + 3) // 4
    QTd, OTd = B.QTd, B.OTd
    with ExitStack() as s2:
        tmpm = B.sb("tmpm", [128, 1 + NH], F32, s2)
        MQ = B.sb("MQ", [128, NH], F32, s2)
        MK = B.sb("MK", [128, NH], F32, s2)
        negM = B.sb("negM", [128, NH], F32, s2)
        bcast_partition_max(B, qmax, MQ, ps[0], tmpm)
        bcast_partition_max(B, kmax, MK, ps[0], tmpm)
        B.dve(lambda e: e.tensor_tensor(out=negM[:], in0=MQ[:], in1=MK[:], op=ALU.mult), r=[MQ, MK], w=[negM])
        B.act(lambda e: e.activation(out=negM[:], in_=negM[:], func=AF.Sqrt), r=[negM], w=[negM])
        B.dve(lambda e: e.tensor_scalar(out=negM[:], in0=negM[:], scalar1=-scale, scalar2=None, op0=ALU.mult), r=[negM], w=[negM])
        masks = []
        mf = B.sb("mf", [128, 512], F32, s2)
        for i in range(4):
            mk = B.sb("mask", [128, 512], BF16, s2)
            B.pool(lambda e: e.memset(mf[:], 1.0), w=[mf])
            B.pool(lambda e, i=i: e.affine_select(out=mf[:], in_=mf[:], pattern=[[1, 512]], compare_op=ALU.is_ge, fill=0.0, base=-128 * i, channel_multiplier=-1),
                   r=[mf], w=[mf])
            B.pool(lambda e, mk=mk: e.tensor_copy(out=mk[:], in_=mf[:]), r=[mf], w=[mk])
            masks.append(mk)
        KTh = [B.sb("KTh", [QK, NTOK], BF16, s2) for _ in range(2)]
        Vh = [B.sb("Vh", [128, ntp, VD + 1], BF16, s2) for _ in range(2)]
        for v_ in Vh:
            B.pool(lambda e, v_=v_: e.memset(v_[:], 1.0), w=[v_])
        QTq = [B.sb("QTq", [QK, 512], BF16, s2) for _ in range(2)]
        PT = [B.sb("PT", [128, 512], BF16, s2) for _ in range(3)]
        Osb = [B.sb("Osb", [VD + 1, 512], F32, s2) for _ in range(2)]
        rl = B.sb("rl", [VD, 512], F32, s2)
        On = [B.sb("On", [VD, 512], BF16, s2) for _ in range(2)]
        onesr = B.sb("onesr", [VD + 1, VD], F32, s2)
        B.pool(lambda e: e.memset(onesr[:], 1.0), w=[onesr])
        nstep = 0
        nq_i = 0
        for h in range(NH):
            KT, V = KTh[h % 2], Vh[h % 2]
            for b in range(nblk):
                n = min(512, NTOK - b * 512)
                bank = ps[1 + b % 2]
                for kc in range(2):
                    B.pe(lambda e, kc=kc, b=b, n=n, bank=bank: e.matmul(bank[0:QK, 0:n], lhsT=wukx[:, kc, h, :], rhs=cT[:, kc, b * 512:b * 512 + n],
                                                                        start=(kc == 0), stop=False), r=[wukx, cT], w=[bank])
                B.pe(lambda e, b=b, n=n, bank=bank: e.matmul(bank[0:QK, 0:n], lhsT=sel[:, :], rhs=kpeT[:, b * 512:b * 512 + n], start=False, stop=True),
                     r=[sel, kpeT], w=[bank])
                B.dve(lambda e, b=b, n=n, bank=bank: e.tensor_copy(out=KT[:, b * 512:b * 512 + n], in_=bank[0:QK, 0:n]), r=[bank], w=[KT])
            for t0 in range(0, ntp, 8):
                nt8 = min(8, ntp - t0)
                bank = ps[3]
                for i in range(nt8):
                    for kc in range(2):
                        B.pe(lambda e, i=i, kc=kc, t0=t0, bank=bank: e.matmul(bank[:, i * VD:(i + 1) * VD], lhsT=cT[:, kc, (t0 + i) * 128:(t0 + i + 1) * 128],
                                                                              rhs=w_uv[:, kc, h * VD:(h + 1) * VD], start=(kc == 0), stop=(kc == 1)),
                             r=[cT, w_uv], w=[bank])
                B.dve(lambda e, t0=t0, nt8=nt8, bank=bank: e.tensor_copy(out=V[:, t0:t0 + nt8, 0:VD], in_=bank[:, 0:nt8 * VD].rearrange("p (t v) -> p t v", v=VD)),
                      r=[bank], w=[V])
            for b in range(nblk):
                tiles = list(range(4 * b, min(4 * b + 4, ntp)))
                nq = 128 * len(tiles)
                Q = QTq[nq_i % 2]
                Ob = ps[6 + nq_i % 2]
                B.dma("sp", Q[:, 0:nq], QTd.t[h, :, b * 512:b * 512 + nq], r=[QTd], w=[Q])
                last_kt = tiles[-1]
                for kt in range(last_kt + 1):
                    kr = cfg.rows(kt)
                    Sb = ps[4 + nstep % 2]
                    P = PT[nstep % 3]
                    nstep += 1
                    B.pe(lambda e, kt=kt, kr=kr, Sb=Sb: e.matmul(Sb[0:kr, 0:nq], lhsT=KT[:, kt * 128:kt * 128 + kr], rhs=Q[:, 0:nq], start=True, stop=True),
                         r=[KT, Q], w=[Sb])
                    B.act(lambda e, kr=kr, Sb=Sb, P=P: e.activation(out=P[0:kr, 0:nq], in_=Sb[0:kr, 0:nq], func=AF.Exp, bias=negM[0:kr, h:h + 1], scale=scale),
                          r=[Sb, negM], w=[P])
                    if kt >= 4 * b:
                        mk = masks[kt - 4 * b]
                        B.dve(lambda e, kr=kr, P=P, mk=mk: e.tensor_tensor(out=P[0:kr, 0:nq], in0=P[0:kr, 0:nq], in1=mk[0:kr, 0:nq], op=ALU.mult), r=[P, mk], w=[P])
                    B.pe(lambda e, kt=kt, kr=kr, P=P: e.matmul(Ob[0:VD + 1, 0:nq], lhsT=V[0:kr, kt, :], rhs=P[0:kr, 0:nq], start=(kt == 0), stop=(kt == last_kt)),
                         r=[V, P], w=[Ob])
                O_ = Osb[nq_i % 2]
                On_ = On[nq_i % 2]
                B.act(lambda e: e.copy(out=O_[:, 0:nq], in_=Ob[0:VD + 1, 0:nq]), r=[Ob], w=[O_])
                B.pe(lambda e: e.matmul(ps[0][0:VD, 0:nq], lhsT=onesr[VD:VD + 1, :], rhs=O_[VD:VD + 1, 0:nq], start=True, stop=True), r=[onesr, O_], w=[ps[0]])
                B.dve(lambda e: e.reciprocal(out=rl[:, 0:nq], in_=ps[0][0:VD, 0:nq]), r=[ps[0]], w=[rl])
                B.dve(lambda e: e.tensor_tensor(out=On_[:, 0:nq], in0=O_[0:VD, 0:nq], in1=rl[:, 0:nq], op=ALU.mult), r=[O_, rl], w=[On_])
                B.dma("sp", OTd.t[h, :, b * 512:b * 512 + nq], On_[:, 0:nq], r=[On_], w=[OTd])
                nq_i += 1
        B.sy.barrier()


def mla_sample_attention(B, st, m, qs_b, w_uk, w_uv, scale):
    cfg, d, ps = B.cfg, B.d, B.ps
    nsq, srows, npg = cfg.nsq, cfg.srows, cfg.npages
    NK = npg * 128 + DEC_SEQ
    R = KVR + ROPE
    OTd, CNd = B.OTd, B.CNd
    nblk = (cfg.ntp + 3) // 4
    with ExitStack() as s2:
        wukT = B.sb("wukT", [NOPE, NH, KVR], BF16, s2)
        for h in range(NH):
            pv = B.psb(h % 2)
            for kc in range(2):
                B.pe(lambda e, h=h, kc=kc, pv=pv: e.transpose(pv[0:NOPE, kc * 128:(kc + 1) * 128], w_uk[:, kc, h * NOPE:(h + 1) * NOPE], B.identb[:, :]),
                     r=[w_uk, B.identb], w=[ps[h % 2]])
            B.act(lambda e, h=h, pv=pv: e.copy(out=wukT[:, h, :], in_=pv[0:NOPE, 0:KVR]), r=[ps[h % 2]], w=[wukT])
        qnT = B.sb("qnT", [NOPE, NH, srows], BF16, s2)
        pv = B.psb(2)
        for h in range(NH):
            B.pe(lambda e, h=h: e.transpose(pv[0:NOPE, h * srows:(h + 1) * srows], qs_b[0:srows, h, 0:NOPE], B.identb[0:srows, 0:srows]),
                 r=[qs_b, B.identb], w=[ps[2]])
        B.act(lambda e: e.copy(out=qnT[:, :, :], in_=pv[0:NOPE, 0:NH * srows].rearrange("p (h t) -> p h t", t=srows)), r=[ps[2]], w=[qnT])
        QL = B.sb("QL", [srows, NH, R], BF16, s2)
        for h in range(NH):
            bank = ps[3 + h % 2]
            B.pe(lambda e, h=h, bank=bank: e.matmul(bank[0:srows, 0:KVR], lhsT=qnT[:, h, :], rhs=wukT[:, h, :], start=True, stop=True), r=[qnT, wukT], w=[bank])
            B.act(lambda e, h=h, bank=bank: e.copy(out=QL[:, h, 0:KVR], in_=bank[0:srows, 0:KVR]), r=[bank], w=[QL])
        B.dve(lambda e: e.tensor_copy(out=QL[:, :, KVR:R], in_=qs_b[0:srows, :, NOPE:QK]), r=[qs_b], w=[QL])
        QLT = B.sb("QLT", [128, 3, srows, NH], BF16, s2)
        for c, (c0, cw) in enumerate(((0, 128), (128, 128), (256, 32))):
            for h0 in range(0, NH, 8):
                bank = ps[5 + (c + h0 // 8) % 2]
                pv = B.psb(5 + (c + h0 // 8) % 2)
                for hh in range(8):
                    h = h0 + hh
                    B.pe(lambda e, h=h, hh=hh, c0=c0, cw=cw, pv=pv: e.transpose(pv[0:cw, hh * srows:(hh + 1) * srows], QL[:, h, c0:c0 + cw], B.identb[0:srows, 0:srows]),
                         r=[QL, B.identb], w=[bank])
                B.act(lambda e, c=c, h0=h0, cw=cw, pv=pv: e.copy(out=QLT[0:cw, c, :, h0:h0 + 8].rearrange("p r h -> p h r"),
                                                              in_=pv[0:cw, 0:8 * srows].rearrange("p (h r) -> p h r", r=srows)), r=[bank], w=[QLT])
        ptb = B.sb("ptb", [128, nsq * npg], I32, s2)
        idx = B.sb("idx", [128, nsq * npg], I32, s2)
        pidx = B.sb("pidx", [128, 1], I32, s2)
        B.dma("sp", ptb[:, :], d["page_table"].t.rearrange("s g -> (s g)").rearrange("(o n) -> o n", o=1).broadcast_to([128, nsq * npg]), w=[ptb])
        B.pool(lambda e: e.iota(pidx[:], pattern=[[0, 1]], base=m * cfg.npool * 128, channel_multiplier=1), w=[pidx])
        B.pool(lambda e: e.tensor_scalar(out=idx[:], in0=ptb[:], scalar1=128, scalar2=None, op0=ALU.mult), r=[ptb], w=[idx])
        B.pool(lambda e: e.tensor_tensor(out=idx[:], in0=idx[:], in1=pidx[:].broadcast_to([128, nsq * npg]), op=ALU.add), r=[idx, pidx], w=[idx])
        mskf = B.sb("mskf", [64, DEC_SEQ], F32, s2)
        B.pool(lambda e: e.memset(mskf[:], 1.0), w=[mskf])
        B.pool(lambda e: e.affine_select(out=mskf[:], in_=mskf[:], pattern=[[-NH, DEC_SEQ]], compare_op=ALU.is_ge, fill=0.0, base=0, channel_multiplier=1),
               r=[mskf], w=[mskf])
        CP = [B.sb("CP", [128, npg + 1, R], BF16, s2) for _ in range(2)]
        CTs = [B.sb("CTs", [128, 3, 128], BF16, s2) for _ in range(3)]
        S_all = B.sb("S_all", [64, NK], F32, s2)
        Pb = B.sb("Pb", [64, NK], BF16, s2)
        Pn = B.sb("Pn", [64, DEC_SEQ], F32, s2)
        PTs = B.sb("PTs", [128, npg + 1, 64], BF16, s2)
        sm = B.sb("sm", [64, 8], F32, s2)
        OLs = B.sb("OLs", [64, KVR], BF16, s2)
        OLT = B.sb("OLT", [128, 2, NH, srows], BF16, s2)
        lat, kr_ = d["cache_mla_latent"], d["cache_mla_krope"]
        nct = 0
        for s in range(nsq):
            C = CP[s % 2]
            for g in range(npg):
                B.sy.op("pool", lambda e, g=g: e.indirect_dma_start(out=C[:, g, 0:KVR], out_offset=None, in_=lat.t,
                        in_offset=bass.IndirectOffsetOnAxis(ap=idx[:, s * npg + g:s * npg + g + 1], axis=0)), [idx, lat], [C], dma=True)
                B.sy.op("pool", lambda e, g=g: e.indirect_dma_start(out=C[:, g, KVR:R], out_offset=None, in_=kr_.t,
                        in_offset=bass.IndirectOffsetOnAxis(ap=idx[:, s * npg + g:s * npg + g + 1], axis=0)), [idx, kr_], [C], dma=True)
            B.dma("pool", C[0:DEC_SEQ, npg, :], CNd.t[s * DEC_SEQ:(s + 1) * DEC_SEQ, :], r=[CNd], w=[C])
            qcols = lambda c, kw: QLT[0:kw, c, s * DEC_SEQ:(s + 1) * DEC_SEQ, :].rearrange("p t h -> p (t h)")
            for g in range(npg + 1):
                kr = 128 if g < npg else DEC_SEQ
                CT_ = CTs[nct % 3]
                nct += 1
                bi = 1 + (g % 2)
                pv = B.psb(bi)
                for c, (c0, cw) in enumerate(((0, 128), (128, 128), (256, 32))):
                    B.pe(lambda e, g=g, kr=kr, c=c, c0=c0, cw=cw, pv=pv: e.transpose(pv[0:cw, c * 128:c * 128 + kr], C[0:kr, g, c0:c0 + cw], B.identb[0:kr, 0:kr]),
                         r=[C, B.identb], w=[ps[bi]])
                B.act(lambda e, kr=kr, CT_=CT_, pv=pv: e.copy(out=CT_[:, :, 0:kr], in_=pv[:, 0:384].rearrange("p (c k) -> p c k", k=128)[:, :, 0:kr]), r=[ps[bi]], w=[CT_])
                sb_i = 3 + (g // 4) % 2
                off = (g % 4) * 128
                for c, (c0, cw) in enumerate(((0, 128), (128, 128), (256, 32))):
                    B.pe(lambda e, c=c, cw=cw, kr=kr, CT_=CT_, sb_i=sb_i, off=off: e.matmul(ps[sb_i][0:64, off:off + kr], lhsT=qcols(c, cw), rhs=CT_[0:cw, c, 0:kr],
                                                                                        start=(c == 0), stop=(c == 2)), r=[QLT, CT_], w=[ps[sb_i]])
                if g % 4 == 3 or g == npg:
                    g0 = (g // 4) * 4
                    n = off + kr
                    B.dve(lambda e, g0=g0, n=n, sb_i=sb_i: e.tensor_scalar(out=S_all[:, g0 * 128:g0 * 128 + n], in0=ps[sb_i][0:64, 0:n], scalar1=scale, scalar2=None, op0=ALU.mult),
                          r=[ps[sb_i]], w=[S_all])
            B.dve(lambda e: e.tensor_reduce(out=sm[:, 0:1], in_=S_all[:, 0:NK], axis=AX.X, op=ALU.max), r=[S_all], w=[sm])
            B.dve(lambda e: e.tensor_scalar(out=sm[:, 1:2], in0=sm[:, 0:1], scalar1=-1.0, scalar2=None, op0=ALU.mult), r=[sm], w=[sm])
            B.act(lambda e: e.activation(out=Pb[:, 0:npg * 128], in_=S_all[:, 0:npg * 128], func=AF.Exp, bias=sm[:, 1:2], scale=1.0, accum_out=sm[:, 2:3]),
                  r=[S_all, sm], w=[Pb, sm])
            B.act(lambda e: e.activation(out=Pn[:, :], in_=S_all[:, npg * 128:NK], func=AF.Exp, bias=sm[:, 1:2], scale=1.0), r=[S_all, sm], w=[Pn])
            B.dve(lambda e: e.tensor_tensor(out=Pn[:, :], in0=Pn[:, :], in1=mskf[:, :], op=ALU.mult), r=[Pn, mskf], w=[Pn])
            B.dve(lambda e: e.tensor_reduce(out=sm[:, 3:4], in_=Pn[:, :], axis=AX.X, op=ALU.add), r=[Pn], w=[sm])
            B.dve(lambda e: e.tensor_copy(out=Pb[:, npg * 128:NK], in_=Pn[:, :]), r=[Pn], w=[Pb])
            B.dve(lambda e: e.tensor_tensor(out=sm[:, 4:5], in0=sm[:, 2:3], in1=sm[:, 3:4], op=ALU.add), r=[sm], w=[sm])
            B.dve(lambda e: e.reciprocal(out=sm[:, 5:6], in_=sm[:, 4:5]), r=[sm], w=[sm])
            for g0 in range(0, npg + 1, 8):
                n8 = min(8, npg + 1 - g0)
                bi = 5 + (g0 // 8) % 2
                pv = B.psb(bi)
                for i in range(n8):
                    g = g0 + i
                    kr = 128 if g < npg else DEC_SEQ
                    B.pe(lambda e, g=g, i=i, kr=kr, pv=pv: e.transpose(pv[0:kr, i * 64:(i + 1) * 64], Pb[:, g * 128:g * 128 + kr], B.identb[0:64, 0:64]),
                         r=[Pb, B.identb], w=[ps[bi]])
                nfull = n8 if g0 + n8 <= npg else n8 - 1
                if nfull:
                    B.act(lambda e, g0=g0, nfull=nfull, pv=pv: e.copy(out=PTs[:, g0:g0 + nfull, :], in_=pv[:, 0:nfull * 64].rearrange("p (g q) -> p g q", q=64)), r=[ps[bi]], w=[PTs])
                if nfull != n8:
                    B.act(lambda e, pv=pv, nfull=nfull: e.copy(out=PTs[0:DEC_SEQ, npg, :], in_=pv[0:DEC_SEQ, nfull * 64:(nfull + 1) * 64]), r=[ps[bi]], w=[PTs])
            for g in range(npg + 1):
                kr = 128 if g < npg else DEC_SEQ
                B.pe(lambda e, g=g, kr=kr: e.matmul(ps[7][0:64, 0:KVR], lhsT=PTs[0:kr, g, :], rhs=C[0:kr, g, 0:KVR], start=(g == 0), stop=(g == npg)), r=[PTs, C], w=[ps[7]])
            B.dve(lambda e: e.tensor_scalar(out=OLs[:, :], in0=ps[7][0:64, 0:KVR], scalar1=sm[:, 5:6], scalar2=None, op0=ALU.mult), r=[ps[7], sm], w=[OLs])
            pv = B.psb(0)
            for kc in range(2):
                B.pe(lambda e, kc=kc: e.transpose(pv[:, kc * 64:(kc + 1) * 64], OLs[:, kc * 128:(kc + 1) * 128], B.identb[0:64, 0:64]), r=[OLs, B.identb], w=[ps[0]])
            B.act(lambda e: e.copy(out=OLT[:, :, :, s * DEC_SEQ:(s + 1) * DEC_SEQ].rearrange("p k h t -> p k t h"),
                                   in_=pv[:, 0:128].rearrange("p (k t h) -> p k t h", k=2, t=DEC_SEQ)), r=[ps[0]], w=[OLT])
        OTs = B.sb("OTs", [VD, NH, srows], BF16, s2)
        for h in range(NH):
            bank = ps[1 + h % 2]
            for kc in range(2):
                B.pe(lambda e, h=h, kc=kc, bank=bank: e.matmul(bank[0:VD, 0:srows], lhsT=w_uv[:, kc, h * VD:(h + 1) * VD], rhs=OLT[:, kc, h, :], start=(kc == 0), stop=(kc == 1)),
                     r=[w_uv, OLT], w=[bank])
            B.act(lambda e, h=h, bank=bank: e.copy(out=OTs[:, h, :], in_=bank[0:VD, 0:srows]), r=[bank], w=[OTs])
        B.dma("sp", OTd.t.rearrange("h v t -> v h t")[:, :, nblk * 512:nblk * 512 + srows], OTs[:, :, :], r=[OTs], w=[OTd])
        B.sy.barrier()


def mla_outproj(B, li, m, src, dst):
    cfg, d, ps = B.cfg, B.d, B.ps
    ntp, nt = cfg.ntp, cfg.nt
    nblk = (ntp + 3) // 4
    OTd = B.OTd
    with ExitStack() as st:
        w_o = B.sb("w_o", [VD, NH, D], BF16, st)
        wv = d["mla_w_o"].t[m].rearrange("(h v) n -> v h n", v=VD)
        for h in range(NH):
            B.dma("pool", w_o[:, h, :], wv[:, h, :], w=[w_o])
        G = B.sb("G", [128, D], F32, st)
        Bt = B.sb("Bt", [128, D], F32, st)
        B.load_bcast(G, d["ln_g"].t[li, 1])
        B.load_bcast(Bt, d["ln_b"].t[li, 1])
        xs = [B.sb("xs", [128, D], F32, st) for _ in range(2)]
        OTg = [B.sb("OTg", [VD, NH, 512], BF16, st) for _ in range(2)]
        tmp = dict(xa=B.sb("xa", [128, D], F32, st), y=B.sb("y", [128, D], F32, st), junk=B.sb("junk", [128, D], F32, st),
                   st=B.sb("st", [128, 8], F32, st))
        xo = [B.sb("xo", [128, D], F32, st) for _ in range(2)]
        for t_ in xs:
            B.dve(lambda e, t_=t_: e.memset(t_[:], 0.0), w=[t_])
        for j in range(nt):
            x = xs[j % 2]
            for (r0, r1), ap, tt in src(j):
                B.dma("sp", x[r0:r1, :], ap, r=[tt], w=[x])
            if j < ntp:
                blk, off = j // 4, (j % 4) * 128
                OT = OTg[blk % 2]
                if j % 4 == 0:
                    n = min(512, ntp * 128 - blk * 512)
                    B.dma("sp", OT[:, :, 0:n], OTd.t.rearrange("h v t -> v h t")[:, :, blk * 512:blk * 512 + n], r=[OTd], w=[OT])
                M = 128
            else:
                OT = OTg[(nblk) % 2]
                off, M = 0, cfg.srows
                B.dma("sp", OT[:, :, 0:M], OTd.t.rearrange("h v t -> v h t")[:, :, nblk * 512:nblk * 512 + M], r=[OTd], w=[OT])
            for hf in range(2):
                for h in range(NH):
                    B.pe(lambda e, h=h, hf=hf, OT=OT, off=off, M=M: e.matmul(ps[6 + hf][0:M, :], lhsT=OT[:, h, off:off + M], rhs=w_o[:, h, hf * 512:(hf + 1) * 512],
                                                                           start=(h == 0), stop=(h == NH - 1)), r=[OT, w_o], w=[ps[6 + hf]])
            o = xo[j % 2]
            B.resid_ln(x, [ps[6], ps[7]], 1.0, G, Bt, o, tmp)
            for (r0, r1), ap, tt in dst(j):
                B.dma("sp", ap, o[r0:r1, :], r=[o], w=[tt])
        B.sy.barrier()


def resid_ln_sb(B, x, h, G, Bt, out, tmp):
    y, junk, st = tmp["y"], tmp["junk"], tmp["st"]
    B.dve(lambda e: e.memset(st[:, 1:2], 0.0), w=[st])
    B.dve(lambda e: e.scalar_tensor_tensor(out=y[:], in0=x[:], scalar=float(B.cfg.alpha), in1=h[:], op0=ALU.mult, op1=ALU.add, accum_out=st[:, 0:1]),
          r=[x, h], w=[y, st])
    B.ln_core(y, G, Bt, out, junk, st)


def stage_s5(B, li, m, src, dst):
    cfg, d, ps = B.cfg, B.d, B.ps
    ntp, nt, nsq, srows = cfg.ntp, cfg.nt, cfg.nsq, cfg.srows
    with ExitStack() as st:
        BbT = [B.sb("BbT", [128, 32, 128], BF16, st) for _ in range(2)]
        CTm = [B.sb("CTm", [128, 32, 128], BF16, st) for _ in range(2)]
        prm = B.sb("prm", [128, 12, 32], F32, st)
        Dp = B.sb("Dp", [128, 8], F32, st)
        wv = B.sb("wv", [128, 8, D], BF16, st)
        wg = B.sb("wg", [128, 8, D], BF16, st)
        B.load_w(wv, d["s5_wv"].t[m])
        B.load_w(wg, d["s5_wg"].t[m])
        G = B.sb("G", [128, D], F32, st)
        Bt = B.sb("Bt", [128, D], F32, st)
        B.load_bcast(G, d["ln_g"].t[li, 1])
        B.load_bcast(Bt, d["ln_b"].t[li, 1])
        B.load_T(Dp, Dp[:, :], d["s5_d"].t[m].rearrange("(c p) -> c p", p=128), 8)
        P = lambda k: prm[:, k, :]
        bufs = s5_alloc(B, st)
        sp = ExitStack()
        CS = B.sb("CS", [128, 32 * 128], F32, sp)
        SN = B.sb("SN", [128, 32 * 128], F32, sp)
        with ExitStack() as s1:
            raw = B.sb("raw", [32, 3, 128], F32, s1)
            ldt = B.sb("ldt", [32, 2], F32, s1)
            B.dma("sp", raw[:, 0, :], d["s5_lam_re"].t[m].rearrange("(s g) p -> s (g p)", g=2), w=[raw])
            B.dma("sp", raw[:, 1, :], d["s5_lam_im"].t[m].rearrange("(s g) p -> s (g p)", g=2), w=[raw])
            B.dma("sp", ldt[:, :], d["s5_log_dt"].t[m].rearrange("(s g) -> s g", g=2), w=[ldt])
            B.dve(lambda e: e.tensor_copy(out=raw[:, 2, :].rearrange("s (g p) -> s g p", g=2), in_=ldt[:, :].unsqueeze(2).broadcast_to([32, 2, 64])), r=[ldt], w=[raw])
            for k in range(3):
                B.pe(lambda e, k=k: e.transpose(ps[0][:, k * 32:(k + 1) * 32], raw[:, k, :], B.ident[0:32, 0:32]), r=[raw, B.ident], w=[ps[0]])
            B.dve(lambda e: e.tensor_copy(out=prm[:, 0:3, :], in_=ps[0][:, 0:96].rearrange("p (k s) -> p k s", k=3)), r=[ps[0]], w=[prm])
            B.act(lambda e: e.activation(out=P(2), in_=P(2), func=AF.Exp), r=[prm], w=[prm])
            B.dve(lambda e: e.tensor_tensor(out=P(9), in0=P(0), in1=P(2), op=ALU.mult), r=[prm], w=[prm])
            B.act(lambda e: e.activation(out=P(3), in_=P(9), func=AF.Exp), r=[prm], w=[prm])
            B.dve(lambda e: e.tensor_tensor(out=P(4), in0=P(1), in1=P(2), op=ALU.mult), r=[prm], w=[prm])
            tf = B.sb("tf", [128, 4096], F32, s1)
            ti = B.sb("ti", [128, 4096], I32, s1)
            a32 = B.sb("a32", [128, 4, 32], F32, s1)
            B.dve(lambda e: e.tensor_copy(out=a32[:, 0, :], in_=P(4)), r=[prm], w=[a32])
            B.dve(lambda e: e.tensor_scalar(out=a32[:, 1, :], in0=P(4), scalar1=math.pi / 2, scalar2=None, op0=ALU.add), r=[prm], w=[a32])
            a32f = T(a32.t[:].rearrange("p k s -> p (k s)"), "a32f")
            a32f.wr, a32f.rd = a32.wr, a32.rd
            sc_ = B.sb("sc_", [128, 64], F32, s1)
            range_reduce_sin_signed(B, a32f, sc_, 64, tf, ti)
            B.dve(lambda e: e.tensor_tensor(out=P(6), in0=P(3), in1=sc_[:, 0:32], op=ALU.mult), r=[prm, sc_], w=[prm])
            B.dve(lambda e: e.tensor_tensor(out=P(5), in0=P(3), in1=sc_[:, 32:64], op=ALU.mult), r=[prm, sc_], w=[prm])
            B.dve(lambda e: e.tensor_tensor(out=P(9), in0=P(0), in1=P(0), op=ALU.mult), r=[prm], w=[prm])
            B.dve(lambda e: e.tensor_tensor(out=P(10), in0=P(1), in1=P(1), op=ALU.mult), r=[prm], w=[prm])
            B.dve(lambda e: e.tensor_tensor(out=P(9), in0=P(9), in1=P(10), op=ALU.add), r=[prm], w=[prm])
            B.dve(lambda e: e.reciprocal(out=P(9), in_=P(9)), r=[prm], w=[prm])
            B.dve(lambda e: e.tensor_scalar(out=P(10), in0=P(5), scalar1=-1.0, scalar2=None, op0=ALU.add), r=[prm], w=[prm])
            B.dve(lambda e: e.tensor_tensor(out=P(7), in0=P(10), in1=P(0), op=ALU.mult), r=[prm], w=[prm])
            B.dve(lambda e: e.tensor_tensor(out=P(11), in0=P(6), in1=P(1), op=ALU.mult), r=[prm], w=[prm])
            B.dve(lambda e: e.tensor_tensor(out=P(7), in0=P(7), in1=P(11), op=ALU.add), r=[prm], w=[prm])
            B.dve(lambda e: e.tensor_tensor(out=P(7), in0=P(7), in1=P(9), op=ALU.mult), r=[prm], w=[prm])
            B.dve(lambda e: e.tensor_tensor(out=P(8), in0=P(6), in1=P(0), op=ALU.mult), r=[prm], w=[prm])
            B.dve(lambda e: e.tensor_tensor(out=P(11), in0=P(10), in1=P(1), op=ALU.mult), r=[prm], w=[prm])
            B.dve(lambda e: e.tensor_tensor(out=P(8), in0=P(8), in1=P(11), op=ALU.subtract), r=[prm], w=[prm])
            B.dve(lambda e: e.tensor_tensor(out=P(8), in0=P(8), in1=P(9), op=ALU.mult), r=[prm], w=[prm])
            t1 = B.sb("t1", [128, 128], F32, s1)
            B.pool(lambda e: e.iota(t1[:], pattern=[[1, 128]], base=1, channel_multiplier=0, allow_small_or_imprecise_dtypes=True), w=[t1])
            ang = B.sb("ang", [128, 4096], F32, s1)
            for s_ in range(32):
                B.dve(lambda e, s_=s_: e.tensor_scalar(out=ang[:, s_ * 128:(s_ + 1) * 128], in0=t1[:], scalar1=prm[:, 4, s_:s_ + 1], scalar2=None, op0=ALU.mult), r=[t1, prm], w=[ang])
            range_reduce_sin_signed(B, ang, SN, 4096, tf, ti)
            B.dve(lambda e: e.tensor_scalar(out=ang[:], in0=ang[:], scalar1=math.pi / 2, scalar2=None, op0=ALU.add), r=[ang], w=[ang])
            range_reduce_sin_signed(B, ang, CS, 4096, tf, ti)
            B.sy.barrier()
        with ExitStack() as s1:
            Braw = [B.sb("Braw", [128, 32, 16], F32, s1) for _ in range(2)]
            bb = [B.sb("bb", [128, 32, 16], F32, s1) for _ in range(2)]
            tb = B.sb("tb", [128, 32, 16], F32, s1)
            for k, nm in enumerate(("s5_b_re", "s5_b_im")):
                src_v = d[nm].t[m].rearrange("(s g) p i -> g p s i", g=2)
                for g2 in range(2):
                    B.dma("sp", Braw[k][g2 * 64:(g2 + 1) * 64, :, :], src_v[g2], w=[Braw[k]])
            cre = prm[:, 7, :].unsqueeze(2).broadcast_to([128, 32, 16])
            cim = prm[:, 8, :].unsqueeze(2).broadcast_to([128, 32, 16])
            B.dve(lambda e: e.tensor_tensor(out=bb[0][:], in0=Braw[0][:], in1=cre, op=ALU.mult), r=[Braw[0], prm], w=[bb[0]])
            B.dve(lambda e: e.tensor_tensor(out=tb[:], in0=Braw[1][:], in1=cim, op=ALU.mult), r=[Braw[1], prm], w=[tb])
            B.dve(lambda e: e.tensor_tensor(out=bb[0][:], in0=bb[0][:], in1=tb[:], op=ALU.subtract), r=[bb[0], tb], w=[bb[0]])
            B.dve(lambda e: e.tensor_tensor(out=bb[1][:], in0=Braw[1][:], in1=cre, op=ALU.mult), r=[Braw[1], prm], w=[bb[1]])
            B.dve(lambda e: e.tensor_tensor(out=tb[:], in0=Braw[0][:], in1=cim, op=ALU.mult), r=[Braw[0], prm], w=[tb])
            B.dve(lambda e: e.tensor_tensor(out=bb[1][:], in0=bb[1][:], in1=tb[:], op=ALU.add), r=[bb[1], tb], w=[bb[1]])
            Ep = B.sb("Ep", [128, 32, 128], F32, s1)
            for k in range(2):
                B.pool(lambda e: e.memset(Ep[:], 0.0), w=[Ep])
                E4 = Ep[:].rearrange("p (c q) n -> p c q n", q=4)
                b4 = bb[k][:].rearrange("p (c q) i -> p c q i", q=4)
                for g2 in range(2):
                    for q in range(4):
                        B.pool(lambda e, g2=g2, q=q, E4=E4, b4=b4: e.tensor_copy(out=E4[g2 * 64:(g2 + 1) * 64, :, q, q * 32 + g2 * 16:q * 32 + g2 * 16 + 16],
                                                                                 in_=b4[g2 * 64:(g2 + 1) * 64, :, q, :]), r=[bb[k]], w=[Ep])
                for s0 in range(0, 32, 4):
                    bank = ps[1 + (s0 // 4) % 2]
                    for i in range(4):
                        B.pe(lambda e, s0=s0, i=i, bank=bank: e.transpose(bank[:, i * 128:(i + 1) * 128], Ep[:, s0 + i, :], B.ident[:, :]), r=[Ep, B.ident], w=[bank])
                    B.act(lambda e, s0=s0, k=k, bank=bank: e.copy(out=BbT[k][:, s0:s0 + 4, :], in_=bank[:, :].rearrange("p (s n) -> p s n", n=128)), r=[bank], w=[BbT[k]])
            B.sy.barrier()
        with ExitStack() as s1:
            selc = B.sb("selc", [16, 8, 128], BF16, s1)
            B.pool(lambda e: e.memset(selc[:], 0.0), w=[selc])
            for q in range(4):
                for g2 in range(2):
                    B.pool(lambda e, q=q, g2=g2: e.tensor_copy(out=selc[:, q * 2 + g2, q * 32 + g2 * 16:q * 32 + g2 * 16 + 16], in_=B.identb[0:16, 0:16]), r=[B.identb], w=[selc])
            F = [B.sb("F", [16, 32, 128], BF16, s1) for _ in range(2)]
            for k, nm in enumerate(("s5_c_re", "s5_c_im")):
                src_v = d[nm].t[m].rearrange("(s g) o p -> g o s p", g=2)
                for g2 in range(2):
                    B.pool(lambda e, g2=g2: e.memset(F[g2][:], 0.0), w=[F[g2]])
                    B.dma("pool", F[g2][:, :, g2 * 64:(g2 + 1) * 64], src_v[g2], w=[F[g2]])
                for s0 in range(0, 32, 4):
                    bank = ps[3 + (s0 // 4) % 2]
                    for i in range(4):
                        s_ = s0 + i
                        q = s_ % 4
                        for g2 in range(2):
                            B.pe(lambda e, s_=s_, i=i, g2=g2, q=q, bank=bank: e.matmul(bank[:, i * 128:(i + 1) * 128], lhsT=F[g2][:, s_, :], rhs=selc[:, q * 2 + g2, :],
                                                                                   start=(g2 == 0), stop=(g2 == 1)), r=[F[g2], selc], w=[bank])
                    B.act(lambda e, s0=s0, k=k, bank=bank: e.activation(out=CTm[k][:, s0:s0 + 4, :], in_=bank[:, :].rearrange("p (s n) -> p s n", n=128), func=AF.Copy,
                                                                        scale=(1.0 if k == 0 else -1.0)), r=[bank], w=[CTm[k]])
            B.sy.barrier()
        s5_body(B, st, sp, bufs, li, m, src, dst, BbT, CTm, CS, SN, prm, Dp, wv, wg, G, Bt)


def range_reduce_sin_signed(B, ang, out, n, tmpf, tmpi):
    range_reduce_sin(B, ang, out, n, tmpf, tmpi)


def s5_alloc(B, st):
    bufs = {}
    bufs["xs"] = [B.sb("xs", [128, D], F32, st) for _ in range(2)]
    bufs["uT"] = B.sb("uT", [128, 8, 128], F32, st)
    bufs["uTb"] = B.sb("uTb", [128, 8, 128], BF16, st)
    bufs["HS"] = [B.sb("HS", [128, 32], F32, st) for _ in range(2)]
    bufs["yT"] = B.sb("yT", [128, 8, 128], F32, st)
    bufs["gt"] = B.sb("gt", [128, D], F32, st)
    bufs["zT"] = B.sb("zT", [128, 8, 128], BF16, st)
    bufs["sgt"] = B.sb("sgt", [128, D], F32, st)
    bufs["hbuf"] = B.sb("hbuf", [128, D], F32, st)
    bufs["y"] = B.sb("y", [128, D], F32, st)
    bufs["st"] = B.sb("st", [128, 8], F32, st)
    bufs["xo"] = [B.sb("xo", [128, D], F32, st) for _ in range(2)]
    return bufs


def s5_body(B, st, sp, bufs, li, m, src, dst, BbT, CTm, CS, SN, prm, Dp, wv, wg, G, Bt):
    cfg, d, ps = B.cfg, B.d, B.ps
    ntp, nt, nsq, srows = cfg.ntp, cfg.nt, cfg.nsq, cfg.srows
    xs, uT, uTb, HS, yT, gt, zT, sgt, hbuf, xo = (bufs[k] for k in ("xs", "uT", "uTb", "HS", "yT", "gt", "zT", "sgt", "hbuf", "xo"))
    SETS = []
    for si in range(2):
        SETS.append(dict(bu=[B.sb("bu", [128, 512], F32, sp) for _ in range(2)], mt=[B.sb("mt", [128, 512], F32, sp) for _ in range(4)],
                         z=[B.sb("z", [128, 512], F32, sp) for _ in range(2)], gs=[B.sb("gs", [128, 512], F32, sp) for _ in range(2)],
                         hh=[B.sb("hh", [128, 512], F32, sp) for _ in range(2)], hb=[B.sb("hb", [128, 512], BF16, sp) for _ in range(2)],
                         bb=([ps[3], ps[4]] if si == 0 else [ps[0], ps[7]]), yb=(ps[5] if si == 0 else ps[6])))
    tmp = dict(y=bufs["y"], junk=gt, st=bufs["st"])
    for t_ in xs + HS:
        B.dve(lambda e, t_=t_: e.memset(t_[:], 0.0), w=[t_])
    ss = ExitStack()
    Hs = Hall = BUs = sm_ = None

    def load(j, buf):
        for (r0, r1), ap, tt in src(j):
            B.dma("sp", buf[r0:r1, :], ap, r=[tt], w=[buf])

    load(0, xs[0])
    for j in range(nt):
        x = xs[j % 2]
        rows = cfg.rows(j)
        sample = (j == ntp)
        N = srows if sample else 128
        if j + 1 < nt:
            load(j + 1, xs[(j + 1) % 2])
        if sample:
            B.sy.barrier()
            sp.close()
            Hs = [B.sb("Hs", [128, 32, nsq], F32, ss) for _ in range(2)]
            Hall = [B.sb("Hall", [128, 32, srows], F32, ss) for _ in range(2)]
            BUs = [B.sb("BUs", [128, 32, srows], F32, ss) for _ in range(2)]
            sm_ = [B.sb("sm_", [128, 32, nsq], F32, ss) for _ in range(4)]
            hin = B.sb("hin", [nsq, 4096], F32, ss)
            for k, nm in enumerate(("state_s5_re", "state_s5_im")):
                B.dma("sp", hin[:, :], d[nm].t[m].rearrange("s g p -> s (g p)"), w=[hin])
                for s_ in range(32):
                    B.pe(lambda e, s_=s_: e.transpose(ps[0][:, s_ * nsq:(s_ + 1) * nsq], hin[:, s_ * 128:(s_ + 1) * 128], B.ident[0:nsq, 0:nsq]), r=[hin, B.ident], w=[ps[0]])
                B.dve(lambda e, k=k: e.tensor_copy(out=Hs[k][:, :, :], in_=ps[0][:, 0:32 * nsq].rearrange("p (s q) -> p s q", q=nsq)), r=[ps[0]], w=[Hs[k]])
        for h2 in range(2):
            for c in range(4):
                B.pe(lambda e, h2=h2, c=c: e.transpose(ps[1 + h2][:, c * 128:(c + 1) * 128], x[:, (h2 * 4 + c) * 128:(h2 * 4 + c + 1) * 128], B.ident[:, :]), r=[x, B.ident], w=[ps[1 + h2]])
            B.act(lambda e, h2=h2: e.copy(out=uT[:, h2 * 4:(h2 + 1) * 4, :], in_=ps[1 + h2][:, :].rearrange("p (c t) -> p c t", t=128)), r=[ps[1 + h2]], w=[uT])
        B.pool(lambda e: e.tensor_copy(out=uTb[:], in_=uT[:]), r=[uT], w=[uTb])
        def chunk_steps(c, S):
            bu, mt, z, gs, hh, hb = S["bu"], S["mt"], S["z"], S["gs"], S["hh"], S["hb"]
            bb, yb = S["bb"], S["yb"]
            for k in range(2):
                for q in range(4):
                    B.pe(lambda e, k=k, q=q: e.matmul(bb[k][:, q * 128:q * 128 + N], lhsT=BbT[k][:, 4 * c + q, :], rhs=uTb[:, c, 0:N], start=True, stop=True),
                         r=[BbT[k], uTb], w=[bb[k]])
            yield
            if sample:
                for k in range(2):
                    B.act(lambda e, k=k: e.copy(out=BUs[k][:, 4 * c:4 * c + 4, :], in_=bb[k][:, :].rearrange("p (q t) -> p q t", t=128)[:, :, 0:N]), r=[bb[k]], w=[BUs[k]])
                yield
                return
            for k in range(2):
                B.act(lambda e, k=k: e.copy(out=bu[k][:], in_=bb[k][:, :]), r=[bb[k]], w=[bu[k]])
            yield
            cs = CS[:, c * 512:(c + 1) * 512]
            sn = SN[:, c * 512:(c + 1) * 512]
            B.dve(lambda e: e.tensor_tensor(out=mt[0][:], in0=bu[0][:], in1=cs, op=ALU.mult), r=[bu[0], CS], w=[mt[0]])
            B.pool(lambda e: e.tensor_tensor(out=mt[1][:], in0=bu[1][:], in1=sn, op=ALU.mult), r=[bu[1], SN], w=[mt[1]])
            B.dve(lambda e: e.tensor_tensor(out=mt[2][:], in0=bu[1][:], in1=cs, op=ALU.mult), r=[bu[1], CS], w=[mt[2]])
            B.pool(lambda e: e.tensor_tensor(out=mt[3][:], in0=bu[0][:], in1=sn, op=ALU.mult), r=[bu[0], SN], w=[mt[3]])
            yield
            B.dve(lambda e: e.tensor_tensor(out=z[0][:], in0=mt[0][:], in1=mt[1][:], op=ALU.add), r=[mt[0], mt[1]], w=[z[0]])
            B.dve(lambda e: e.tensor_tensor(out=z[1][:], in0=mt[2][:], in1=mt[3][:], op=ALU.subtract), r=[mt[2], mt[3]], w=[z[1]])
            yield
            for k in range(2):
                for q in range(4):
                    s_ = 4 * c + q
                    B.dve(lambda e, k=k, q=q, s_=s_: e.tensor_tensor_scan(out=gs[k][:, q * 128:(q + 1) * 128], data0=prm[:, 3, s_:s_ + 1].broadcast_to([128, 128]),
                                                                          data1=z[k][:, q * 128:(q + 1) * 128], initial=HS[k][:, s_:s_ + 1], op0=ALU.mult, op1=ALU.add),
                          r=[prm, z[k], HS[k]], w=[gs[k]])
                yield
            B.dve(lambda e: e.tensor_tensor(out=mt[0][:], in0=gs[0][:], in1=cs, op=ALU.mult), r=[gs[0], CS], w=[mt[0]])
            B.pool(lambda e: e.tensor_tensor(out=mt[1][:], in0=gs[1][:], in1=sn, op=ALU.mult), r=[gs[1], SN], w=[mt[1]])
            B.dve(lambda e: e.tensor_tensor(out=mt[2][:], in0=gs[0][:], in1=sn, op=ALU.mult), r=[gs[0], SN], w=[mt[2]])
            B.pool(lambda e: e.tensor_tensor(out=mt[3][:], in0=gs[1][:], in1=cs, op=ALU.mult), r=[gs[1], CS], w=[mt[3]])
            yield
            B.dve(lambda e: e.tensor_tensor(out=hh[0][:], in0=mt[0][:], in1=mt[1][:], op=ALU.subtract), r=[mt[0], mt[1]], w=[hh[0]])
            B.dve(lambda e: e.tensor_tensor(out=hh[1][:], in0=mt[2][:], in1=mt[3][:], op=ALU.add), r=[mt[2], mt[3]], w=[hh[1]])
            yield
            for k in range(2):
                B.dve(lambda e, k=k: e.tensor_copy(out=HS[k][:, 4 * c:4 * c + 4], in_=hh[k][:].rearrange("p (q t) -> p q t", t=128)[:, :, rows - 1]), r=[hh[k]], w=[HS[k]])
                B.act(lambda e, k=k: e.copy(out=hb[k][:], in_=hh[k][:]), r=[hh[k]], w=[hb[k]])
            yield
            for q in range(4):
                for k in range(2):
                    B.pe(lambda e, q=q, k=k: e.matmul(yb[:, 0:128], lhsT=CTm[k][:, 4 * c + q, :], rhs=hb[k][:, q * 128:(q + 1) * 128],
                                                     start=(q == 0 and k == 0), stop=(q == 3 and k == 1)), r=[CTm[k], hb[k]], w=[yb])
            yield
            B.dve(lambda e: e.scalar_tensor_tensor(out=yT[:, c, :], in0=uT[:, c, :], scalar=Dp[:, c:c + 1], in1=yb[:, 0:128], op0=ALU.mult, op1=ALU.add),
                  r=[uT, Dp, yb], w=[yT])
            yield

        for c0 in range(0, 8, 2):
            gens = [chunk_steps(c0, SETS[0]), chunk_steps(c0 + 1, SETS[1])]
            alive = [True, True]
            while any(alive):
                for gi, g_ in enumerate(gens):
                    if alive[gi]:
                        try:
                            next(g_)
                        except StopIteration:
                            alive[gi] = False
        if sample:
            are = prm[:, 5, :].unsqueeze(2).broadcast_to([128, 32, nsq])
            aim = prm[:, 6, :].unsqueeze(2).broadcast_to([128, 32, nsq])
            for t in range(DEC_SEQ):
                bt = [BUs[k][:].rearrange("p s (q t) -> p s q t", t=DEC_SEQ)[:, :, :, t] for k in range(2)]
                B.dve(lambda e: e.tensor_tensor(out=sm_[0][:], in0=Hs[0][:], in1=are, op=ALU.mult), r=[Hs[0], prm], w=[sm_[0]])
                B.dve(lambda e: e.tensor_tensor(out=sm_[1][:], in0=Hs[1][:], in1=aim, op=ALU.mult), r=[Hs[1], prm], w=[sm_[1]])
                B.dve(lambda e: e.tensor_tensor(out=sm_[2][:], in0=Hs[1][:], in1=are, op=ALU.mult), r=[Hs[1], prm], w=[sm_[2]])
                B.dve(lambda e: e.tensor_tensor(out=sm_[3][:], in0=Hs[0][:], in1=aim, op=ALU.mult), r=[Hs[0], prm], w=[sm_[3]])
                B.dve(lambda e: e.tensor_tensor(out=sm_[0][:], in0=sm_[0][:], in1=sm_[1][:], op=ALU.subtract), r=[sm_[0], sm_[1]], w=[sm_[0]])
                B.dve(lambda e: e.tensor_tensor(out=sm_[2][:], in0=sm_[2][:], in1=sm_[3][:], op=ALU.add), r=[sm_[2], sm_[3]], w=[sm_[2]])
                B.dve(lambda e, bt=bt: e.tensor_tensor(out=Hs[0][:], in0=sm_[0][:], in1=bt[0], op=ALU.add), r=[sm_[0], BUs[0]], w=[Hs[0]])
                B.dve(lambda e, bt=bt: e.tensor_tensor(out=Hs[1][:], in0=sm_[2][:], in1=bt[1], op=ALU.add), r=[sm_[2], BUs[1]], w=[Hs[1]])
                for k in range(2):
                    B.dve(lambda e, k=k, t=t: e.tensor_copy(out=Hall[k][:].rearrange("p s (q t) -> p s q t", t=DEC_SEQ)[:, :, :, t], in_=Hs[k][:]), r=[Hs[k]], w=[Hall[k]])
            hbs = [B.sb("hbs", [128, 32, srows], BF16, ss) for _ in range(2)]
            for k in range(2):
                B.act(lambda e, k=k: e.copy(out=hbs[k][:], in_=Hall[k][:]), r=[Hall[k]], w=[hbs[k]])
            for c in range(8):
                yb = ps[5 + c % 2]
                for q in range(4):
                    for k in range(2):
                        B.pe(lambda e, q=q, k=k, c=c, yb=yb: e.matmul(yb[:, 0:N], lhsT=CTm[k][:, 4 * c + q, :], rhs=hbs[k][:, 4 * c + q, :],
                                                                     start=(q == 0 and k == 0), stop=(q == 3 and k == 1)), r=[CTm[k], hbs[k]], w=[yb])
                B.dve(lambda e, c=c, yb=yb: e.scalar_tensor_tensor(out=yT[:, c, 0:N], in0=uT[:, c, 0:N], scalar=Dp[:, c:c + 1], in1=yb[:, 0:N], op0=ALU.mult, op1=ALU.add),
                      r=[uT, Dp, yb], w=[yT])
        yf = yT[:].rearrange("p c t -> p (c t)")
        B.pool(lambda e: e.tensor_tensor(out=gt[:], in0=yf, in1=yf, op=ALU.mult), r=[yT], w=[gt])
        B.dve(lambda e: e.tensor_scalar(out=gt[:], in0=gt[:], scalar1=0.044715, scalar2=1.0, op0=ALU.mult, op1=ALU.add), r=[gt], w=[gt])
        B.dve(lambda e: e.tensor_tensor(out=gt[:], in0=gt[:], in1=yf, op=ALU.mult), r=[gt, yT], w=[gt])
        B.act(lambda e: e.activation(out=gt[:], in_=gt[:], func=AF.Sigmoid, scale=2.0 * math.sqrt(2.0 / math.pi)), r=[gt], w=[gt])
        B.dve(lambda e: e.tensor_tensor(out=zT[:].rearrange("p c t -> p (c t)"), in0=gt[:], in1=yf, op=ALU.mult), r=[gt, yT], w=[zT])
        for hf in range(2):
            B.linear(ps[1 + hf], zT, wg, hf * 512, (hf + 1) * 512, 8)
            B.act(lambda e, hf=hf: e.activation(out=sgt[:, hf * 512:(hf + 1) * 512], in_=ps[1 + hf][:, :], func=AF.Sigmoid), r=[ps[1 + hf]], w=[sgt])
            B.linear(ps[6 + hf], zT, wv, hf * 512, (hf + 1) * 512, 8)
            B.dve(lambda e, hf=hf: e.tensor_tensor(out=hbuf[:, hf * 512:(hf + 1) * 512], in0=ps[6 + hf][:, :], in1=sgt[:, hf * 512:(hf + 1) * 512], op=ALU.mult),
                  r=[ps[6 + hf], sgt], w=[hbuf])
        o = xo[j % 2]
        resid_ln_sb(B, x, hbuf, G, Bt, o, tmp)
        for (r0, r1), ap, tt in dst(j):
            B.dma("sp", ap, o[r0:r1, :], r=[o], w=[tt])
    for k, nm in enumerate(("re_p", "im_p")):
        B.pe(lambda e, k=k: e.transpose(ps[0][0:32, k * 128:(k + 1) * 128], HS[k][:, :], B.ident[:, :]), r=[HS[k], B.ident], w=[ps[0]])
    hso = B.sb("hso", [32, 256], F32, ss)
    B.dve(lambda e: e.tensor_copy(out=hso[:], in_=ps[0][0:32, 0:256]), r=[ps[0]], w=[hso])
    for k, nm in enumerate(("re_p", "im_p")):
        B.dma("sp", d[nm].t[m].rearrange("(s g) p -> s (g p)", g=2), hso[:, k * 128:(k + 1) * 128], r=[hso], w=[d[nm]])
    hout = hin
    for k, nm in enumerate(("re_s", "im_s")):
        for s0 in range(0, 32, 4):
            bank = ps[1 + (s0 // 4) % 2]
            for i in range(4):
                B.pe(lambda e, k=k, s0=s0, i=i, bank=bank: e.transpose(bank[0:nsq, i * 128:(i + 1) * 128], Hs[k][:, s0 + i, :], B.ident[:, :]), r=[Hs[k], B.ident], w=[bank])
            B.act(lambda e, s0=s0, bank=bank: e.copy(out=hout[:, s0 * 128:(s0 + 4) * 128], in_=bank[0:nsq, :]), r=[bank], w=[hout])
        B.dma("sp", d[nm].t[m].rearrange("s g p -> s (g p)"), hout[:, :], r=[hout], w=[d[nm]])
    B.sy.barrier()
    ss.close()


def stage_rwkv(B, li, m, src, dst):
    cfg, d, ps = B.cfg, B.d, B.ps
    ntp, nt, nsq, srows = cfg.ntp, cfg.nt, cfg.nsq, cfg.srows
    Xin, Xint = src.X, src.Xt
    if not hasattr(B, "RWd"):
        B.RWd = B.dram_scr("RWd", [6, 128, D], F32)
        B.YSd = B.dram_scr("YSd", [128, D], F32)
    RWd, YSd = B.RWd, B.YSd
    with ExitStack() as st:
        W = {}
        for nm in ("wr", "wk", "wv", "wo"):
            W[nm] = B.sb(nm, [128, 8, D], BF16, st)
            B.load_w(W[nm], d["rw_" + nm].t[m])
        for nm, n in (("w1", 64), ("a1", 64), ("g1", 128)):
            W[nm] = B.sb(nm, [128, 8, n], BF16, st)
            B.load_w(W[nm], d["rw_" + nm].t[m])
        for nm, k in (("w2", 64), ("a2", 64), ("g2", 128)):
            W[nm] = B.sb(nm, [k, 1, D], BF16, st)
            B.load_w(W[nm], d["rw_" + nm].t[m])
        MU = B.sb("MU", [128, 6, 8], F32, st)
        B.load_T(MU, MU[:, :, :].rearrange("p j c -> p (j c)"), d["rw_mu"].t[m].rearrange("j (c p) -> (j c) p", p=128), 48)
        R_ = {}
        for nm in ("w0", "a0", "k_k", "k_a", "r_k", "lnx_g", "lnx_b"):
            R_[nm] = B.sb("r_" + nm, [128, D], F32, st)
            B.load_bcast(R_[nm], d["rw_" + nm].t[m])
        G = B.sb("G", [128, D], F32, st)
        Bt = B.sb("Bt", [128, D], F32, st)
        B.load_bcast(G, d["ln_g"].t[li, 1])
        B.load_bcast(Bt, d["ln_b"].t[li, 1])
        tri = B.sb("tri", [128, 128], F32, st)
        m2 = B.sb("m2", [128, 256], F32, st)
        sl = B.sb("sl", [128, 128], F32, st)
        B.pool(lambda e: e.memset(tri[:], 1.0), w=[tri])
        B.pool(lambda e: e.affine_select(out=tri[:], in_=tri[:], pattern=[[1, 128]], compare_op=ALU.is_ge, fill=0.0, base=0, channel_multiplier=-1), r=[tri], w=[tri])
        B.pool(lambda e: e.memset(m2[:], 1.0), w=[m2])
        B.pool(lambda e: e.affine_select(out=m2[:, 0:128], in_=m2[:, 0:128], pattern=[[1, 128]], compare_op=ALU.is_gt, fill=0.0, base=0, channel_multiplier=-1), r=[m2], w=[m2])
        B.pool(lambda e: e.affine_select(out=m2[:, 128:256], in_=m2[:, 128:256], pattern=[[1, 128]], compare_op=ALU.is_ge, fill=0.0, base=0, channel_multiplier=-1), r=[m2], w=[m2])
        B.pool(lambda e: e.memset(sl[:], 1.0), w=[sl])
        B.pool(lambda e: e.affine_select(out=sl[:], in_=sl[:], pattern=[[-1, 128]], compare_op=ALU.is_gt, fill=0.0, base=0, channel_multiplier=1), r=[sl], w=[sl])
        vmask = B.sb("vmask", [128, 1], F32, st)
        B.pool(lambda e: e.memset(vmask[:], 1.0), w=[vmask])
        B.pool(lambda e: e.affine_select(out=vmask[:], in_=vmask[:], pattern=[[0, 1]], compare_op=ALU.is_gt, fill=0.0, base=cfg.rows(ntp - 1), channel_multiplier=-1), r=[vmask], w=[vmask])
        ST = B.sb("ST", [64, NH, 64], F32, st)
        STb = B.sb("STb", [64, NH, 64], BF16, st)
        B.dve(lambda e: e.memset(ST[:], 0.0), w=[ST])
        B.dve(lambda e: e.memset(STb[:], 0.0), w=[STb])

        s2 = ExitStack()
        f32 = lambda nm: B.sb(nm, [128, D], F32, s2)
        x, xp, t0, t1 = f32("x"), f32("xp"), f32("t0"), f32("t1")
        r32, k32, v32, a32, ld, kk = f32("r32"), f32("k32"), f32("v32"), f32("a32"), f32("ld"), f32("kk")
        g32 = xp
        xT = B.sb("xT", [128, 8, 128], F32, s2)
        xxT = B.sb("xxT", [128, 8, 128], F32, s2)
        mixT = [B.sb("mixT", [128, 8, 128], BF16, s2) for _ in range(2)]
        lo = B.sb("lo", [128, 128], BF16, s2)
        loT = B.sb("loT", [128, 1, 128], BF16, s2)
        ss16 = B.sb("ss16", [128, 4, NH], F32, s2)
        bfs = {nm: B.sb(nm, [128, D], BF16, s2) for nm in ("rt", "at", "bt", "kt", "vb")}
        tmp = dict(xa=t0, y=t1, junk=kk, st=B.sb("st", [128, 8], F32, s2))
        xo = [B.sb("xo", [128, D], F32, s2)] * 2

        def mix(jx, buf):
            B.pool(lambda e: e.tensor_tensor(out=t0[:].rearrange("p (c t) -> p c t", t=128), in0=xxT[:], in1=MU[:, jx, :].unsqueeze(2).broadcast_to([128, 8, 128]), op=ALU.mult),
                   r=[xxT, MU], w=[t0])
            B.dve(lambda e: e.tensor_tensor(out=buf[:], in0=t0[:].rearrange("p (c t) -> p c t", t=128), in1=xT[:], op=ALU.add), r=[t0, xT], w=[buf])

        def proj_full(jx, wname, out32):
            buf = mixT[jx % 2]
            mix(jx, buf)
            for hf in range(2):
                B.linear(ps[1 + hf], buf, W[wname], hf * 512, (hf + 1) * 512, 8)
                B.act(lambda e, hf=hf: e.copy(out=out32[:, hf * 512:(hf + 1) * 512], in_=ps[1 + hf][:, :]), r=[ps[1 + hf]], w=[out32])

        def proj_lora(jx, w1n, w2n, n1, mid_func, bias_row, out_func, out32, scale=1.0):
            buf = mixT[jx % 2]
            mix(jx, buf)
            B.linear(ps[3], buf, W[w1n], 0, n1, 8)
            B.act(lambda e: e.activation(out=lo[:, 0:n1], in_=ps[3][:, 0:n1], func=mid_func), r=[ps[3]], w=[lo])
            B.transpose_to(lo, 1, loT, loT[0:n1, :, :], ps[0], cw=n1)
            for hf in range(2):
                B.pe(lambda e, hf=hf: e.matmul(ps[1 + hf][:, :], lhsT=loT[0:n1, 0, :], rhs=W[w2n][0:n1, 0, hf * 512:(hf + 1) * 512], start=True, stop=True),
                     r=[loT, W[w2n]], w=[ps[1 + hf]])
                if bias_row is not None:
                    B.dve(lambda e, hf=hf: e.tensor_tensor(out=out32[:, hf * 512:(hf + 1) * 512], in0=ps[1 + hf][:, :], in1=bias_row[:, hf * 512:(hf + 1) * 512], op=ALU.add),
                          r=[ps[1 + hf], bias_row], w=[out32])
                    B.act(lambda e, hf=hf: e.activation(out=out32[:, hf * 512:(hf + 1) * 512], in_=out32[:, hf * 512:(hf + 1) * 512], func=out_func), r=[out32], w=[out32])
                else:
                    B.act(lambda e, hf=hf: e.copy(out=out32[:, hf * 512:(hf + 1) * 512], in_=ps[1 + hf][:, :]), r=[ps[1 + hf]], w=[out32])

        def v3(t_, n=64):
            return t_[:].rearrange("p (h k) -> p h k", k=n)

        def bc16(col_ap):
            return col_ap.unsqueeze(2).broadcast_to([128, NH, 64])

        def front(j):
            rows = cfg.rows(j)
            base = 128 * j
            sample = (j == ntp)
            if rows < 128:
                B.dve(lambda e: e.memset(x[:], 0.0), w=[x])
            B.dve(lambda e: e.memset(xp[:], 0.0), w=[xp])
            B.dma("sp", x[0:rows, :], Xin[base:base + rows, :], r=[Xint[j]], w=[x])
            if not sample:
                if j > 0:
                    B.dma("sp", xp[0:rows, :], Xin[base - 1:base - 1 + rows, :], r=[Xint[j], Xint[j - 1]], w=[xp])
                else:
                    B.dma("sp", xp[1:rows, :], Xin[0:rows - 1, :], r=[Xint[j]], w=[xp])
            else:
                xv = Xin[base:base + rows, :].rearrange("(s t) n -> s t n", t=DEC_SEQ)
                for s_ in range(nsq):
                    B.dma("sp", xp[s_ * DEC_SEQ:s_ * DEC_SEQ + 1, :], d["state_rwkv_shift"].t[m, s_:s_ + 1, :], w=[xp])
                    B.dma("sp", xp[s_ * DEC_SEQ + 1:(s_ + 1) * DEC_SEQ, :], xv[s_, 0:DEC_SEQ - 1, :], r=[Xint[j]], w=[xp])
            B.dve(lambda e: e.tensor_tensor(out=xp[:], in0=xp[:], in1=x[:], op=ALU.subtract), r=[xp, x], w=[xp])
            for src_, dstT in ((x, xT), (xp, xxT)):
                for h2 in range(2):
                    for c in range(4):
                        B.pe(lambda e, h2=h2, c=c, src_=src_: e.transpose(ps[4 + h2][:, c * 128:(c + 1) * 128], src_[:, (h2 * 4 + c) * 128:(h2 * 4 + c + 1) * 128], B.ident[:, :]),
                             r=[src_, B.ident], w=[ps[4 + h2]])
                    B.act(lambda e, h2=h2, dstT=dstT: e.copy(out=dstT[:, h2 * 4:(h2 + 1) * 4, :], in_=ps[4 + h2][:, :].rearrange("p (c t) -> p c t", t=128)), r=[ps[4 + h2]], w=[dstT])
            proj_full(0, "wr", r32)
            proj_lora(1, "w1", "w2", 64, AF.Tanh, R_["w0"], AF.Sigmoid, ld)
            proj_full(2, "wk", k32)
            proj_full(3, "wv", v32)
            proj_lora(4, "a1", "a2", 64, AF.Copy, R_["a0"], AF.Sigmoid, a32)
            proj_lora(5, "g1", "g2", 128, AF.Sigmoid, None, None, g32)
            B.act(lambda e: e.activation(out=ld[:], in_=ld[:], func=AF.Copy, scale=-math.exp(-0.5)), r=[ld], w=[ld])
            if rows < 128 and not sample:
                B.dve(lambda e: e.tensor_scalar(out=ld[:], in0=ld[:], scalar1=vmask[:, 0:1], scalar2=None, op0=ALU.mult), r=[ld, vmask], w=[ld])
            B.dve(lambda e: e.tensor_tensor(out=kk[:], in0=k32[:], in1=R_["k_k"][:], op=ALU.mult), r=[k32, R_["k_k"]], w=[kk])
            B.pool(lambda e: e.tensor_tensor(out=t0[:], in0=kk[:], in1=kk[:], op=ALU.mult), r=[kk], w=[t0])
            B.dve(lambda e: e.tensor_reduce(out=ss16[:, 0, :], in_=v3(t0), axis=AX.X, op=ALU.add), r=[t0], w=[ss16])
            B.dve(lambda e: e.tensor_scalar(out=ss16[:, 0, :], in0=ss16[:, 0, :], scalar1=1e-24, scalar2=None, op0=ALU.max), r=[ss16], w=[ss16])
            B.act(lambda e: e.activation(out=ss16[:, 0, :], in_=ss16[:, 0, :], func=AF.Sqrt), r=[ss16], w=[ss16])
            B.dve(lambda e: e.reciprocal(out=ss16[:, 0, :], in_=ss16[:, 0, :]), r=[ss16], w=[ss16])
            B.dve(lambda e: e.tensor_tensor(out=v3(kk), in0=v3(kk), in1=bc16(ss16[:, 0, :]), op=ALU.mult), r=[kk, ss16], w=[kk])
            B.dve(lambda e: e.scalar_tensor_tensor(out=t0[:], in0=a32[:], scalar=-1.0, in1=R_["k_a"][:], op0=ALU.add, op1=ALU.mult), r=[a32, R_["k_a"]], w=[t0])
            B.dve(lambda e: e.scalar_tensor_tensor(out=k32[:], in0=t0[:], scalar=1.0, in1=k32[:], op0=ALU.add, op1=ALU.mult), r=[t0, k32], w=[k32])
            B.pool(lambda e: e.tensor_tensor(out=t0[:], in0=r32[:], in1=k32[:], op=ALU.mult), r=[r32, k32], w=[t0])
            B.dve(lambda e: e.tensor_tensor(out=t0[:], in0=t0[:], in1=R_["r_k"][:], op=ALU.mult), r=[t0, R_["r_k"]], w=[t0])
            B.dve(lambda e: e.tensor_reduce(out=ss16[:, 1, :], in_=v3(t0), axis=AX.X, op=ALU.add), r=[t0], w=[ss16])
            B.dve(lambda e: e.tensor_tensor(out=v3(t0), in0=v3(v32), in1=bc16(ss16[:, 1, :]), op=ALU.mult), r=[v32, ss16], w=[t0])
            B.pool(lambda e: e.tensor_tensor(out=t1[:], in0=kk[:], in1=a32[:], op=ALU.mult), r=[kk, a32], w=[t1])

        def post(j, y32):
            B.dve(lambda e: e.tensor_reduce(out=ss16[:, 2, :], in_=v3(y32), axis=AX.X, op=ALU.add), r=[y32], w=[ss16])
            B.dve(lambda e: e.tensor_scalar(out=ss16[:, 2, :], in0=ss16[:, 2, :], scalar1=-1.0 / 64, scalar2=None, op0=ALU.mult), r=[ss16], w=[ss16])
            B.dve(lambda e: e.tensor_tensor(out=v3(y32), in0=v3(y32), in1=bc16(ss16[:, 2, :]), op=ALU.add), r=[y32, ss16], w=[y32])
            B.pool(lambda e: e.tensor_tensor(out=t1[:], in0=y32[:], in1=y32[:], op=ALU.mult), r=[y32], w=[t1])
            B.dve(lambda e: e.tensor_reduce(out=ss16[:, 3, :], in_=v3(t1), axis=AX.X, op=ALU.add), r=[t1], w=[ss16])
            B.act(lambda e: e.activation(out=ss16[:, 3, :], in_=ss16[:, 3, :], func=AF.Sqrt, bias=B.eps[:, 2:3], scale=1.0 / 64), r=[ss16, B.eps], w=[ss16])
            B.dve(lambda e: e.reciprocal(out=ss16[:, 3, :], in_=ss16[:, 3, :]), r=[ss16], w=[ss16])
            B.dve(lambda e: e.tensor_tensor(out=v3(y32), in0=v3(y32), in1=bc16(ss16[:, 3, :]), op=ALU.mult), r=[y32, ss16], w=[y32])
            B.dve(lambda e: e.tensor_tensor(out=y32[:], in0=y32[:], in1=R_["lnx_g"][:], op=ALU.mult), r=[y32, R_["lnx_g"]], w=[y32])
            B.pool(lambda e: e.tensor_tensor(out=y32[:], in0=y32[:], in1=R_["lnx_b"][:], op=ALU.add), r=[y32, R_["lnx_b"]], w=[y32])
            B.dve(lambda e: e.tensor_tensor(out=y32[:], in0=y32[:], in1=t0[:], op=ALU.add), r=[y32, t0], w=[y32])
            yg = bfs["rt"]
            B.dve(lambda e: e.tensor_tensor(out=yg[:], in0=y32[:], in1=g32[:], op=ALU.mult), r=[y32, g32], w=[yg])
            buf = mixT[0]
            B.transpose_to(yg, 8, buf, buf[:, :, :], ps[0])
            for hf in range(2):
                B.linear(ps[6 + hf], buf, W["wo"], hf * 512, (hf + 1) * 512, 8)
            o = xo[j % 2]
            B.resid_ln(x, [ps[6], ps[7]], 1.0, G, Bt, o, tmp)
            for (r0, r1), ap, tt in dst(j):
                B.dma("sp", ap, o[r0:r1, :], r=[o], w=[tt])

        rwkv_prompt_loop(B, m, front, post, locals())
        front(ntp)
        B.act(lambda e: e.activation(out=ld[:], in_=ld[:], func=AF.Exp), r=[ld], w=[ld])
        B.dve(lambda e: e.tensor_scalar(out=kk[:], in0=kk[:], scalar1=-1.0, scalar2=None, op0=ALU.mult), r=[kk], w=[kk])
        for i, t_ in enumerate((r32, ld, k32, v32, kk, t1)):
            B.dma("sp", RWd.t[i, 0:srows, :], t_[0:srows, :], r=[t_], w=[RWd])
        rwkv_sample_rec(B, m, RWd, YSd)
        y32 = r32
        B.dma("sp", y32[0:srows, :], YSd.t[0:srows, :], r=[YSd], w=[y32])
        post(ntp, y32)
        B.dma("sp", d["sh_p"].t[m:m + 1, :], Xin[cfg.L - 1:cfg.L, :], r=[Xint[ntp - 1]], w=[d["sh_p"]])
        B.dma("sp", d["sh_s"].t[m], Xin[128 * ntp:128 * ntp + srows, :].rearrange("(s t) n -> s t n", t=DEC_SEQ)[:, DEC_SEQ - 1, :], r=[Xint[ntp]], w=[d["sh_s"]])
        B.sy.barrier()
        s2.close()


def rwkv_prompt_loop(B, m, front, post, L):
    cfg, d, ps = B.cfg, B.d, B.ps
    ntp = cfg.ntp
    r32, k32, v32, ld, kk, t0, t1, a32 = (L[k] for k in ("r32", "k32", "v32", "ld", "kk", "t0", "t1", "a32"))
    bfs, tri, m2, sl, ST, STb = (L[k] for k in ("bfs", "tri", "m2", "sl", "ST", "STb"))
    with ExitStack() as s3:
        HT = B.sb("HT", [64, NH, 4, 128], BF16, s3)
        WC = B.sb("WC", [64, NH], F32, s3)
        HB = []
        for par in range(2):
            HB.append((B.sb("AKm", [128, 256], BF16, s3), B.sb("ABm", [128, 256], BF16, s3),
                       [B.sb("NX", [128, 256], BF16, s3) for _ in range(2)], [B.sb("NT", [128, 128], BF16, s3) for _ in range(2)],
                       B.sb("Zb", [128, 64], BF16, s3), B.sb("Ub", [128, 64], BF16, s3)))
        ones1 = B.ones
        for j in range(ntp):
            front(j)
            for hf in range(2):
                B.pe(lambda e, hf=hf: e.matmul(ps[1 + hf][:, :], lhsT=tri[:, :], rhs=ld[:, hf * 512:(hf + 1) * 512], start=True, stop=True), r=[tri, ld], w=[ps[1 + hf]])
            for h in range(NH):
                B.pe(lambda e, h=h: e.matmul(ps[3][0:64, h:h + 1], lhsT=ld[:, h * 64:(h + 1) * 64], rhs=ones1[:, 0:1], start=True, stop=True), r=[ld, ones1], w=[ps[3]])
            B.act(lambda e: e.activation(out=WC[:, :], in_=ps[3][0:64, 0:NH], func=AF.Exp), r=[ps[3]], w=[WC])
            y32 = a32
            for hf in range(2):
                sl_ = slice(hf * 512, (hf + 1) * 512)
                cum = ps[1 + hf]
                B.act(lambda e, sl_=sl_, cum=cum: e.activation(out=y32[:, sl_], in_=cum[:, :], func=AF.Exp), r=[cum], w=[y32])
                B.dve(lambda e, sl_=sl_: e.tensor_tensor(out=bfs["rt"][:, sl_], in0=r32[:, sl_], in1=y32[:, sl_], op=ALU.mult), r=[r32, y32], w=[bfs["rt"]])
                B.act(lambda e, sl_=sl_, cum=cum: e.activation(out=y32[:, sl_], in_=cum[:, :], func=AF.Exp, scale=-1.0), r=[cum], w=[y32])
                B.dve(lambda e, sl_=sl_: e.tensor_tensor(out=bfs["kt"][:, sl_], in0=k32[:, sl_], in1=y32[:, sl_], op=ALU.mult), r=[k32, y32], w=[bfs["kt"]])
                B.pool(lambda e, sl_=sl_: e.tensor_tensor(out=bfs["bt"][:, sl_], in0=t1[:, sl_], in1=y32[:, sl_], op=ALU.mult), r=[t1, y32], w=[bfs["bt"]])
                B.dve(lambda e, sl_=sl_, cum=cum: e.tensor_tensor(out=y32[:, sl_], in0=cum[:, :], in1=ld[:, sl_], op=ALU.subtract), r=[cum, ld], w=[y32])
                B.act(lambda e, sl_=sl_: e.activation(out=y32[:, sl_], in_=y32[:, sl_], func=AF.Exp), r=[y32], w=[y32])
                B.dve(lambda e, sl_=sl_: e.scalar_tensor_tensor(out=bfs["at"][:, sl_], in0=kk[:, sl_], scalar=-1.0, in1=y32[:, sl_], op0=ALU.mult, op1=ALU.mult), r=[kk, y32], w=[bfs["at"]])
            B.pool(lambda e: e.tensor_copy(out=bfs["vb"][:], in_=v32[:]), r=[v32], w=[bfs["vb"]])
            for h in range(NH):
                bi = 4 + h % 2
                pv = B.psb(bi)
                for i, nm in enumerate(("at", "rt", "kt", "bt")):
                    B.pe(lambda e, h=h, i=i, nm=nm, pv=pv: e.transpose(pv[0:64, i * 128:(i + 1) * 128], bfs[nm][:, h * 64:(h + 1) * 64], B.identb[:, :]), r=[bfs[nm], B.identb], w=[ps[bi]])
                B.act(lambda e, h=h, pv=pv: e.copy(out=HT[:, h, :, :], in_=pv[0:64, 0:512].rearrange("p (i t) -> p i t", t=128)), r=[ps[bi]], w=[HT])
            def head_steps(h, par):
                bA, bB, bM = (ps[1], ps[2], ps[3]) if par == 0 else (ps[4], ps[5], ps[0])
                AKm, ABm, NX, NT, Zb, Ub = HB[par]
                hs = slice(h * 64, (h + 1) * 64)
                rhsAR = HT[:, h, 0:2, :].rearrange("p i t -> p (i t)")
                B.pe(lambda e: e.matmul(bA[:, 0:256], lhsT=HT[:, h, 2, :], rhs=rhsAR, start=True, stop=True), r=[HT], w=[bA])
                B.pe(lambda e: e.matmul(bB[:, 0:256], lhsT=HT[:, h, 3, :], rhs=rhsAR, start=True, stop=True), r=[HT], w=[bB])
                B.pe(lambda e: e.matmul(bM[:, 0:128], lhsT=HT[:, h, 0, :], rhs=HT[:, h, 3, :], start=True, stop=True), r=[HT], w=[bM])
                yield
                B.dve(lambda e: e.tensor_tensor(out=AKm[:], in0=bA[:, 0:256], in1=m2[:], op=ALU.mult), r=[bA, m2], w=[AKm])
                B.dve(lambda e: e.tensor_tensor(out=ABm[:], in0=bB[:, 0:256], in1=m2[:], op=ALU.mult), r=[bB, m2], w=[ABm])
                B.dve(lambda e: e.tensor_tensor(out=NT[0][:], in0=bM[:, 0:128], in1=sl[:], op=ALU.mult), r=[bM, sl], w=[NT[0]])
                yield
                B.pool(lambda e: e.tensor_copy(out=NX[0][:, 0:128], in_=ABm[:, 0:128]), r=[ABm], w=[NX[0]])
                B.pool(lambda e: e.tensor_tensor(out=NX[0][:, 128:256], in0=ABm[:, 0:128], in1=B.identb[:, :], op=ALU.add), r=[ABm, B.identb], w=[NX[0]])
                yield
                cur = 0
                for lvl in range(7):
                    nx, nt_ = NX[cur], NT[cur]
                    nx2, nt2 = NX[1 - cur], NT[1 - cur]
                    if lvl == 0:
                        B.pe(lambda e: e.matmul(bA[:, 0:128], lhsT=nt_[:, :], rhs=nx[:, 0:128], start=True, stop=True), r=[nt_, nx], w=[bA])
                        B.pe(lambda e: e.matmul(bB[:, 0:128], lhsT=nx[:, 0:128], rhs=nt_[:, :], start=True, stop=True), r=[nt_, nx], w=[bB])
                        yield
                        B.act(lambda e: e.copy(out=nx2[:, 0:128], in_=bA[:, 0:128]), r=[bA], w=[nx2])
                        B.dve(lambda e: e.tensor_copy(out=nx2[:, 128:256], in_=nx[:, 128:256]), r=[nx], w=[nx2])
                        B.act(lambda e: e.copy(out=nt2[:, :], in_=bB[:, 0:128]), r=[bB], w=[nt2])
                    elif lvl < 6:
                        B.pe(lambda e: e.matmul(bA[:, 0:256], lhsT=nt_[:, :], rhs=nx[:, :], start=True, stop=True), r=[nt_, nx], w=[bA])
                        B.pe(lambda e: e.matmul(bB[:, 0:128], lhsT=nx[:, 0:128], rhs=nt_[:, :], start=True, stop=True), r=[nt_, nx], w=[bB])
                        yield
                        B.act(lambda e: e.copy(out=nx2[:, 0:128], in_=bA[:, 0:128]), r=[bA], w=[nx2])
                        B.dve(lambda e: e.tensor_tensor(out=nx2[:, 128:256], in0=bA[:, 128:256], in1=nx[:, 128:256], op=ALU.add), r=[bA, nx], w=[nx2])
                        B.act(lambda e: e.copy(out=nt2[:, :], in_=bB[:, 0:128]), r=[bB], w=[nt2])
                    else:
                        B.pe(lambda e: e.matmul(bA[:, 0:128], lhsT=nt_[:, :], rhs=nx[:, 128:256], start=True, stop=True), r=[nt_, nx], w=[bA])
                        yield
                        B.dve(lambda e: e.tensor_tensor(out=nx2[:, 128:256], in0=bA[:, 0:128], in1=nx[:, 128:256], op=ALU.add), r=[bA, nx], w=[nx2])
                    cur = 1 - cur
                    yield
                XT = NX[cur]
                B.pe(lambda e: e.matmul(bM[:, 0:64], lhsT=HT[:, h, 0, :], rhs=STb[:, h, :], start=True, stop=False), r=[HT, STb], w=[bM])
                B.pe(lambda e: e.matmul(bM[:, 0:64], lhsT=AKm[:, 0:128], rhs=bfs["vb"][:, hs], start=False, stop=True), r=[AKm, bfs["vb"]], w=[bM])
                yield
                B.act(lambda e: e.copy(out=Zb[:, :], in_=bM[:, 0:64]), r=[bM], w=[Zb])
                yield
                B.pe(lambda e: e.matmul(bM[:, 64:128], lhsT=XT[:, 128:256], rhs=Zb[:, :], start=True, stop=True), r=[XT, Zb], w=[bM])
                yield
                B.act(lambda e: e.copy(out=Ub[:, :], in_=bM[:, 64:128]), r=[bM], w=[Ub])
                yield
                yb = ps[6 + (h // 8) % 2]
                yc = slice((h % 8) * 64, (h % 8 + 1) * 64)
                B.pe(lambda e: e.matmul(yb[:, yc], lhsT=HT[:, h, 1, :], rhs=STb[:, h, :], start=True, stop=False), r=[HT, STb], w=[yb])
                B.pe(lambda e: e.matmul(yb[:, yc], lhsT=ABm[:, 128:256], rhs=Ub[:, :], start=False, stop=False), r=[ABm, Ub], w=[yb])
                B.pe(lambda e: e.matmul(yb[:, yc], lhsT=AKm[:, 128:256], rhs=bfs["vb"][:, hs], start=False, stop=True), r=[AKm, bfs["vb"]], w=[yb])
                B.pe(lambda e: e.matmul(bM[0:64, 128:192], lhsT=bfs["bt"][:, hs], rhs=Ub[:, :], start=True, stop=False), r=[bfs["bt"], Ub], w=[bM])
                B.pe(lambda e: e.matmul(bM[0:64, 128:192], lhsT=bfs["kt"][:, hs], rhs=bfs["vb"][:, hs], start=False, stop=True), r=[bfs["kt"], bfs["vb"]], w=[bM])
                yield
                B.dve(lambda e: e.tensor_scalar(out=ST[:, h, :], in0=ST[:, h, :], scalar1=WC[:, h:h + 1], scalar2=None, op0=ALU.mult), r=[ST, WC], w=[ST])
                B.dve(lambda e: e.scalar_tensor_tensor(out=ST[:, h, :], in0=bM[0:64, 128:192], scalar=WC[:, h:h + 1], in1=ST[:, h, :], op0=ALU.mult, op1=ALU.add), r=[bM, WC, ST], w=[ST])
                B.act(lambda e: e.copy(out=STb[:, h, :], in_=ST[:, h, :]), r=[ST], w=[STb])
                yield

            for h0 in range(0, NH, 2):
                gens = [head_steps(h0, 0), head_steps(h0 + 1, 1)]
                alive = [True, True]
                while any(alive):
                    for gi, g_ in enumerate(gens):
                        if alive[gi]:
                            try:
                                next(g_)
                            except StopIteration:
                                alive[gi] = False
                if h0 % 8 == 6:
                    yb = ps[6 + (h0 // 8) % 2]
                    B.act(lambda e, yb=yb, h0=h0: e.copy(out=y32[:, (h0 - 6) * 64:(h0 + 2) * 64], in_=yb[:, :]), r=[yb], w=[y32])
            post(j, y32)
        so = T(HT.t[:].rearrange("p h i t -> p (h i t)").bitcast(F32)[:, 0:NH * 64].rearrange("p (h k) -> p h k", k=64), "so")
        so.wr, so.rd = HT.wr, HT.rd
        for h0 in range(0, NH, 8):
            bank = ps[1 + (h0 // 8) % 2]
            for i in range(8):
                B.pe(lambda e, h0=h0, i=i, bank=bank: e.transpose(bank[0:64, i * 64:(i + 1) * 64], ST[:, h0 + i, :], B.ident[0:64, 0:64]), r=[ST, B.ident], w=[bank])
            B.act(lambda e, h0=h0, bank=bank: e.copy(out=so[:, h0:h0 + 8, :], in_=bank[0:64, :].rearrange("p (h k) -> p h k", k=64)), r=[bank], w=[so])
        B.dma("sp", d["wkv_p"].t[m].rearrange("h v k -> v h k"), so[:, :, :], r=[so], w=[d["wkv_p"]])
        B.sy.barrier()


def rwkv_sample_rec(B, m, RWd, YSd):
    cfg, d, ps = B.cfg, B.d, B.ps
    nsq, srows = cfg.nsq, cfg.srows
    P = nsq * 8
    VS = 8
    with ExitStack() as s3:
        vec = [B.sb("vec", [P, DEC_SEQ, 128], F32, s3) for _ in range(6)]
        for i in range(6):
            for s_ in range(nsq):
                B.dma("sp", vec[i][s_ * 8:(s_ + 1) * 8, :, :], RWd.t[i, s_ * DEC_SEQ:(s_ + 1) * DEC_SEQ, :].rearrange("t (g c) -> g t c", c=128), r=[RWd], w=[vec[i]])
        Yall = B.sb("Yall", [P, DEC_SEQ, 128], F32, s3)
        S = B.sb("S", [P, VS, 64], F32, s3)
        tmp = B.sb("tmp", [P, VS, 64], F32, s3)
        sa = B.sb("sa", [P, VS], F32, s3)
        wkv_in = d["state_rwkv_wkv"].t[m].rearrange("s (g h2) v k -> (s g) h2 v k", h2=2)
        wkv_out = d["wkv_s"].t[m].rearrange("s (g h2) v k -> (s g) h2 v k", h2=2)
        n = 0
        for h2 in range(2):
            kvec = lambda i, t: vec[i][:, t, h2 * 64:(h2 + 1) * 64].unsqueeze(1).broadcast_to([P, VS, 64])
            for v0 in range(0, 64, VS):
                B.dma("sp", S[:, :, :], wkv_in[:, h2, v0:v0 + VS, :], w=[S])
                for t in range(DEC_SEQ):
                    vv = vec[3][:, t, h2 * 64 + v0:h2 * 64 + v0 + VS]
                    e1, e2 = ("dve", "pool") if n % 2 == 0 else ("pool", "dve")
                    n += 1
                    B.dve(lambda e: e.tensor_tensor(out=tmp[:], in0=S[:], in1=kvec(4, t), op=ALU.mult), r=[S, vec[4]], w=[tmp])
                    B.dve(lambda e: e.tensor_reduce(out=sa[:, :], in_=tmp[:], axis=AX.X, op=ALU.add), r=[tmp], w=[sa])
                    B.dve(lambda e: e.tensor_tensor(out=S[:], in0=S[:], in1=kvec(1, t), op=ALU.mult), r=[S, vec[1]], w=[S])
                    B.dve(lambda e: e.tensor_tensor(out=tmp[:], in0=sa[:, :].unsqueeze(2).broadcast_to([P, VS, 64]), in1=kvec(5, t), op=ALU.mult), r=[sa, vec[5]], w=[tmp])
                    B.dve(lambda e: e.tensor_tensor(out=S[:], in0=S[:], in1=tmp[:], op=ALU.add), r=[S, tmp], w=[S])
                    B.dve(lambda e, vv=vv: e.tensor_tensor(out=tmp[:], in0=vv.unsqueeze(2).broadcast_to([P, VS, 64]), in1=kvec(2, t), op=ALU.mult), r=[vec[3], vec[2]], w=[tmp])
                    B.dve(lambda e: e.tensor_tensor(out=S[:], in0=S[:], in1=tmp[:], op=ALU.add), r=[S, tmp], w=[S])
                    B.dve(lambda e: e.tensor_tensor(out=tmp[:], in0=S[:], in1=kvec(0, t), op=ALU.mult), r=[S, vec[0]], w=[tmp])
                    B.dve(lambda e, t=t: e.tensor_reduce(out=Yall[:, t, h2 * 64 + v0:h2 * 64 + v0 + VS], in_=tmp[:], axis=AX.X, op=ALU.add), r=[tmp], w=[Yall])
                B.dma("sp", wkv_out[:, h2, v0:v0 + VS, :], S[:, :, :], r=[S], w=[d["wkv_s"]])
        for s_ in range(nsq):
            B.dma("sp", YSd.t[s_ * DEC_SEQ:(s_ + 1) * DEC_SEQ, :].rearrange("t (g c) -> g t c", c=128), Yall[s_ * 8:(s_ + 1) * 8, :, :], r=[Yall], w=[YSd])
        B.sy.barrier()
```
